# Optimizing a Trainium2 kernel written in Bass

```python
import jax, jax.numpy as jnp
from jax import lax
import numpy as np

D_MODEL = 1024
BATCH = 8
SEQ = 2048
DEPTH = 2

N_META = 16
D_FF = (11 * D_MODEL) // 4
LRU_WIDTH = D_MODEL // 4
LRU_BLOCKS = 4
LRU_BLOCK = LRU_WIDTH // LRU_BLOCKS
LRU_CONV = 4
LRU_C = 8.0
SC_WIDTH = D_MODEL // 4
SC_GROUPS = 4
SC_CONV = 3
RWKV_WIDTH = D_MODEL // 2
RWKV_HEAD = 64
RWKV_HEADS = RWKV_WIDTH // RWKV_HEAD
DECAY_LORA = 32
ICL_LORA = 32
GATE_LORA = 64
MIX_WIDTH = LRU_WIDTH + SC_WIDTH + RWKV_WIDTH
RWKV_IN = 3 * RWKV_WIDTH + DECAY_LORA + ICL_LORA + GATE_LORA
N_IN = 2 * LRU_WIDTH + 3 * SC_WIDTH + RWKV_IN
IN_SPLIT_IDX = (LRU_WIDTH, 2 * LRU_WIDTH, 2 * LRU_WIDTH + SC_WIDTH,
                2 * LRU_WIDTH + 2 * SC_WIDTH, 2 * LRU_WIDTH + 3 * SC_WIDTH)
RWKV_SPLIT_IDX = (RWKV_WIDTH, 2 * RWKV_WIDTH, 3 * RWKV_WIDTH,
                  3 * RWKV_WIDTH + DECAY_LORA, 3 * RWKV_WIDTH + DECAY_LORA + ICL_LORA)
RMS_EPS = 1e-6
LNX_EPS = 64e-5

kernel_name = "hybrid_rglru_shortconv_rwkv7_macaron"


def rms_norm(x, g):
    xf = x.astype(jnp.float32)
    y = xf * lax.rsqrt(jnp.mean(xf * xf, axis=-1, keepdims=True) + RMS_EPS)
    return (y * g).astype(x.dtype)


def group_rms_norm(x, g, n_groups):
    shp = x.shape
    xf = x.astype(jnp.float32).reshape(shp[:-1] + (n_groups, shp[-1] // n_groups))
    y = xf * lax.rsqrt(jnp.mean(xf * xf, axis=-1, keepdims=True) + RMS_EPS)
    return (y.reshape(shp) * g).astype(x.dtype)


def causal_dwconv(x, w):
    k_w, t = w.shape[0], x.shape[1]
    xp = jnp.pad(x, ((0, 0), (k_w - 1, 0), (0, 0)))
    y = xp[:, 0:t] * w[0]
    for k in range(1, k_w):
        y = y + xp[:, k:k + t] * w[k]
    return y


def token_shift(x):
    return jnp.pad(x, ((0, 0), (1, 0), (0, 0)))[:, :-1]


def swiglu(x, w_in, w_out):
    gate, up = jnp.split(x @ w_in, 2, axis=-1)
    return (jax.nn.silu(gate) * up) @ w_out


def _linear_rec_combine(c1, c2):
    a1, b1 = c1
    a2, b2 = c2
    return a1 * a2, a2 * b1 + b2


def rg_lru_mixer(xb, gb, conv_w, conv_b, wa, ba, wx, bx, lam):
    bsz, t, _ = xb.shape
    u = (causal_dwconv(xb, conv_w) + conv_b).astype(jnp.float32)
    ub = u.reshape(bsz, t, LRU_BLOCKS, LRU_BLOCK)
    r = jax.nn.sigmoid(jnp.einsum('btgi,gij->btgj', ub, wa).reshape(bsz, t, LRU_WIDTH) + ba)
    i = jax.nn.sigmoid(jnp.einsum('btgi,gij->btgj', ub, wx).reshape(bsz, t, LRU_WIDTH) + bx)
    log_a = -LRU_C * r * jax.nn.softplus(-lam)
    a = jnp.exp(log_a)
    b = jnp.sqrt(-jnp.expm1(2.0 * log_a)) * (i * u)
    _, h = lax.associative_scan(_linear_rec_combine, (a, b), axis=1)
    return jax.nn.gelu(gb.astype(jnp.float32)) * h


def short_conv_mixer(sc_b, sc_c, sc_x, conv_w):
    return sc_b * causal_dwconv(sc_c * sc_x, conv_w)


def rwkv7_mixer(z, mu, w0, w2, a0, a2, g2, k_k, k_a, r_k, lnx_w, lnx_b):
    bsz, t, _ = z.shape
    z = (z + (token_shift(z) - z) * mu).astype(jnp.float32)
    r, k, v, w_lo, a_lo, g_lo = jnp.split(z, RWKV_SPLIT_IDX, axis=-1)
    w_log = -jax.nn.softplus(-(w0 + jnp.tanh(w_lo) @ w2)) - 0.5
    decay = jnp.exp(-jnp.exp(w_log))
    a = jax.nn.sigmoid(a0 + a_lo @ a2)
    g = jax.nn.sigmoid(g_lo) @ g2
    kk = k * k_k
    k = k * (1.0 + (a - 1.0) * k_a)

    def heads(q):
        return q.reshape(bsz, t, RWKV_HEADS, RWKV_HEAD)

    r, k, v, kk, a, decay = (heads(q) for q in (r, k, v, kk, a, decay))
    kk = kk * lax.rsqrt(jnp.maximum(jnp.sum(kk * kk, axis=-1, keepdims=True), 1e-24))

    def step(s, inp):
        r_t, w_t, k_t, v_t, kk_t, a_t = inp
        sa = jnp.einsum('bhvk,bhk->bhv', s, -kk_t)
        s = (s * w_t[:, :, None, :] + sa[..., None] * (kk_t * a_t)[:, :, None, :]
             + v_t[..., None] * k_t[:, :, None, :])
        return s, jnp.einsum('bhvk,bhk->bhv', s, r_t)

    seq = tuple(jnp.moveaxis(q, 1, 0) for q in (r, decay, k, v, kk, a))
    s0 = jnp.zeros((bsz, RWKV_HEADS, RWKV_HEAD, RWKV_HEAD), jnp.float32)
    _, y = lax.scan(step, s0, seq)
    y = jnp.moveaxis(y, 0, 1)
    mean = jnp.mean(y, axis=-1, keepdims=True)
    var = jnp.mean(jnp.square(y - mean), axis=-1, keepdims=True)
    y = ((y - mean) * lax.rsqrt(var + LNX_EPS)).reshape(bsz, t, RWKV_WIDTH) * lnx_w + lnx_b
    bonus = jnp.sum(r * k * r_k.reshape(RWKV_HEADS, RWKV_HEAD), axis=-1, keepdims=True) * v
    y = y + bonus.reshape(bsz, t, RWKV_WIDTH)
    return y * g


def hybrid_mixer(u, w_in, w_out, lru_conv_w, lru_conv_b, lru_wa, lru_ba, lru_wx, lru_bx,
                 lru_lambda, lru_norm_g, sc_conv_w, sc_norm_g, rwkv_mu, rwkv_w0, rwkv_w2,
                 rwkv_a0, rwkv_a2, rwkv_g2, rwkv_k_k, rwkv_k_a, rwkv_r_k, rwkv_lnx_w, rwkv_lnx_b):
    p = u @ w_in
    lru_x, lru_g, sc_b, sc_c, sc_x, rw = jnp.split(p, IN_SPLIT_IDX, axis=-1)
    y_lru = group_rms_norm(
        rg_lru_mixer(lru_x, lru_g, lru_conv_w, lru_conv_b, lru_wa, lru_ba, lru_wx, lru_bx, lru_lambda),
        lru_norm_g, LRU_BLOCKS)
    y_sc = group_rms_norm(short_conv_mixer(sc_b, sc_c, sc_x, sc_conv_w), sc_norm_g, SC_GROUPS)
    y_rw = rwkv7_mixer(rw, rwkv_mu, rwkv_w0, rwkv_w2, rwkv_a0, rwkv_a2, rwkv_g2,
                       rwkv_k_k, rwkv_k_a, rwkv_r_k, rwkv_lnx_w, rwkv_lnx_b)
    y = jnp.concatenate([y_lru.astype(u.dtype), y_sc.astype(u.dtype), y_rw.astype(u.dtype)], axis=-1)
    return y @ w_out


def setup_inputs(seed: int = 0) -> dict:
    key = jax.random.key(seed)
    ks = jax.random.split(key, 32)
    L, D = DEPTH, D_MODEL
    nrm = jax.random.normal
    u01 = jax.random.uniform(ks[14], (L, LRU_WIDTH), minval=0.9, maxval=0.999)
    a_base = u01 ** (1.0 / LRU_C)
    return {
        'x': nrm(ks[0], (BATCH, SEQ, D), jnp.float32),
        'meta_tokens': nrm(ks[1], (N_META, D), jnp.float32),
        'norm_g': 1.0 + 0.02 * nrm(ks[2], (L, 6, D), jnp.float32),
        'ffn1_w_in': nrm(ks[3], (L, D, 2 * D_FF), jnp.float32) * D ** -0.5,
        'ffn1_w_out': nrm(ks[4], (L, D_FF, D), jnp.float32) * D_FF ** -0.5,
        'ffn2_w_in': nrm(ks[5], (L, D, 2 * D_FF), jnp.float32) * D ** -0.5,
        'ffn2_w_out': nrm(ks[6], (L, D_FF, D), jnp.float32) * D_FF ** -0.5,
        'mix_w_in': nrm(ks[7], (L, D, N_IN), jnp.float32) * D ** -0.5,
        'mix_w_out': nrm(ks[8], (L, MIX_WIDTH, D), jnp.float32) * MIX_WIDTH ** -0.5,
        'lru_conv_w': nrm(ks[9], (L, LRU_CONV, LRU_WIDTH), jnp.float32) * LRU_CONV ** -0.5,
        'lru_conv_b': 0.02 * nrm(ks[10], (L, LRU_WIDTH), jnp.float32),
        'lru_wa': nrm(ks[11], (L, LRU_BLOCKS, LRU_BLOCK, LRU_BLOCK), jnp.float32) * LRU_BLOCK ** -0.5,
        'lru_ba': 0.1 * nrm(ks[12], (L, LRU_WIDTH), jnp.float32),
        'lru_wx': nrm(ks[13], (L, LRU_BLOCKS, LRU_BLOCK, LRU_BLOCK), jnp.float32) * LRU_BLOCK ** -0.5,
        'lru_bx': 0.1 * nrm(ks[15], (L, LRU_WIDTH), jnp.float32),
        'lru_lambda': jnp.log(a_base) - jnp.log1p(-a_base),
        'lru_norm_g': 1.0 + 0.02 * nrm(ks[16], (L, LRU_WIDTH), jnp.float32),
        'sc_conv_w': nrm(ks[17], (L, SC_CONV, SC_WIDTH), jnp.float32) * SC_CONV ** -0.5,
        'sc_norm_g': 1.0 + 0.02 * nrm(ks[18], (L, SC_WIDTH), jnp.float32),
        'rwkv_mu': jax.random.uniform(ks[19], (L, RWKV_IN), jnp.float32),
        'rwkv_w0': jnp.linspace(-6.0, -1.0, RWKV_WIDTH, dtype=jnp.float32)[None, :]
                   + 0.1 * nrm(ks[20], (L, RWKV_WIDTH), jnp.float32),
        'rwkv_w2': nrm(ks[21], (L, DECAY_LORA, RWKV_WIDTH), jnp.float32) * DECAY_LORA ** -0.5,
        'rwkv_a0': 0.1 * nrm(ks[22], (L, RWKV_WIDTH), jnp.float32),
        'rwkv_a2': nrm(ks[23], (L, ICL_LORA, RWKV_WIDTH), jnp.float32) * ICL_LORA ** -0.5,
        'rwkv_g2': nrm(ks[24], (L, GATE_LORA, RWKV_WIDTH), jnp.float32) * GATE_LORA ** -0.5,
        'rwkv_k_k': 0.85 + 0.02 * nrm(ks[25], (L, RWKV_WIDTH), jnp.float32),
        'rwkv_k_a': 1.0 + 0.02 * nrm(ks[26], (L, RWKV_WIDTH), jnp.float32),
        'rwkv_r_k': 0.1 * nrm(ks[27], (L, RWKV_WIDTH), jnp.float32),
        'rwkv_lnx_w': 1.0 + 0.02 * nrm(ks[28], (L, RWKV_WIDTH), jnp.float32),
        'rwkv_lnx_b': 0.02 * nrm(ks[29], (L, RWKV_WIDTH), jnp.float32),
    }


def reference(x, meta_tokens, norm_g, ffn1_w_in, ffn1_w_out, ffn2_w_in, ffn2_w_out,
              mix_w_in, mix_w_out, lru_conv_w, lru_conv_b, lru_wa, lru_ba, lru_wx, lru_bx,
              lru_lambda, lru_norm_g, sc_conv_w, sc_norm_g, rwkv_mu, rwkv_w0, rwkv_w2,
              rwkv_a0, rwkv_a2, rwkv_g2, rwkv_k_k, rwkv_k_a, rwkv_r_k, rwkv_lnx_w, rwkv_lnx_b):
    bsz = x.shape[0]
    meta = jnp.broadcast_to(meta_tokens.astype(x.dtype)[None], (bsz, N_META, x.shape[-1]))
    h = jnp.concatenate([meta, x], axis=1)
    for l in range(DEPTH):
        g = norm_g[l]
        h = h + 0.5 * rms_norm(swiglu(rms_norm(h, g[0]), ffn1_w_in[l], ffn1_w_out[l]), g[1])
        m = hybrid_mixer(rms_norm(h, g[2]), mix_w_in[l], mix_w_out[l],
                         lru_conv_w[l], lru_conv_b[l], lru_wa[l], lru_ba[l], lru_wx[l], lru_bx[l],
                         lru_lambda[l], lru_norm_g[l], sc_conv_w[l], sc_norm_g[l],
                         rwkv_mu[l], rwkv_w0[l], rwkv_w2[l], rwkv_a0[l], rwkv_a2[l], rwkv_g2[l],
                         rwkv_k_k[l], rwkv_k_a[l], rwkv_r_k[l], rwkv_lnx_w[l], rwkv_lnx_b[l])
        h = h + rms_norm(m, g[3])
        h = h + 0.5 * rms_norm(swiglu(rms_norm(h, g[4]), ffn2_w_in[l], ffn2_w_out[l]), g[5])
    return h[:, N_META:]
```

```python
import numpy as np
import concourse.bass as bass
import concourse.mybir as mybir
from concourse.bass_utils import run_bass_kernel_spmd

F32, BF16 = mybir.dt.float32, mybir.dt.bfloat16
AF = mybir.ActivationFunctionType
ALU = mybir.AluOpType

NCORES = 8
T = 2064
NMETA = 16
NB = 344
NBLK = 6
D = 1024
KC = 8
DFF = 2816
JT = 22
NL = 2
SB = 2
MIXT = 23
RMS_EPS = 1e-6
LNX_EPS = 64e-5
NPV = 67
CH = 43
NCHB = 8
C2 = 2 * CH
CO_ID, CO_M1, CO_M3, CO_M2, CO_01 = 0, 128, 300, 472, 558
NCONST = 558 + 344
SLOTW = 348


class Op:
    __slots__ = ("eng", "fn", "deps", "is_dma", "sem", "done", "signal", "rank", "tag", "rank_pos")


class _Rec:
    def __init__(self):
        self.call = None

    def __getattr__(self, name):
        def f(*args, **kwargs):
            self.call = (name, args, kwargs)
            return self
        return f


class Prog:
    CE = ("pe", "act", "dve", "pool")

    def __init__(self, nc):
        self.nc = nc
        self.streams = {e: [] for e in ("pe", "act", "dve", "pool", "sp")}
        self.lastw = {}
        self.readers = {}
        self.dma_cnt = {}
        self.barrier_ops = []
        self.final_dma = []

    def add(self, eng, fn, reads=(), writes=(), dma_sem=None, tag=None, nodeps=False):
        op = Op()
        rec = _Rec()
        fn(rec)
        assert rec.call is not None
        op.eng, op.fn, op.is_dma, op.sem = eng, rec.call, dma_sem is not None, dma_sem
        op.signal, op.rank, op.done, op.tag = False, None, None, tag
        if op.is_dma:
            self.dma_cnt[dma_sem] = self.dma_cnt.get(dma_sem, 0) + 16
            op.done = self.dma_cnt[dma_sem]
        deps = {}

        def want(d, raw):
            if d is op:
                return
            if d.is_dma:
                if op.is_dma and d.sem == op.sem and d.sem in ("msm",):
                    return
                key = ("dma", d.sem)
                if key not in deps or deps[key].done < d.done:
                    deps[key] = d
                return
            if (not op.is_dma) and d.eng == eng:
                if eng == "pe" or (not raw and eng != "pool"):
                    return
            key = ("ce", d.eng)
            if key not in deps or deps[key].rank_pos < d.rank_pos:
                deps[key] = d

        for k in reads:
            w = self.lastw.get(k)
            if w is not None:
                want(w, True)
            if isinstance(k, tuple) and k[0] == "bank":
                rd = self.readers.get(k)
                if rd:
                    for r in rd.values():
                        if r.eng != eng:
                            want(r, True)
        for k in writes:
            w = self.lastw.get(k)
            if w is not None:
                want(w, True)
            rd = self.readers.get(k)
            if rd:
                for r in rd.values():
                    want(r, False)
        for b in self.barrier_ops:
            want(b, True)
        if nodeps:
            deps = {}
        op.deps = list(deps.values())
        op.rank_pos = len(self.streams[eng])
        for k in reads:
            rd = self.readers.setdefault(k, {})
            rd[("dma", id(op)) if op.is_dma else eng] = op
        for k in writes:
            self.lastw[k] = op
            self.readers[k] = {}
        self.streams[eng].append(op)
        return op

    def barrier(self):
        ops = []
        for e in self.CE:
            for op in reversed(self.streams[e]):
                if not op.is_dma:
                    ops.append(op)
                    break
        last_dma = {}
        for e in self.streams:
            for op in self.streams[e]:
                if op.is_dma:
                    if op.sem not in last_dma or last_dma[op.sem].done < op.done:
                        last_dma[op.sem] = op
        ops.extend(last_dma.values())
        self.barrier_ops = ops

    def emit(self, block, eng_sems, dma_sems):
        for st in self.streams.values():
            for op in st:
                for d in op.deps:
                    if not d.is_dma:
                        d.signal = True
        for e in self.CE:
            r = 0
            for op in self.streams[e]:
                if op.signal and not op.is_dma:
                    r += 1
                op.rank = r

        def run(ename, eng):
            waited = {}
            for op in self.streams[ename]:
                for d in op.deps:
                    if d.is_dma:
                        sname, val, sem = ("dma", d.sem), d.done, dma_sems[d.sem]
                    else:
                        sname, val, sem = ("ce", d.eng), d.rank, eng_sems[d.eng]
                    if waited.get(sname, 0) >= val:
                        continue
                    eng.wait_ge(sem, val)
                    waited[sname] = val
                name, args, kwargs = op.fn
                ins = getattr(eng, name)(*args, **kwargs)
                if op.is_dma:
                    ins.then_inc(dma_sems[op.sem], 16)
                elif op.signal:
                    ins.then_inc(eng_sems[ename], 1)
            if ename == "sp":
                for s, cnt in self.final_dma:
                    eng.wait_ge(dma_sems[s], cnt)

        @block.sync
        def _(e):
            run("sp", e)

        @block.tensor
        def _(e):
            run("pe", e)

        @block.scalar
        def _(e):
            run("act", e)

        @block.vector
        def _(e):
            run("dve", e)

        @block.gpsimd
        def _(e):
            run("pool", e)


class Arena:
    def __init__(self, nc, nbytes):
        self.t = nc.alloc_sbuf_tensor("arena", [128, nbytes // 4], F32)
        self.ap = self.t.ap()
        self.off = 0
        self.cap = nbytes
        self.peak = 0

    def alloc(self, shape, dtype):
        n = int(np.prod(shape))
        esz = 4 if dtype == F32 else 2
        nbytes = (n * esz + 63) // 64 * 64
        assert self.off + nbytes <= self.cap, ("arena overflow", self.off, nbytes, self.cap)
        a = self.ap[:, self.off // 4:(self.off + nbytes) // 4]
        if dtype != F32:
            a = a.bitcast(dtype)
        a = a[:, 0:n]
        if len(shape) == 2:
            a = a.rearrange("p (a b) -> p a b", a=shape[0])
        elif len(shape) == 3:
            a = a.rearrange("p (a b c) -> p a b c", a=shape[0], b=shape[1])
        self.off += nbytes
        self.peak = max(self.peak, self.off)
        return a

    def mark(self):
        return self.off

    def release(self, m):
        self.off = m


def build_program(stages=None, dbg_h=False):
    if stages is None:
        stages = [(l, s) for l in range(NL) for s in range(3)]
    nc = bass.Bass("TRN2", target_bir_lowering=False)
    dt = {}

    def din(name, shape):
        dt[name] = nc.dram_tensor(name, list(shape), F32, kind="ExternalInput").ap()
        return dt[name]

    xT = din("xT", [KC, 128, T - NMETA])
    metaT = din("metaT", [KC, 128, NMETA])
    gvec = din("gvec", [128, NL * 6 * KC])
    ffn_win = [din(f"ffn{i}_win", [NL, JT, 128, KC, 256]) for i in (1, 2)]
    ffn_wout = [din(f"ffn{i}_wout", [NL, KC, 128, JT, 128]) for i in (1, 2)]
    mix_win = din("mix_win", [NL, MIXT, 128, KC, 128])
    mix_wout = din("mix_wout", [NL, KC, 128, KC, 128])
    pv_d = din("pv", [128, NL * NPV])
    lora_d = din("lora_w", [NL, 128, 512])
    lruw_d = din("lru_w", [NL, 2, 4, 64, 64])
    consts_d = din("consts", [128, NCONST])
    if dbg_h:
        outT = nc.dram_tensor("outT", [KC, 128, T], F32, kind="ExternalOutput").ap()
    else:
        outT = nc.dram_tensor("outT", [KC, 128, T - NMETA], F32, kind="ExternalOutput").ap()

    P = Prog(nc)
    A = Arena(nc, 204 * 1024)
    banks = [nc.alloc_psum_tensor(f"bank{i}", [128, 512], F32).ap() for i in range(8)]

    h = A.alloc([KC, T], F32)
    gsb = A.alloc([NL * 6 * KC], F32)
    ghalf = A.alloc([NL * 6 * KC], F32)
    ones_bf = A.alloc([128], BF16)

    def g_ap(l, i, c, half=False):
        idx = (l * 6 + i) * KC + c
        return (ghalf if half else gsb)[:, idx:idx + 1]

    def hk(c, b):
        return ("h", c, b)

    grp = []
    for c in range(KC):
        grp.append(P.add("sp", lambda e, c=c: e.dma_start(out=h[:, c, NMETA:], in_=xT[c]),
                         writes=[hk(c, b) for b in range(NBLK)], dma_sem="ld", nodeps=True))
        grp.append(P.add("sp", lambda e, c=c: e.dma_start(out=h[:, c, 0:NMETA], in_=metaT[c]),
                         writes=[hk(c, 0)], dma_sem="ld", nodeps=True))
    grp.append(P.add("sp", lambda e: e.dma_start(out=gsb, in_=gvec), writes=["gsb"], dma_sem="ld", nodeps=True))
    P.add("pool", lambda e: e.memset(ones_bf, 1.0), writes=["ones"])
    P.add("dve", lambda e: e.tensor_scalar(out=ghalf, in0=gsb, scalar1=0.5, scalar2=None, op0=ALU.mult),
          reads=["gsb"], writes=["ghalf"])

    def rstd_from_sumsq(ps_ap, rstd_ap, n, pskey, rkey, tmpkey=None):
        P.add("act", lambda e: e.activation(out=rstd_ap, in_=ps_ap, func=AF.Ln, bias=RMS_EPS, scale=1.0 / n),
              reads=[pskey], writes=[rkey])
        P.add("act", lambda e: e.activation(out=rstd_ap, in_=rstd_ap, func=AF.Exp, scale=-0.5), reads=[rkey], writes=[rkey])

    def ffn_stage(l, which):
        gi_pre, gi_post = (0, 1) if which == 0 else (4, 5)
        win_d, wout_d = ffn_win[which], ffn_wout[which]
        P.barrier()
        m0 = A.mark()
        NT = SB * NB
        xn2 = A.alloc([2, KC, NT], BF16)
        hid = A.alloc([JT, NT], BF16)
        o = A.alloc([KC, NT], F32)
        sq = A.alloc([2, KC, NB], BF16)
        rstd = A.alloc([2, NB], F32)
        sg = A.alloc([2, NB], F32)
        tmp = A.alloc([2, NB], F32)
        NWI, NWO = 3, 2
        wi = A.alloc([NWI, KC, 256], BF16)
        wo = A.alloc([NWO, JT, 128], BF16)
        psA = [banks[0], banks[1]]
        psB = [banks[2], banks[3]]
        psO = [banks[4], banks[5]]
        psN = [banks[6], banks[7]]
        uid = ("f", l, which)
        wn = {"i": 0, "o": 0}
        cnt = {"n": 0, "a": 0, "o": 0}
        def blocks_of(sbi):
            return list(range(sbi * SB, (sbi + 1) * SB))

        def pre(sbi):
            blocks = blocks_of(sbi)
            par = sbi % 2
            xn = xn2[:, par]
            for bi, b in enumerate(blocks):
                n = cnt["n"]; cnt["n"] += 1
                s = n % 2
                tsl = slice(b * NB, (b + 1) * NB)
                P.add("act", lambda e, s=s, tsl=tsl: e.activation(out=sq[:, s], in_=h[:, :, tsl], func=AF.Square),
                      reads=[hk(c, b) for c in range(KC)], writes=[(uid, "sq", s)])
                for c in range(KC):
                    P.add("pe", lambda e, s=s, c=c: e.matmul(psN[s][:, 0:NB], lhsT=ones_bf, rhs=sq[:, s, c, :],
                                                            start=(c == 0), stop=(c == KC - 1)),
                          reads=[(uid, "sq", s), "ones"], writes=[(uid, "psN", s)])
                rstd_from_sumsq(psN[s][:, 0:NB], rstd[:, s], D, (uid, "psN", s), (uid, "rstd", s))
                for c in range(KC):
                    P.add("dve", lambda e, s=s, c=c, tsl=tsl, bi=bi: e.scalar_tensor_tensor(
                        out=xn[:, c, bi * NB:(bi + 1) * NB], in0=h[:, c, tsl], scalar=g_ap(l, gi_pre, c),
                        in1=rstd[:, s], op0=ALU.mult, op1=ALU.mult),
                        reads=[hk(c, b), (uid, "rstd", s), "gsb"], writes=[(uid, "xn", par, c, bi)])

        def inp(sbi):
            blocks = blocks_of(sbi)
            par = sbi % 2
            xn = xn2[:, par]
            for j in range(JT):
                ws = wn["i"] % NWI; wn["i"] += 1
                P.add("pool", lambda e, ws=ws, j=j: e.dma_start(out=wi[:, ws], in_=win_d[l, j], max_dma_last_dim=4096),
                      writes=[(uid, "wi", ws)], dma_sem=f"wi{ws}")
                for bi, b in enumerate(blocks):
                    n = cnt["a"]; cnt["a"] += 1
                    s = n % 2
                    bsl = slice(bi * NB, (bi + 1) * NB)
                    for half, ps in ((0, psA), (1, psB)):
                        for k in range(KC):
                            P.add("pe", lambda e, ps=ps, s=s, k=k, ws=ws, half=half, bsl=bsl: e.matmul(
                                ps[s][:, 0:NB], lhsT=wi[:, ws, k, half * 128:(half + 1) * 128], rhs=xn[:, k, bsl],
                                start=(k == 0), stop=(k == KC - 1)),
                                reads=[(uid, "wi", ws), (uid, "xn", par, k, bi)], writes=[(uid, "ps", half, s)])
                    P.add("act", lambda e, s=s: e.activation(out=sg[:, s], in_=psA[s][:, 0:NB], func=AF.Silu),
                          reads=[(uid, "ps", 0, s)], writes=[(uid, "sg", s)])
                    P.add("dve", lambda e, s=s, j=j, bsl=bsl: e.tensor_tensor(
                        out=hid[:, j, bsl], in0=sg[:, s], in1=psB[s][:, 0:NB], op=ALU.mult),
                        reads=[(uid, "sg", s), (uid, "ps", 1, s)], writes=[(uid, "hid", j, bi)])

        def outp(sbi):
            blocks = blocks_of(sbi)
            for m in range(KC):
                ws = wn["o"] % NWO; wn["o"] += 1
                P.add("pool", lambda e, ws=ws, m=m: e.dma_start(out=wo[:, ws], in_=wout_d[l, m], max_dma_last_dim=4096),
                      writes=[(uid, "wo", ws)], dma_sem=f"wo{ws}")
                for bi, b in enumerate(blocks):
                    n = cnt["o"]; cnt["o"] += 1
                    s = n % 2
                    bsl = slice(bi * NB, (bi + 1) * NB)
                    for k in range(JT):
                        P.add("pe", lambda e, s=s, k=k, ws=ws, bsl=bsl: e.matmul(
                            psO[s][:, 0:NB], lhsT=wo[:, ws, k, :], rhs=hid[:, k, bsl],
                            start=(k == 0), stop=(k == JT - 1)),
                            reads=[(uid, "wo", ws), (uid, "hid", k, bi)], writes=[(uid, "psO", s)])
                    P.add("act", lambda e, s=s, m=m, bsl=bsl: e.activation(out=o[:, m, bsl], in_=psO[s][:, 0:NB], func=AF.Copy),
                          reads=[(uid, "psO", s)], writes=[(uid, "o", m, bi)])

        def post(sbi):
            blocks = blocks_of(sbi)
            for bi, b in enumerate(blocks):
                n = cnt["n"]; cnt["n"] += 1
                s = n % 2
                bsl = slice(bi * NB, (bi + 1) * NB)
                tsl = slice(b * NB, (b + 1) * NB)
                P.add("act", lambda e, s=s, bsl=bsl: e.activation(out=sq[:, s], in_=o[:, :, bsl], func=AF.Square),
                      reads=[(uid, "o", m, bi) for m in range(KC)], writes=[(uid, "sq", s)])
                for c in range(KC):
                    P.add("pe", lambda e, s=s, c=c: e.matmul(psN[s][:, 0:NB], lhsT=ones_bf, rhs=sq[:, s, c, :],
                                                            start=(c == 0), stop=(c == KC - 1)),
                          reads=[(uid, "sq", s), "ones"], writes=[(uid, "psN", s)])
                rstd_from_sumsq(psN[s][:, 0:NB], rstd[:, s], D, (uid, "psN", s), (uid, "rstd", s))
                for m in range(KC):
                    ts_ = (n * KC + m) % 2
                    P.add("dve", lambda e, s=s, m=m, bsl=bsl, ts_=ts_: e.scalar_tensor_tensor(
                        out=tmp[:, ts_], in0=o[:, m, bsl], scalar=g_ap(l, gi_post, m, half=True),
                        in1=rstd[:, s], op0=ALU.mult, op1=ALU.mult),
                        reads=[(uid, "o", m, bi), (uid, "rstd", s), "ghalf"], writes=[(uid, "tmp", ts_)])
                    P.add("pool", lambda e, m=m, tsl=tsl, ts_=ts_: e.tensor_tensor(
                        out=h[:, m, tsl], in0=h[:, m, tsl], in1=tmp[:, ts_], op=ALU.add),
                        reads=[hk(m, b), (uid, "tmp", ts_)], writes=[hk(m, b)])
        NSB = NBLK // SB
        pre(0)
        inp(0)
        for sbi in range(NSB):
            if sbi + 1 < NSB:
                pre(sbi + 1)
            outp(sbi)
            if sbi + 1 < NSB:
                inp(sbi + 1)
            post(sbi)
        A.release(m0)

    pv = A.alloc([NL * NPV], F32)
    consts = A.alloc([NCONST], F32)
    ident_f = consts[:, CO_ID:CO_ID + 128]
    mask1 = consts[0:C2, CO_M1:CO_M1 + 2 * C2]
    mask3 = consts[0:C2, CO_M3:CO_M3 + 2 * C2]
    mask2 = consts[0:C2, CO_M2:CO_M2 + C2]
    mask01 = consts[:, CO_01:CO_01 + NB]
    ident_bf = A.alloc([128], BF16)
    ones_bd = A.alloc([128], BF16)
    lru_halo = A.alloc([2, 3], F32)
    sc_halo = A.alloc([2, 2], F32)
    rw_halo = A.alloc([13], F32)
    lru_state = A.alloc([2], F32)
    S_buf = A.alloc([4, 2, 128], BF16)
    clam = A.alloc([4], F32)
    grp2 = []
    grp2.append(P.add("sp", lambda e: e.dma_start(out=pv, in_=pv_d), writes=["pv"], dma_sem="ld", nodeps=True))
    grp2.append(P.add("sp", lambda e: e.dma_start(out=consts, in_=consts_d), writes=["consts"], dma_sem="ld", nodeps=True))
    for op in grp + grp2:
        op.done = P.dma_cnt["ld"]
    P.add("act", lambda e: e.activation(out=ident_bf, in_=ident_f, func=AF.Copy), reads=["consts"], writes=["identbf"])
    P.add("pool", lambda e: e.memset(ones_bd, 0.0), writes=["ones_bd"])
    P.add("pool", lambda e: e.memset(ones_bd[0:64, 0:64], 1.0), writes=["ones_bd"])
    P.add("pool", lambda e: e.memset(ones_bd[64:128, 64:128], 1.0), writes=["ones_bd"])

    npv = A.alloc([NL * NPV], F32)
    P.add("dve", lambda e: e.tensor_scalar(out=npv, in0=pv, scalar1=-1.0, scalar2=None, op0=ALU.mult), reads=["pv"], writes=["pv"])

    def pvc(l, j):
        return pv[:, l * NPV + j:l * NPV + j + 1]

    def npvc(l, j):
        return npv[:, l * NPV + j:l * NPV + j + 1]

    def act_sigmoid(dst_ap, src_ap, reads, wkey, nbias=None, scale=1.0):
        if nbias is None:
            P.add("act", lambda e: e.activation(out=dst_ap, in_=src_ap, func=AF.Exp, scale=-scale), reads=reads, writes=[wkey])
        else:
            P.add("act", lambda e: e.activation(out=dst_ap, in_=src_ap, func=AF.Exp, scale=-scale, bias=nbias), reads=reads + ["pv"], writes=[wkey])
        P.add("act", lambda e: e.activation(out=dst_ap, in_=dst_ap, func=AF.Ln, bias=1.0), reads=[wkey], writes=[wkey])
        P.add("act", lambda e: e.activation(out=dst_ap, in_=dst_ap, func=AF.Exp, scale=-1.0), reads=[wkey], writes=[wkey])

    class Slot:
        def __init__(self, i, ap):
            self.i, self.ap, self.key = i, ap, ("slot", i)
            self.bf = ap.bitcast(BF16)

    def mixer_stage(l):
        P.barrier()
        m0 = A.mark()
        NSLOT = 29
        slots_ap = A.alloc([NSLOT, SLOTW], F32)
        free_slots = list(range(NSLOT))
        cur = {"tid": None}
        live = {}
        quota = {}

        def getslot():
            assert free_slots, "slot pool exhausted"
            i = free_slots.pop(0)
            sl = Slot(i, slots_ap[:, i, :])
            sl.owner = cur["tid"]
            live[sl.owner] = live.get(sl.owner, 0) + 1
            return sl

        def putslot(*ss):
            for s_ in ss:
                free_slots.append(s_.i)
                if s_.owner is not None or None in live:
                    live[s_.owner] = live.get(s_.owner, 0) - 1

        def endphase():
            t = cur["tid"]
            quota[t] = live.get(t, 0)

        def disown(*ss):
            for s_ in ss:
                live[s_.owner] = live.get(s_.owner, 0) - 1
                s_.owner = "handoff"
                live["handoff"] = live.get("handoff", 0) + 1

        u = A.alloc([KC, NB], BF16)
        ycat = A.alloc([2, KC, NB], BF16)
        sqb = A.alloc([KC, NB], BF16)
        sqo = A.alloc([KC, NB], BF16)
        NWI, NWO = 3, 2
        wi = A.alloc([NWI, KC, 128], BF16)
        wo = A.alloc([NWO, KC, 128], BF16)
        lora_sb = A.alloc([512], BF16)
        gate_w = A.alloc([2, 2, 128], BF16)
        AR_pad = A.alloc([2, NCHB, 2 * C2], BF16)
        BK_pad = A.alloc([NCHB, 2 * C2], BF16)
        Bh_pad = A.alloc([NCHB, C2], BF16)
        Kh_pad = A.alloc([NCHB, C2], BF16)
        V_pad = A.alloc([NCHB, C2], BF16)
        NRB = A.alloc([NCHB, 300], BF16)
        NTY = A.alloc([NCHB, 300], BF16)
        RKV = A.alloc([2, NCHB, 342], BF16)
        NNb = A.alloc([2, NCHB, 2 * C2], BF16)
        PT = A.alloc([NCHB, C2], BF16)
        QT = A.alloc([2, NCHB, C2], BF16)
        Mm = A.alloc([2, NCHB, 128], BF16)
        GHb = A.alloc([2, NCHB, 214], BF16)
        Pend = A.alloc([2, NCHB], F32)
        uid = ("m", l)
        pbank = {"n": 0}

        def nextbank():
            b = pbank["n"] % 8
            pbank["n"] += 1
            return banks[b], ("bank", b)

        P.add("pool", lambda e: e.memset(AR_pad, 0.0), writes=[(uid, "AR", q_, c) for q_ in range(2) for c in range(NCHB)])
        for buf, nm in ((BK_pad, "BK"), (Bh_pad, "Bh"), (Kh_pad, "Kh"), (V_pad, "Vp")):
            P.add("pool", lambda e, buf=buf: e.memset(buf, 0.0), writes=[(uid, nm, c) for c in range(NCHB)])
        P.add("pool", lambda e: e.memset(S_buf, 0.0), writes=[("S", hp, i) for hp in range(4) for i in range(2)])
        P.add("pool", lambda e: e.memset(lru_halo, 0.0), writes=["lru_halo"])
        P.add("pool", lambda e: e.memset(sc_halo, 0.0), writes=["sc_halo"])
        P.add("pool", lambda e: e.memset(rw_halo, 0.0), writes=[("rw_halo", m_) for m_ in range(13)])
        P.add("pool", lambda e: e.memset(lru_state, 0.0), writes=["lru_state"])
        P.add("pool", lambda e: e.memset(gate_w, 0.0), writes=[(uid, "gate_w")])
        mg = [P.add("pool", lambda e: e.dma_start(out=lora_sb, in_=lora_d[l]), writes=[(uid, "lora")], dma_sem="msm")]
        for ax in range(2):
            for g in range(4):
                i, hh = g // 2, g % 2
                mg.append(P.add("pool", lambda e, ax=ax, g=g, i=i, hh=hh: e.dma_start(
                    out=gate_w[hh * 64:(hh + 1) * 64, ax, i, hh * 64:(hh + 1) * 64], in_=lruw_d[l, ax, g]),
                    reads=[], writes=[(uid, "gate_w")], dma_sem="msm"))
        for op_ in mg:
            op_.done = P.dma_cnt["msm"]
        st_ = getslot()
        P.add("act", lambda e: e.activation(out=st_.ap[:, 0:2], in_=pv[:, l * NPV + 14:l * NPV + 16], func=AF.Exp, scale=-1.0),
              reads=["pv"], writes=[st_.key])
        P.add("act", lambda e: e.activation(out=st_.ap[:, 2:4], in_=st_.ap[:, 0:2], func=AF.Ln, bias=1.0),
              reads=[st_.key], writes=[st_.key])
        P.add("dve", lambda e: e.tensor_scalar(out=clam[:, 0:2], in0=st_.ap[:, 2:4], scalar1=-8.0, scalar2=None, op0=ALU.mult),
              reads=[st_.key], writes=["clam"])
        P.add("dve", lambda e: e.tensor_scalar(out=clam[:, 2:4], in0=st_.ap[:, 2:4], scalar1=-16.0, scalar2=None, op0=ALU.mult),
              reads=[st_.key], writes=["clam"])
        putslot(st_)

        wi_n = {"n": 0}
        wo_n = {"n": 0}

        def inproj_tile(m, dst_ap, dst_key):
            ws = wi_n["n"] % NWI
            wi_n["n"] += 1
            P.add("pool", lambda e: e.dma_start(out=wi[:, ws], in_=mix_win[l, m], max_dma_last_dim=4096),
                  writes=[(uid, "wi", ws)], dma_sem=f"wi{ws}")
            bk, bkey = nextbank()
            for k in range(KC):
                P.add("pe", lambda e, k=k: e.matmul(bk[:, 0:NB], lhsT=wi[:, ws, k, :], rhs=u[:, k, :],
                                                   start=(k == 0), stop=(k == KC - 1)),
                      reads=[(uid, "wi", ws), (uid, "u")], writes=[bkey])
            P.add("act", lambda e: e.activation(out=dst_ap, in_=bk[:, 0:NB], func=AF.Copy), reads=[bkey], writes=[dst_key])

        def group_rstd(src_ap, src_key, n, eps):
            sq_ = getslot()
            P.add("act", lambda e: e.activation(out=sq_.bf[:, 0:NB], in_=src_ap, func=AF.Square),
                  reads=[src_key], writes=[sq_.key])
            bk, bkey = nextbank()
            P.add("pe", lambda e: e.matmul(bk[:, 0:NB], lhsT=ones_bd, rhs=sq_.bf[:, 0:NB], start=True, stop=True),
                  reads=[sq_.key, "ones_bd"], writes=[bkey])
            rs = getslot()
            P.add("act", lambda e: e.activation(out=rs.ap[:, 0:NB], in_=bk[:, 0:NB], func=AF.Ln, bias=eps, scale=1.0 / n),
                  reads=[bkey], writes=[rs.key])
            P.add("act", lambda e: e.activation(out=rs.ap[:, 0:NB], in_=rs.ap[:, 0:NB], func=AF.Exp, scale=-0.5), reads=[rs.key], writes=[rs.key])
            putslot(sq_)
            return rs

        class Ev:
            def __init__(self):
                self.done = False

        def wait(ev):
            while not ev.done:
                yield "wait"

        def need(n):
            tid = cur["tid"]
            while True:
                others = sum(max(0, quota.get(t, 0) - live.get(t, 0)) for t in quota if t != tid)
                if len(free_slots) >= n + others:
                    break
                yield "wait"
            quota[tid] = live.get(tid, 0) + n

        NG = NBLK * 4
        ev_prep = [Ev() for _ in range(NG)]
        ev_A = [Ev() for _ in range(NG)]
        ev_QM = [Ev() for _ in range(NG)]
        ev_state = [Ev() for _ in range(NG)]
        ev_out = [Ev() for _ in range(NBLK)]
        ev_u = [Ev() for _ in range(NBLK)]
        ev_ls = [Ev() for _ in range(NBLK)]
        handoff = {}
        gen_banks = [0, 1, 2, 3, 4]
        gb = {"n": 0}

        def nextbank():
            b = gen_banks[gb["n"] % len(gen_banks)]
            gb["n"] += 1
            return banks[b], ("bank", b)

        def NRBk(c):
            return [(uid, "NRBm", c), (uid, "NRBt", c)]

        def NTYk(c):
            return [(uid, "NTYm", c), (uid, "NTYt", c)]

        def RKVk(c, q):
            return [(uid, "RKVm", q, c), (uid, "RKVt", q, c)]

        def inproj_tile(m, dst_ap, dst_key):
            ws = wi_n["n"] % NWI
            wi_n["n"] += 1
            P.add("pool", lambda e: e.dma_start(out=wi[:, ws], in_=mix_win[l, m], max_dma_last_dim=4096),
                  writes=[(uid, "wi", ws)], dma_sem=f"wi{ws}")
            bk, bkey = nextbank()
            for k in range(KC):
                P.add("pe", lambda e, k=k: e.matmul(bk[:, 0:NB], lhsT=wi[:, ws, k, :], rhs=u[:, k, :],
                                                   start=(k == 0), stop=(k == KC - 1)),
                      reads=[(uid, "wi", ws), (uid, "u")], writes=[bkey])
            P.add("act", lambda e: e.activation(out=dst_ap, in_=bk[:, 0:NB], func=AF.Copy), reads=[bkey], writes=[dst_key])

        def group_rstd(src_ap, src_key, n, eps):
            sq_ = getslot()
            P.add("act", lambda e: e.activation(out=sq_.bf[:, 0:NB], in_=src_ap, func=AF.Square),
                  reads=[src_key], writes=[sq_.key])
            bk, bkey = nextbank()
            P.add("pe", lambda e: e.matmul(bk[:, 0:NB], lhsT=ones_bd, rhs=sq_.bf[:, 0:NB], start=True, stop=True),
                  reads=[sq_.key, "ones_bd"], writes=[bkey])
            rs = getslot()
            P.add("act", lambda e: e.activation(out=rs.ap[:, 0:NB], in_=bk[:, 0:NB], func=AF.Ln, bias=eps, scale=1.0 / n),
                  reads=[bkey], writes=[rs.key])
            P.add("act", lambda e: e.activation(out=rs.ap[:, 0:NB], in_=rs.ap[:, 0:NB], func=AF.Exp, scale=-0.5), reads=[rs.key], writes=[rs.key])
            putslot(sq_)
            return rs

        def rw_tile(m):
            zz = getslot()
            inproj_tile(10 + m, zz.ap[:, 1:1 + NB], zz.key)
            P.add("act", lambda e: e.activation(out=zz.ap[:, 0:1], in_=rw_halo[:, m:m + 1], func=AF.Copy), reads=[("rw_halo", m)], writes=[zz.key])
            P.add("act", lambda e: e.activation(out=rw_halo[:, m:m + 1], in_=zz.ap[:, NB:NB + 1], func=AF.Copy), reads=[zz.key], writes=[("rw_halo", m)])
            dd = getslot()
            P.add("dve", lambda e: e.tensor_tensor(out=dd.ap[:, 0:NB], in0=zz.ap[:, 0:NB], in1=zz.ap[:, 1:1 + NB], op=ALU.subtract),
                  reads=[zz.key], writes=[dd.key])
            P.add("dve", lambda e: e.scalar_tensor_tensor(
                out=zz.ap[:, 1:1 + NB], in0=dd.ap[:, 0:NB], scalar=pvc(l, 26 + m), in1=zz.ap[:, 1:1 + NB], op0=ALU.mult, op1=ALU.add),
                reads=[zz.key, dd.key, "pv"], writes=[zz.key])
            putslot(dd)
            return zz

        def front():
            for b in range(NBLK):
                tsl = slice(b * NB, (b + 1) * NB)
                yq = b % 2
                if b >= 1:
                    yield from wait(ev_ls[b - 1])
                yield from need(1)
                P.add("act", lambda e: e.activation(out=sqb, in_=h[:, :, tsl], func=AF.Square),
                      reads=[hk(c, b) for c in range(KC)], writes=[(uid, "sqb")])
                bk, bkey = nextbank()
                for c in range(KC):
                    P.add("pe", lambda e, c=c: e.matmul(bk[:, 0:NB], lhsT=ones_bf, rhs=sqb[:, c, :], start=(c == 0), stop=(c == KC - 1)),
                          reads=[(uid, "sqb"), "ones"], writes=[bkey])
                rs = getslot()
                rstd_from_sumsq(bk[:, 0:NB], rs.ap[:, 0:NB], D, bkey, rs.key)
                for c in range(KC):
                    P.add("dve", lambda e, c=c: e.scalar_tensor_tensor(
                        out=u[:, c, :], in0=h[:, c, tsl], scalar=g_ap(l, 2, c), in1=rs.ap[:, 0:NB], op0=ALU.mult, op1=ALU.mult),
                        reads=[hk(c, b), rs.key, "gsb"], writes=[(uid, "u")])
                putslot(rs)
                endphase()
                yield
                ev_u[b].done = True
                yield from need(5)
                zlo = rw_tile(12)
                lob = getslot()
                ltmp = getslot()
                act_sigmoid(ltmp.ap[0:32, 0:NB], zlo.ap[0:32, 1:1 + NB], [zlo.key], ltmp.key, scale=2.0)
                P.add("dve", lambda e: e.tensor_scalar(out=lob.bf[0:32, 0:NB], in0=ltmp.ap[0:32, 0:NB], scalar1=2.0, scalar2=-1.0, op0=ALU.mult, op1=ALU.add),
                      reads=[ltmp.key], writes=[lob.key])
                P.add("act", lambda e: e.activation(out=lob.bf[32:64, 0:NB], in_=zlo.ap[32:64, 1:1 + NB], func=AF.Copy), reads=[zlo.key], writes=[lob.key])
                act_sigmoid(ltmp.ap[64:128, 0:NB], zlo.ap[64:128, 1:1 + NB], [zlo.key], ltmp.key)
                P.add("act", lambda e: e.activation(out=lob.bf[64:128, 0:NB], in_=ltmp.ap[64:128, 0:NB], func=AF.Copy), reads=[ltmp.key], writes=[lob.key])
                putslot(zlo, ltmp)
                endphase()
                yield
                for hp in range(4):
                    g = b * 4 + hp
                    q = g % 2
                    yield from need(13)
                    rr = rw_tile(hp)
                    yield
                    kx = rw_tile(4 + hp)
                    yield
                    vv = rw_tile(8 + hp)
                    yield
                    r_ap, k_ap, v_ap = rr.ap[:, 1:1 + NB], kx.ap[:, 1:1 + NB], vv.ap[:, 1:1 + NB]
                    cols = slice(hp * 128, (hp + 1) * 128)
                    logw, av, gv = getslot(), getslot(), getslot()
                    for (p0, p1, dst, fn, bcol) in ((0, 32, logw, AF.Sigmoid, 39 + hp), (32, 64, av, AF.Sigmoid, 43 + hp), (64, 128, gv, AF.Copy, None)):
                        bk, bkey = nextbank()
                        P.add("pe", lambda e, p0=p0, p1=p1, bk=bk: e.matmul(bk[:, 0:NB], lhsT=lora_sb[p0:p1, cols], rhs=lob.bf[p0:p1, 0:NB],
                                                                           start=True, stop=True),
                              reads=[(uid, "lora"), lob.key], writes=[bkey])
                        if bcol is None:
                            P.add("act", lambda e, dst=dst, bk=bk: e.activation(out=dst.ap[:, 0:NB], in_=bk[:, 0:NB], func=AF.Copy),
                                  reads=[bkey], writes=[dst.key])
                        else:
                            act_sigmoid(dst.ap[:, 0:NB], bk[:, 0:NB], [bkey], dst.key, nbias=npvc(l, bcol))
                    P.add("dve", lambda e: e.tensor_scalar(out=logw.ap[:, 0:NB], in0=logw.ap[:, 0:NB], scalar1=-0.6065306597126334, scalar2=None, op0=ALU.mult),
                          reads=[logw.key], writes=[logw.key])
                    yield
                    kk = getslot()
                    P.add("dve", lambda e: e.tensor_scalar(out=kk.ap[:, 0:NB], in0=k_ap, scalar1=pvc(l, 47 + hp), scalar2=None, op0=ALU.mult),
                          reads=[kx.key, "pv"], writes=[kk.key])
                    sq_ = getslot()
                    P.add("act", lambda e: e.activation(out=sq_.bf[:, 0:NB], in_=kk.ap[:, 0:NB], func=AF.Square),
                          reads=[kk.key], writes=[sq_.key])
                    bk, bkey = nextbank()
                    P.add("pe", lambda e, bk=bk: e.matmul(bk[:, 0:NB], lhsT=ones_bd, rhs=sq_.bf[:, 0:NB], start=True, stop=True),
                          reads=[sq_.key, "ones_bd"], writes=[bkey])
                    rn = getslot()
                    P.add("dve", lambda e, bk=bk: e.tensor_scalar(out=rn.ap[:, 0:NB], in0=bk[:, 0:NB], scalar1=1e-24, scalar2=None, op0=ALU.max),
                          reads=[bkey], writes=[rn.key])
                    P.add("act", lambda e: e.activation(out=rn.ap[:, 0:NB], in_=rn.ap[:, 0:NB], func=AF.Ln), reads=[rn.key], writes=[rn.key])
                    P.add("act", lambda e: e.activation(out=rn.ap[:, 0:NB], in_=rn.ap[:, 0:NB], func=AF.Exp, scale=-0.5), reads=[rn.key], writes=[rn.key])
                    P.add("dve", lambda e: e.tensor_tensor(out=kk.ap[:, 0:NB], in0=kk.ap[:, 0:NB], in1=rn.ap[:, 0:NB], op=ALU.mult),
                          reads=[rn.key, kk.key], writes=[kk.key])
                    putslot(sq_, rn)
                    yield
                    kp = getslot()
                    P.add("dve", lambda e: e.tensor_scalar(out=kp.ap[:, 0:NB], in0=av.ap[:, 0:NB], scalar1=-1.0, scalar2=pvc(l, 51 + hp),
                                                           op0=ALU.add, op1=ALU.mult), reads=[av.key, "pv"], writes=[kp.key])
                    P.add("dve", lambda e: e.scalar_tensor_tensor(out=kp.ap[:, 0:NB], in0=kp.ap[:, 0:NB], scalar=1.0, in1=k_ap,
                                                                  op0=ALU.add, op1=ALU.mult), reads=[kp.key, kx.key], writes=[kp.key])
                    bvec = getslot()
                    P.add("dve", lambda e: e.tensor_tensor(out=bvec.ap[:, 0:NB], in0=kk.ap[:, 0:NB], in1=av.ap[:, 0:NB], op=ALU.mult),
                          reads=[kk.key, av.key], writes=[bvec.key])
                    putslot(av)
                    rkb = getslot()
                    P.add("dve", lambda e: e.scalar_tensor_tensor(
                        out=rkb.bf[:, 0:NB], in0=r_ap, scalar=pvc(l, 55 + hp), in1=kp.ap[:, 0:NB], op0=ALU.mult, op1=ALU.mult),
                        reads=[rr.key, kp.key, "pv"], writes=[rkb.key])
                    bk, bkey = nextbank()
                    P.add("pe", lambda e, bk=bk: e.matmul(bk[:, 0:NB], lhsT=ones_bd, rhs=rkb.bf[:, 0:NB], start=True, stop=True),
                          reads=[rkb.key, "ones_bd"], writes=[bkey])
                    bonus = getslot()
                    P.add("dve", lambda e, bk=bk: e.tensor_tensor(out=bonus.ap[:, 0:NB], in0=bk[:, 0:NB], in1=v_ap, op=ALU.mult),
                          reads=[bkey, vv.key], writes=[bonus.key])
                    putslot(rkb)
                    yield
                    Ls = getslot()
                    P.add("dve", lambda e: e.tensor_tensor_scan(out=Ls.ap[:, 0:NB], data0=mask01, data1=logw.ap[:, 0:NB], initial=0.0,
                                                                op0=ALU.mult, op1=ALU.add), reads=[logw.key, "consts"], writes=[Ls.key])
                    L3 = Ls.ap[:, 0:NB].rearrange("p (c t) -> p c t", t=CH)
                    Lend = L3[:, :, CH - 1:CH]
                    if g >= 1:
                        yield from wait(ev_A[g - 1])
                    if g >= 2:
                        yield from wait(ev_QM[g - 2])

                    def padded(dst, col0, in0_ap, in1_ap, keys_in, wkeys, neg=False, eng="dve"):
                        for hh in range(2):
                            ps_ = slice(hh * 64, (hh + 1) * 64)
                            o_ap = dst[ps_, :, col0 + hh * CH:col0 + (hh + 1) * CH]
                            a0 = in0_ap[ps_].rearrange("p (c t) -> p c t", t=CH)
                            if in1_ap is None:
                                P.add(eng, lambda e: e.activation(out=o_ap, in_=a0, func=AF.Copy), reads=keys_in, writes=wkeys)
                            else:
                                a1 = in1_ap[ps_].rearrange("p (c t) -> p c t", t=CH)
                                if neg:
                                    P.add(eng, lambda e: e.scalar_tensor_tensor(out=o_ap, in0=a0, scalar=-1.0, in1=a1, op0=ALU.mult, op1=ALU.mult),
                                          reads=keys_in, writes=wkeys)
                                else:
                                    P.add(eng, lambda e: e.tensor_tensor(out=o_ap, in0=a0, in1=a1, op=ALU.mult), reads=keys_in, writes=wkeys)

                    ARq = AR_pad[:, q]
                    ARkeys = [(uid, "AR", q, c) for c in range(NCHB)]
                    E = getslot()
                    P.add("act", lambda e: e.activation(out=E.ap[:, 0:NB], in_=Ls.ap[:, 0:NB], func=AF.Exp), reads=[Ls.key], writes=[E.key])
                    padded(ARq, C2, r_ap, E.ap[:, 0:NB], [rr.key, E.key], ARkeys)
                    P.add("act", lambda e: e.activation(out=Pend[:, q, :], in_=L3[:, :, CH - 1], func=AF.Exp), reads=[Ls.key], writes=[(uid, "Pend", q)])
                    yield
                    E2 = getslot()
                    P.add("act", lambda e: e.activation(out=E2.ap[:, 0:NB], in_=Ls.ap[:, 0:NB], func=AF.Exp, scale=-1.0), reads=[Ls.key], writes=[E2.key])
                    padded(BK_pad, 0, bvec.ap[:, 0:NB], E2.ap[:, 0:NB], [bvec.key, E2.key], [(uid, "BK", c) for c in range(NCHB)])
                    padded(BK_pad, C2, kp.ap[:, 0:NB], E2.ap[:, 0:NB], [kp.key, E2.key], [(uid, "BK", c) for c in range(NCHB)], eng="dve")
                    yield
                    P.add("dve", lambda e: e.tensor_tensor(out=E.ap[:, 0:NB], in0=Ls.ap[:, 0:NB], in1=logw.ap[:, 0:NB], op=ALU.subtract),
                          reads=[Ls.key, logw.key], writes=[E.key])
                    P.add("act", lambda e: e.activation(out=E.ap[:, 0:NB], in_=E.ap[:, 0:NB], func=AF.Exp), reads=[E.key], writes=[E.key])
                    padded(ARq, 0, kk.ap[:, 0:NB], E.ap[:, 0:NB], [kk.key, E.key], ARkeys, neg=True)
                    yield
                    E3 = E2.ap[:, 0:NB].rearrange("p (c t) -> p c t", t=CH)
                    P.add("dve", lambda e: e.tensor_tensor(out=E3, in0=Lend.to_broadcast([128, NCHB, CH]), in1=L3, op=ALU.subtract),
                          reads=[Ls.key], writes=[E2.key])
                    P.add("act", lambda e: e.activation(out=E2.ap[:, 0:NB], in_=E2.ap[:, 0:NB], func=AF.Exp), reads=[E2.key], writes=[E2.key])
                    padded(Bh_pad, 0, bvec.ap[:, 0:NB], E2.ap[:, 0:NB], [bvec.key, E2.key], [(uid, "Bh", c) for c in range(NCHB)])
                    padded(Kh_pad, 0, kp.ap[:, 0:NB], E2.ap[:, 0:NB], [kp.key, E2.key], [(uid, "Kh", c) for c in range(NCHB)], eng="dve")
                    padded(V_pad, 0, v_ap, None, [vv.key], [(uid, "Vp", c) for c in range(NCHB)], eng="act")
                    putslot(E, E2, Ls, logw, kk, kp, bvec, rr, kx, vv)
                    disown(gv, bonus)
                    handoff[g] = (gv, bonus)
                    endphase()
                    ev_prep[g].done = True
                    yield
                putslot(lob)

        def ls_thread():
            for b in range(NBLK):
                tsl = slice(b * NB, (b + 1) * NB)
                yq = b % 2
                yield from wait(ev_u[b])
                if b >= 2:
                    yield from wait(ev_out[b - 2])
                for i in range(2):
                    yield from need(7)
                    xs = getslot()
                    gx = getslot()
                    inproj_tile(i, xs.ap[:, 3:3 + NB], xs.key)
                    inproj_tile(2 + i, gx.ap[:, 0:NB], gx.key)
                    yield
                    P.add("act", lambda e: e.activation(out=xs.ap[:, 0:3], in_=lru_halo[:, i, :], func=AF.Copy), reads=["lru_halo"], writes=[xs.key])
                    P.add("act", lambda e: e.activation(out=lru_halo[:, i, :], in_=xs.ap[:, NB:NB + 3], func=AF.Copy), reads=[xs.key], writes=["lru_halo"])
                    uc = getslot()
                    P.add("dve", lambda e: e.tensor_scalar(out=uc.ap[:, 0:NB], in0=xs.ap[:, 3:3 + NB], scalar1=pvc(l, i * 4 + 3), scalar2=pvc(l, 8 + i),
                                                           op0=ALU.mult, op1=ALU.add), reads=[xs.key, "pv"], writes=[uc.key])
                    for k in (2, 1, 0):
                        P.add("dve", lambda e, k=k: e.scalar_tensor_tensor(
                            out=uc.ap[:, 0:NB], in0=xs.ap[:, k:k + NB], scalar=pvc(l, i * 4 + k), in1=uc.ap[:, 0:NB],
                            op0=ALU.mult, op1=ALU.add), reads=[xs.key, uc.key, "pv"], writes=[uc.key])
                    putslot(xs)
                    yield
                    ucb = getslot()
                    P.add("act", lambda e: e.activation(out=ucb.bf[:, 0:NB], in_=uc.ap[:, 0:NB], func=AF.Copy), reads=[uc.key], writes=[ucb.key])
                    rg = getslot()
                    ig = getslot()
                    for ax, dst, bcol in ((0, rg, 10 + i), (1, ig, 12 + i)):
                        bk, bkey = nextbank()
                        P.add("pe", lambda e, ax=ax, bk=bk: e.matmul(bk[:, 0:NB], lhsT=gate_w[:, ax, i, :], rhs=ucb.bf[:, 0:NB], start=True, stop=True),
                              reads=[(uid, "gate_w"), ucb.key], writes=[bkey])
                        act_sigmoid(dst.ap[:, 0:NB], bk[:, 0:NB], [bkey], dst.key, nbias=npvc(l, bcol))
                    putslot(ucb)
                    yield
                    av = getslot()
                    a2 = getslot()
                    P.add("act", lambda e: e.activation(out=av.ap[:, 0:NB], in_=rg.ap[:, 0:NB], func=AF.Exp, scale=clam[:, i:i + 1]),
                          reads=[rg.key, "clam"], writes=[av.key])
                    P.add("act", lambda e: e.activation(out=a2.ap[:, 0:NB], in_=rg.ap[:, 0:NB], func=AF.Exp, scale=clam[:, 2 + i:3 + i]),
                          reads=[rg.key, "clam"], writes=[a2.key])
                    P.add("act", lambda e: e.activation(out=a2.ap[:, 0:NB], in_=a2.ap[:, 0:NB], func=AF.Ln, bias=1.0, scale=-1.0),
                          reads=[a2.key], writes=[a2.key])
                    P.add("act", lambda e: e.activation(out=a2.ap[:, 0:NB], in_=a2.ap[:, 0:NB], func=AF.Exp, scale=0.5),
                          reads=[a2.key], writes=[a2.key])
                    putslot(rg)
                    P.add("dve", lambda e: e.tensor_tensor(out=ig.ap[:, 0:NB], in0=ig.ap[:, 0:NB], in1=uc.ap[:, 0:NB], op=ALU.mult),
                          reads=[ig.key, uc.key], writes=[ig.key])
                    P.add("dve", lambda e: e.tensor_tensor(out=ig.ap[:, 0:NB], in0=ig.ap[:, 0:NB], in1=a2.ap[:, 0:NB], op=ALU.mult),
                          reads=[ig.key, a2.key], writes=[ig.key])
                    putslot(uc, a2)
                    hs = getslot()
                    P.add("dve", lambda e: e.tensor_tensor_scan(
                        out=hs.ap[:, 0:NB], data0=av.ap[:, 0:NB], data1=ig.ap[:, 0:NB], initial=lru_state[:, i:i + 1],
                        op0=ALU.mult, op1=ALU.add), reads=[av.key, ig.key, "lru_state"], writes=[hs.key])
                    P.add("act", lambda e: e.activation(out=lru_state[:, i:i + 1], in_=hs.ap[:, NB - 1:NB], func=AF.Copy), reads=[hs.key], writes=["lru_state"])
                    putslot(av, ig)
                    yield
                    t1 = getslot()
                    P.add("act", lambda e: e.activation(out=t1.ap[:, 0:NB], in_=gx.ap[:, 0:NB], func=AF.Square), reads=[gx.key], writes=[t1.key])
                    P.add("dve", lambda e: e.tensor_scalar(out=t1.ap[:, 0:NB], in0=t1.ap[:, 0:NB], scalar1=0.044715, scalar2=1.0,
                                                           op0=ALU.mult, op1=ALU.add), reads=[t1.key], writes=[t1.key])
                    P.add("dve", lambda e: e.tensor_tensor(out=t1.ap[:, 0:NB], in0=t1.ap[:, 0:NB], in1=gx.ap[:, 0:NB], op=ALU.mult),
                          reads=[gx.key, t1.key], writes=[t1.key])
                    act_sigmoid(t1.ap[:, 0:NB], t1.ap[:, 0:NB], [t1.key], t1.key, scale=1.5957691216)
                    P.add("dve", lambda e: e.tensor_tensor(out=t1.ap[:, 0:NB], in0=t1.ap[:, 0:NB], in1=gx.ap[:, 0:NB], op=ALU.mult),
                          reads=[gx.key, t1.key], writes=[t1.key])
                    P.add("dve", lambda e: e.tensor_tensor(out=t1.ap[:, 0:NB], in0=t1.ap[:, 0:NB], in1=hs.ap[:, 0:NB], op=ALU.mult),
                          reads=[hs.key, t1.key], writes=[t1.key])
                    putslot(gx, hs)
                    yield
                    rs = group_rstd(t1.ap[:, 0:NB], t1.key, 64, RMS_EPS)
                    P.add("dve", lambda e: e.scalar_tensor_tensor(
                        out=ycat[:, yq, i, :], in0=t1.ap[:, 0:NB], scalar=pvc(l, 16 + i), in1=rs.ap[:, 0:NB], op0=ALU.mult, op1=ALU.mult),
                        reads=[t1.key, rs.key, "pv"], writes=[(uid, "ycat", yq, i)])
                    putslot(t1, rs)
                    endphase()
                    yield
                for i in range(2):
                    yield from need(5)
                    Bs, Cs, Xs = getslot(), getslot(), getslot()
                    inproj_tile(4 + i, Bs.ap[:, 0:NB], Bs.key)
                    inproj_tile(6 + i, Cs.ap[:, 0:NB], Cs.key)
                    inproj_tile(8 + i, Xs.ap[:, 0:NB], Xs.key)
                    yield
                    vs = getslot()
                    P.add("dve", lambda e: e.tensor_tensor(out=vs.ap[:, 2:2 + NB], in0=Cs.ap[:, 0:NB], in1=Xs.ap[:, 0:NB], op=ALU.mult),
                          reads=[Cs.key, Xs.key], writes=[vs.key])
                    P.add("act", lambda e: e.activation(out=vs.ap[:, 0:2], in_=sc_halo[:, i, :], func=AF.Copy), reads=["sc_halo"], writes=[vs.key])
                    P.add("act", lambda e: e.activation(out=sc_halo[:, i, :], in_=vs.ap[:, NB:NB + 2], func=AF.Copy), reads=[vs.key], writes=["sc_halo"])
                    putslot(Cs, Xs)
                    yc = getslot()
                    P.add("dve", lambda e: e.tensor_scalar(out=yc.ap[:, 0:NB], in0=vs.ap[:, 2:2 + NB], scalar1=pvc(l, 18 + i * 3 + 2),
                                                           scalar2=None, op0=ALU.mult), reads=[vs.key, "pv"], writes=[yc.key])
                    for k in (1, 0):
                        P.add("dve", lambda e, k=k: e.scalar_tensor_tensor(
                            out=yc.ap[:, 0:NB], in0=vs.ap[:, k:k + NB], scalar=pvc(l, 18 + i * 3 + k), in1=yc.ap[:, 0:NB],
                            op0=ALU.mult, op1=ALU.add), reads=[vs.key, yc.key, "pv"], writes=[yc.key])
                    P.add("dve", lambda e: e.tensor_tensor(out=yc.ap[:, 0:NB], in0=yc.ap[:, 0:NB], in1=Bs.ap[:, 0:NB], op=ALU.mult),
                          reads=[Bs.key, yc.key], writes=[yc.key])
                    putslot(vs, Bs)
                    yield
                    rs = group_rstd(yc.ap[:, 0:NB], yc.key, 64, RMS_EPS)
                    P.add("dve", lambda e: e.scalar_tensor_tensor(
                        out=ycat[:, yq, 2 + i, :], in0=yc.ap[:, 0:NB], scalar=pvc(l, 24 + i), in1=rs.ap[:, 0:NB], op0=ALU.mult, op1=ALU.mult),
                        reads=[yc.key, rs.key, "pv"], writes=[(uid, "ycat", yq, 2 + i)])
                    putslot(yc, rs)
                    endphase()
                    yield
                ev_ls[b].done = True
                yield

        def chunk_thread():
            ev = {"n": 0}

            def evac_copy(out_ap, in_ap, reads, writes, wt=2):
                ev["n"] += 1
                if ev["n"] % wt != 0:
                    P.add("act", lambda e: e.activation(out=out_ap, in_=in_ap, func=AF.Copy), reads=reads, writes=writes)
                else:
                    P.add("dve", lambda e: e.tensor_copy(out=out_ap, in_=in_ap), reads=reads, writes=writes)

            for g in range(NG):
                q = g % 2
                yield from wait(ev_prep[g])
                if g >= 2:
                    yield from wait(ev_state[g - 2])
                ARq = AR_pad[:, q]
                for c in range(NCHB):
                    ak = (uid, "AR", q, c)
                    bk, bkey = nextbank()
                    P.add("pe", lambda e: e.matmul(bk[0:C2, 0:2 * C2], lhsT=BK_pad[:, c, 0:C2], rhs=ARq[:, c, :], start=True, stop=True),
                          reads=[(uid, "BK", c), ak], writes=[bkey])
                    P.add("pe", lambda e: e.matmul(bk[0:C2, 2 * C2:300], lhsT=Bh_pad[:, c, :], rhs=ident_bf, start=True, stop=True),
                          reads=[(uid, "Bh", c), "identbf"], writes=[bkey])
                    P.add("dve", lambda e: e.tensor_tensor(out=NRB[0:C2, c, 0:2 * C2], in0=bk[0:C2, 0:2 * C2], in1=mask1[:, 0:2 * C2], op=ALU.mult),
                          reads=[bkey, "consts"], writes=[(uid, "NRBm", c)])
                    P.add("act", lambda e: e.activation(out=NRB[0:C2, c, 2 * C2:300], in_=bk[0:C2, 2 * C2:300], func=AF.Copy),
                          reads=[bkey], writes=[(uid, "NRBt", c)])
                    bk2, bkey2 = nextbank()
                    P.add("pe", lambda e: e.matmul(bk2[0:C2, 0:2 * C2], lhsT=ARq[:, c, 0:C2], rhs=BK_pad[:, c, :], start=True, stop=True),
                          reads=[(uid, "BK", c), ak], writes=[bkey2])
                    P.add("pe", lambda e: e.matmul(bk2[0:C2, 2 * C2:300], lhsT=ARq[:, c, 0:C2], rhs=ident_bf, start=True, stop=True),
                          reads=[ak, "identbf"], writes=[bkey2])
                    P.add("dve", lambda e: e.tensor_tensor(out=NTY[0:C2, c, 0:2 * C2], in0=bk2[0:C2, 0:2 * C2], in1=mask3[:, 0:2 * C2], op=ALU.mult),
                          reads=[bkey2, "consts"], writes=[(uid, "NTYm", c)])
                    P.add("act", lambda e: e.activation(out=NTY[0:C2, c, 2 * C2:300], in_=bk2[0:C2, 2 * C2:300], func=AF.Copy),
                          reads=[bkey2], writes=[(uid, "NTYt", c)])
                    bk3, bkey3 = nextbank()
                    P.add("pe", lambda e: e.matmul(bk3[0:C2, 0:C2], lhsT=BK_pad[:, c, C2:2 * C2], rhs=ARq[:, c, C2:2 * C2], start=True, stop=True),
                          reads=[(uid, "BK", c), ak], writes=[bkey3])
                    P.add("pe", lambda e: e.matmul(bk3[0:C2, C2:C2 + 128], lhsT=Kh_pad[:, c, :], rhs=ident_bf, start=True, stop=True),
                          reads=[(uid, "Kh", c), "identbf"], writes=[bkey3])
                    P.add("pe", lambda e: e.matmul(bk3[0:C2, C2 + 128:342], lhsT=V_pad[:, c, :], rhs=ident_bf, start=True, stop=True),
                          reads=[(uid, "Vp", c), "identbf"], writes=[bkey3])
                    P.add("dve", lambda e: e.tensor_tensor(out=RKV[0:C2, q, c, 0:C2], in0=bk3[0:C2, 0:C2], in1=mask2[:, 0:C2], op=ALU.mult),
                          reads=[bkey3, "consts"], writes=[(uid, "RKVm", q, c)])
                    P.add("act", lambda e: e.activation(out=RKV[0:C2, q, c, C2:342], in_=bk3[0:C2, C2:342], func=AF.Copy),
                          reads=[bkey3], writes=[(uid, "RKVt", q, c)])
                    yield
                ev_A[g].done = True
                for lev in range(6):
                    for c in range(NCHB):
                        if lev == 0:
                            Nk, NTk, nkey = NRB[0:C2, c, 0:C2], NTY[0:C2, c, 0:C2], [(uid, "NRBm", c), (uid, "NTYm", c)]
                        else:
                            Nk, NTk, nkey = NNb[0:C2, lev % 2, c, 0:C2], NNb[0:C2, lev % 2, c, C2:2 * C2], [(uid, "NN", lev % 2, c)]
                        Pc = PT[0:C2, c, :]
                        pkey = (uid, "PT", c)
                        rhsP = ident_bf[0:C2, 0:C2] if lev == 0 else Pc
                        bk, bkey = nextbank()
                        P.add("pe", lambda e: e.matmul(bk[0:C2, 0:C2], lhsT=ident_bf[0:C2, 0:C2], rhs=rhsP, start=True, stop=False),
                              reads=[pkey, "identbf"], writes=[bkey])
                        P.add("pe", lambda e: e.matmul(bk[0:C2, 0:C2], lhsT=NTk, rhs=rhsP, start=False, stop=True),
                              reads=[pkey, "identbf"] + nkey, writes=[bkey])
                        evac_copy(Pc, bk[0:C2, 0:C2], [bkey], [pkey])
                        if lev < 5:
                            bk2, bkey2 = nextbank()
                            if lev < 4:
                                P.add("pe", lambda e: e.matmul(bk2[0:C2, 0:C2], lhsT=NTk, rhs=Nk, start=True, stop=True), reads=nkey, writes=[bkey2])
                                P.add("pe", lambda e: e.matmul(bk2[0:C2, C2:2 * C2], lhsT=Nk, rhs=NTk, start=True, stop=True), reads=nkey, writes=[bkey2])
                                evac_copy(NNb[0:C2, (lev + 1) % 2, c, 0:2 * C2], bk2[0:C2, 0:2 * C2], [bkey2], [(uid, "NN", (lev + 1) % 2, c)])
                            else:
                                P.add("pe", lambda e: e.matmul(bk2[0:C2, 0:C2], lhsT=Nk, rhs=NTk, start=True, stop=True), reads=nkey, writes=[bkey2])
                                evac_copy(NNb[0:C2, (lev + 1) % 2, c, C2:2 * C2], bk2[0:C2, 0:C2], [bkey2], [(uid, "NN", (lev + 1) % 2, c)])
                        yield
                for c in range(NCHB):
                    YT = NTY[0:C2, c, C2:300]
                    ykeys = [(uid, "NTYm", c), (uid, "NTYt", c)]
                    bk, bkey = nextbank()
                    P.add("pe", lambda e: e.matmul(bk[0:C2, 0:214], lhsT=PT[0:C2, c, :], rhs=YT, start=True, stop=True),
                          reads=ykeys + [(uid, "PT", c)], writes=[bkey])
                    evac_copy(YT, bk[0:C2, 0:214], [bkey], ykeys)
                    yield
                for c in range(NCHB):
                    ak = (uid, "AR", q, c)
                    bk, bkey = nextbank()
                    P.add("pe", lambda e: e.matmul(bk[:, 0:C2], lhsT=NTY[0:C2, c, 2 * C2:300], rhs=NRB[0:C2, c, C2:2 * C2], start=True, stop=False),
                          reads=NTYk(c) + NRBk(c), writes=[bkey])
                    P.add("pe", lambda e: e.matmul(bk[:, 0:C2], lhsT=ident_bf, rhs=ARq[:, c, C2:2 * C2], start=False, stop=True),
                          reads=[ak, "identbf"], writes=[bkey])
                    P.add("pe", lambda e: e.matmul(bk[:, C2:214], lhsT=NTY[0:C2, c, 2 * C2:300], rhs=NRB[0:C2, c, 2 * C2:300], start=True, stop=True),
                          reads=NTYk(c) + NRBk(c), writes=[bkey])
                    evac_copy(QT[:, q, c, :], bk[:, 0:C2], [bkey], [(uid, "QT", q, c)], wt=2)
                    P.add("dve", lambda e: e.scalar_tensor_tensor(out=Mm[:, q, c, :], in0=ident_f, scalar=Pend[:, q, c:c + 1], in1=bk[:, C2:214],
                                                                  op0=ALU.mult, op1=ALU.add),
                          reads=[bkey, "consts", (uid, "Pend", q)], writes=[(uid, "Mm", q, c)])
                    bk2, bkey2 = nextbank()
                    P.add("pe", lambda e: e.matmul(bk2[0:C2, 0:214], lhsT=NTY[0:C2, c, C2:2 * C2], rhs=NRB[0:C2, c, C2:300], start=True, stop=False),
                          reads=NTYk(c) + NRBk(c), writes=[bkey2])
                    P.add("pe", lambda e: e.matmul(bk2[0:C2, 0:214], lhsT=ident_bf[0:C2, 0:C2], rhs=RKV[0:C2, q, c, 0:214], start=False, stop=True),
                          reads=RKVk(c, q) + ["identbf"], writes=[bkey2])
                    evac_copy(GHb[0:C2, q, c, :], bk2[0:C2, 0:214], [bkey2], [(uid, "GH", q, c)], wt=2)
                    yield
                ev_QM[g].done = True

        def state_thread():
            sb = {"n": 0}
            for g in range(NG):
                b, hp = g // 4, g % 4
                q = g % 2
                yq = b % 2
                tsl = slice(b * NB, (b + 1) * NB)
                yield from wait(ev_QM[g])
                yield from need(5)
                gv, bonus = handoff.pop(g)
                yrw = getslot()
                ybk, ybkey = banks[5], ("bank", 5)
                for c in range(NCHB):
                    gidx = (b * NCHB + c)
                    cur, nxt = gidx % 2, (gidx + 1) % 2
                    S_cur, S_nxt = S_buf[:, hp, cur, :], S_buf[:, hp, nxt, :]
                    j = c % 4
                    P.add("pe", lambda e: e.matmul(ybk[:, j * C2:(j + 1) * C2], lhsT=RKV[0:C2, q, c, 214:342], rhs=GHb[0:C2, q, c, 0:C2], start=True, stop=False),
                          reads=RKVk(c, q) + [(uid, "GH", q, c)], writes=[ybkey])
                    P.add("pe", lambda e: e.matmul(ybk[:, j * C2:(j + 1) * C2], lhsT=S_cur, rhs=QT[:, q, c, :], start=False, stop=True),
                          reads=[("S", hp, cur), (uid, "QT", q, c)], writes=[ybkey])
                    if j == 3:
                        c0 = c - 3
                        for hh in range(2):
                            ps_ = slice(hh * 64, (hh + 1) * 64)
                            src = ybk[ps_, 0:4 * C2].rearrange("p (c x) -> p c x", x=C2)[:, :, hh * CH:(hh + 1) * CH]
                            dst = yrw.ap[ps_, c0 * CH:(c0 + 4) * CH].rearrange("p (c t) -> p c t", t=CH)
                            P.add("act", lambda e, src=src, dst=dst: e.activation(out=dst, in_=src, func=AF.Copy), reads=[ybkey], writes=[yrw.key])
                    sbi = 6 + (sb["n"] % 2)
                    sb["n"] += 1
                    sbk, sbkey = banks[sbi], ("bank", sbi)
                    P.add("pe", lambda e: e.matmul(sbk[:, 0:128], lhsT=GHb[0:C2, q, c, C2:214], rhs=RKV[0:C2, q, c, 214:342], start=True, stop=False),
                          reads=RKVk(c, q) + [(uid, "GH", q, c)], writes=[sbkey])
                    P.add("pe", lambda e: e.matmul(sbk[:, 0:128], lhsT=Mm[:, q, c, :], rhs=S_cur, start=False, stop=True),
                          reads=[("S", hp, cur), (uid, "Mm", q, c)], writes=[sbkey])
                    P.add("act", lambda e: e.activation(out=S_nxt, in_=sbk[:, 0:128], func=AF.Copy), reads=[sbkey], writes=[("S", hp, nxt)])
                    yield
                ev_state[g].done = True
                yb = getslot()
                P.add("act", lambda e: e.activation(out=yb.bf[:, 0:NB], in_=yrw.ap[:, 0:NB], func=AF.Copy), reads=[yrw.key], writes=[yb.key])
                bk, bkey = nextbank()
                P.add("pe", lambda e: e.matmul(bk[:, 0:NB], lhsT=ones_bd, rhs=yb.bf[:, 0:NB], start=True, stop=True),
                      reads=[yb.key, "ones_bd"], writes=[bkey])
                P.add("dve", lambda e: e.scalar_tensor_tensor(out=yrw.ap[:, 0:NB], in0=bk[:, 0:NB], scalar=-1.0 / 64, in1=yrw.ap[:, 0:NB],
                                                              op0=ALU.mult, op1=ALU.add), reads=[bkey, yrw.key], writes=[yrw.key])
                putslot(yb)
                yield
                rs = group_rstd(yrw.ap[:, 0:NB], yrw.key, 64, LNX_EPS)
                P.add("dve", lambda e: e.tensor_tensor(out=yrw.ap[:, 0:NB], in0=yrw.ap[:, 0:NB], in1=rs.ap[:, 0:NB], op=ALU.mult),
                      reads=[yrw.key, rs.key], writes=[yrw.key])
                P.add("dve", lambda e: e.tensor_scalar(out=yrw.ap[:, 0:NB], in0=yrw.ap[:, 0:NB], scalar1=pvc(l, 59 + hp), scalar2=pvc(l, 63 + hp),
                                                       op0=ALU.mult, op1=ALU.add), reads=[yrw.key, "pv"], writes=[yrw.key])
                P.add("dve", lambda e: e.tensor_tensor(out=yrw.ap[:, 0:NB], in0=yrw.ap[:, 0:NB], in1=bonus.ap[:, 0:NB], op=ALU.add),
                      reads=[yrw.key, bonus.key], writes=[yrw.key])
                P.add("dve", lambda e: e.tensor_tensor(out=ycat[:, yq, 4 + hp, :], in0=yrw.ap[:, 0:NB], in1=gv.ap[:, 0:NB], op=ALU.mult),
                      reads=[yrw.key, gv.key], writes=[(uid, "ycat", yq, 4 + hp)])
                putslot(rs, yrw, bonus, gv)
                endphase()
                yield
                if hp == 3:
                    yield from need(10)
                    om = [getslot() for _ in range(KC)]
                    for m in range(KC):
                        ws = wo_n["n"] % NWO
                        wo_n["n"] += 1
                        P.add("pool", lambda e: e.dma_start(out=wo[:, ws], in_=mix_wout[l, m], max_dma_last_dim=4096),
                              writes=[(uid, "wo", ws)], dma_sem=f"wo{ws}")
                        bk, bkey = nextbank()
                        for k in range(KC):
                            P.add("pe", lambda e, k=k: e.matmul(bk[:, 0:NB], lhsT=wo[:, ws, k, :], rhs=ycat[:, yq, k, :], start=(k == 0), stop=(k == KC - 1)),
                                  reads=[(uid, "wo", ws), (uid, "ycat", yq, k)], writes=[bkey])
                        P.add("act", lambda e: e.activation(out=om[m].ap[:, 0:NB], in_=bk[:, 0:NB], func=AF.Copy), reads=[bkey], writes=[om[m].key])
                        P.add("act", lambda e: e.activation(out=sqo[:, m, :], in_=bk[:, 0:NB], func=AF.Square), reads=[bkey], writes=[(uid, "sqo")])
                        yield
                    bk, bkey = nextbank()
                    for c in range(KC):
                        P.add("pe", lambda e, c=c: e.matmul(bk[:, 0:NB], lhsT=ones_bf, rhs=sqo[:, c, :], start=(c == 0), stop=(c == KC - 1)),
                              reads=[(uid, "sqo"), "ones"], writes=[bkey])
                    rs = getslot()
                    rstd_from_sumsq(bk[:, 0:NB], rs.ap[:, 0:NB], D, bkey, rs.key)
                    for m in range(KC):
                        P.add("dve", lambda e, m=m: e.scalar_tensor_tensor(out=om[m].ap[:, 0:NB], in0=om[m].ap[:, 0:NB], scalar=g_ap(l, 3, m),
                                                                           in1=rs.ap[:, 0:NB], op0=ALU.mult, op1=ALU.mult),
                              reads=[om[m].key, rs.key, "gsb"], writes=[om[m].key])
                        P.add("pool", lambda e, m=m: e.tensor_tensor(out=h[:, m, tsl], in0=h[:, m, tsl], in1=om[m].ap[:, 0:NB], op=ALU.add),
                              reads=[hk(m, b), om[m].key], writes=[hk(m, b)])
                    putslot(rs, *om)
                    endphase()
                    ev_out[b].done = True
                    yield

        threads = [("front", front()), ("ls", ls_thread()), ("chunk", chunk_thread()), ("state", state_thread())]
        stall = 0
        while threads:
            progressed = False
            for tid_th in list(threads):
                tid, th = tid_th
                cur["tid"] = tid
                try:
                    r = next(th)
                    if r != "wait":
                        progressed = True
                except StopIteration:
                    threads.remove(tid_th)
                    quota.pop(tid, None)
                    progressed = True
            stall = 0 if progressed else stall + 1
            assert stall < 10, "thread scheduler deadlock"
        cur["tid"] = None
        A.release(m0)

    for (l, s) in stages:
        if s == 0:
            ffn_stage(l, 0)
        elif s == 2:
            ffn_stage(l, 1)
        else:
            mixer_stage(l)

    P.barrier()
    for c in range(KC):
        if dbg_h:
            P.add("sp", lambda e, c=c: e.dma_start(out=outT[c], in_=h[:, c, :]),
                  reads=[hk(c, b) for b in range(NBLK)], dma_sem="st")
        else:
            P.add("sp", lambda e, c=c: e.dma_start(out=outT[c], in_=h[:, c, NMETA:]),
                  reads=[hk(c, b) for b in range(NBLK)], dma_sem="st")
    P.final_dma.append(("st", P.dma_cnt["st"]))

    sem_names = sorted(P.dma_cnt.keys())
    from contextlib import ExitStack
    with ExitStack() as es:
        eng_sems = {e: es.enter_context(nc.semaphore(f"s_{e}")) for e in Prog.CE}
        dma_sems = {s: es.enter_context(nc.semaphore(f"d_{s}")) for s in sem_names}
        block = es.enter_context(nc.Block())
        P.emit(block, eng_sems, dma_sems)
    print("arena peak bytes/partition:", A.peak, " ops:", {e: len(s) for e, s in P.streams.items()})
    return nc


def _tile_w_in_ffn(w):
    Lh = w.shape[0]
    w5 = w.reshape(Lh, KC, 128, 2, JT, 128)
    return np.ascontiguousarray(w5.transpose(0, 4, 2, 1, 3, 5)).reshape(Lh, JT, 128, KC, 256)


def _tile_w_out_ffn(w):
    Lh = w.shape[0]
    w5 = w.reshape(Lh, JT, 128, KC, 128)
    return np.ascontiguousarray(w5.transpose(0, 3, 2, 1, 4))


def _make_consts():
    cst = np.zeros((128, NCONST), np.float32)
    cst[:, CO_ID:CO_ID + 128] = np.eye(128, dtype=np.float32)
    su = np.triu(np.ones((CH, CH), np.float32), 1)
    ue = np.triu(np.ones((CH, CH), np.float32), 0)
    bd = lambda a: np.block([[a, np.zeros_like(a)], [np.zeros_like(a), a]])
    cst[0:C2, CO_M1:CO_M1 + C2] = bd(su)
    cst[0:C2, CO_M1 + C2:CO_M1 + 2 * C2] = bd(ue)
    cst[0:C2, CO_M3:CO_M3 + C2] = bd(su.T)
    cst[0:C2, CO_M3 + C2:CO_M3 + 2 * C2] = bd(su.T)
    cst[0:C2, CO_M2:CO_M2 + C2] = bd(ue)
    m01 = np.ones(NB, np.float32)
    m01[0::CH] = 0.0
    cst[:, CO_01:CO_01 + NB] = m01[None, :]
    return cst


def prep_inputs(inputs):
    x = np.asarray(inputs["x"], dtype=np.float32)
    shared = {}
    shared["metaT"] = np.ascontiguousarray(np.asarray(inputs["meta_tokens"], np.float32).T).reshape(KC, 128, NMETA)
    g = np.asarray(inputs["norm_g"], np.float32).reshape(NL, 6, KC, 128)
    shared["gvec"] = np.ascontiguousarray(g.transpose(3, 0, 1, 2)).reshape(128, NL * 6 * KC)
    shared["ffn1_win"] = _tile_w_in_ffn(np.asarray(inputs["ffn1_w_in"], np.float32))
    shared["ffn2_win"] = _tile_w_in_ffn(np.asarray(inputs["ffn2_w_in"], np.float32))
    shared["ffn1_wout"] = _tile_w_out_ffn(np.asarray(inputs["ffn1_w_out"], np.float32))
    shared["ffn2_wout"] = _tile_w_out_ffn(np.asarray(inputs["ffn2_w_out"], np.float32))
    f = lambda k: np.asarray(inputs[k], np.float32)
    mw = f("mix_w_in").reshape(NL, KC, 128, MIXT, 128)
    shared["mix_win"] = np.ascontiguousarray(mw.transpose(0, 3, 2, 1, 4))
    mo = f("mix_w_out").reshape(NL, KC, 128, KC, 128)
    shared["mix_wout"] = np.ascontiguousarray(mo.transpose(0, 3, 2, 1, 4))
    pvt = np.zeros((NL, NPV, 128), np.float32)
    for l in range(NL):
        cw = f("lru_conv_w")[l].reshape(4, 2, 128)
        for i in range(2):
            for k in range(4):
                pvt[l, i * 4 + k] = cw[k, i]
        pvt[l, 8:10] = f("lru_conv_b")[l].reshape(2, 128)
        pvt[l, 10:12] = f("lru_ba")[l].reshape(2, 128)
        pvt[l, 12:14] = f("lru_bx")[l].reshape(2, 128)
        pvt[l, 14:16] = f("lru_lambda")[l].reshape(2, 128)
        pvt[l, 16:18] = f("lru_norm_g")[l].reshape(2, 128)
        sw = f("sc_conv_w")[l].reshape(3, 2, 128)
        for i in range(2):
            for k in range(3):
                pvt[l, 18 + i * 3 + k] = sw[k, i]
        pvt[l, 24:26] = f("sc_norm_g")[l].reshape(2, 128)
        pvt[l, 26:39] = f("rwkv_mu")[l].reshape(13, 128)
        for j, nm in enumerate(("rwkv_w0", "rwkv_a0", "rwkv_k_k", "rwkv_k_a", "rwkv_r_k", "rwkv_lnx_w", "rwkv_lnx_b")):
            pvt[l, 39 + 4 * j:43 + 4 * j] = f(nm)[l].reshape(4, 128)
    shared["pv"] = np.ascontiguousarray(pvt.transpose(2, 0, 1)).reshape(128, NL * NPV)
    shared["lora_w"] = np.ascontiguousarray(np.concatenate([f("rwkv_w2"), f("rwkv_a2"), f("rwkv_g2")], axis=1))
    shared["lru_w"] = np.ascontiguousarray(np.stack([f("lru_wa"), f("lru_wx")], axis=1))
    shared["consts"] = _make_consts()
    in_maps = []
    for b in range(x.shape[0]):
        m = dict(shared)
        m["xT"] = np.ascontiguousarray(x[b].T).reshape(KC, 128, T - NMETA)
        in_maps.append(m)
    return in_maps


def kernel(**inputs):
    in_maps = prep_inputs(inputs)
    nc = build_program()
    res = run_bass_kernel_spmd(nc, in_maps, core_ids=list(range(NCORES)))
    outs = [np.asarray(r["outT"]).reshape(D, T - NMETA).T for r in res.results]
    return np.ascontiguousarray(np.stack(outs, axis=0)).astype(np.float32)
```

```python
import numpy as np
import concourse.bass as bass
import concourse.mybir as mybir
from concourse.bass_utils import run_bass_kernel_spmd

F32, BF16 = mybir.dt.float32, mybir.dt.bfloat16
AF = mybir.ActivationFunctionType
ALU = mybir.AluOpType

NCORES = 8
T = 2064
NMETA = 16
NB = 344
NBLK = 6
D = 1024
KC = 8
DFF = 2816
JT = 22
NL = 2
SB = 3
MIXT = 23
RMS_EPS = 1e-6
LNX_EPS = 64e-5
NPV = 67
CH = 43
NCHB = 8
C2 = 2 * CH
CO_ID, CO_M1, CO_M3, CO_M2, CO_01 = 0, 128, 300, 472, 558
NCONST = 558 + 344
SLOTW = 348


class Op:
    __slots__ = ("eng", "fn", "deps", "is_dma", "sem", "done", "signal", "rank", "tag", "rank_pos")


class _Rec:
    def __init__(self):
        self.call = None

    def __getattr__(self, name):
        def f(*args, **kwargs):
            self.call = (name, args, kwargs)
            return self
        return f


class Prog:
    CE = ("pe", "act", "dve", "pool")

    def __init__(self, nc):
        self.nc = nc
        self.streams = {e: [] for e in ("pe", "act", "dve", "pool", "sp")}
        self.lastw = {}
        self.readers = {}
        self.dma_cnt = {}
        self.barrier_ops = []
        self.final_dma = []

    def add(self, eng, fn, reads=(), writes=(), dma_sem=None, tag=None, nodeps=False):
        op = Op()
        rec = _Rec()
        fn(rec)
        assert rec.call is not None
        op.eng, op.fn, op.is_dma, op.sem = eng, rec.call, dma_sem is not None, dma_sem
        op.signal, op.rank, op.done, op.tag = False, None, None, tag
        if op.is_dma:
            self.dma_cnt[dma_sem] = self.dma_cnt.get(dma_sem, 0) + 16
            op.done = self.dma_cnt[dma_sem]
        deps = {}

        def want(d, raw):
            if d is op:
                return
            if d.is_dma:
                if op.is_dma and d.sem == op.sem and d.sem in ("msm",):
                    return
                key = ("dma", d.sem)
                if key not in deps or deps[key].done < d.done:
                    deps[key] = d
                return
            if (not op.is_dma) and d.eng == eng:
                if eng == "pe" or (not raw and eng != "pool"):
                    return
            key = ("ce", d.eng)
            if key not in deps or deps[key].rank_pos < d.rank_pos:
                deps[key] = d

        for k in reads:
            w = self.lastw.get(k)
            if w is not None:
                want(w, True)
            if isinstance(k, tuple) and k[0] == "bank":
                rd = self.readers.get(k)
                if rd:
                    for r in rd.values():
                        if r.eng != eng:
                            want(r, True)
        for k in writes:
            w = self.lastw.get(k)
            if w is not None:
                want(w, True)
            rd = self.readers.get(k)
            if rd:
                for r in rd.values():
                    want(r, False)
        for b in self.barrier_ops:
            want(b, True)
        if nodeps:
            deps = {}
        op.deps = list(deps.values())
        op.rank_pos = len(self.streams[eng])
        for k in reads:
            rd = self.readers.setdefault(k, {})
            rd[("dma", id(op)) if op.is_dma else eng] = op
        for k in writes:
            self.lastw[k] = op
            self.readers[k] = {}
        self.streams[eng].append(op)
        return op

    def barrier(self):
        ops = []
        for e in self.CE:
            for op in reversed(self.streams[e]):
                if not op.is_dma:
                    ops.append(op)
                    break
        last_dma = {}
        for e in self.streams:
            for op in self.streams[e]:
                if op.is_dma:
                    if op.sem not in last_dma or last_dma[op.sem].done < op.done:
                        last_dma[op.sem] = op
        ops.extend(last_dma.values())
        self.barrier_ops = ops

    def emit(self, block, eng_sems, dma_sems):
        for st in self.streams.values():
            for op in st:
                for d in op.deps:
                    if not d.is_dma:
                        d.signal = True
        for e in self.CE:
            r = 0
            for op in self.streams[e]:
                if op.signal and not op.is_dma:
                    r += 1
                op.rank = r

        def run(ename, eng):
            waited = {}
            for op in self.streams[ename]:
                for d in op.deps:
                    if d.is_dma:
                        sname, val, sem = ("dma", d.sem), d.done, dma_sems[d.sem]
                    else:
                        sname, val, sem = ("ce", d.eng), d.rank, eng_sems[d.eng]
                    if waited.get(sname, 0) >= val:
                        continue
                    eng.wait_ge(sem, val)
                    waited[sname] = val
                name, args, kwargs = op.fn
                ins = getattr(eng, name)(*args, **kwargs)
                if op.is_dma:
                    ins.then_inc(dma_sems[op.sem], 16)
                elif op.signal:
                    ins.then_inc(eng_sems[ename], 1)
            if ename == "sp":
                for s, cnt in self.final_dma:
                    eng.wait_ge(dma_sems[s], cnt)

        @block.sync
        def _(e):
            run("sp", e)

        @block.tensor
        def _(e):
            run("pe", e)

        @block.scalar
        def _(e):
            run("act", e)

        @block.vector
        def _(e):
            run("dve", e)

        @block.gpsimd
        def _(e):
            run("pool", e)


class Arena:
    def __init__(self, nc, nbytes):
        self.t = nc.alloc_sbuf_tensor("arena", [128, nbytes // 4], F32)
        self.ap = self.t.ap()
        self.off = 0
        self.cap = nbytes
        self.peak = 0

    def alloc(self, shape, dtype):
        n = int(np.prod(shape))
        esz = 4 if dtype == F32 else 2
        nbytes = (n * esz + 63) // 64 * 64
        assert self.off + nbytes <= self.cap, ("arena overflow", self.off, nbytes, self.cap)
        a = self.ap[:, self.off // 4:(self.off + nbytes) // 4]
        if dtype != F32:
            a = a.bitcast(dtype)
        a = a[:, 0:n]
        if len(shape) == 2:
            a = a.rearrange("p (a b) -> p a b", a=shape[0])
        elif len(shape) == 3:
            a = a.rearrange("p (a b c) -> p a b c", a=shape[0], b=shape[1])
        self.off += nbytes
        self.peak = max(self.peak, self.off)
        return a

    def mark(self):
        return self.off

    def release(self, m):
        self.off = m


def build_program(stages=None, dbg_h=False):
    if stages is None:
        stages = [(l, s) for l in range(NL) for s in range(3)]
    nc = bass.Bass("TRN2", target_bir_lowering=False)
    dt = {}

    def din(name, shape):
        dt[name] = nc.dram_tensor(name, list(shape), F32, kind="ExternalInput").ap()
        return dt[name]

    xT = din("xT", [KC, 128, T - NMETA])
    metaT = din("metaT", [KC, 128, NMETA])
    gvec = din("gvec", [128, NL * 6 * KC])
    ffn_win = [din(f"ffn{i}_win", [NL, JT, 128, KC, 256]) for i in (1, 2)]
    ffn_wout = [din(f"ffn{i}_wout", [NL, KC, 128, JT, 128]) for i in (1, 2)]
    mix_win = din("mix_win", [NL, MIXT, 128, KC, 128])
    mix_wout = din("mix_wout", [NL, KC, 128, KC, 128])
    pv_d = din("pv", [128, NL * NPV])
    lora_d = din("lora_w", [NL, 128, 512])
    lruw_d = din("lru_w", [NL, 2, 4, 64, 64])
    consts_d = din("consts", [128, NCONST])
    if dbg_h:
        outT = nc.dram_tensor("outT", [KC, 128, T], F32, kind="ExternalOutput").ap()
    else:
        outT = nc.dram_tensor("outT", [KC, 128, T - NMETA], F32, kind="ExternalOutput").ap()

    P = Prog(nc)
    A = Arena(nc, 204 * 1024)
    banks = [nc.alloc_psum_tensor(f"bank{i}", [128, 512], F32).ap() for i in range(8)]

    h = A.alloc([KC, T], F32)
    gsb = A.alloc([NL * 6 * KC], F32)
    ghalf = A.alloc([NL * 6 * KC], F32)
    ones_bf = A.alloc([128], BF16)

    def g_ap(l, i, c, half=False):
        idx = (l * 6 + i) * KC + c
        return (ghalf if half else gsb)[:, idx:idx + 1]

    def hk(c, b):
        return ("h", c, b)

    grp = []
    for c in range(KC):
        grp.append(P.add("sp", lambda e, c=c: e.dma_start(out=h[:, c, NMETA:], in_=xT[c]),
                         writes=[hk(c, b) for b in range(NBLK)], dma_sem="ld", nodeps=True))
        grp.append(P.add("sp", lambda e, c=c: e.dma_start(out=h[:, c, 0:NMETA], in_=metaT[c]),
                         writes=[hk(c, 0)], dma_sem="ld", nodeps=True))
    grp.append(P.add("sp", lambda e: e.dma_start(out=gsb, in_=gvec), writes=["gsb"], dma_sem="ld", nodeps=True))
    P.add("pool", lambda e: e.memset(ones_bf, 1.0), writes=["ones"])
    P.add("dve", lambda e: e.tensor_scalar(out=ghalf, in0=gsb, scalar1=0.5, scalar2=None, op0=ALU.mult),
          reads=["gsb"], writes=["ghalf"])

    def rstd_from_sumsq(ps_ap, rstd_ap, n, pskey, rkey, tmpkey=None):
        P.add("act", lambda e: e.activation(out=rstd_ap, in_=ps_ap, func=AF.Ln, bias=RMS_EPS, scale=1.0 / n),
              reads=[pskey], writes=[rkey])
        P.add("act", lambda e: e.activation(out=rstd_ap, in_=rstd_ap, func=AF.Exp, scale=-0.5), reads=[rkey], writes=[rkey])

    def ffn_stage(l, which):
        gi_pre, gi_post = (0, 1) if which == 0 else (4, 5)
        win_d, wout_d = ffn_win[which], ffn_wout[which]
        P.barrier()
        m0 = A.mark()
        NT = SB * NB
        xn2 = A.alloc([1, KC, NT], BF16)
        hid = A.alloc([JT, NT], BF16)
        o = A.alloc([KC, NT], F32)
        sq = A.alloc([1, KC, NB], BF16)
        rstd = A.alloc([2, NB], F32)
        sg = A.alloc([2, NB], F32)
        tmp = A.alloc([2, NB], F32)
        NWI, NWO = 2, 2
        wi = A.alloc([NWI, KC, 256], BF16)
        wo = A.alloc([NWO, JT, 128], BF16)
        psA = [banks[0], banks[1]]
        psB = [banks[2], banks[3]]
        psO = [banks[4], banks[5]]
        psN = [banks[6], banks[7]]
        uid = ("f", l, which)
        wn = {"i": 0, "o": 0}
        cnt = {"n": 0, "a": 0, "o": 0}
        def blocks_of(sbi):
            return list(range(sbi * SB, (sbi + 1) * SB))

        def pre(sbi):
            blocks = blocks_of(sbi)
            par = 0
            xn = xn2[:, par]
            for bi, b in enumerate(blocks):
                n = cnt["n"]; cnt["n"] += 1
                s = n % 2
                tsl = slice(b * NB, (b + 1) * NB)
                P.add("act", lambda e, s=s, tsl=tsl: e.activation(out=sq[:, 0], in_=h[:, :, tsl], func=AF.Square),
                      reads=[hk(c, b) for c in range(KC)], writes=[(uid, "sq", 0)])
                for c in range(KC):
                    P.add("pe", lambda e, s=s, c=c: e.matmul(psN[s][:, 0:NB], lhsT=ones_bf, rhs=sq[:, 0, c, :],
                                                            start=(c == 0), stop=(c == KC - 1)),
                          reads=[(uid, "sq", 0), "ones"], writes=[(uid, "psN", s)])
                rstd_from_sumsq(psN[s][:, 0:NB], rstd[:, s], D, (uid, "psN", s), (uid, "rstd", s))
                for c in range(KC):
                    P.add("dve", lambda e, s=s, c=c, tsl=tsl, bi=bi: e.scalar_tensor_tensor(
                        out=xn[:, c, bi * NB:(bi + 1) * NB], in0=h[:, c, tsl], scalar=g_ap(l, gi_pre, c),
                        in1=rstd[:, s], op0=ALU.mult, op1=ALU.mult),
                        reads=[hk(c, b), (uid, "rstd", s), "gsb"], writes=[(uid, "xn", par, c, bi)])

        def inp(sbi):
            blocks = blocks_of(sbi)
            par = 0
            xn = xn2[:, par]
            for j in range(JT):
                ws = wn["i"] % NWI; wn["i"] += 1
                P.add("pool", lambda e, ws=ws, j=j: e.dma_start(out=wi[:, ws], in_=win_d[l, j], max_dma_last_dim=4096),
                      writes=[(uid, "wi", ws)], dma_sem=f"wi{ws}")
                for bi, b in enumerate(blocks):
                    n = cnt["a"]; cnt["a"] += 1
                    s = n % 2
                    bsl = slice(bi * NB, (bi + 1) * NB)
                    for half, ps in ((0, psA), (1, psB)):
                        for k in range(KC):
                            P.add("pe", lambda e, ps=ps, s=s, k=k, ws=ws, half=half, bsl=bsl: e.matmul(
                                ps[s][:, 0:NB], lhsT=wi[:, ws, k, half * 128:(half + 1) * 128], rhs=xn[:, k, bsl],
                                start=(k == 0), stop=(k == KC - 1)),
                                reads=[(uid, "wi", ws), (uid, "xn", par, k, bi)], writes=[(uid, "ps", half, s)])
                    P.add("act", lambda e, s=s: e.activation(out=sg[:, s], in_=psA[s][:, 0:NB], func=AF.Silu),
                          reads=[(uid, "ps", 0, s)], writes=[(uid, "sg", s)])
                    P.add("dve", lambda e, s=s, j=j, bsl=bsl: e.tensor_tensor(
                        out=hid[:, j, bsl], in0=sg[:, s], in1=psB[s][:, 0:NB], op=ALU.mult),
                        reads=[(uid, "sg", s), (uid, "ps", 1, s)], writes=[(uid, "hid", j, bi)])

        def outp(sbi):
            blocks = blocks_of(sbi)
            for m in range(KC):
                ws = wn["o"] % NWO; wn["o"] += 1
                P.add("pool", lambda e, ws=ws, m=m: e.dma_start(out=wo[:, ws], in_=wout_d[l, m], max_dma_last_dim=4096),
                      writes=[(uid, "wo", ws)], dma_sem=f"wo{ws}")
                for bi, b in enumerate(blocks):
                    n = cnt["o"]; cnt["o"] += 1
                    s = n % 2
                    bsl = slice(bi * NB, (bi + 1) * NB)
                    for k in range(JT):
                        P.add("pe", lambda e, s=s, k=k, ws=ws, bsl=bsl: e.matmul(
                            psO[s][:, 0:NB], lhsT=wo[:, ws, k, :], rhs=hid[:, k, bsl],
                            start=(k == 0), stop=(k == JT - 1)),
                            reads=[(uid, "wo", ws), (uid, "hid", k, bi)], writes=[(uid, "psO", s)])
                    P.add("act", lambda e, s=s, m=m, bsl=bsl: e.activation(out=o[:, m, bsl], in_=psO[s][:, 0:NB], func=AF.Copy),
                          reads=[(uid, "psO", s)], writes=[(uid, "o", m, bi)])

        def post(sbi):
            blocks = blocks_of(sbi)
            for bi, b in enumerate(blocks):
                n = cnt["n"]; cnt["n"] += 1
                s = n % 2
                bsl = slice(bi * NB, (bi + 1) * NB)
                tsl = slice(b * NB, (b + 1) * NB)
                P.add("act", lambda e, s=s, bsl=bsl: e.activation(out=sq[:, 0], in_=o[:, :, bsl], func=AF.Square),
                      reads=[(uid, "o", m, bi) for m in range(KC)], writes=[(uid, "sq", 0)])
                for c in range(KC):
                    P.add("pe", lambda e, s=s, c=c: e.matmul(psN[s][:, 0:NB], lhsT=ones_bf, rhs=sq[:, 0, c, :],
                                                            start=(c == 0), stop=(c == KC - 1)),
                          reads=[(uid, "sq", 0), "ones"], writes=[(uid, "psN", s)])
                rstd_from_sumsq(psN[s][:, 0:NB], rstd[:, s], D, (uid, "psN", s), (uid, "rstd", s))
                for m in range(KC):
                    ts_ = (n * KC + m) % 2
                    P.add("dve", lambda e, s=s, m=m, bsl=bsl, ts_=ts_: e.scalar_tensor_tensor(
                        out=tmp[:, ts_], in0=o[:, m, bsl], scalar=g_ap(l, gi_post, m, half=True),
                        in1=rstd[:, s], op0=ALU.mult, op1=ALU.mult),
                        reads=[(uid, "o", m, bi), (uid, "rstd", s), "ghalf"], writes=[(uid, "tmp", ts_)])
                    P.add("pool", lambda e, m=m, tsl=tsl, ts_=ts_: e.tensor_tensor(
                        out=h[:, m, tsl], in0=h[:, m, tsl], in1=tmp[:, ts_], op=ALU.add),
                        reads=[hk(m, b), (uid, "tmp", ts_)], writes=[hk(m, b)])
        NSB = NBLK // SB
        pre(0)
        inp(0)
        for sbi in range(NSB):
            if sbi + 1 < NSB:
                pre(sbi + 1)
            outp(sbi)
            if sbi + 1 < NSB:
                inp(sbi + 1)
            post(sbi)
        A.release(m0)

    pv = A.alloc([NL * NPV], F32)
    consts = A.alloc([NCONST], F32)
    ident_f = consts[:, CO_ID:CO_ID + 128]
    mask1 = consts[0:C2, CO_M1:CO_M1 + 2 * C2]
    mask3 = consts[0:C2, CO_M3:CO_M3 + 2 * C2]
    mask2 = consts[0:C2, CO_M2:CO_M2 + C2]
    mask01 = consts[:, CO_01:CO_01 + NB]
    ident_bf = A.alloc([128], BF16)
    ones_bd = A.alloc([128], BF16)
    lru_halo = A.alloc([2, 3], F32)
    sc_halo = A.alloc([2, 2], F32)
    rw_halo = A.alloc([13], F32)
    lru_state = A.alloc([2], F32)
    S_buf = A.alloc([4, 2, 128], BF16)
    clam = A.alloc([4], F32)
    grp2 = []
    grp2.append(P.add("sp", lambda e: e.dma_start(out=pv, in_=pv_d), writes=["pv"], dma_sem="ld", nodeps=True))
    grp2.append(P.add("sp", lambda e: e.dma_start(out=consts, in_=consts_d), writes=["consts"], dma_sem="ld", nodeps=True))
    for op in grp + grp2:
        op.done = P.dma_cnt["ld"]
    P.add("act", lambda e: e.activation(out=ident_bf, in_=ident_f, func=AF.Copy), reads=["consts"], writes=["identbf"])
    P.add("pool", lambda e: e.memset(ones_bd, 0.0), writes=["ones_bd"])
    P.add("pool", lambda e: e.memset(ones_bd[0:64, 0:64], 1.0), writes=["ones_bd"])
    P.add("pool", lambda e: e.memset(ones_bd[64:128, 64:128], 1.0), writes=["ones_bd"])

    npv = A.alloc([NL * NPV], F32)
    P.add("dve", lambda e: e.tensor_scalar(out=npv, in0=pv, scalar1=-1.0, scalar2=None, op0=ALU.mult), reads=["pv"], writes=["pv"])

    def pvc(l, j):
        return pv[:, l * NPV + j:l * NPV + j + 1]

    def npvc(l, j):
        return npv[:, l * NPV + j:l * NPV + j + 1]

    def act_sigmoid(dst_ap, src_ap, reads, wkey, nbias=None, scale=1.0):
        if nbias is None:
            P.add("act", lambda e: e.activation(out=dst_ap, in_=src_ap, func=AF.Exp, scale=-scale), reads=reads, writes=[wkey])
        else:
            P.add("act", lambda e: e.activation(out=dst_ap, in_=src_ap, func=AF.Exp, scale=-scale, bias=nbias), reads=reads + ["pv"], writes=[wkey])
        P.add("act", lambda e: e.activation(out=dst_ap, in_=dst_ap, func=AF.Ln, bias=1.0), reads=[wkey], writes=[wkey])
        P.add("act", lambda e: e.activation(out=dst_ap, in_=dst_ap, func=AF.Exp, scale=-1.0), reads=[wkey], writes=[wkey])

    class Slot:
        def __init__(self, i, ap):
            self.i, self.ap, self.key = i, ap, ("slot", i)
            self.bf = ap.bitcast(BF16)

    def mixer_stage(l):
        P.barrier()
        m0 = A.mark()
        NSLOT = 29
        slots_ap = A.alloc([NSLOT, SLOTW], F32)
        free_slots = list(range(NSLOT))
        cur = {"tid": None}
        live = {}
        quota = {}

        def getslot():
            assert free_slots, "slot pool exhausted"
            i = free_slots.pop(0)
            sl = Slot(i, slots_ap[:, i, :])
            sl.owner = cur["tid"]
            live[sl.owner] = live.get(sl.owner, 0) + 1
            return sl

        def putslot(*ss):
            for s_ in ss:
                free_slots.append(s_.i)
                if s_.owner is not None or None in live:
                    live[s_.owner] = live.get(s_.owner, 0) - 1

        def endphase():
            t = cur["tid"]
            quota[t] = live.get(t, 0)

        def disown(*ss):
            for s_ in ss:
                live[s_.owner] = live.get(s_.owner, 0) - 1
                s_.owner = "handoff"
                live["handoff"] = live.get("handoff", 0) + 1

        u = A.alloc([KC, NB], BF16)
        ycat = A.alloc([2, KC, NB], BF16)
        sqb = A.alloc([KC, NB], BF16)
        sqo = A.alloc([KC, NB], BF16)
        NWI, NWO = 3, 2
        wi = A.alloc([NWI, KC, 128], BF16)
        wo = A.alloc([NWO, KC, 128], BF16)
        lora_sb = A.alloc([512], BF16)
        gate_w = A.alloc([2, 2, 128], BF16)
        AR_pad = A.alloc([2, NCHB, 2 * C2], BF16)
        BK_pad = A.alloc([NCHB, 2 * C2], BF16)
        Bh_pad = A.alloc([NCHB, C2], BF16)
        Kh_pad = A.alloc([NCHB, C2], BF16)
        V_pad = A.alloc([NCHB, C2], BF16)
        NRB = A.alloc([NCHB, 300], BF16)
        NTY = A.alloc([NCHB, 300], BF16)
        RKV = A.alloc([2, NCHB, 342], BF16)
        NNb = A.alloc([2, NCHB, 2 * C2], BF16)
        PT = A.alloc([NCHB, C2], BF16)
        QT = A.alloc([2, NCHB, C2], BF16)
        Mm = A.alloc([2, NCHB, 128], BF16)
        GHb = A.alloc([2, NCHB, 214], BF16)
        Pend = A.alloc([2, NCHB], F32)
        uid = ("m", l)
        pbank = {"n": 0}

        def nextbank():
            b = pbank["n"] % 8
            pbank["n"] += 1
            return banks[b], ("bank", b)

        P.add("pool", lambda e: e.memset(AR_pad, 0.0), writes=[(uid, "AR", q_, c) for q_ in range(2) for c in range(NCHB)])
        for buf, nm in ((BK_pad, "BK"), (Bh_pad, "Bh"), (Kh_pad, "Kh"), (V_pad, "Vp")):
            P.add("pool", lambda e, buf=buf: e.memset(buf, 0.0), writes=[(uid, nm, c) for c in range(NCHB)])
        P.add("pool", lambda e: e.memset(S_buf, 0.0), writes=[("S", hp, i) for hp in range(4) for i in range(2)])
        P.add("pool", lambda e: e.memset(lru_halo, 0.0), writes=["lru_halo"])
        P.add("pool", lambda e: e.memset(sc_halo, 0.0), writes=["sc_halo"])
        P.add("pool", lambda e: e.memset(rw_halo, 0.0), writes=[("rw_halo", m_) for m_ in range(13)])
        P.add("pool", lambda e: e.memset(lru_state, 0.0), writes=["lru_state"])
        P.add("pool", lambda e: e.memset(gate_w, 0.0), writes=[(uid, "gate_w")])
        mg = [P.add("pool", lambda e: e.dma_start(out=lora_sb, in_=lora_d[l]), writes=[(uid, "lora")], dma_sem="msm")]
        for ax in range(2):
            for g in range(4):
                i, hh = g // 2, g % 2
                mg.append(P.add("pool", lambda e, ax=ax, g=g, i=i, hh=hh: e.dma_start(
                    out=gate_w[hh * 64:(hh + 1) * 64, ax, i, hh * 64:(hh + 1) * 64], in_=lruw_d[l, ax, g]),
                    reads=[], writes=[(uid, "gate_w")], dma_sem="msm"))
        for op_ in mg:
            op_.done = P.dma_cnt["msm"]
        st_ = getslot()
        P.add("act", lambda e: e.activation(out=st_.ap[:, 0:2], in_=pv[:, l * NPV + 14:l * NPV + 16], func=AF.Exp, scale=-1.0),
              reads=["pv"], writes=[st_.key])
        P.add("act", lambda e: e.activation(out=st_.ap[:, 2:4], in_=st_.ap[:, 0:2], func=AF.Ln, bias=1.0),
              reads=[st_.key], writes=[st_.key])
        P.add("dve", lambda e: e.tensor_scalar(out=clam[:, 0:2], in0=st_.ap[:, 2:4], scalar1=-8.0, scalar2=None, op0=ALU.mult),
              reads=[st_.key], writes=["clam"])
        P.add("dve", lambda e: e.tensor_scalar(out=clam[:, 2:4], in0=st_.ap[:, 2:4], scalar1=-16.0, scalar2=None, op0=ALU.mult),
              reads=[st_.key], writes=["clam"])
        putslot(st_)

        wi_n = {"n": 0}
        wo_n = {"n": 0}

        def inproj_tile(m, dst_ap, dst_key):
            ws = wi_n["n"] % NWI
            wi_n["n"] += 1
            P.add("pool", lambda e: e.dma_start(out=wi[:, ws], in_=mix_win[l, m], max_dma_last_dim=4096),
                  writes=[(uid, "wi", ws)], dma_sem=f"wi{ws}")
            bk, bkey = nextbank()
            for k in range(KC):
                P.add("pe", lambda e, k=k: e.matmul(bk[:, 0:NB], lhsT=wi[:, ws, k, :], rhs=u[:, k, :],
                                                   start=(k == 0), stop=(k == KC - 1)),
                      reads=[(uid, "wi", ws), (uid, "u")], writes=[bkey])
            P.add("act", lambda e: e.activation(out=dst_ap, in_=bk[:, 0:NB], func=AF.Copy), reads=[bkey], writes=[dst_key])

        def group_rstd(src_ap, src_key, n, eps):
            sq_ = getslot()
            P.add("act", lambda e: e.activation(out=sq_.bf[:, 0:NB], in_=src_ap, func=AF.Square),
                  reads=[src_key], writes=[sq_.key])
            bk, bkey = nextbank()
            P.add("pe", lambda e: e.matmul(bk[:, 0:NB], lhsT=ones_bd, rhs=sq_.bf[:, 0:NB], start=True, stop=True),
                  reads=[sq_.key, "ones_bd"], writes=[bkey])
            rs = getslot()
            P.add("act", lambda e: e.activation(out=rs.ap[:, 0:NB], in_=bk[:, 0:NB], func=AF.Ln, bias=eps, scale=1.0 / n),
                  reads=[bkey], writes=[rs.key])
            P.add("act", lambda e: e.activation(out=rs.ap[:, 0:NB], in_=rs.ap[:, 0:NB], func=AF.Exp, scale=-0.5), reads=[rs.key], writes=[rs.key])
            putslot(sq_)
            return rs

        class Ev:
            def __init__(self):
                self.done = False

        def wait(ev):
            while not ev.done:
                yield "wait"

        def need(n):
            tid = cur["tid"]
            while True:
                others = sum(max(0, quota.get(t, 0) - live.get(t, 0)) for t in quota if t != tid)
                if len(free_slots) >= n + others:
                    break
                yield "wait"
            quota[tid] = live.get(tid, 0) + n

        NG = NBLK * 4
        ev_prep = [Ev() for _ in range(NG)]
        ev_A = [Ev() for _ in range(NG)]
        ev_QM = [Ev() for _ in range(NG)]
        ev_state = [Ev() for _ in range(NG)]
        ev_out = [Ev() for _ in range(NBLK)]
        ev_u = [Ev() for _ in range(NBLK)]
        ev_ls = [Ev() for _ in range(NBLK)]
        handoff = {}
        gen_banks = [0, 1, 2, 3, 4]
        gb = {"n": 0}

        def nextbank():
            b = gen_banks[gb["n"] % len(gen_banks)]
            gb["n"] += 1
            return banks[b], ("bank", b)

        def NRBk(c):
            return [(uid, "NRBm", c), (uid, "NRBt", c)]

        def NTYk(c):
            return [(uid, "NTYm", c), (uid, "NTYt", c)]

        def RKVk(c, q):
            return [(uid, "RKVm", q, c), (uid, "RKVt", q, c)]

        def inproj_tile(m, dst_ap, dst_key):
            ws = wi_n["n"] % NWI
            wi_n["n"] += 1
            P.add("pool", lambda e: e.dma_start(out=wi[:, ws], in_=mix_win[l, m], max_dma_last_dim=4096),
                  writes=[(uid, "wi", ws)], dma_sem=f"wi{ws}")
            bk, bkey = nextbank()
            for k in range(KC):
                P.add("pe", lambda e, k=k: e.matmul(bk[:, 0:NB], lhsT=wi[:, ws, k, :], rhs=u[:, k, :],
                                                   start=(k == 0), stop=(k == KC - 1)),
                      reads=[(uid, "wi", ws), (uid, "u")], writes=[bkey])
            P.add("act", lambda e: e.activation(out=dst_ap, in_=bk[:, 0:NB], func=AF.Copy), reads=[bkey], writes=[dst_key])

        def group_rstd(src_ap, src_key, n, eps):
            sq_ = getslot()
            P.add("act", lambda e: e.activation(out=sq_.bf[:, 0:NB], in_=src_ap, func=AF.Square),
                  reads=[src_key], writes=[sq_.key])
            bk, bkey = nextbank()
            P.add("pe", lambda e: e.matmul(bk[:, 0:NB], lhsT=ones_bd, rhs=sq_.bf[:, 0:NB], start=True, stop=True),
                  reads=[sq_.key, "ones_bd"], writes=[bkey])
            rs = getslot()
            P.add("act", lambda e: e.activation(out=rs.ap[:, 0:NB], in_=bk[:, 0:NB], func=AF.Ln, bias=eps, scale=1.0 / n),
                  reads=[bkey], writes=[rs.key])
            P.add("act", lambda e: e.activation(out=rs.ap[:, 0:NB], in_=rs.ap[:, 0:NB], func=AF.Exp, scale=-0.5), reads=[rs.key], writes=[rs.key])
            putslot(sq_)
            return rs

        def rw_tile(m):
            zz = getslot()
            inproj_tile(10 + m, zz.ap[:, 1:1 + NB], zz.key)
            P.add("act", lambda e: e.activation(out=zz.ap[:, 0:1], in_=rw_halo[:, m:m + 1], func=AF.Copy), reads=[("rw_halo", m)], writes=[zz.key])
            P.add("act", lambda e: e.activation(out=rw_halo[:, m:m + 1], in_=zz.ap[:, NB:NB + 1], func=AF.Copy), reads=[zz.key], writes=[("rw_halo", m)])
            dd = getslot()
            P.add("dve", lambda e: e.tensor_tensor(out=dd.ap[:, 0:NB], in0=zz.ap[:, 0:NB], in1=zz.ap[:, 1:1 + NB], op=ALU.subtract),
                  reads=[zz.key], writes=[dd.key])
            P.add("dve", lambda e: e.scalar_tensor_tensor(
                out=zz.ap[:, 1:1 + NB], in0=dd.ap[:, 0:NB], scalar=pvc(l, 26 + m), in1=zz.ap[:, 1:1 + NB], op0=ALU.mult, op1=ALU.add),
                reads=[zz.key, dd.key, "pv"], writes=[zz.key])
            putslot(dd)
            return zz

        def front():
            for b in range(NBLK):
                tsl = slice(b * NB, (b + 1) * NB)
                yq = b % 2
                if b >= 1:
                    yield from wait(ev_ls[b - 1])
                yield from need(1)
                P.add("act", lambda e: e.activation(out=sqb, in_=h[:, :, tsl], func=AF.Square),
                      reads=[hk(c, b) for c in range(KC)], writes=[(uid, "sqb")])
                bk, bkey = nextbank()
                for c in range(KC):
                    P.add("pe", lambda e, c=c: e.matmul(bk[:, 0:NB], lhsT=ones_bf, rhs=sqb[:, c, :], start=(c == 0), stop=(c == KC - 1)),
                          reads=[(uid, "sqb"), "ones"], writes=[bkey])
                rs = getslot()
                rstd_from_sumsq(bk[:, 0:NB], rs.ap[:, 0:NB], D, bkey, rs.key)
                for c in range(KC):
                    P.add("dve", lambda e, c=c: e.scalar_tensor_tensor(
                        out=u[:, c, :], in0=h[:, c, tsl], scalar=g_ap(l, 2, c), in1=rs.ap[:, 0:NB], op0=ALU.mult, op1=ALU.mult),
                        reads=[hk(c, b), rs.key, "gsb"], writes=[(uid, "u")])
                putslot(rs)
                endphase()
                yield
                ev_u[b].done = True
                yield from need(5)
                zlo = rw_tile(12)
                lob = getslot()
                ltmp = getslot()
                act_sigmoid(ltmp.ap[0:32, 0:NB], zlo.ap[0:32, 1:1 + NB], [zlo.key], ltmp.key, scale=2.0)
                P.add("dve", lambda e: e.tensor_scalar(out=lob.bf[0:32, 0:NB], in0=ltmp.ap[0:32, 0:NB], scalar1=2.0, scalar2=-1.0, op0=ALU.mult, op1=ALU.add),
                      reads=[ltmp.key], writes=[lob.key])
                P.add("act", lambda e: e.activation(out=lob.bf[32:64, 0:NB], in_=zlo.ap[32:64, 1:1 + NB], func=AF.Copy), reads=[zlo.key], writes=[lob.key])
                act_sigmoid(ltmp.ap[64:128, 0:NB], zlo.ap[64:128, 1:1 + NB], [zlo.key], ltmp.key)
                P.add("act", lambda e: e.activation(out=lob.bf[64:128, 0:NB], in_=ltmp.ap[64:128, 0:NB], func=AF.Copy), reads=[ltmp.key], writes=[lob.key])
                putslot(zlo, ltmp)
                endphase()
                yield
                for hp in range(4):
                    g = b * 4 + hp
                    q = g % 2
                    yield from need(13)
                    rr = rw_tile(hp)
                    yield
                    kx = rw_tile(4 + hp)
                    yield
                    vv = rw_tile(8 + hp)
                    yield
                    r_ap, k_ap, v_ap = rr.ap[:, 1:1 + NB], kx.ap[:, 1:1 + NB], vv.ap[:, 1:1 + NB]
                    cols = slice(hp * 128, (hp + 1) * 128)
                    logw, av, gv = getslot(), getslot(), getslot()
                    for (p0, p1, dst, fn, bcol) in ((0, 32, logw, AF.Sigmoid, 39 + hp), (32, 64, av, AF.Sigmoid, 43 + hp), (64, 128, gv, AF.Copy, None)):
                        bk, bkey = nextbank()
                        P.add("pe", lambda e, p0=p0, p1=p1, bk=bk: e.matmul(bk[:, 0:NB], lhsT=lora_sb[p0:p1, cols], rhs=lob.bf[p0:p1, 0:NB],
                                                                           start=True, stop=True),
                              reads=[(uid, "lora"), lob.key], writes=[bkey])
                        if bcol is None:
                            P.add("act", lambda e, dst=dst, bk=bk: e.activation(out=dst.ap[:, 0:NB], in_=bk[:, 0:NB], func=AF.Copy),
                                  reads=[bkey], writes=[dst.key])
                        else:
                            act_sigmoid(dst.ap[:, 0:NB], bk[:, 0:NB], [bkey], dst.key, nbias=npvc(l, bcol))
                    P.add("dve", lambda e: e.tensor_scalar(out=logw.ap[:, 0:NB], in0=logw.ap[:, 0:NB], scalar1=-0.6065306597126334, scalar2=None, op0=ALU.mult),
                          reads=[logw.key], writes=[logw.key])
                    yield
                    kk = getslot()
                    P.add("dve", lambda e: e.tensor_scalar(out=kk.ap[:, 0:NB], in0=k_ap, scalar1=pvc(l, 47 + hp), scalar2=None, op0=ALU.mult),
                          reads=[kx.key, "pv"], writes=[kk.key])
                    sq_ = getslot()
                    P.add("act", lambda e: e.activation(out=sq_.bf[:, 0:NB], in_=kk.ap[:, 0:NB], func=AF.Square),
                          reads=[kk.key], writes=[sq_.key])
                    bk, bkey = nextbank()
                    P.add("pe", lambda e, bk=bk: e.matmul(bk[:, 0:NB], lhsT=ones_bd, rhs=sq_.bf[:, 0:NB], start=True, stop=True),
                          reads=[sq_.key, "ones_bd"], writes=[bkey])
                    rn = getslot()
                    P.add("dve", lambda e, bk=bk: e.tensor_scalar(out=rn.ap[:, 0:NB], in0=bk[:, 0:NB], scalar1=1e-24, scalar2=None, op0=ALU.max),
                          reads=[bkey], writes=[rn.key])
                    P.add("act", lambda e: e.activation(out=rn.ap[:, 0:NB], in_=rn.ap[:, 0:NB], func=AF.Ln), reads=[rn.key], writes=[rn.key])
                    P.add("act", lambda e: e.activation(out=rn.ap[:, 0:NB], in_=rn.ap[:, 0:NB], func=AF.Exp, scale=-0.5), reads=[rn.key], writes=[rn.key])
                    P.add("dve", lambda e: e.tensor_tensor(out=kk.ap[:, 0:NB], in0=kk.ap[:, 0:NB], in1=rn.ap[:, 0:NB], op=ALU.mult),
                          reads=[rn.key, kk.key], writes=[kk.key])
                    putslot(sq_, rn)
                    yield
                    kp = getslot()
                    P.add("dve", lambda e: e.tensor_scalar(out=kp.ap[:, 0:NB], in0=av.ap[:, 0:NB], scalar1=-1.0, scalar2=pvc(l, 51 + hp),
                                                           op0=ALU.add, op1=ALU.mult), reads=[av.key, "pv"], writes=[kp.key])
                    P.add("dve", lambda e: e.scalar_tensor_tensor(out=kp.ap[:, 0:NB], in0=kp.ap[:, 0:NB], scalar=1.0, in1=k_ap,
                                                                  op0=ALU.add, op1=ALU.mult), reads=[kp.key, kx.key], writes=[kp.key])
                    bvec = getslot()
                    P.add("dve", lambda e: e.tensor_tensor(out=bvec.ap[:, 0:NB], in0=kk.ap[:, 0:NB], in1=av.ap[:, 0:NB], op=ALU.mult),
                          reads=[kk.key, av.key], writes=[bvec.key])
                    putslot(av)
                    rkb = getslot()
                    P.add("dve", lambda e: e.scalar_tensor_tensor(
                        out=rkb.bf[:, 0:NB], in0=r_ap, scalar=pvc(l, 55 + hp), in1=kp.ap[:, 0:NB], op0=ALU.mult, op1=ALU.mult),
                        reads=[rr.key, kp.key, "pv"], writes=[rkb.key])
                    bk, bkey = nextbank()
                    P.add("pe", lambda e, bk=bk: e.matmul(bk[:, 0:NB], lhsT=ones_bd, rhs=rkb.bf[:, 0:NB], start=True, stop=True),
                          reads=[rkb.key, "ones_bd"], writes=[bkey])
                    bonus = getslot()
                    P.add("dve", lambda e, bk=bk: e.tensor_tensor(out=bonus.ap[:, 0:NB], in0=bk[:, 0:NB], in1=v_ap, op=ALU.mult),
                          reads=[bkey, vv.key], writes=[bonus.key])
                    putslot(rkb)
                    yield
                    Ls = getslot()
                    P.add("dve", lambda e: e.tensor_tensor_scan(out=Ls.ap[:, 0:NB], data0=mask01, data1=logw.ap[:, 0:NB], initial=0.0,
                                                                op0=ALU.mult, op1=ALU.add), reads=[logw.key, "consts"], writes=[Ls.key])
                    L3 = Ls.ap[:, 0:NB].rearrange("p (c t) -> p c t", t=CH)
                    Lend = L3[:, :, CH - 1:CH]
                    if g >= 1:
                        yield from wait(ev_A[g - 1])
                    if g >= 2:
                        yield from wait(ev_QM[g - 2])

                    def padded(dst, col0, in0_ap, in1_ap, keys_in, wkeys, neg=False, eng="dve"):
                        for hh in range(2):
                            ps_ = slice(hh * 64, (hh + 1) * 64)
                            o_ap = dst[ps_, :, col0 + hh * CH:col0 + (hh + 1) * CH]
                            a0 = in0_ap[ps_].rearrange("p (c t) -> p c t", t=CH)
                            if in1_ap is None:
                                P.add(eng, lambda e: e.activation(out=o_ap, in_=a0, func=AF.Copy), reads=keys_in, writes=wkeys)
                            else:
                                a1 = in1_ap[ps_].rearrange("p (c t) -> p c t", t=CH)
                                if neg:
                                    P.add(eng, lambda e: e.scalar_tensor_tensor(out=o_ap, in0=a0, scalar=-1.0, in1=a1, op0=ALU.mult, op1=ALU.mult),
                                          reads=keys_in, writes=wkeys)
                                else:
                                    P.add(eng, lambda e: e.tensor_tensor(out=o_ap, in0=a0, in1=a1, op=ALU.mult), reads=keys_in, writes=wkeys)

                    ARq = AR_pad[:, q]
                    ARkeys = [(uid, "AR", q, c) for c in range(NCHB)]
                    E = getslot()
                    P.add("act", lambda e: e.activation(out=E.ap[:, 0:NB], in_=Ls.ap[:, 0:NB], func=AF.Exp), reads=[Ls.key], writes=[E.key])
                    padded(ARq, C2, r_ap, E.ap[:, 0:NB], [rr.key, E.key], ARkeys)
                    P.add("act", lambda e: e.activation(out=Pend[:, q, :], in_=L3[:, :, CH - 1], func=AF.Exp), reads=[Ls.key], writes=[(uid, "Pend", q)])
                    yield
                    E2 = getslot()
                    P.add("act", lambda e: e.activation(out=E2.ap[:, 0:NB], in_=Ls.ap[:, 0:NB], func=AF.Exp, scale=-1.0), reads=[Ls.key], writes=[E2.key])
                    padded(BK_pad, 0, bvec.ap[:, 0:NB], E2.ap[:, 0:NB], [bvec.key, E2.key], [(uid, "BK", c) for c in range(NCHB)])
                    padded(BK_pad, C2, kp.ap[:, 0:NB], E2.ap[:, 0:NB], [kp.key, E2.key], [(uid, "BK", c) for c in range(NCHB)], eng="dve")
                    yield
                    P.add("dve", lambda e: e.tensor_tensor(out=E.ap[:, 0:NB], in0=Ls.ap[:, 0:NB], in1=logw.ap[:, 0:NB], op=ALU.subtract),
                          reads=[Ls.key, logw.key], writes=[E.key])
                    P.add("act", lambda e: e.activation(out=E.ap[:, 0:NB], in_=E.ap[:, 0:NB], func=AF.Exp), reads=[E.key], writes=[E.key])
                    padded(ARq, 0, kk.ap[:, 0:NB], E.ap[:, 0:NB], [kk.key, E.key], ARkeys, neg=True)
                    yield
                    E3 = E2.ap[:, 0:NB].rearrange("p (c t) -> p c t", t=CH)
                    P.add("dve", lambda e: e.tensor_tensor(out=E3, in0=Lend.to_broadcast([128, NCHB, CH]), in1=L3, op=ALU.subtract),
                          reads=[Ls.key], writes=[E2.key])
                    P.add("act", lambda e: e.activation(out=E2.ap[:, 0:NB], in_=E2.ap[:, 0:NB], func=AF.Exp), reads=[E2.key], writes=[E2.key])
                    padded(Bh_pad, 0, bvec.ap[:, 0:NB], E2.ap[:, 0:NB], [bvec.key, E2.key], [(uid, "Bh", c) for c in range(NCHB)])
                    padded(Kh_pad, 0, kp.ap[:, 0:NB], E2.ap[:, 0:NB], [kp.key, E2.key], [(uid, "Kh", c) for c in range(NCHB)], eng="dve")
                    padded(V_pad, 0, v_ap, None, [vv.key], [(uid, "Vp", c) for c in range(NCHB)], eng="act")
                    putslot(E, E2, Ls, logw, kk, kp, bvec, rr, kx, vv)
                    disown(gv, bonus)
                    handoff[g] = (gv, bonus)
                    endphase()
                    ev_prep[g].done = True
                    yield
                putslot(lob)

        def ls_thread():
            for b in range(NBLK):
                tsl = slice(b * NB, (b + 1) * NB)
                yq = b % 2
                yield from wait(ev_u[b])
                if b >= 2:
                    yield from wait(ev_out[b - 2])
                for i in range(2):
                    yield from need(7)
                    xs = getslot()
                    gx = getslot()
                    inproj_tile(i, xs.ap[:, 3:3 + NB], xs.key)
                    inproj_tile(2 + i, gx.ap[:, 0:NB], gx.key)
                    yield
                    P.add("act", lambda e: e.activation(out=xs.ap[:, 0:3], in_=lru_halo[:, i, :], func=AF.Copy), reads=["lru_halo"], writes=[xs.key])
                    P.add("act", lambda e: e.activation(out=lru_halo[:, i, :], in_=xs.ap[:, NB:NB + 3], func=AF.Copy), reads=[xs.key], writes=["lru_halo"])
                    uc = getslot()
                    P.add("dve", lambda e: e.tensor_scalar(out=uc.ap[:, 0:NB], in0=xs.ap[:, 3:3 + NB], scalar1=pvc(l, i * 4 + 3), scalar2=pvc(l, 8 + i),
                                                           op0=ALU.mult, op1=ALU.add), reads=[xs.key, "pv"], writes=[uc.key])
                    for k in (2, 1, 0):
                        P.add("dve", lambda e, k=k: e.scalar_tensor_tensor(
                            out=uc.ap[:, 0:NB], in0=xs.ap[:, k:k + NB], scalar=pvc(l, i * 4 + k), in1=uc.ap[:, 0:NB],
                            op0=ALU.mult, op1=ALU.add), reads=[xs.key, uc.key, "pv"], writes=[uc.key])
                    putslot(xs)
                    yield
                    ucb = getslot()
                    P.add("act", lambda e: e.activation(out=ucb.bf[:, 0:NB], in_=uc.ap[:, 0:NB], func=AF.Copy), reads=[uc.key], writes=[ucb.key])
                    rg = getslot()
                    ig = getslot()
                    for ax, dst, bcol in ((0, rg, 10 + i), (1, ig, 12 + i)):
                        bk, bkey = nextbank()
                        P.add("pe", lambda e, ax=ax, bk=bk: e.matmul(bk[:, 0:NB], lhsT=gate_w[:, ax, i, :], rhs=ucb.bf[:, 0:NB], start=True, stop=True),
                              reads=[(uid, "gate_w"), ucb.key], writes=[bkey])
                        act_sigmoid(dst.ap[:, 0:NB], bk[:, 0:NB], [bkey], dst.key, nbias=npvc(l, bcol))
                    putslot(ucb)
                    yield
                    av = getslot()
                    a2 = getslot()
                    P.add("act", lambda e: e.activation(out=av.ap[:, 0:NB], in_=rg.ap[:, 0:NB], func=AF.Exp, scale=clam[:, i:i + 1]),
                          reads=[rg.key, "clam"], writes=[av.key])
                    P.add("act", lambda e: e.activation(out=a2.ap[:, 0:NB], in_=rg.ap[:, 0:NB], func=AF.Exp, scale=clam[:, 2 + i:3 + i]),
                          reads=[rg.key, "clam"], writes=[a2.key])
                    P.add("act", lambda e: e.activation(out=a2.ap[:, 0:NB], in_=a2.ap[:, 0:NB], func=AF.Ln, bias=1.0, scale=-1.0),
                          reads=[a2.key], writes=[a2.key])
                    P.add("act", lambda e: e.activation(out=a2.ap[:, 0:NB], in_=a2.ap[:, 0:NB], func=AF.Exp, scale=0.5),
                          reads=[a2.key], writes=[a2.key])
                    putslot(rg)
                    P.add("dve", lambda e: e.tensor_tensor(out=ig.ap[:, 0:NB], in0=ig.ap[:, 0:NB], in1=uc.ap[:, 0:NB], op=ALU.mult),
                          reads=[ig.key, uc.key], writes=[ig.key])
                    P.add("dve", lambda e: e.tensor_tensor(out=ig.ap[:, 0:NB], in0=ig.ap[:, 0:NB], in1=a2.ap[:, 0:NB], op=ALU.mult),
                          reads=[ig.key, a2.key], writes=[ig.key])
                    putslot(uc, a2)
                    hs = getslot()
                    P.add("dve", lambda e: e.tensor_tensor_scan(
                        out=hs.ap[:, 0:NB], data0=av.ap[:, 0:NB], data1=ig.ap[:, 0:NB], initial=lru_state[:, i:i + 1],
                        op0=ALU.mult, op1=ALU.add), reads=[av.key, ig.key, "lru_state"], writes=[hs.key])
                    P.add("act", lambda e: e.activation(out=lru_state[:, i:i + 1], in_=hs.ap[:, NB - 1:NB], func=AF.Copy), reads=[hs.key], writes=["lru_state"])
                    putslot(av, ig)
                    yield
                    t1 = getslot()
                    P.add("act", lambda e: e.activation(out=t1.ap[:, 0:NB], in_=gx.ap[:, 0:NB], func=AF.Square), reads=[gx.key], writes=[t1.key])
                    P.add("dve", lambda e: e.tensor_scalar(out=t1.ap[:, 0:NB], in0=t1.ap[:, 0:NB], scalar1=0.044715, scalar2=1.0,
                                                           op0=ALU.mult, op1=ALU.add), reads=[t1.key], writes=[t1.key])
                    P.add("dve", lambda e: e.tensor_tensor(out=t1.ap[:, 0:NB], in0=t1.ap[:, 0:NB], in1=gx.ap[:, 0:NB], op=ALU.mult),
                          reads=[gx.key, t1.key], writes=[t1.key])
                    act_sigmoid(t1.ap[:, 0:NB], t1.ap[:, 0:NB], [t1.key], t1.key, scale=1.5957691216)
                    P.add("dve", lambda e: e.tensor_tensor(out=t1.ap[:, 0:NB], in0=t1.ap[:, 0:NB], in1=gx.ap[:, 0:NB], op=ALU.mult),
                          reads=[gx.key, t1.key], writes=[t1.key])
                    P.add("dve", lambda e: e.tensor_tensor(out=t1.ap[:, 0:NB], in0=t1.ap[:, 0:NB], in1=hs.ap[:, 0:NB], op=ALU.mult),
                          reads=[hs.key, t1.key], writes=[t1.key])
                    putslot(gx, hs)
                    yield
                    rs = group_rstd(t1.ap[:, 0:NB], t1.key, 64, RMS_EPS)
                    P.add("dve", lambda e: e.scalar_tensor_tensor(
                        out=ycat[:, yq, i, :], in0=t1.ap[:, 0:NB], scalar=pvc(l, 16 + i), in1=rs.ap[:, 0:NB], op0=ALU.mult, op1=ALU.mult),
                        reads=[t1.key, rs.key, "pv"], writes=[(uid, "ycat", yq, i)])
                    putslot(t1, rs)
                    endphase()
                    yield
                for i in range(2):
                    yield from need(5)
                    Bs, Cs, Xs = getslot(), getslot(), getslot()
                    inproj_tile(4 + i, Bs.ap[:, 0:NB], Bs.key)
                    inproj_tile(6 + i, Cs.ap[:, 0:NB], Cs.key)
                    inproj_tile(8 + i, Xs.ap[:, 0:NB], Xs.key)
                    yield
                    vs = getslot()
                    P.add("dve", lambda e: e.tensor_tensor(out=vs.ap[:, 2:2 + NB], in0=Cs.ap[:, 0:NB], in1=Xs.ap[:, 0:NB], op=ALU.mult),
                          reads=[Cs.key, Xs.key], writes=[vs.key])
                    P.add("act", lambda e: e.activation(out=vs.ap[:, 0:2], in_=sc_halo[:, i, :], func=AF.Copy), reads=["sc_halo"], writes=[vs.key])
                    P.add("act", lambda e: e.activation(out=sc_halo[:, i, :], in_=vs.ap[:, NB:NB + 2], func=AF.Copy), reads=[vs.key], writes=["sc_halo"])
                    putslot(Cs, Xs)
                    yc = getslot()
                    P.add("dve", lambda e: e.tensor_scalar(out=yc.ap[:, 0:NB], in0=vs.ap[:, 2:2 + NB], scalar1=pvc(l, 18 + i * 3 + 2),
                                                           scalar2=None, op0=ALU.mult), reads=[vs.key, "pv"], writes=[yc.key])
                    for k in (1, 0):
                        P.add("dve", lambda e, k=k: e.scalar_tensor_tensor(
                            out=yc.ap[:, 0:NB], in0=vs.ap[:, k:k + NB], scalar=pvc(l, 18 + i * 3 + k), in1=yc.ap[:, 0:NB],
                            op0=ALU.mult, op1=ALU.add), reads=[vs.key, yc.key, "pv"], writes=[yc.key])
                    P.add("dve", lambda e: e.tensor_tensor(out=yc.ap[:, 0:NB], in0=yc.ap[:, 0:NB], in1=Bs.ap[:, 0:NB], op=ALU.mult),
                          reads=[Bs.key, yc.key], writes=[yc.key])
                    putslot(vs, Bs)
                    yield
                    rs = group_rstd(yc.ap[:, 0:NB], yc.key, 64, RMS_EPS)
                    P.add("dve", lambda e: e.scalar_tensor_tensor(
                        out=ycat[:, yq, 2 + i, :], in0=yc.ap[:, 0:NB], scalar=pvc(l, 24 + i), in1=rs.ap[:, 0:NB], op0=ALU.mult, op1=ALU.mult),
                        reads=[yc.key, rs.key, "pv"], writes=[(uid, "ycat", yq, 2 + i)])
                    putslot(yc, rs)
                    endphase()
                    yield
                ev_ls[b].done = True
                yield

        def chunk_thread():
            ev = {"n": 0}

            def evac_copy(out_ap, in_ap, reads, writes, wt=2):
                ev["n"] += 1
                if ev["n"] % wt != 0:
                    P.add("act", lambda e: e.activation(out=out_ap, in_=in_ap, func=AF.Copy), reads=reads, writes=writes)
                else:
                    P.add("dve", lambda e: e.tensor_copy(out=out_ap, in_=in_ap), reads=reads, writes=writes)

            for g in range(NG):
                q = g % 2
                yield from wait(ev_prep[g])
                if g >= 2:
                    yield from wait(ev_state[g - 2])
                ARq = AR_pad[:, q]
                for c in range(NCHB):
                    ak = (uid, "AR", q, c)
                    bk, bkey = nextbank()
                    P.add("pe", lambda e: e.matmul(bk[0:C2, 0:2 * C2], lhsT=BK_pad[:, c, 0:C2], rhs=ARq[:, c, :], start=True, stop=True),
                          reads=[(uid, "BK", c), ak], writes=[bkey])
                    P.add("pe", lambda e: e.matmul(bk[0:C2, 2 * C2:300], lhsT=Bh_pad[:, c, :], rhs=ident_bf, start=True, stop=True),
                          reads=[(uid, "Bh", c), "identbf"], writes=[bkey])
                    P.add("dve", lambda e: e.tensor_tensor(out=NRB[0:C2, c, 0:2 * C2], in0=bk[0:C2, 0:2 * C2], in1=mask1[:, 0:2 * C2], op=ALU.mult),
                          reads=[bkey, "consts"], writes=[(uid, "NRBm", c)])
                    P.add("act", lambda e: e.activation(out=NRB[0:C2, c, 2 * C2:300], in_=bk[0:C2, 2 * C2:300], func=AF.Copy),
                          reads=[bkey], writes=[(uid, "NRBt", c)])
                    bk2, bkey2 = nextbank()
                    P.add("pe", lambda e: e.matmul(bk2[0:C2, 0:2 * C2], lhsT=ARq[:, c, 0:C2], rhs=BK_pad[:, c, :], start=True, stop=True),
                          reads=[(uid, "BK", c), ak], writes=[bkey2])
                    P.add("pe", lambda e: e.matmul(bk2[0:C2, 2 * C2:300], lhsT=ARq[:, c, 0:C2], rhs=ident_bf, start=True, stop=True),
                          reads=[ak, "identbf"], writes=[bkey2])
                    P.add("dve", lambda e: e.tensor_tensor(out=NTY[0:C2, c, 0:2 * C2], in0=bk2[0:C2, 0:2 * C2], in1=mask3[:, 0:2 * C2], op=ALU.mult),
                          reads=[bkey2, "consts"], writes=[(uid, "NTYm", c)])
                    P.add("act", lambda e: e.activation(out=NTY[0:C2, c, 2 * C2:300], in_=bk2[0:C2, 2 * C2:300], func=AF.Copy),
                          reads=[bkey2], writes=[(uid, "NTYt", c)])
                    bk3, bkey3 = nextbank()
                    P.add("pe", lambda e: e.matmul(bk3[0:C2, 0:C2], lhsT=BK_pad[:, c, C2:2 * C2], rhs=ARq[:, c, C2:2 * C2], start=True, stop=True),
                          reads=[(uid, "BK", c), ak], writes=[bkey3])
                    P.add("pe", lambda e: e.matmul(bk3[0:C2, C2:C2 + 128], lhsT=Kh_pad[:, c, :], rhs=ident_bf, start=True, stop=True),
                          reads=[(uid, "Kh", c), "identbf"], writes=[bkey3])
                    P.add("pe", lambda e: e.matmul(bk3[0:C2, C2 + 128:342], lhsT=V_pad[:, c, :], rhs=ident_bf, start=True, stop=True),
                          reads=[(uid, "Vp", c), "identbf"], writes=[bkey3])
                    P.add("dve", lambda e: e.tensor_tensor(out=RKV[0:C2, q, c, 0:C2], in0=bk3[0:C2, 0:C2], in1=mask2[:, 0:C2], op=ALU.mult),
                          reads=[bkey3, "consts"], writes=[(uid, "RKVm", q, c)])
                    P.add("act", lambda e: e.activation(out=RKV[0:C2, q, c, C2:342], in_=bk3[0:C2, C2:342], func=AF.Copy),
                          reads=[bkey3], writes=[(uid, "RKVt", q, c)])
                    yield
                ev_A[g].done = True
                for lev in range(6):
                    for c in range(NCHB):
                        if lev == 0:
                            Nk, NTk, nkey = NRB[0:C2, c, 0:C2], NTY[0:C2, c, 0:C2], [(uid, "NRBm", c), (uid, "NTYm", c)]
                        else:
                            Nk, NTk, nkey = NNb[0:C2, lev % 2, c, 0:C2], NNb[0:C2, lev % 2, c, C2:2 * C2], [(uid, "NN", lev % 2, c)]
                        Pc = PT[0:C2, c, :]
                        pkey = (uid, "PT", c)
                        rhsP = ident_bf[0:C2, 0:C2] if lev == 0 else Pc
                        bk, bkey = nextbank()
                        P.add("pe", lambda e: e.matmul(bk[0:C2, 0:C2], lhsT=ident_bf[0:C2, 0:C2], rhs=rhsP, start=True, stop=False),
                              reads=[pkey, "identbf"], writes=[bkey])
                        P.add("pe", lambda e: e.matmul(bk[0:C2, 0:C2], lhsT=NTk, rhs=rhsP, start=False, stop=True),
                              reads=[pkey, "identbf"] + nkey, writes=[bkey])
                        evac_copy(Pc, bk[0:C2, 0:C2], [bkey], [pkey])
                        if lev < 5:
                            bk2, bkey2 = nextbank()
                            if lev < 4:
                                P.add("pe", lambda e: e.matmul(bk2[0:C2, 0:C2], lhsT=NTk, rhs=Nk, start=True, stop=True), reads=nkey, writes=[bkey2])
                                P.add("pe", lambda e: e.matmul(bk2[0:C2, C2:2 * C2], lhsT=Nk, rhs=NTk, start=True, stop=True), reads=nkey, writes=[bkey2])
                                evac_copy(NNb[0:C2, (lev + 1) % 2, c, 0:2 * C2], bk2[0:C2, 0:2 * C2], [bkey2], [(uid, "NN", (lev + 1) % 2, c)])
                            else:
                                P.add("pe", lambda e: e.matmul(bk2[0:C2, 0:C2], lhsT=Nk, rhs=NTk, start=True, stop=True), reads=nkey, writes=[bkey2])
                                evac_copy(NNb[0:C2, (lev + 1) % 2, c, C2:2 * C2], bk2[0:C2, 0:C2], [bkey2], [(uid, "NN", (lev + 1) % 2, c)])
                        yield
                for c in range(NCHB):
                    YT = NTY[0:C2, c, C2:300]
                    ykeys = [(uid, "NTYm", c), (uid, "NTYt", c)]
                    bk, bkey = nextbank()
                    P.add("pe", lambda e: e.matmul(bk[0:C2, 0:214], lhsT=PT[0:C2, c, :], rhs=YT, start=True, stop=True),
                          reads=ykeys + [(uid, "PT", c)], writes=[bkey])
                    evac_copy(YT, bk[0:C2, 0:214], [bkey], ykeys)
                    yield
                for c in range(NCHB):
                    ak = (uid, "AR", q, c)
                    bk, bkey = nextbank()
                    P.add("pe", lambda e: e.matmul(bk[:, 0:C2], lhsT=NTY[0:C2, c, 2 * C2:300], rhs=NRB[0:C2, c, C2:2 * C2], start=True, stop=False),
                          reads=NTYk(c) + NRBk(c), writes=[bkey])
                    P.add("pe", lambda e: e.matmul(bk[:, 0:C2], lhsT=ident_bf, rhs=ARq[:, c, C2:2 * C2], start=False, stop=True),
                          reads=[ak, "identbf"], writes=[bkey])
                    P.add("pe", lambda e: e.matmul(bk[:, C2:214], lhsT=NTY[0:C2, c, 2 * C2:300], rhs=NRB[0:C2, c, 2 * C2:300], start=True, stop=True),
                          reads=NTYk(c) + NRBk(c), writes=[bkey])
                    evac_copy(QT[:, q, c, :], bk[:, 0:C2], [bkey], [(uid, "QT", q, c)], wt=2)
                    P.add("dve", lambda e: e.scalar_tensor_tensor(out=Mm[:, q, c, :], in0=ident_f, scalar=Pend[:, q, c:c + 1], in1=bk[:, C2:214],
                                                                  op0=ALU.mult, op1=ALU.add),
                          reads=[bkey, "consts", (uid, "Pend", q)], writes=[(uid, "Mm", q, c)])
                    bk2, bkey2 = nextbank()
                    P.add("pe", lambda e: e.matmul(bk2[0:C2, 0:214], lhsT=NTY[0:C2, c, C2:2 * C2], rhs=NRB[0:C2, c, C2:300], start=True, stop=False),
                          reads=NTYk(c) + NRBk(c), writes=[bkey2])
                    P.add("pe", lambda e: e.matmul(bk2[0:C2, 0:214], lhsT=ident_bf[0:C2, 0:C2], rhs=RKV[0:C2, q, c, 0:214], start=False, stop=True),
                          reads=RKVk(c, q) + ["identbf"], writes=[bkey2])
                    evac_copy(GHb[0:C2, q, c, :], bk2[0:C2, 0:214], [bkey2], [(uid, "GH", q, c)], wt=2)
                    yield
                ev_QM[g].done = True

        def state_thread():
            sb = {"n": 0}
            for g in range(NG):
                b, hp = g // 4, g % 4
                q = g % 2
                yq = b % 2
                tsl = slice(b * NB, (b + 1) * NB)
                yield from wait(ev_QM[g])
                yield from need(5)
                gv, bonus = handoff.pop(g)
                yrw = getslot()
                ybk, ybkey = banks[5], ("bank", 5)
                for c in range(NCHB):
                    gidx = (b * NCHB + c)
                    cur, nxt = gidx % 2, (gidx + 1) % 2
                    S_cur, S_nxt = S_buf[:, hp, cur, :], S_buf[:, hp, nxt, :]
                    j = c % 4
                    P.add("pe", lambda e: e.matmul(ybk[:, j * C2:(j + 1) * C2], lhsT=RKV[0:C2, q, c, 214:342], rhs=GHb[0:C2, q, c, 0:C2], start=True, stop=False),
                          reads=RKVk(c, q) + [(uid, "GH", q, c)], writes=[ybkey])
                    P.add("pe", lambda e: e.matmul(ybk[:, j * C2:(j + 1) * C2], lhsT=S_cur, rhs=QT[:, q, c, :], start=False, stop=True),
                          reads=[("S", hp, cur), (uid, "QT", q, c)], writes=[ybkey])
                    if j == 3:
                        c0 = c - 3
                        for hh in range(2):
                            ps_ = slice(hh * 64, (hh + 1) * 64)
                            src = ybk[ps_, 0:4 * C2].rearrange("p (c x) -> p c x", x=C2)[:, :, hh * CH:(hh + 1) * CH]
                            dst = yrw.ap[ps_, c0 * CH:(c0 + 4) * CH].rearrange("p (c t) -> p c t", t=CH)
                            P.add("act", lambda e, src=src, dst=dst: e.activation(out=dst, in_=src, func=AF.Copy), reads=[ybkey], writes=[yrw.key])
                    sbi = 6 + (sb["n"] % 2)
                    sb["n"] += 1
                    sbk, sbkey = banks[sbi], ("bank", sbi)
                    P.add("pe", lambda e: e.matmul(sbk[:, 0:128], lhsT=GHb[0:C2, q, c, C2:214], rhs=RKV[0:C2, q, c, 214:342], start=True, stop=False),
                          reads=RKVk(c, q) + [(uid, "GH", q, c)], writes=[sbkey])
                    P.add("pe", lambda e: e.matmul(sbk[:, 0:128], lhsT=Mm[:, q, c, :], rhs=S_cur, start=False, stop=True),
                          reads=[("S", hp, cur), (uid, "Mm", q, c)], writes=[sbkey])
                    P.add("act", lambda e: e.activation(out=S_nxt, in_=sbk[:, 0:128], func=AF.Copy), reads=[sbkey], writes=[("S", hp, nxt)])
                    yield
                ev_state[g].done = True
                yb = getslot()
                P.add("act", lambda e: e.activation(out=yb.bf[:, 0:NB], in_=yrw.ap[:, 0:NB], func=AF.Copy), reads=[yrw.key], writes=[yb.key])
                bk, bkey = nextbank()
                P.add("pe", lambda e: e.matmul(bk[:, 0:NB], lhsT=ones_bd, rhs=yb.bf[:, 0:NB], start=True, stop=True),
                      reads=[yb.key, "ones_bd"], writes=[bkey])
                P.add("dve", lambda e: e.scalar_tensor_tensor(out=yrw.ap[:, 0:NB], in0=bk[:, 0:NB], scalar=-1.0 / 64, in1=yrw.ap[:, 0:NB],
                                                              op0=ALU.mult, op1=ALU.add), reads=[bkey, yrw.key], writes=[yrw.key])
                putslot(yb)
                yield
                rs = group_rstd(yrw.ap[:, 0:NB], yrw.key, 64, LNX_EPS)
                P.add("dve", lambda e: e.tensor_tensor(out=yrw.ap[:, 0:NB], in0=yrw.ap[:, 0:NB], in1=rs.ap[:, 0:NB], op=ALU.mult),
                      reads=[yrw.key, rs.key], writes=[yrw.key])
                P.add("dve", lambda e: e.tensor_scalar(out=yrw.ap[:, 0:NB], in0=yrw.ap[:, 0:NB], scalar1=pvc(l, 59 + hp), scalar2=pvc(l, 63 + hp),
                                                       op0=ALU.mult, op1=ALU.add), reads=[yrw.key, "pv"], writes=[yrw.key])
                P.add("dve", lambda e: e.tensor_tensor(out=yrw.ap[:, 0:NB], in0=yrw.ap[:, 0:NB], in1=bonus.ap[:, 0:NB], op=ALU.add),
                      reads=[yrw.key, bonus.key], writes=[yrw.key])
                P.add("dve", lambda e: e.tensor_tensor(out=ycat[:, yq, 4 + hp, :], in0=yrw.ap[:, 0:NB], in1=gv.ap[:, 0:NB], op=ALU.mult),
                      reads=[yrw.key, gv.key], writes=[(uid, "ycat", yq, 4 + hp)])
                putslot(rs, yrw, bonus, gv)
                endphase()
                yield
                if hp == 3:
                    yield from need(10)
                    om = [getslot() for _ in range(KC)]
                    for m in range(KC):
                        ws = wo_n["n"] % NWO
                        wo_n["n"] += 1
                        P.add("pool", lambda e: e.dma_start(out=wo[:, ws], in_=mix_wout[l, m], max_dma_last_dim=4096),
                              writes=[(uid, "wo", ws)], dma_sem=f"wo{ws}")
                        bk, bkey = nextbank()
                        for k in range(KC):
                            P.add("pe", lambda e, k=k: e.matmul(bk[:, 0:NB], lhsT=wo[:, ws, k, :], rhs=ycat[:, yq, k, :], start=(k == 0), stop=(k == KC - 1)),
                                  reads=[(uid, "wo", ws), (uid, "ycat", yq, k)], writes=[bkey])
                        P.add("act", lambda e: e.activation(out=om[m].ap[:, 0:NB], in_=bk[:, 0:NB], func=AF.Copy), reads=[bkey], writes=[om[m].key])
                        P.add("act", lambda e: e.activation(out=sqo[:, m, :], in_=bk[:, 0:NB], func=AF.Square), reads=[bkey], writes=[(uid, "sqo")])
                        yield
                    bk, bkey = nextbank()
                    for c in range(KC):
                        P.add("pe", lambda e, c=c: e.matmul(bk[:, 0:NB], lhsT=ones_bf, rhs=sqo[:, c, :], start=(c == 0), stop=(c == KC - 1)),
                              reads=[(uid, "sqo"), "ones"], writes=[bkey])
                    rs = getslot()
                    rstd_from_sumsq(bk[:, 0:NB], rs.ap[:, 0:NB], D, bkey, rs.key)
                    for m in range(KC):
                        P.add("dve", lambda e, m=m: e.scalar_tensor_tensor(out=om[m].ap[:, 0:NB], in0=om[m].ap[:, 0:NB], scalar=g_ap(l, 3, m),
                                                                           in1=rs.ap[:, 0:NB], op0=ALU.mult, op1=ALU.mult),
                              reads=[om[m].key, rs.key, "gsb"], writes=[om[m].key])
                        P.add("pool", lambda e, m=m: e.tensor_tensor(out=h[:, m, tsl], in0=h[:, m, tsl], in1=om[m].ap[:, 0:NB], op=ALU.add),
                              reads=[hk(m, b), om[m].key], writes=[hk(m, b)])
                    putslot(rs, *om)
                    endphase()
                    ev_out[b].done = True
                    yield

        threads = [("front", front()), ("ls", ls_thread()), ("chunk", chunk_thread()), ("state", state_thread())]
        stall = 0
        while threads:
            progressed = False
            for tid_th in list(threads):
                tid, th = tid_th
                cur["tid"] = tid
                try:
                    r = next(th)
                    if r != "wait":
                        progressed = True
                except StopIteration:
                    threads.remove(tid_th)
                    quota.pop(tid, None)
                    progressed = True
            stall = 0 if progressed else stall + 1
            assert stall < 10, "thread scheduler deadlock"
        cur["tid"] = None
        A.release(m0)

    for (l, s) in stages:
        if s == 0:
            ffn_stage(l, 0)
        elif s == 2:
            ffn_stage(l, 1)
        else:
            mixer_stage(l)

    P.barrier()
    for c in range(KC):
        if dbg_h:
            P.add("sp", lambda e, c=c: e.dma_start(out=outT[c], in_=h[:, c, :]),
                  reads=[hk(c, b) for b in range(NBLK)], dma_sem="st")
        else:
            P.add("sp", lambda e, c=c: e.dma_start(out=outT[c], in_=h[:, c, NMETA:]),
                  reads=[hk(c, b) for b in range(NBLK)], dma_sem="st")
    P.final_dma.append(("st", P.dma_cnt["st"]))

    sem_names = sorted(P.dma_cnt.keys())
    from contextlib import ExitStack
    with ExitStack() as es:
        eng_sems = {e: es.enter_context(nc.semaphore(f"s_{e}")) for e in Prog.CE}
        dma_sems = {s: es.enter_context(nc.semaphore(f"d_{s}")) for s in sem_names}
        block = es.enter_context(nc.Block())
        P.emit(block, eng_sems, dma_sems)
    print("arena peak bytes/partition:", A.peak, " ops:", {e: len(s) for e, s in P.streams.items()})
    return nc


def _tile_w_in_ffn(w):
    Lh = w.shape[0]
    w5 = w.reshape(Lh, KC, 128, 2, JT, 128)
    return np.ascontiguousarray(w5.transpose(0, 4, 2, 1, 3, 5)).reshape(Lh, JT, 128, KC, 256)


def _tile_w_out_ffn(w):
    Lh = w.shape[0]
    w5 = w.reshape(Lh, JT, 128, KC, 128)
    return np.ascontiguousarray(w5.transpose(0, 3, 2, 1, 4))


def _make_consts():
    cst = np.zeros((128, NCONST), np.float32)
    cst[:, CO_ID:CO_ID + 128] = np.eye(128, dtype=np.float32)
    su = np.triu(np.ones((CH, CH), np.float32), 1)
    ue = np.triu(np.ones((CH, CH), np.float32), 0)
    bd = lambda a: np.block([[a, np.zeros_like(a)], [np.zeros_like(a), a]])
    cst[0:C2, CO_M1:CO_M1 + C2] = bd(su)
    cst[0:C2, CO_M1 + C2:CO_M1 + 2 * C2] = bd(ue)
    cst[0:C2, CO_M3:CO_M3 + C2] = bd(su.T)
    cst[0:C2, CO_M3 + C2:CO_M3 + 2 * C2] = bd(su.T)
    cst[0:C2, CO_M2:CO_M2 + C2] = bd(ue)
    m01 = np.ones(NB, np.float32)
    m01[0::CH] = 0.0
    cst[:, CO_01:CO_01 + NB] = m01[None, :]
    return cst


def prep_inputs(inputs):
    x = np.asarray(inputs["x"], dtype=np.float32)
    shared = {}
    shared["metaT"] = np.ascontiguousarray(np.asarray(inputs["meta_tokens"], np.float32).T).reshape(KC, 128, NMETA)
    g = np.asarray(inputs["norm_g"], np.float32).reshape(NL, 6, KC, 128)
    shared["gvec"] = np.ascontiguousarray(g.transpose(3, 0, 1, 2)).reshape(128, NL * 6 * KC)
    shared["ffn1_win"] = _tile_w_in_ffn(np.asarray(inputs["ffn1_w_in"], np.float32))
    shared["ffn2_win"] = _tile_w_in_ffn(np.asarray(inputs["ffn2_w_in"], np.float32))
    shared["ffn1_wout"] = _tile_w_out_ffn(np.asarray(inputs["ffn1_w_out"], np.float32))
    shared["ffn2_wout"] = _tile_w_out_ffn(np.asarray(inputs["ffn2_w_out"], np.float32))
    f = lambda k: np.asarray(inputs[k], np.float32)
    mw = f("mix_w_in").reshape(NL, KC, 128, MIXT, 128)
    shared["mix_win"] = np.ascontiguousarray(mw.transpose(0, 3, 2, 1, 4))
    mo = f("mix_w_out").reshape(NL, KC, 128, KC, 128)
    shared["mix_wout"] = np.ascontiguousarray(mo.transpose(0, 3, 2, 1, 4))
    pvt = np.zeros((NL, NPV, 128), np.float32)
    for l in range(NL):
        cw = f("lru_conv_w")[l].reshape(4, 2, 128)
        for i in range(2):
            for k in range(4):
                pvt[l, i * 4 + k] = cw[k, i]
        pvt[l, 8:10] = f("lru_conv_b")[l].reshape(2, 128)
        pvt[l, 10:12] = f("lru_ba")[l].reshape(2, 128)
        pvt[l, 12:14] = f("lru_bx")[l].reshape(2, 128)
        pvt[l, 14:16] = f("lru_lambda")[l].reshape(2, 128)
        pvt[l, 16:18] = f("lru_norm_g")[l].reshape(2, 128)
        sw = f("sc_conv_w")[l].reshape(3, 2, 128)
        for i in range(2):
            for k in range(3):
                pvt[l, 18 + i * 3 + k] = sw[k, i]
        pvt[l, 24:26] = f("sc_norm_g")[l].reshape(2, 128)
        pvt[l, 26:39] = f("rwkv_mu")[l].reshape(13, 128)
        for j, nm in enumerate(("rwkv_w0", "rwkv_a0", "rwkv_k_k", "rwkv_k_a", "rwkv_r_k", "rwkv_lnx_w", "rwkv_lnx_b")):
            pvt[l, 39 + 4 * j:43 + 4 * j] = f(nm)[l].reshape(4, 128)
    shared["pv"] = np.ascontiguousarray(pvt.transpose(2, 0, 1)).reshape(128, NL * NPV)
    shared["lora_w"] = np.ascontiguousarray(np.concatenate([f("rwkv_w2"), f("rwkv_a2"), f("rwkv_g2")], axis=1))
    shared["lru_w"] = np.ascontiguousarray(np.stack([f("lru_wa"), f("lru_wx")], axis=1))
    shared["consts"] = _make_consts()
    in_maps = []
    for b in range(x.shape[0]):
        m = dict(shared)
        m["xT"] = np.ascontiguousarray(x[b].T).reshape(KC, 128, T - NMETA)
        in_maps.append(m)
    return in_maps


def kernel(**inputs):
    in_maps = prep_inputs(inputs)
    nc = build_program()
    res = run_bass_kernel_spmd(nc, in_maps, core_ids=list(range(NCORES)))
    outs = [np.asarray(r["outT"]).reshape(D, T - NMETA).T for r in res.results]
    return np.ascontiguousarray(np.stack(outs, axis=0)).astype(np.float32)
```

```python
import numpy as np
import concourse.bass as bass
import concourse.mybir as mybir
from concourse.bass_utils import run_bass_kernel_spmd

F32, BF16 = mybir.dt.float32, mybir.dt.bfloat16
AF = mybir.ActivationFunctionType
ALU = mybir.AluOpType

NCORES = 8
T = 2064
NMETA = 16
NB = 344
NBLK = 6
D = 1024
KC = 8
DFF = 2816
JT = 22
NL = 2
SB = 3
MIXT = 23
RMS_EPS = 1e-6
LNX_EPS = 64e-5
NPV = 67
CH = 43
NCHB = 8
C2 = 2 * CH
CO_ID, CO_M1, CO_M3, CO_M2, CO_01 = 0, 128, 300, 472, 558
NCONST = 558 + 344
SLOTW = 348


class Op:
    __slots__ = ("eng", "fn", "deps", "is_dma", "sem", "done", "signal", "rank", "tag", "rank_pos")


class _Rec:
    def __init__(self):
        self.call = None

    def __getattr__(self, name):
        def f(*args, **kwargs):
            self.call = (name, args, kwargs)
            return self
        return f


class Prog:
    CE = ("pe", "act", "dve", "pool")

    def __init__(self, nc):
        self.nc = nc
        self.streams = {e: [] for e in ("pe", "act", "dve", "pool", "sp")}
        self.lastw = {}
        self.readers = {}
        self.dma_cnt = {}
        self.barrier_ops = []
        self.final_dma = []

    def add(self, eng, fn, reads=(), writes=(), dma_sem=None, tag=None, nodeps=False):
        op = Op()
        rec = _Rec()
        fn(rec)
        assert rec.call is not None
        op.eng, op.fn, op.is_dma, op.sem = eng, rec.call, dma_sem is not None, dma_sem
        op.signal, op.rank, op.done, op.tag = False, None, None, tag
        if op.is_dma:
            self.dma_cnt[dma_sem] = self.dma_cnt.get(dma_sem, 0) + 16
            op.done = self.dma_cnt[dma_sem]
        deps = {}

        def want(d, raw):
            if d is op:
                return
            if d.is_dma:
                if op.is_dma and d.sem == op.sem and d.sem in ("msm",):
                    return
                key = ("dma", d.sem)
                if key not in deps or deps[key].done < d.done:
                    deps[key] = d
                return
            if (not op.is_dma) and d.eng == eng:
                if eng == "pe" or (not raw and eng != "pool"):
                    return
            key = ("ce", d.eng)
            if key not in deps or deps[key].rank_pos < d.rank_pos:
                deps[key] = d

        for k in reads:
            w = self.lastw.get(k)
            if w is not None:
                want(w, True)
            if isinstance(k, tuple) and k[0] == "bank":
                rd = self.readers.get(k)
                if rd:
                    for r in rd.values():
                        if r.eng != eng:
                            want(r, True)
        for k in writes:
            w = self.lastw.get(k)
            if w is not None:
                want(w, True)
            rd = self.readers.get(k)
            if rd:
                for r in rd.values():
                    want(r, False)
        for b in self.barrier_ops:
            want(b, True)
        if nodeps:
            deps = {}
        op.deps = list(deps.values())
        op.rank_pos = len(self.streams[eng])
        for k in reads:
            rd = self.readers.setdefault(k, {})
            rd[("dma", id(op)) if op.is_dma else eng] = op
        for k in writes:
            self.lastw[k] = op
            self.readers[k] = {}
        self.streams[eng].append(op)
        return op

    def barrier(self):
        ops = []
        for e in self.CE:
            for op in reversed(self.streams[e]):
                if not op.is_dma:
                    ops.append(op)
                    break
        last_dma = {}
        for e in self.streams:
            for op in self.streams[e]:
                if op.is_dma:
                    if op.sem not in last_dma or last_dma[op.sem].done < op.done:
                        last_dma[op.sem] = op
        ops.extend(last_dma.values())
        self.barrier_ops = ops

    def emit(self, block, eng_sems, dma_sems):
        for st in self.streams.values():
            for op in st:
                for d in op.deps:
                    if not d.is_dma:
                        d.signal = True
        for e in self.CE:
            r = 0
            for op in self.streams[e]:
                if op.signal and not op.is_dma:
                    r += 1
                op.rank = r

        def run(ename, eng):
            waited = {}
            for op in self.streams[ename]:
                for d in op.deps:
                    if d.is_dma:
                        sname, val, sem = ("dma", d.sem), d.done, dma_sems[d.sem]
                    else:
                        sname, val, sem = ("ce", d.eng), d.rank, eng_sems[d.eng]
                    if waited.get(sname, 0) >= val:
                        continue
                    eng.wait_ge(sem, val)
                    waited[sname] = val
                name, args, kwargs = op.fn
                ins = getattr(eng, name)(*args, **kwargs)
                if op.is_dma:
                    ins.then_inc(dma_sems[op.sem], 16)
                elif op.signal:
                    ins.then_inc(eng_sems[ename], 1)
            if ename == "sp":
                for s, cnt in self.final_dma:
                    eng.wait_ge(dma_sems[s], cnt)

        @block.sync
        def _(e):
            run("sp", e)

        @block.tensor
        def _(e):
            run("pe", e)

        @block.scalar
        def _(e):
            run("act", e)

        @block.vector
        def _(e):
            run("dve", e)

        @block.gpsimd
        def _(e):
            run("pool", e)


class Arena:
    def __init__(self, nc, nbytes):
        self.t = nc.alloc_sbuf_tensor("arena", [128, nbytes // 4], F32)
        self.ap = self.t.ap()
        self.off = 0
        self.cap = nbytes
        self.peak = 0

    def alloc(self, shape, dtype):
        n = int(np.prod(shape))
        esz = 4 if dtype == F32 else 2
        nbytes = (n * esz + 63) // 64 * 64
        assert self.off + nbytes <= self.cap, ("arena overflow", self.off, nbytes, self.cap)
        a = self.ap[:, self.off // 4:(self.off + nbytes) // 4]
        if dtype != F32:
            a = a.bitcast(dtype)
        a = a[:, 0:n]
        if len(shape) == 2:
            a = a.rearrange("p (a b) -> p a b", a=shape[0])
        elif len(shape) == 3:
            a = a.rearrange("p (a b c) -> p a b c", a=shape[0], b=shape[1])
        self.off += nbytes
        self.peak = max(self.peak, self.off)
        return a

    def mark(self):
        return self.off

    def release(self, m):
        self.off = m


def build_program(stages=None, dbg_h=False):
    if stages is None:
        stages = [(l, s) for l in range(NL) for s in range(3)]
    nc = bass.Bass("TRN2", target_bir_lowering=False)
    dt = {}

    def din(name, shape):
        dt[name] = nc.dram_tensor(name, list(shape), F32, kind="ExternalInput").ap()
        return dt[name]

    xT = din("xT", [KC, 128, T - NMETA])
    metaT = din("metaT", [KC, 128, NMETA])
    gvec = din("gvec", [128, NL * 6 * KC])
    ffn_win = [din(f"ffn{i}_win", [NL, JT, 128, KC, 256]) for i in (1, 2)]
    ffn_wout = [din(f"ffn{i}_wout", [NL, KC, 128, JT, 128]) for i in (1, 2)]
    mix_win = din("mix_win", [NL, MIXT, 128, KC, 128])
    mix_wout = din("mix_wout", [NL, KC, 128, KC, 128])
    pv_d = din("pv", [128, NL * NPV])
    lora_d = din("lora_w", [NL, 128, 512])
    lruw_d = din("lru_w", [NL, 2, 4, 64, 64])
    consts_d = din("consts", [128, NCONST])
    if dbg_h:
        outT = nc.dram_tensor("outT", [KC, 128, T], F32, kind="ExternalOutput").ap()
    else:
        outT = nc.dram_tensor("outT", [KC, 128, T - NMETA], F32, kind="ExternalOutput").ap()

    P = Prog(nc)
    A = Arena(nc, 204 * 1024)
    banks = [nc.alloc_psum_tensor(f"bank{i}", [128, 512], F32).ap() for i in range(8)]

    h = A.alloc([KC, T], F32)
    gsb = A.alloc([NL * 6 * KC], F32)
    ghalf = A.alloc([NL * 6 * KC], F32)
    ones_bf = A.alloc([128], BF16)

    def g_ap(l, i, c, half=False):
        idx = (l * 6 + i) * KC + c
        return (ghalf if half else gsb)[:, idx:idx + 1]

    def hk(c, b):
        return ("h", c, b)

    grp = []
    for c in range(KC):
        grp.append(P.add("sp", lambda e, c=c: e.dma_start(out=h[:, c, NMETA:], in_=xT[c]),
                         writes=[hk(c, b) for b in range(NBLK)], dma_sem="ld", nodeps=True))
        grp.append(P.add("sp", lambda e, c=c: e.dma_start(out=h[:, c, 0:NMETA], in_=metaT[c]),
                         writes=[hk(c, 0)], dma_sem="ld", nodeps=True))
    grp.append(P.add("sp", lambda e: e.dma_start(out=gsb, in_=gvec), writes=["gsb"], dma_sem="ld", nodeps=True))
    P.add("pool", lambda e: e.memset(ones_bf, 1.0), writes=["ones"])
    P.add("dve", lambda e: e.tensor_scalar(out=ghalf, in0=gsb, scalar1=0.5, scalar2=None, op0=ALU.mult),
          reads=["gsb"], writes=["ghalf"])

    def rstd_from_sumsq(ps_ap, rstd_ap, n, pskey, rkey, tmpkey=None):
        P.add("act", lambda e: e.activation(out=rstd_ap, in_=ps_ap, func=AF.Ln, bias=RMS_EPS, scale=1.0 / n),
              reads=[pskey], writes=[rkey])
        P.add("act", lambda e: e.activation(out=rstd_ap, in_=rstd_ap, func=AF.Exp, scale=-0.5), reads=[rkey], writes=[rkey])

    def ffn_stage(l, which):
        gi_pre, gi_post = (0, 1) if which == 0 else (4, 5)
        win_d, wout_d = ffn_win[which], ffn_wout[which]
        P.barrier()
        m0 = A.mark()
        NT = SB * NB
        xn2 = A.alloc([1, KC, NT], BF16)
        hid = A.alloc([JT, NT], BF16)
        o = A.alloc([KC, NT], F32)
        sq = A.alloc([1, KC, NB], BF16)
        rstd = A.alloc([2, NB], F32)
        sg = A.alloc([2, NB], F32)
        tmp = A.alloc([2, NB], F32)
        NWI, NWO = 2, 2
        wi = A.alloc([NWI, KC, 256], BF16)
        wo = A.alloc([NWO, JT, 128], BF16)
        psA = [banks[0], banks[1]]
        psB = [banks[2], banks[3]]
        psO = [banks[4], banks[5]]
        psN = [banks[6], banks[7]]
        uid = ("f", l, which)
        wn = {"i": 0, "o": 0}
        cnt = {"n": 0, "a": 0, "o": 0}
        def blocks_of(sbi):
            return list(range(sbi * SB, (sbi + 1) * SB))

        def pre(sbi):
            blocks = blocks_of(sbi)
            par = 0
            xn = xn2[:, par]
            for bi, b in enumerate(blocks):
                n = cnt["n"]; cnt["n"] += 1
                s = n % 2
                tsl = slice(b * NB, (b + 1) * NB)
                P.add("act", lambda e, s=s, tsl=tsl: e.activation(out=sq[:, 0], in_=h[:, :, tsl], func=AF.Square),
                      reads=[hk(c, b) for c in range(KC)], writes=[(uid, "sq", 0)])
                for c in range(KC):
                    P.add("pe", lambda e, s=s, c=c: e.matmul(psN[s][:, 0:NB], lhsT=ones_bf, rhs=sq[:, 0, c, :],
                                                            start=(c == 0), stop=(c == KC - 1)),
                          reads=[(uid, "sq", 0), "ones"], writes=[(uid, "psN", s)])
                rstd_from_sumsq(psN[s][:, 0:NB], rstd[:, s], D, (uid, "psN", s), (uid, "rstd", s))
                for c in range(KC):
                    P.add("dve", lambda e, s=s, c=c, tsl=tsl, bi=bi: e.scalar_tensor_tensor(
                        out=xn[:, c, bi * NB:(bi + 1) * NB], in0=h[:, c, tsl], scalar=g_ap(l, gi_pre, c),
                        in1=rstd[:, s], op0=ALU.mult, op1=ALU.mult),
                        reads=[hk(c, b), (uid, "rstd", s), "gsb"], writes=[(uid, "xn", par, c, bi)])

        def inp(sbi):
            blocks = blocks_of(sbi)
            par = 0
            xn = xn2[:, par]
            for j in range(JT):
                ws = wn["i"] % NWI; wn["i"] += 1
                P.add("pool", lambda e, ws=ws, j=j: e.dma_start(out=wi[:, ws], in_=win_d[l, j], max_dma_last_dim=4096),
                      writes=[(uid, "wi", ws)], dma_sem=f"wi{ws}")
                for bi, b in enumerate(blocks):
                    n = cnt["a"]; cnt["a"] += 1
                    s = n % 2
                    bsl = slice(bi * NB, (bi + 1) * NB)
                    for half, ps in ((0, psA), (1, psB)):
                        for k in range(KC):
                            P.add("pe", lambda e, ps=ps, s=s, k=k, ws=ws, half=half, bsl=bsl: e.matmul(
                                ps[s][:, 0:NB], lhsT=wi[:, ws, k, half * 128:(half + 1) * 128], rhs=xn[:, k, bsl],
                                start=(k == 0), stop=(k == KC - 1)),
                                reads=[(uid, "wi", ws), (uid, "xn", par, k, bi)], writes=[(uid, "ps", half, s)])
                    P.add("act", lambda e, s=s: e.activation(out=sg[:, s], in_=psA[s][:, 0:NB], func=AF.Silu),
                          reads=[(uid, "ps", 0, s)], writes=[(uid, "sg", s)])
                    P.add("dve", lambda e, s=s, j=j, bsl=bsl: e.tensor_tensor(
                        out=hid[:, j, bsl], in0=sg[:, s], in1=psB[s][:, 0:NB], op=ALU.mult),
                        reads=[(uid, "sg", s), (uid, "ps", 1, s)], writes=[(uid, "hid", j, bi)])

        def outp(sbi):
            blocks = blocks_of(sbi)
            for m in range(KC):
                ws = wn["o"] % NWO; wn["o"] += 1
                P.add("pool", lambda e, ws=ws, m=m: e.dma_start(out=wo[:, ws], in_=wout_d[l, m], max_dma_last_dim=4096),
                      writes=[(uid, "wo", ws)], dma_sem=f"wo{ws}")
                for bi, b in enumerate(blocks):
                    n = cnt["o"]; cnt["o"] += 1
                    s = n % 2
                    bsl = slice(bi * NB, (bi + 1) * NB)
                    for k in range(JT):
                        P.add("pe", lambda e, s=s, k=k, ws=ws, bsl=bsl: e.matmul(
                            psO[s][:, 0:NB], lhsT=wo[:, ws, k, :], rhs=hid[:, k, bsl],
                            start=(k == 0), stop=(k == JT - 1)),
                            reads=[(uid, "wo", ws), (uid, "hid", k, bi)], writes=[(uid, "psO", s)])
                    P.add("act", lambda e, s=s, m=m, bsl=bsl: e.activation(out=o[:, m, bsl], in_=psO[s][:, 0:NB], func=AF.Copy),
                          reads=[(uid, "psO", s)], writes=[(uid, "o", m, bi)])

        def post(sbi):
            blocks = blocks_of(sbi)
            for bi, b in enumerate(blocks):
                n = cnt["n"]; cnt["n"] += 1
                s = n % 2
                bsl = slice(bi * NB, (bi + 1) * NB)
                tsl = slice(b * NB, (b + 1) * NB)
                P.add("act", lambda e, s=s, bsl=bsl: e.activation(out=sq[:, 0], in_=o[:, :, bsl], func=AF.Square),
                      reads=[(uid, "o", m, bi) for m in range(KC)], writes=[(uid, "sq", 0)])
                for c in range(KC):
                    P.add("pe", lambda e, s=s, c=c: e.matmul(psN[s][:, 0:NB], lhsT=ones_bf, rhs=sq[:, 0, c, :],
                                                            start=(c == 0), stop=(c == KC - 1)),
                          reads=[(uid, "sq", 0), "ones"], writes=[(uid, "psN", s)])
                rstd_from_sumsq(psN[s][:, 0:NB], rstd[:, s], D, (uid, "psN", s), (uid, "rstd", s))
                for m in range(KC):
                    ts_ = (n * KC + m) % 2
                    P.add("dve", lambda e, s=s, m=m, bsl=bsl, ts_=ts_: e.scalar_tensor_tensor(
                        out=tmp[:, ts_], in0=o[:, m, bsl], scalar=g_ap(l, gi_post, m, half=True),
                        in1=rstd[:, s], op0=ALU.mult, op1=ALU.mult),
                        reads=[(uid, "o", m, bi), (uid, "rstd", s), "ghalf"], writes=[(uid, "tmp", ts_)])
                    P.add("pool", lambda e, m=m, tsl=tsl, ts_=ts_: e.tensor_tensor(
                        out=h[:, m, tsl], in0=h[:, m, tsl], in1=tmp[:, ts_], op=ALU.add),
                        reads=[hk(m, b), (uid, "tmp", ts_)], writes=[hk(m, b)])
        NSB = NBLK // SB
        pre(0)
        inp(0)
        for sbi in range(NSB):
            if sbi + 1 < NSB:
                pre(sbi + 1)
            outp(sbi)
            if sbi + 1 < NSB:
                inp(sbi + 1)
            post(sbi)
        A.release(m0)

    pv = A.alloc([NL * NPV], F32)
    consts = A.alloc([NCONST], F32)
    ident_f = consts[:, CO_ID:CO_ID + 128]
    mask1 = consts[0:C2, CO_M1:CO_M1 + 2 * C2]
    mask3 = consts[0:C2, CO_M3:CO_M3 + 2 * C2]
    mask2 = consts[0:C2, CO_M2:CO_M2 + C2]
    mask01 = consts[:, CO_01:CO_01 + NB]
    ident_bf = A.alloc([128], BF16)
    ones_bd = A.alloc([128], BF16)
    lru_halo = A.alloc([2, 3], F32)
    sc_halo = A.alloc([2, 2], F32)
    rw_halo = A.alloc([13], F32)
    lru_state = A.alloc([2], F32)
    S_buf = A.alloc([4, 2, 128], BF16)
    clam = A.alloc([4], F32)
    grp2 = []
    grp2.append(P.add("sp", lambda e: e.dma_start(out=pv, in_=pv_d), writes=["pv"], dma_sem="ld", nodeps=True))
    grp2.append(P.add("sp", lambda e: e.dma_start(out=consts, in_=consts_d), writes=["consts"], dma_sem="ld", nodeps=True))
    for op in grp + grp2:
        op.done = P.dma_cnt["ld"]
    P.add("act", lambda e: e.activation(out=ident_bf, in_=ident_f, func=AF.Copy), reads=["consts"], writes=["identbf"])
    P.add("pool", lambda e: e.memset(ones_bd, 0.0), writes=["ones_bd"])
    P.add("pool", lambda e: e.memset(ones_bd[0:64, 0:64], 1.0), writes=["ones_bd"])
    P.add("pool", lambda e: e.memset(ones_bd[64:128, 64:128], 1.0), writes=["ones_bd"])

    npv = A.alloc([NL * NPV], F32)
    P.add("dve", lambda e: e.tensor_scalar(out=npv, in0=pv, scalar1=-1.0, scalar2=None, op0=ALU.mult), reads=["pv"], writes=["pv"])

    def pvc(l, j):
        return pv[:, l * NPV + j:l * NPV + j + 1]

    def npvc(l, j):
        return npv[:, l * NPV + j:l * NPV + j + 1]

    def act_sigmoid(dst_ap, src_ap, reads, wkey, nbias=None, scale=1.0):
        if nbias is None:
            P.add("act", lambda e: e.activation(out=dst_ap, in_=src_ap, func=AF.Exp, scale=-scale), reads=reads, writes=[wkey])
        else:
            P.add("act", lambda e: e.activation(out=dst_ap, in_=src_ap, func=AF.Exp, scale=-scale, bias=nbias), reads=reads + ["pv"], writes=[wkey])
        P.add("act", lambda e: e.activation(out=dst_ap, in_=dst_ap, func=AF.Ln, bias=1.0), reads=[wkey], writes=[wkey])
        P.add("act", lambda e: e.activation(out=dst_ap, in_=dst_ap, func=AF.Exp, scale=-1.0), reads=[wkey], writes=[wkey])

    class Slot:
        def __init__(self, i, ap):
            self.i, self.ap, self.key = i, ap, ("slot", i)
            self.bf = ap.bitcast(BF16)

    def mixer_stage(l):
        P.barrier()
        m0 = A.mark()
        NSLOT = 29
        slots_ap = A.alloc([NSLOT, SLOTW], F32)
        free_slots = list(range(NSLOT))
        cur = {"tid": None}
        live = {}
        quota = {}

        def getslot():
            assert free_slots, "slot pool exhausted"
            i = free_slots.pop(0)
            sl = Slot(i, slots_ap[:, i, :])
            sl.owner = cur["tid"]
            live[sl.owner] = live.get(sl.owner, 0) + 1
            return sl

        def putslot(*ss):
            for s_ in ss:
                free_slots.append(s_.i)
                if s_.owner is not None or None in live:
                    live[s_.owner] = live.get(s_.owner, 0) - 1

        def endphase():
            t = cur["tid"]
            quota[t] = live.get(t, 0)

        def disown(*ss):
            for s_ in ss:
                live[s_.owner] = live.get(s_.owner, 0) - 1
                s_.owner = "handoff"
                live["handoff"] = live.get("handoff", 0) + 1

        u = A.alloc([KC, NB], BF16)
        ycat = A.alloc([2, KC, NB], BF16)
        sqb = A.alloc([KC, NB], BF16)
        sqo = A.alloc([KC, NB], BF16)
        NWI, NWO = 3, 2
        wi = A.alloc([NWI, KC, 128], BF16)
        wo = A.alloc([NWO, KC, 128], BF16)
        lora_sb = A.alloc([512], BF16)
        gate_w = A.alloc([2, 2, 128], BF16)
        AR_pad = A.alloc([2, NCHB, 2 * C2], BF16)
        BK_pad = A.alloc([NCHB, 2 * C2], BF16)
        Bh_pad = A.alloc([NCHB, C2], BF16)
        Kh_pad = A.alloc([NCHB, C2], BF16)
        V_pad = A.alloc([NCHB, C2], BF16)
        NRB = A.alloc([NCHB, 300], BF16)
        NTY = A.alloc([NCHB, 300], BF16)
        RKV = A.alloc([2, NCHB, 342], BF16)
        NNb = A.alloc([2, NCHB, 2 * C2], BF16)
        PT = A.alloc([NCHB, C2], BF16)
        QT = A.alloc([2, NCHB, C2], BF16)
        Mm = A.alloc([2, NCHB, 128], BF16)
        GHb = A.alloc([2, NCHB, 214], BF16)
        Pend = A.alloc([2, NCHB], F32)
        uid = ("m", l)
        pbank = {"n": 0}

        def nextbank():
            b = pbank["n"] % 8
            pbank["n"] += 1
            return banks[b], ("bank", b)

        P.add("pool", lambda e: e.memset(AR_pad, 0.0), writes=[(uid, "AR", q_, c) for q_ in range(2) for c in range(NCHB)])
        for buf, nm in ((BK_pad, "BK"), (Bh_pad, "Bh"), (Kh_pad, "Kh"), (V_pad, "Vp")):
            P.add("pool", lambda e, buf=buf: e.memset(buf, 0.0), writes=[(uid, nm, c) for c in range(NCHB)])
        P.add("pool", lambda e: e.memset(S_buf, 0.0), writes=[("S", hp, i) for hp in range(4) for i in range(2)])
        P.add("pool", lambda e: e.memset(lru_halo, 0.0), writes=["lru_halo"])
        P.add("pool", lambda e: e.memset(sc_halo, 0.0), writes=["sc_halo"])
        P.add("pool", lambda e: e.memset(rw_halo, 0.0), writes=[("rw_halo", m_) for m_ in range(13)])
        P.add("pool", lambda e: e.memset(lru_state, 0.0), writes=["lru_state"])
        P.add("pool", lambda e: e.memset(gate_w, 0.0), writes=[(uid, "gate_w")])
        mg = [P.add("pool", lambda e: e.dma_start(out=lora_sb, in_=lora_d[l]), writes=[(uid, "lora")], dma_sem="msm")]
        for ax in range(2):
            for g in range(4):
                i, hh = g // 2, g % 2
                mg.append(P.add("pool", lambda e, ax=ax, g=g, i=i, hh=hh: e.dma_start(
                    out=gate_w[hh * 64:(hh + 1) * 64, ax, i, hh * 64:(hh + 1) * 64], in_=lruw_d[l, ax, g]),
                    reads=[], writes=[(uid, "gate_w")], dma_sem="msm"))
        for op_ in mg:
            op_.done = P.dma_cnt["msm"]
        st_ = getslot()
        P.add("act", lambda e: e.activation(out=st_.ap[:, 0:2], in_=pv[:, l * NPV + 14:l * NPV + 16], func=AF.Exp, scale=-1.0),
              reads=["pv"], writes=[st_.key])
        P.add("act", lambda e: e.activation(out=st_.ap[:, 2:4], in_=st_.ap[:, 0:2], func=AF.Ln, bias=1.0),
              reads=[st_.key], writes=[st_.key])
        P.add("dve", lambda e: e.tensor_scalar(out=clam[:, 0:2], in0=st_.ap[:, 2:4], scalar1=-8.0, scalar2=None, op0=ALU.mult),
              reads=[st_.key], writes=["clam"])
        P.add("dve", lambda e: e.tensor_scalar(out=clam[:, 2:4], in0=st_.ap[:, 2:4], scalar1=-16.0, scalar2=None, op0=ALU.mult),
              reads=[st_.key], writes=["clam"])
        putslot(st_)

        wi_n = {"n": 0}
        wo_n = {"n": 0}

        def inproj_tile(m, dst_ap, dst_key):
            ws = wi_n["n"] % NWI
            wi_n["n"] += 1
            P.add("pool", lambda e: e.dma_start(out=wi[:, ws], in_=mix_win[l, m], max_dma_last_dim=4096),
                  writes=[(uid, "wi", ws)], dma_sem=f"wi{ws}")
            bk, bkey = nextbank()
            for k in range(KC):
                P.add("pe", lambda e, k=k: e.matmul(bk[:, 0:NB], lhsT=wi[:, ws, k, :], rhs=u[:, k, :],
                                                   start=(k == 0), stop=(k == KC - 1)),
                      reads=[(uid, "wi", ws), (uid, "u")], writes=[bkey])
            P.add("act", lambda e: e.activation(out=dst_ap, in_=bk[:, 0:NB], func=AF.Copy), reads=[bkey], writes=[dst_key])

        def group_rstd(src_ap, src_key, n, eps):
            sq_ = getslot()
            P.add("act", lambda e: e.activation(out=sq_.bf[:, 0:NB], in_=src_ap, func=AF.Square),
                  reads=[src_key], writes=[sq_.key])
            bk, bkey = nextbank()
            P.add("pe", lambda e: e.matmul(bk[:, 0:NB], lhsT=ones_bd, rhs=sq_.bf[:, 0:NB], start=True, stop=True),
                  reads=[sq_.key, "ones_bd"], writes=[bkey])
            rs = getslot()
            P.add("act", lambda e: e.activation(out=rs.ap[:, 0:NB], in_=bk[:, 0:NB], func=AF.Ln, bias=eps, scale=1.0 / n),
                  reads=[bkey], writes=[rs.key])
            P.add("act", lambda e: e.activation(out=rs.ap[:, 0:NB], in_=rs.ap[:, 0:NB], func=AF.Exp, scale=-0.5), reads=[rs.key], writes=[rs.key])
            putslot(sq_)
            return rs

        class Ev:
            def __init__(self):
                self.done = False

        def wait(ev):
            while not ev.done:
                yield "wait"

        def need(n):
            tid = cur["tid"]
            while True:
                others = sum(max(0, quota.get(t, 0) - live.get(t, 0)) for t in quota if t != tid)
                if len(free_slots) >= n + others:
                    break
                yield "wait"
            quota[tid] = live.get(tid, 0) + n

        NG = NBLK * 4
        ev_prep = [Ev() for _ in range(NG)]
        ev_A = [Ev() for _ in range(NG)]
        ev_QM = [Ev() for _ in range(NG)]
        ev_state = [Ev() for _ in range(NG)]
        ev_out = [Ev() for _ in range(NBLK)]
        ev_u = [Ev() for _ in range(NBLK)]
        ev_ls = [Ev() for _ in range(NBLK)]
        handoff = {}
        gen_banks = [0, 1, 2, 3, 4]
        gb = {"n": 0}

        def nextbank():
            b = gen_banks[gb["n"] % len(gen_banks)]
            gb["n"] += 1
            return banks[b], ("bank", b)

        def NRBk(c):
            return [(uid, "NRBm", c), (uid, "NRBt", c)]

        def NTYk(c):
            return [(uid, "NTYm", c), (uid, "NTYt", c)]

        def RKVk(c, q):
            return [(uid, "RKVm", q, c), (uid, "RKVt", q, c)]

        def inproj_tile(m, dst_ap, dst_key):
            ws = wi_n["n"] % NWI
            wi_n["n"] += 1
            P.add("pool", lambda e: e.dma_start(out=wi[:, ws], in_=mix_win[l, m], max_dma_last_dim=4096),
                  writes=[(uid, "wi", ws)], dma_sem=f"wi{ws}")
            bk, bkey = nextbank()
            for k in range(KC):
                P.add("pe", lambda e, k=k: e.matmul(bk[:, 0:NB], lhsT=wi[:, ws, k, :], rhs=u[:, k, :],
                                                   start=(k == 0), stop=(k == KC - 1)),
                      reads=[(uid, "wi", ws), (uid, "u")], writes=[bkey])
            P.add("act", lambda e: e.activation(out=dst_ap, in_=bk[:, 0:NB], func=AF.Copy), reads=[bkey], writes=[dst_key])

        def group_rstd(src_ap, src_key, n, eps):
            sq_ = getslot()
            P.add("act", lambda e: e.activation(out=sq_.bf[:, 0:NB], in_=src_ap, func=AF.Square),
                  reads=[src_key], writes=[sq_.key])
            bk, bkey = nextbank()
            P.add("pe", lambda e: e.matmul(bk[:, 0:NB], lhsT=ones_bd, rhs=sq_.bf[:, 0:NB], start=True, stop=True),
                  reads=[sq_.key, "ones_bd"], writes=[bkey])
            rs = getslot()
            P.add("act", lambda e: e.activation(out=rs.ap[:, 0:NB], in_=bk[:, 0:NB], func=AF.Ln, bias=eps, scale=1.0 / n),
                  reads=[bkey], writes=[rs.key])
            P.add("act", lambda e: e.activation(out=rs.ap[:, 0:NB], in_=rs.ap[:, 0:NB], func=AF.Exp, scale=-0.5), reads=[rs.key], writes=[rs.key])
            putslot(sq_)
            return rs

        def rw_tile(m):
            zz = getslot()
            inproj_tile(10 + m, zz.ap[:, 1:1 + NB], zz.key)
            P.add("act", lambda e: e.activation(out=zz.ap[:, 0:1], in_=rw_halo[:, m:m + 1], func=AF.Copy), reads=[("rw_halo", m)], writes=[zz.key])
            P.add("act", lambda e: e.activation(out=rw_halo[:, m:m + 1], in_=zz.ap[:, NB:NB + 1], func=AF.Copy), reads=[zz.key], writes=[("rw_halo", m)])
            dd = getslot()
            P.add("dve", lambda e: e.tensor_tensor(out=dd.ap[:, 0:NB], in0=zz.ap[:, 0:NB], in1=zz.ap[:, 1:1 + NB], op=ALU.subtract),
                  reads=[zz.key], writes=[dd.key])
            P.add("dve", lambda e: e.scalar_tensor_tensor(
                out=zz.ap[:, 1:1 + NB], in0=dd.ap[:, 0:NB], scalar=pvc(l, 26 + m), in1=zz.ap[:, 1:1 + NB], op0=ALU.mult, op1=ALU.add),
                reads=[zz.key, dd.key, "pv"], writes=[zz.key])
            putslot(dd)
            return zz

        def front():
            for b in range(NBLK):
                tsl = slice(b * NB, (b + 1) * NB)
                yq = b % 2
                if b >= 1:
                    yield from wait(ev_ls[b - 1])
                yield from need(1)
                P.add("act", lambda e: e.activation(out=sqb, in_=h[:, :, tsl], func=AF.Square),
                      reads=[hk(c, b) for c in range(KC)], writes=[(uid, "sqb")])
                bk, bkey = nextbank()
                for c in range(KC):
                    P.add("pe", lambda e, c=c: e.matmul(bk[:, 0:NB], lhsT=ones_bf, rhs=sqb[:, c, :], start=(c == 0), stop=(c == KC - 1)),
                          reads=[(uid, "sqb"), "ones"], writes=[bkey])
                rs = getslot()
                rstd_from_sumsq(bk[:, 0:NB], rs.ap[:, 0:NB], D, bkey, rs.key)
                for c in range(KC):
                    P.add("dve", lambda e, c=c: e.scalar_tensor_tensor(
                        out=u[:, c, :], in0=h[:, c, tsl], scalar=g_ap(l, 2, c), in1=rs.ap[:, 0:NB], op0=ALU.mult, op1=ALU.mult),
                        reads=[hk(c, b), rs.key, "gsb"], writes=[(uid, "u")])
                putslot(rs)
                endphase()
                yield
                ev_u[b].done = True
                yield from need(5)
                zlo = rw_tile(12)
                lob = getslot()
                ltmp = getslot()
                act_sigmoid(ltmp.ap[0:32, 0:NB], zlo.ap[0:32, 1:1 + NB], [zlo.key], ltmp.key, scale=2.0)
                P.add("dve", lambda e: e.tensor_scalar(out=lob.bf[0:32, 0:NB], in0=ltmp.ap[0:32, 0:NB], scalar1=2.0, scalar2=-1.0, op0=ALU.mult, op1=ALU.add),
                      reads=[ltmp.key], writes=[lob.key])
                P.add("act", lambda e: e.activation(out=lob.bf[32:64, 0:NB], in_=zlo.ap[32:64, 1:1 + NB], func=AF.Copy), reads=[zlo.key], writes=[lob.key])
                act_sigmoid(ltmp.ap[64:128, 0:NB], zlo.ap[64:128, 1:1 + NB], [zlo.key], ltmp.key)
                P.add("act", lambda e: e.activation(out=lob.bf[64:128, 0:NB], in_=ltmp.ap[64:128, 0:NB], func=AF.Copy), reads=[ltmp.key], writes=[lob.key])
                putslot(zlo, ltmp)
                endphase()
                yield
                for hp in range(4):
                    g = b * 4 + hp
                    q = g % 2
                    yield from need(13)
                    rr = rw_tile(hp)
                    yield
                    kx = rw_tile(4 + hp)
                    yield
                    vv = rw_tile(8 + hp)
                    yield
                    r_ap, k_ap, v_ap = rr.ap[:, 1:1 + NB], kx.ap[:, 1:1 + NB], vv.ap[:, 1:1 + NB]
                    cols = slice(hp * 128, (hp + 1) * 128)
                    logw, av, gv = getslot(), getslot(), getslot()
                    for (p0, p1, dst, fn, bcol) in ((0, 32, logw, AF.Sigmoid, 39 + hp), (32, 64, av, AF.Sigmoid, 43 + hp), (64, 128, gv, AF.Copy, None)):
                        bk, bkey = nextbank()
                        P.add("pe", lambda e, p0=p0, p1=p1, bk=bk: e.matmul(bk[:, 0:NB], lhsT=lora_sb[p0:p1, cols], rhs=lob.bf[p0:p1, 0:NB],
                                                                           start=True, stop=True),
                              reads=[(uid, "lora"), lob.key], writes=[bkey])
                        if bcol is None:
                            P.add("act", lambda e, dst=dst, bk=bk: e.activation(out=dst.ap[:, 0:NB], in_=bk[:, 0:NB], func=AF.Copy),
                                  reads=[bkey], writes=[dst.key])
                        else:
                            act_sigmoid(dst.ap[:, 0:NB], bk[:, 0:NB], [bkey], dst.key, nbias=npvc(l, bcol))
                    P.add("dve", lambda e: e.tensor_scalar(out=logw.ap[:, 0:NB], in0=logw.ap[:, 0:NB], scalar1=-0.6065306597126334, scalar2=None, op0=ALU.mult),
                          reads=[logw.key], writes=[logw.key])
                    yield
                    kk = getslot()
                    P.add("dve", lambda e: e.tensor_scalar(out=kk.ap[:, 0:NB], in0=k_ap, scalar1=pvc(l, 47 + hp), scalar2=None, op0=ALU.mult),
                          reads=[kx.key, "pv"], writes=[kk.key])
                    sq_ = getslot()
                    P.add("act", lambda e: e.activation(out=sq_.bf[:, 0:NB], in_=kk.ap[:, 0:NB], func=AF.Square),
                          reads=[kk.key], writes=[sq_.key])
                    bk, bkey = nextbank()
                    P.add("pe", lambda e, bk=bk: e.matmul(bk[:, 0:NB], lhsT=ones_bd, rhs=sq_.bf[:, 0:NB], start=True, stop=True),
                          reads=[sq_.key, "ones_bd"], writes=[bkey])
                    rn = getslot()
                    P.add("dve", lambda e, bk=bk: e.tensor_scalar(out=rn.ap[:, 0:NB], in0=bk[:, 0:NB], scalar1=1e-24, scalar2=None, op0=ALU.max),
                          reads=[bkey], writes=[rn.key])
                    P.add("act", lambda e: e.activation(out=rn.ap[:, 0:NB], in_=rn.ap[:, 0:NB], func=AF.Ln), reads=[rn.key], writes=[rn.key])
                    P.add("act", lambda e: e.activation(out=rn.ap[:, 0:NB], in_=rn.ap[:, 0:NB], func=AF.Exp, scale=-0.5), reads=[rn.key], writes=[rn.key])
                    P.add("dve", lambda e: e.tensor_tensor(out=kk.ap[:, 0:NB], in0=kk.ap[:, 0:NB], in1=rn.ap[:, 0:NB], op=ALU.mult),
                          reads=[rn.key, kk.key], writes=[kk.key])
                    putslot(sq_, rn)
                    yield
                    kp = getslot()
                    P.add("dve", lambda e: e.tensor_scalar(out=kp.ap[:, 0:NB], in0=av.ap[:, 0:NB], scalar1=-1.0, scalar2=pvc(l, 51 + hp),
                                                           op0=ALU.add, op1=ALU.mult), reads=[av.key, "pv"], writes=[kp.key])
                    P.add("dve", lambda e: e.scalar_tensor_tensor(out=kp.ap[:, 0:NB], in0=kp.ap[:, 0:NB], scalar=1.0, in1=k_ap,
                                                                  op0=ALU.add, op1=ALU.mult), reads=[kp.key, kx.key], writes=[kp.key])
                    bvec = getslot()
                    P.add("dve", lambda e: e.tensor_tensor(out=bvec.ap[:, 0:NB], in0=kk.ap[:, 0:NB], in1=av.ap[:, 0:NB], op=ALU.mult),
                          reads=[kk.key, av.key], writes=[bvec.key])
                    putslot(av)
                    rkb = getslot()
                    P.add("dve", lambda e: e.scalar_tensor_tensor(
                        out=rkb.bf[:, 0:NB], in0=r_ap, scalar=pvc(l, 55 + hp), in1=kp.ap[:, 0:NB], op0=ALU.mult, op1=ALU.mult),
                        reads=[rr.key, kp.key, "pv"], writes=[rkb.key])
                    bk, bkey = nextbank()
                    P.add("pe", lambda e, bk=bk: e.matmul(bk[:, 0:NB], lhsT=ones_bd, rhs=rkb.bf[:, 0:NB], start=True, stop=True),
                          reads=[rkb.key, "ones_bd"], writes=[bkey])
                    bonus = getslot()
                    P.add("dve", lambda e, bk=bk: e.tensor_tensor(out=bonus.ap[:, 0:NB], in0=bk[:, 0:NB], in1=v_ap, op=ALU.mult),
                          reads=[bkey, vv.key], writes=[bonus.key])
                    putslot(rkb)
                    yield
                    Ls = getslot()
                    P.add("dve", lambda e: e.tensor_tensor_scan(out=Ls.ap[:, 0:NB], data0=mask01, data1=logw.ap[:, 0:NB], initial=0.0,
                                                                op0=ALU.mult, op1=ALU.add), reads=[logw.key, "consts"], writes=[Ls.key])
                    L3 = Ls.ap[:, 0:NB].rearrange("p (c t) -> p c t", t=CH)
                    Lend = L3[:, :, CH - 1:CH]
                    if g >= 1:
                        yield from wait(ev_A[g - 1])
                    if g >= 2:
                        yield from wait(ev_QM[g - 2])

                    def padded(dst, col0, in0_ap, in1_ap, keys_in, wkeys, neg=False, eng="dve"):
                        for hh in range(2):
                            ps_ = slice(hh * 64, (hh + 1) * 64)
                            o_ap = dst[ps_, :, col0 + hh * CH:col0 + (hh + 1) * CH]
                            a0 = in0_ap[ps_].rearrange("p (c t) -> p c t", t=CH)
                            if in1_ap is None:
                                P.add(eng, lambda e: e.activation(out=o_ap, in_=a0, func=AF.Copy), reads=keys_in, writes=wkeys)
                            else:
                                a1 = in1_ap[ps_].rearrange("p (c t) -> p c t", t=CH)
                                if neg:
                                    P.add(eng, lambda e: e.scalar_tensor_tensor(out=o_ap, in0=a0, scalar=-1.0, in1=a1, op0=ALU.mult, op1=ALU.mult),
                                          reads=keys_in, writes=wkeys)
                                else:
                                    P.add(eng, lambda e: e.tensor_tensor(out=o_ap, in0=a0, in1=a1, op=ALU.mult), reads=keys_in, writes=wkeys)

                    ARq = AR_pad[:, q]
                    ARkeys = [(uid, "AR", q, c) for c in range(NCHB)]
                    E = getslot()
                    P.add("act", lambda e: e.activation(out=E.ap[:, 0:NB], in_=Ls.ap[:, 0:NB], func=AF.Exp), reads=[Ls.key], writes=[E.key])
                    padded(ARq, C2, r_ap, E.ap[:, 0:NB], [rr.key, E.key], ARkeys)
                    P.add("act", lambda e: e.activation(out=Pend[:, q, :], in_=L3[:, :, CH - 1], func=AF.Exp), reads=[Ls.key], writes=[(uid, "Pend", q)])
                    yield
                    E2 = getslot()
                    P.add("act", lambda e: e.activation(out=E2.ap[:, 0:NB], in_=Ls.ap[:, 0:NB], func=AF.Exp, scale=-1.0), reads=[Ls.key], writes=[E2.key])
                    padded(BK_pad, 0, bvec.ap[:, 0:NB], E2.ap[:, 0:NB], [bvec.key, E2.key], [(uid, "BK", c) for c in range(NCHB)])
                    padded(BK_pad, C2, kp.ap[:, 0:NB], E2.ap[:, 0:NB], [kp.key, E2.key], [(uid, "BK", c) for c in range(NCHB)], eng="dve")
                    yield
                    P.add("dve", lambda e: e.tensor_tensor(out=E.ap[:, 0:NB], in0=Ls.ap[:, 0:NB], in1=logw.ap[:, 0:NB], op=ALU.subtract),
                          reads=[Ls.key, logw.key], writes=[E.key])
                    P.add("act", lambda e: e.activation(out=E.ap[:, 0:NB], in_=E.ap[:, 0:NB], func=AF.Exp), reads=[E.key], writes=[E.key])
                    padded(ARq, 0, kk.ap[:, 0:NB], E.ap[:, 0:NB], [kk.key, E.key], ARkeys, neg=True)
                    yield
                    E3 = E2.ap[:, 0:NB].rearrange("p (c t) -> p c t", t=CH)
                    P.add("dve", lambda e: e.tensor_tensor(out=E3, in0=Lend.to_broadcast([128, NCHB, CH]), in1=L3, op=ALU.subtract),
                          reads=[Ls.key], writes=[E2.key])
                    P.add("act", lambda e: e.activation(out=E2.ap[:, 0:NB], in_=E2.ap[:, 0:NB], func=AF.Exp), reads=[E2.key], writes=[E2.key])
                    padded(Bh_pad, 0, bvec.ap[:, 0:NB], E2.ap[:, 0:NB], [bvec.key, E2.key], [(uid, "Bh", c) for c in range(NCHB)])
                    padded(Kh_pad, 0, kp.ap[:, 0:NB], E2.ap[:, 0:NB], [kp.key, E2.key], [(uid, "Kh", c) for c in range(NCHB)], eng="dve")
                    padded(V_pad, 0, v_ap, None, [vv.key], [(uid, "Vp", c) for c in range(NCHB)], eng="act")
                    putslot(E, E2, Ls, logw, kk, kp, bvec, rr, kx, vv)
                    disown(gv, bonus)
                    handoff[g] = (gv, bonus)
                    endphase()
                    ev_prep[g].done = True
                    yield
                putslot(lob)

        def ls_thread():
            for b in range(NBLK):
                tsl = slice(b * NB, (b + 1) * NB)
                yq = b % 2
                yield from wait(ev_u[b])
                if b >= 2:
                    yield from wait(ev_out[b - 2])
                for i in range(2):
                    yield from need(7)
                    xs = getslot()
                    gx = getslot()
                    inproj_tile(i, xs.ap[:, 3:3 + NB], xs.key)
                    inproj_tile(2 + i, gx.ap[:, 0:NB], gx.key)
                    yield
                    P.add("act", lambda e: e.activation(out=xs.ap[:, 0:3], in_=lru_halo[:, i, :], func=AF.Copy), reads=["lru_halo"], writes=[xs.key])
                    P.add("act", lambda e: e.activation(out=lru_halo[:, i, :], in_=xs.ap[:, NB:NB + 3], func=AF.Copy), reads=[xs.key], writes=["lru_halo"])
                    uc = getslot()
                    P.add("dve", lambda e: e.tensor_scalar(out=uc.ap[:, 0:NB], in0=xs.ap[:, 3:3 + NB], scalar1=pvc(l, i * 4 + 3), scalar2=pvc(l, 8 + i),
                                                           op0=ALU.mult, op1=ALU.add), reads=[xs.key, "pv"], writes=[uc.key])
                    for k in (2, 1, 0):
                        P.add("dve", lambda e, k=k: e.scalar_tensor_tensor(
                            out=uc.ap[:, 0:NB], in0=xs.ap[:, k:k + NB], scalar=pvc(l, i * 4 + k), in1=uc.ap[:, 0:NB],
                            op0=ALU.mult, op1=ALU.add), reads=[xs.key, uc.key, "pv"], writes=[uc.key])
                    putslot(xs)
                    yield
                    ucb = getslot()
                    P.add("act", lambda e: e.activation(out=ucb.bf[:, 0:NB], in_=uc.ap[:, 0:NB], func=AF.Copy), reads=[uc.key], writes=[ucb.key])
                    rg = getslot()
                    ig = getslot()
                    for ax, dst, bcol in ((0, rg, 10 + i), (1, ig, 12 + i)):
                        bk, bkey = nextbank()
                        P.add("pe", lambda e, ax=ax, bk=bk: e.matmul(bk[:, 0:NB], lhsT=gate_w[:, ax, i, :], rhs=ucb.bf[:, 0:NB], start=True, stop=True),
                              reads=[(uid, "gate_w"), ucb.key], writes=[bkey])
                        act_sigmoid(dst.ap[:, 0:NB], bk[:, 0:NB], [bkey], dst.key, nbias=npvc(l, bcol))
                    putslot(ucb)
                    yield
                    av = getslot()
                    a2 = getslot()
                    P.add("act", lambda e: e.activation(out=av.ap[:, 0:NB], in_=rg.ap[:, 0:NB], func=AF.Exp, scale=clam[:, i:i + 1]),
                          reads=[rg.key, "clam"], writes=[av.key])
                    P.add("act", lambda e: e.activation(out=a2.ap[:, 0:NB], in_=rg.ap[:, 0:NB], func=AF.Exp, scale=clam[:, 2 + i:3 + i]),
                          reads=[rg.key, "clam"], writes=[a2.key])
                    P.add("act", lambda e: e.activation(out=a2.ap[:, 0:NB], in_=a2.ap[:, 0:NB], func=AF.Ln, bias=1.0, scale=-1.0),
                          reads=[a2.key], writes=[a2.key])
                    P.add("act", lambda e: e.activation(out=a2.ap[:, 0:NB], in_=a2.ap[:, 0:NB], func=AF.Exp, scale=0.5),
                          reads=[a2.key], writes=[a2.key])
                    putslot(rg)
                    P.add("dve", lambda e: e.tensor_tensor(out=ig.ap[:, 0:NB], in0=ig.ap[:, 0:NB], in1=uc.ap[:, 0:NB], op=ALU.mult),
                          reads=[ig.key, uc.key], writes=[ig.key])
                    P.add("dve", lambda e: e.tensor_tensor(out=ig.ap[:, 0:NB], in0=ig.ap[:, 0:NB], in1=a2.ap[:, 0:NB], op=ALU.mult),
                          reads=[ig.key, a2.key], writes=[ig.key])
                    putslot(uc, a2)
                    hs = getslot()
                    P.add("dve", lambda e: e.tensor_tensor_scan(
                        out=hs.ap[:, 0:NB], data0=av.ap[:, 0:NB], data1=ig.ap[:, 0:NB], initial=lru_state[:, i:i + 1],
                        op0=ALU.mult, op1=ALU.add), reads=[av.key, ig.key, "lru_state"], writes=[hs.key])
                    P.add("act", lambda e: e.activation(out=lru_state[:, i:i + 1], in_=hs.ap[:, NB - 1:NB], func=AF.Copy), reads=[hs.key], writes=["lru_state"])
                    putslot(av, ig)
                    yield
                    t1 = getslot()
                    P.add("act", lambda e: e.activation(out=t1.ap[:, 0:NB], in_=gx.ap[:, 0:NB], func=AF.Square), reads=[gx.key], writes=[t1.key])
                    P.add("dve", lambda e: e.tensor_scalar(out=t1.ap[:, 0:NB], in0=t1.ap[:, 0:NB], scalar1=0.044715, scalar2=1.0,
                                                           op0=ALU.mult, op1=ALU.add), reads=[t1.key], writes=[t1.key])
                    P.add("dve", lambda e: e.tensor_tensor(out=t1.ap[:, 0:NB], in0=t1.ap[:, 0:NB], in1=gx.ap[:, 0:NB], op=ALU.mult),
                          reads=[gx.key, t1.key], writes=[t1.key])
                    act_sigmoid(t1.ap[:, 0:NB], t1.ap[:, 0:NB], [t1.key], t1.key, scale=1.5957691216)
                    P.add("dve", lambda e: e.tensor_tensor(out=t1.ap[:, 0:NB], in0=t1.ap[:, 0:NB], in1=gx.ap[:, 0:NB], op=ALU.mult),
                          reads=[gx.key, t1.key], writes=[t1.key])
                    P.add("dve", lambda e: e.tensor_tensor(out=t1.ap[:, 0:NB], in0=t1.ap[:, 0:NB], in1=hs.ap[:, 0:NB], op=ALU.mult),
                          reads=[hs.key, t1.key], writes=[t1.key])
                    putslot(gx, hs)
                    yield
                    rs = group_rstd(t1.ap[:, 0:NB], t1.key, 64, RMS_EPS)
                    P.add("dve", lambda e: e.scalar_tensor_tensor(
                        out=ycat[:, yq, i, :], in0=t1.ap[:, 0:NB], scalar=pvc(l, 16 + i), in1=rs.ap[:, 0:NB], op0=ALU.mult, op1=ALU.mult),
                        reads=[t1.key, rs.key, "pv"], writes=[(uid, "ycat", yq, i)])
                    putslot(t1, rs)
                    endphase()
                    yield
                for i in range(2):
                    yield from need(5)
                    Bs, Cs, Xs = getslot(), getslot(), getslot()
                    inproj_tile(4 + i, Bs.ap[:, 0:NB], Bs.key)
                    inproj_tile(6 + i, Cs.ap[:, 0:NB], Cs.key)
                    inproj_tile(8 + i, Xs.ap[:, 0:NB], Xs.key)
                    yield
                    vs = getslot()
                    P.add("dve", lambda e: e.tensor_tensor(out=vs.ap[:, 2:2 + NB], in0=Cs.ap[:, 0:NB], in1=Xs.ap[:, 0:NB], op=ALU.mult),
                          reads=[Cs.key, Xs.key], writes=[vs.key])
                    P.add("act", lambda e: e.activation(out=vs.ap[:, 0:2], in_=sc_halo[:, i, :], func=AF.Copy), reads=["sc_halo"], writes=[vs.key])
                    P.add("act", lambda e: e.activation(out=sc_halo[:, i, :], in_=vs.ap[:, NB:NB + 2], func=AF.Copy), reads=[vs.key], writes=["sc_halo"])
                    putslot(Cs, Xs)
                    yc = getslot()
                    P.add("dve", lambda e: e.tensor_scalar(out=yc.ap[:, 0:NB], in0=vs.ap[:, 2:2 + NB], scalar1=pvc(l, 18 + i * 3 + 2),
                                                           scalar2=None, op0=ALU.mult), reads=[vs.key, "pv"], writes=[yc.key])
                    for k in (1, 0):
                        P.add("dve", lambda e, k=k: e.scalar_tensor_tensor(
                            out=yc.ap[:, 0:NB], in0=vs.ap[:, k:k + NB], scalar=pvc(l, 18 + i * 3 + k), in1=yc.ap[:, 0:NB],
                            op0=ALU.mult, op1=ALU.add), reads=[vs.key, yc.key, "pv"], writes=[yc.key])
                    P.add("dve", lambda e: e.tensor_tensor(out=yc.ap[:, 0:NB], in0=yc.ap[:, 0:NB], in1=Bs.ap[:, 0:NB], op=ALU.mult),
                          reads=[Bs.key, yc.key], writes=[yc.key])
                    putslot(vs, Bs)
                    yield
                    rs = group_rstd(yc.ap[:, 0:NB], yc.key, 64, RMS_EPS)
                    P.add("dve", lambda e: e.scalar_tensor_tensor(
                        out=ycat[:, yq, 2 + i, :], in0=yc.ap[:, 0:NB], scalar=pvc(l, 24 + i), in1=rs.ap[:, 0:NB], op0=ALU.mult, op1=ALU.mult),
                        reads=[yc.key, rs.key, "pv"], writes=[(uid, "ycat", yq, 2 + i)])
                    putslot(yc, rs)
                    endphase()
                    yield
                ev_ls[b].done = True
                yield

        def chunk_thread():
            ev = {"n": 0}

            def evac_copy(out_ap, in_ap, reads, writes, wt=2):
                ev["n"] += 1
                if ev["n"] % wt != 0:
                    P.add("act", lambda e: e.activation(out=out_ap, in_=in_ap, func=AF.Copy), reads=reads, writes=writes)
                else:
                    P.add("dve", lambda e: e.tensor_copy(out=out_ap, in_=in_ap), reads=reads, writes=writes)

            for g in range(NG):
                q = g % 2
                yield from wait(ev_prep[g])
                if g >= 2:
                    yield from wait(ev_state[g - 2])
                ARq = AR_pad[:, q]
                for c in range(NCHB):
                    ak = (uid, "AR", q, c)
                    bk, bkey = nextbank()
                    P.add("pe", lambda e: e.matmul(bk[0:C2, 0:2 * C2], lhsT=BK_pad[:, c, 0:C2], rhs=ARq[:, c, :], start=True, stop=True),
                          reads=[(uid, "BK", c), ak], writes=[bkey])
                    P.add("pe", lambda e: e.matmul(bk[0:C2, 2 * C2:300], lhsT=Bh_pad[:, c, :], rhs=ident_bf, start=True, stop=True),
                          reads=[(uid, "Bh", c), "identbf"], writes=[bkey])
                    P.add("dve", lambda e: e.tensor_tensor(out=NRB[0:C2, c, 0:2 * C2], in0=bk[0:C2, 0:2 * C2], in1=mask1[:, 0:2 * C2], op=ALU.mult),
                          reads=[bkey, "consts"], writes=[(uid, "NRBm", c)])
                    P.add("act", lambda e: e.activation(out=NRB[0:C2, c, 2 * C2:300], in_=bk[0:C2, 2 * C2:300], func=AF.Copy),
                          reads=[bkey], writes=[(uid, "NRBt", c)])
                    bk2, bkey2 = nextbank()
                    P.add("pe", lambda e: e.matmul(bk2[0:C2, 0:2 * C2], lhsT=ARq[:, c, 0:C2], rhs=BK_pad[:, c, :], start=True, stop=True),
                          reads=[(uid, "BK", c), ak], writes=[bkey2])
                    P.add("pe", lambda e: e.matmul(bk2[0:C2, 2 * C2:300], lhsT=ARq[:, c, 0:C2], rhs=ident_bf, start=True, stop=True),
                          reads=[ak, "identbf"], writes=[bkey2])
                    P.add("dve", lambda e: e.tensor_tensor(out=NTY[0:C2, c, 0:2 * C2], in0=bk2[0:C2, 0:2 * C2], in1=mask3[:, 0:2 * C2], op=ALU.mult),
                          reads=[bkey2, "consts"], writes=[(uid, "NTYm", c)])
                    P.add("act", lambda e: e.activation(out=NTY[0:C2, c, 2 * C2:300], in_=bk2[0:C2, 2 * C2:300], func=AF.Copy),
                          reads=[bkey2], writes=[(uid, "NTYt", c)])
                    bk3, bkey3 = nextbank()
                    P.add("pe", lambda e: e.matmul(bk3[0:C2, 0:C2], lhsT=BK_pad[:, c, C2:2 * C2], rhs=ARq[:, c, C2:2 * C2], start=True, stop=True),
                          reads=[(uid, "BK", c), ak], writes=[bkey3])
                    P.add("pe", lambda e: e.matmul(bk3[0:C2, C2:C2 + 128], lhsT=Kh_pad[:, c, :], rhs=ident_bf, start=True, stop=True),
                          reads=[(uid, "Kh", c), "identbf"], writes=[bkey3])
                    P.add("pe", lambda e: e.matmul(bk3[0:C2, C2 + 128:342], lhsT=V_pad[:, c, :], rhs=ident_bf, start=True, stop=True),
                          reads=[(uid, "Vp", c), "identbf"], writes=[bkey3])
                    P.add("dve", lambda e: e.tensor_tensor(out=RKV[0:C2, q, c, 0:C2], in0=bk3[0:C2, 0:C2], in1=mask2[:, 0:C2], op=ALU.mult),
                          reads=[bkey3, "consts"], writes=[(uid, "RKVm", q, c)])
                    P.add("act", lambda e: e.activation(out=RKV[0:C2, q, c, C2:342], in_=bk3[0:C2, C2:342], func=AF.Copy),
                          reads=[bkey3], writes=[(uid, "RKVt", q, c)])
                    yield
                ev_A[g].done = True
                def nk_of(lev, c):
                    if lev == 0:
                        return NRB[0:C2, c, 0:C2], NTY[0:C2, c, 0:C2], [(uid, "NRBm", c), (uid, "NTYm", c)]
                    return NNb[0:C2, lev % 2, c, 0:C2], NNb[0:C2, lev % 2, c, C2:2 * C2], [(uid, "NN", lev % 2, c)]

                for lev in range(6):
                    if lev < 5:
                        for c0 in range(0, NCHB, 2):
                            bk2, bkey2 = nextbank()
                            for j in range(2):
                                c = c0 + j
                                Nk, NTk, nkey = nk_of(lev, c)
                                if lev < 4:
                                    P.add("pe", lambda e: e.matmul(bk2[0:C2, j * 2 * C2:j * 2 * C2 + C2], lhsT=NTk, rhs=Nk, start=True, stop=True),
                                          reads=nkey, writes=[bkey2])
                                    P.add("pe", lambda e: e.matmul(bk2[0:C2, j * 2 * C2 + C2:(j + 1) * 2 * C2], lhsT=Nk, rhs=NTk, start=True, stop=True),
                                          reads=nkey, writes=[bkey2])
                                else:
                                    P.add("pe", lambda e: e.matmul(bk2[0:C2, j * 2 * C2 + C2:(j + 1) * 2 * C2], lhsT=Nk, rhs=NTk, start=True, stop=True),
                                          reads=nkey, writes=[bkey2])
                            nxt_ = (lev + 1) % 2
                            if lev < 4:
                                evac_copy(NNb[0:C2, nxt_, c0:c0 + 2, :], bk2[0:C2, 0:4 * C2].rearrange("p (c x) -> p c x", x=2 * C2), [bkey2],
                                          [(uid, "NN", nxt_, c0 + j) for j in range(2)])
                            else:
                                evac_copy(NNb[0:C2, nxt_, c0:c0 + 2, C2:2 * C2],
                                          bk2[0:C2, 0:4 * C2].rearrange("p (c x) -> p c x", x=2 * C2)[:, :, C2:2 * C2], [bkey2],
                                          [(uid, "NN", nxt_, c0 + j) for j in range(2)])
                            yield
                    for c0 in range(0, NCHB, 4):
                        bk, bkey = nextbank()
                        for j in range(4):
                            c = c0 + j
                            Nk, NTk, nkey = nk_of(lev, c)
                            pkey = (uid, "PT", c)
                            rhsP = ident_bf[0:C2, 0:C2] if lev == 0 else PT[0:C2, c, :]
                            P.add("pe", lambda e: e.matmul(bk[0:C2, j * C2:(j + 1) * C2], lhsT=ident_bf[0:C2, 0:C2], rhs=rhsP, start=True, stop=False),
                                  reads=[pkey, "identbf"], writes=[bkey])
                            P.add("pe", lambda e: e.matmul(bk[0:C2, j * C2:(j + 1) * C2], lhsT=NTk, rhs=rhsP, start=False, stop=True),
                                  reads=[pkey, "identbf"] + nkey, writes=[bkey])
                        evac_copy(PT[0:C2, c0:c0 + 4, :], bk[0:C2, 0:4 * C2].rearrange("p (c x) -> p c x", x=C2), [bkey],
                                  [(uid, "PT", c0 + j) for j in range(4)])
                        yield
                for c0 in range(0, NCHB, 2):
                    bk, bkey = nextbank()
                    for j in range(2):
                        c = c0 + j
                        P.add("pe", lambda e: e.matmul(bk[0:C2, j * 214:(j + 1) * 214], lhsT=PT[0:C2, c, :], rhs=NTY[0:C2, c, C2:300], start=True, stop=True),
                              reads=NTYk(c) + [(uid, "PT", c)], writes=[bkey])
                    evac_copy(NTY[0:C2, c0:c0 + 2, C2:300], bk[0:C2, 0:428].rearrange("p (c x) -> p c x", x=214), [bkey],
                              NTYk(c0) + NTYk(c0 + 1))
                    yield
                for c in range(NCHB):
                    ak = (uid, "AR", q, c)
                    bk, bkey = nextbank()
                    P.add("pe", lambda e: e.matmul(bk[:, 0:C2], lhsT=NTY[0:C2, c, 2 * C2:300], rhs=NRB[0:C2, c, C2:2 * C2], start=True, stop=False),
                          reads=NTYk(c) + NRBk(c), writes=[bkey])
                    P.add("pe", lambda e: e.matmul(bk[:, 0:C2], lhsT=ident_bf, rhs=ARq[:, c, C2:2 * C2], start=False, stop=True),
                          reads=[ak, "identbf"], writes=[bkey])
                    P.add("pe", lambda e: e.matmul(bk[:, C2:214], lhsT=NTY[0:C2, c, 2 * C2:300], rhs=NRB[0:C2, c, 2 * C2:300], start=True, stop=True),
                          reads=NTYk(c) + NRBk(c), writes=[bkey])
                    evac_copy(QT[:, q, c, :], bk[:, 0:C2], [bkey], [(uid, "QT", q, c)], wt=2)
                    P.add("dve", lambda e: e.scalar_tensor_tensor(out=Mm[:, q, c, :], in0=ident_f, scalar=Pend[:, q, c:c + 1], in1=bk[:, C2:214],
                                                                  op0=ALU.mult, op1=ALU.add),
                          reads=[bkey, "consts", (uid, "Pend", q)], writes=[(uid, "Mm", q, c)])
                    bk2, bkey2 = nextbank()
                    P.add("pe", lambda e: e.matmul(bk2[0:C2, 0:214], lhsT=NTY[0:C2, c, C2:2 * C2], rhs=NRB[0:C2, c, C2:300], start=True, stop=False),
                          reads=NTYk(c) + NRBk(c), writes=[bkey2])
                    P.add("pe", lambda e: e.matmul(bk2[0:C2, 0:214], lhsT=ident_bf[0:C2, 0:C2], rhs=RKV[0:C2, q, c, 0:214], start=False, stop=True),
                          reads=RKVk(c, q) + ["identbf"], writes=[bkey2])
                    evac_copy(GHb[0:C2, q, c, :], bk2[0:C2, 0:214], [bkey2], [(uid, "GH", q, c)], wt=2)
                    yield
                ev_QM[g].done = True

        def state_thread():
            sb = {"n": 0}
            for g in range(NG):
                b, hp = g // 4, g % 4
                q = g % 2
                yq = b % 2
                tsl = slice(b * NB, (b + 1) * NB)
                yield from wait(ev_QM[g])
                yield from need(5)
                gv, bonus = handoff.pop(g)
                yrw = getslot()
                ybk, ybkey = banks[5], ("bank", 5)
                for c in range(NCHB):
                    gidx = (b * NCHB + c)
                    cur, nxt = gidx % 2, (gidx + 1) % 2
                    S_cur, S_nxt = S_buf[:, hp, cur, :], S_buf[:, hp, nxt, :]
                    j = c % 4
                    P.add("pe", lambda e: e.matmul(ybk[:, j * C2:(j + 1) * C2], lhsT=RKV[0:C2, q, c, 214:342], rhs=GHb[0:C2, q, c, 0:C2], start=True, stop=False),
                          reads=RKVk(c, q) + [(uid, "GH", q, c)], writes=[ybkey])
                    P.add("pe", lambda e: e.matmul(ybk[:, j * C2:(j + 1) * C2], lhsT=S_cur, rhs=QT[:, q, c, :], start=False, stop=True),
                          reads=[("S", hp, cur), (uid, "QT", q, c)], writes=[ybkey])
                    if j == 3:
                        c0 = c - 3
                        for hh in range(2):
                            ps_ = slice(hh * 64, (hh + 1) * 64)
                            src = ybk[ps_, 0:4 * C2].rearrange("p (c x) -> p c x", x=C2)[:, :, hh * CH:(hh + 1) * CH]
                            dst = yrw.ap[ps_, c0 * CH:(c0 + 4) * CH].rearrange("p (c t) -> p c t", t=CH)
                            P.add("act", lambda e, src=src, dst=dst: e.activation(out=dst, in_=src, func=AF.Copy), reads=[ybkey], writes=[yrw.key])
                    sbi = 6 + (sb["n"] % 2)
                    sb["n"] += 1
                    sbk, sbkey = banks[sbi], ("bank", sbi)
                    P.add("pe", lambda e: e.matmul(sbk[:, 0:128], lhsT=GHb[0:C2, q, c, C2:214], rhs=RKV[0:C2, q, c, 214:342], start=True, stop=False),
                          reads=RKVk(c, q) + [(uid, "GH", q, c)], writes=[sbkey])
                    P.add("pe", lambda e: e.matmul(sbk[:, 0:128], lhsT=Mm[:, q, c, :], rhs=S_cur, start=False, stop=True),
                          reads=[("S", hp, cur), (uid, "Mm", q, c)], writes=[sbkey])
                    P.add("act", lambda e: e.activation(out=S_nxt, in_=sbk[:, 0:128], func=AF.Copy), reads=[sbkey], writes=[("S", hp, nxt)])
                    yield
                ev_state[g].done = True
                yb = getslot()
                P.add("act", lambda e: e.activation(out=yb.bf[:, 0:NB], in_=yrw.ap[:, 0:NB], func=AF.Copy), reads=[yrw.key], writes=[yb.key])
                bk, bkey = nextbank()
                P.add("pe", lambda e: e.matmul(bk[:, 0:NB], lhsT=ones_bd, rhs=yb.bf[:, 0:NB], start=True, stop=True),
                      reads=[yb.key, "ones_bd"], writes=[bkey])
                P.add("dve", lambda e: e.scalar_tensor_tensor(out=yrw.ap[:, 0:NB], in0=bk[:, 0:NB], scalar=-1.0 / 64, in1=yrw.ap[:, 0:NB],
                                                              op0=ALU.mult, op1=ALU.add), reads=[bkey, yrw.key], writes=[yrw.key])
                putslot(yb)
                yield
                rs = group_rstd(yrw.ap[:, 0:NB], yrw.key, 64, LNX_EPS)
                P.add("dve", lambda e: e.tensor_tensor(out=yrw.ap[:, 0:NB], in0=yrw.ap[:, 0:NB], in1=rs.ap[:, 0:NB], op=ALU.mult),
                      reads=[yrw.key, rs.key], writes=[yrw.key])
                P.add("dve", lambda e: e.tensor_scalar(out=yrw.ap[:, 0:NB], in0=yrw.ap[:, 0:NB], scalar1=pvc(l, 59 + hp), scalar2=pvc(l, 63 + hp),
                                                       op0=ALU.mult, op1=ALU.add), reads=[yrw.key, "pv"], writes=[yrw.key])
                P.add("dve", lambda e: e.tensor_tensor(out=yrw.ap[:, 0:NB], in0=yrw.ap[:, 0:NB], in1=bonus.ap[:, 0:NB], op=ALU.add),
                      reads=[yrw.key, bonus.key], writes=[yrw.key])
                P.add("dve", lambda e: e.tensor_tensor(out=ycat[:, yq, 4 + hp, :], in0=yrw.ap[:, 0:NB], in1=gv.ap[:, 0:NB], op=ALU.mult),
                      reads=[yrw.key, gv.key], writes=[(uid, "ycat", yq, 4 + hp)])
                putslot(rs, yrw, bonus, gv)
                endphase()
                yield
                if hp == 3:
                    yield from need(10)
                    om = [getslot() for _ in range(KC)]
                    for m in range(KC):
                        ws = wo_n["n"] % NWO
                        wo_n["n"] += 1
                        P.add("pool", lambda e: e.dma_start(out=wo[:, ws], in_=mix_wout[l, m], max_dma_last_dim=4096),
                              writes=[(uid, "wo", ws)], dma_sem=f"wo{ws}")
                        bk, bkey = nextbank()
                        for k in range(KC):
                            P.add("pe", lambda e, k=k: e.matmul(bk[:, 0:NB], lhsT=wo[:, ws, k, :], rhs=ycat[:, yq, k, :], start=(k == 0), stop=(k == KC - 1)),
                                  reads=[(uid, "wo", ws), (uid, "ycat", yq, k)], writes=[bkey])
                        P.add("act", lambda e: e.activation(out=om[m].ap[:, 0:NB], in_=bk[:, 0:NB], func=AF.Copy), reads=[bkey], writes=[om[m].key])
                        P.add("act", lambda e: e.activation(out=sqo[:, m, :], in_=bk[:, 0:NB], func=AF.Square), reads=[bkey], writes=[(uid, "sqo")])
                        yield
                    bk, bkey = nextbank()
                    for c in range(KC):
                        P.add("pe", lambda e, c=c: e.matmul(bk[:, 0:NB], lhsT=ones_bf, rhs=sqo[:, c, :], start=(c == 0), stop=(c == KC - 1)),
                              reads=[(uid, "sqo"), "ones"], writes=[bkey])
                    rs = getslot()
                    rstd_from_sumsq(bk[:, 0:NB], rs.ap[:, 0:NB], D, bkey, rs.key)
                    for m in range(KC):
                        P.add("dve", lambda e, m=m: e.scalar_tensor_tensor(out=om[m].ap[:, 0:NB], in0=om[m].ap[:, 0:NB], scalar=g_ap(l, 3, m),
                                                                           in1=rs.ap[:, 0:NB], op0=ALU.mult, op1=ALU.mult),
                              reads=[om[m].key, rs.key, "gsb"], writes=[om[m].key])
                        P.add("pool", lambda e, m=m: e.tensor_tensor(out=h[:, m, tsl], in0=h[:, m, tsl], in1=om[m].ap[:, 0:NB], op=ALU.add),
                              reads=[hk(m, b), om[m].key], writes=[hk(m, b)])
                    putslot(rs, *om)
                    endphase()
                    ev_out[b].done = True
                    yield

        threads = [("front", front()), ("ls", ls_thread()), ("chunk", chunk_thread()), ("state", state_thread())]
        stall = 0
        while threads:
            progressed = False
            for tid_th in list(threads):
                tid, th = tid_th
                cur["tid"] = tid
                try:
                    r = next(th)
                    if r != "wait":
                        progressed = True
                except StopIteration:
                    threads.remove(tid_th)
                    quota.pop(tid, None)
                    progressed = True
            stall = 0 if progressed else stall + 1
            assert stall < 10, "thread scheduler deadlock"
        cur["tid"] = None
        A.release(m0)

    for (l, s) in stages:
        if s == 0:
            ffn_stage(l, 0)
        elif s == 2:
            ffn_stage(l, 1)
        else:
            mixer_stage(l)

    P.barrier()
    for c in range(KC):
        if dbg_h:
            P.add("sp", lambda e, c=c: e.dma_start(out=outT[c], in_=h[:, c, :]),
                  reads=[hk(c, b) for b in range(NBLK)], dma_sem="st")
        else:
            P.add("sp", lambda e, c=c: e.dma_start(out=outT[c], in_=h[:, c, NMETA:]),
                  reads=[hk(c, b) for b in range(NBLK)], dma_sem="st")
    P.final_dma.append(("st", P.dma_cnt["st"]))

    sem_names = sorted(P.dma_cnt.keys())
    from contextlib import ExitStack
    with ExitStack() as es:
        eng_sems = {e: es.enter_context(nc.semaphore(f"s_{e}")) for e in Prog.CE}
        dma_sems = {s: es.enter_context(nc.semaphore(f"d_{s}")) for s in sem_names}
        block = es.enter_context(nc.Block())
        P.emit(block, eng_sems, dma_sems)
    print("arena peak bytes/partition:", A.peak, " ops:", {e: len(s) for e, s in P.streams.items()})
    return nc


def _tile_w_in_ffn(w):
    Lh = w.shape[0]
    w5 = w.reshape(Lh, KC, 128, 2, JT, 128)
    return np.ascontiguousarray(w5.transpose(0, 4, 2, 1, 3, 5)).reshape(Lh, JT, 128, KC, 256)


def _tile_w_out_ffn(w):
    Lh = w.shape[0]
    w5 = w.reshape(Lh, JT, 128, KC, 128)
    return np.ascontiguousarray(w5.transpose(0, 3, 2, 1, 4))


def _make_consts():
    cst = np.zeros((128, NCONST), np.float32)
    cst[:, CO_ID:CO_ID + 128] = np.eye(128, dtype=np.float32)
    su = np.triu(np.ones((CH, CH), np.float32), 1)
    ue = np.triu(np.ones((CH, CH), np.float32), 0)
    bd = lambda a: np.block([[a, np.zeros_like(a)], [np.zeros_like(a), a]])
    cst[0:C2, CO_M1:CO_M1 + C2] = bd(su)
    cst[0:C2, CO_M1 + C2:CO_M1 + 2 * C2] = bd(ue)
    cst[0:C2, CO_M3:CO_M3 + C2] = bd(su.T)
    cst[0:C2, CO_M3 + C2:CO_M3 + 2 * C2] = bd(su.T)
    cst[0:C2, CO_M2:CO_M2 + C2] = bd(ue)
    m01 = np.ones(NB, np.float32)
    m01[0::CH] = 0.0
    cst[:, CO_01:CO_01 + NB] = m01[None, :]
    return cst


def prep_inputs(inputs):
    x = np.asarray(inputs["x"], dtype=np.float32)
    shared = {}
    shared["metaT"] = np.ascontiguousarray(np.asarray(inputs["meta_tokens"], np.float32).T).reshape(KC, 128, NMETA)
    g = np.asarray(inputs["norm_g"], np.float32).reshape(NL, 6, KC, 128)
    shared["gvec"] = np.ascontiguousarray(g.transpose(3, 0, 1, 2)).reshape(128, NL * 6 * KC)
    shared["ffn1_win"] = _tile_w_in_ffn(np.asarray(inputs["ffn1_w_in"], np.float32))
    shared["ffn2_win"] = _tile_w_in_ffn(np.asarray(inputs["ffn2_w_in"], np.float32))
    shared["ffn1_wout"] = _tile_w_out_ffn(np.asarray(inputs["ffn1_w_out"], np.float32))
    shared["ffn2_wout"] = _tile_w_out_ffn(np.asarray(inputs["ffn2_w_out"], np.float32))
    f = lambda k: np.asarray(inputs[k], np.float32)
    mw = f("mix_w_in").reshape(NL, KC, 128, MIXT, 128)
    shared["mix_win"] = np.ascontiguousarray(mw.transpose(0, 3, 2, 1, 4))
    mo = f("mix_w_out").reshape(NL, KC, 128, KC, 128)
    shared["mix_wout"] = np.ascontiguousarray(mo.transpose(0, 3, 2, 1, 4))
    pvt = np.zeros((NL, NPV, 128), np.float32)
    for l in range(NL):
        cw = f("lru_conv_w")[l].reshape(4, 2, 128)
        for i in range(2):
            for k in range(4):
                pvt[l, i * 4 + k] = cw[k, i]
        pvt[l, 8:10] = f("lru_conv_b")[l].reshape(2, 128)
        pvt[l, 10:12] = f("lru_ba")[l].reshape(2, 128)
        pvt[l, 12:14] = f("lru_bx")[l].reshape(2, 128)
        pvt[l, 14:16] = f("lru_lambda")[l].reshape(2, 128)
        pvt[l, 16:18] = f("lru_norm_g")[l].reshape(2, 128)
        sw = f("sc_conv_w")[l].reshape(3, 2, 128)
        for i in range(2):
            for k in range(3):
                pvt[l, 18 + i * 3 + k] = sw[k, i]
        pvt[l, 24:26] = f("sc_norm_g")[l].reshape(2, 128)
        pvt[l, 26:39] = f("rwkv_mu")[l].reshape(13, 128)
        for j, nm in enumerate(("rwkv_w0", "rwkv_a0", "rwkv_k_k", "rwkv_k_a", "rwkv_r_k", "rwkv_lnx_w", "rwkv_lnx_b")):
            pvt[l, 39 + 4 * j:43 + 4 * j] = f(nm)[l].reshape(4, 128)
    shared["pv"] = np.ascontiguousarray(pvt.transpose(2, 0, 1)).reshape(128, NL * NPV)
    shared["lora_w"] = np.ascontiguousarray(np.concatenate([f("rwkv_w2"), f("rwkv_a2"), f("rwkv_g2")], axis=1))
    shared["lru_w"] = np.ascontiguousarray(np.stack([f("lru_wa"), f("lru_wx")], axis=1))
    shared["consts"] = _make_consts()
    in_maps = []
    for b in range(x.shape[0]):
        m = dict(shared)
        m["xT"] = np.ascontiguousarray(x[b].T).reshape(KC, 128, T - NMETA)
        in_maps.append(m)
    return in_maps


def kernel(**inputs):
    in_maps = prep_inputs(inputs)
    nc = build_program()
    res = run_bass_kernel_spmd(nc, in_maps, core_ids=list(range(NCORES)))
    outs = [np.asarray(r["outT"]).reshape(D, T - NMETA).T for r in res.results]
    return np.ascontiguousarray(np.stack(outs, axis=0)).astype(np.float32)
```

```python
import numpy as np
import concourse.bass as bass
import concourse.mybir as mybir
from concourse.bass_utils import run_bass_kernel_spmd

F32, BF16 = mybir.dt.float32, mybir.dt.bfloat16
AF = mybir.ActivationFunctionType
ALU = mybir.AluOpType

NCORES = 8
T = 2064
NMETA = 16
NB = 344
NBLK = 6
D = 1024
KC = 8
DFF = 2816
JT = 22
NL = 2
SB = 3
MIXT = 23
RMS_EPS = 1e-6
LNX_EPS = 64e-5
NPV = 67
CH = 43
NCHB = 8
C2 = 2 * CH
CO_ID, CO_M1, CO_M3, CO_M2, CO_01 = 0, 128, 300, 472, 558
NCONST = 558 + 344
SLOTW = 348


class Op:
    __slots__ = ("eng", "fn", "deps", "is_dma", "sem", "done", "signal", "rank", "tag", "rank_pos")


class _Rec:
    def __init__(self):
        self.call = None

    def __getattr__(self, name):
        def f(*args, **kwargs):
            self.call = (name, args, kwargs)
            return self
        return f


class Prog:
    CE = ("pe", "act", "dve", "pool")

    def __init__(self, nc):
        self.nc = nc
        self.streams = {e: [] for e in ("pe", "act", "dve", "pool", "sp")}
        self.lastw = {}
        self.readers = {}
        self.dma_cnt = {}
        self.barrier_ops = []
        self.final_dma = []

    def add(self, eng, fn, reads=(), writes=(), dma_sem=None, tag=None, nodeps=False):
        op = Op()
        rec = _Rec()
        fn(rec)
        assert rec.call is not None
        op.eng, op.fn, op.is_dma, op.sem = eng, rec.call, dma_sem is not None, dma_sem
        op.signal, op.rank, op.done, op.tag = False, None, None, tag
        if op.is_dma:
            self.dma_cnt[dma_sem] = self.dma_cnt.get(dma_sem, 0) + 16
            op.done = self.dma_cnt[dma_sem]
        deps = {}

        def want(d, raw):
            if d is op:
                return
            if d.is_dma:
                if op.is_dma and d.sem == op.sem and d.sem in ("msm",):
                    return
                key = ("dma", d.sem)
                if key not in deps or deps[key].done < d.done:
                    deps[key] = d
                return
            if (not op.is_dma) and d.eng == eng:
                if eng == "pe" or (not raw and eng != "pool"):
                    return
            key = ("ce", d.eng)
            if key not in deps or deps[key].rank_pos < d.rank_pos:
                deps[key] = d

        for k in reads:
            w = self.lastw.get(k)
            if w is not None:
                want(w, True)
            if isinstance(k, tuple) and k[0] == "bank":
                rd = self.readers.get(k)
                if rd:
                    for r in rd.values():
                        if r.eng != eng:
                            want(r, True)
        for k in writes:
            w = self.lastw.get(k)
            if w is not None:
                want(w, True)
            rd = self.readers.get(k)
            if rd:
                for r in rd.values():
                    want(r, False)
        for b in self.barrier_ops:
            want(b, True)
        if nodeps:
            deps = {}
        op.deps = list(deps.values())
        op.rank_pos = len(self.streams[eng])
        for k in reads:
            rd = self.readers.setdefault(k, {})
            rd[("dma", id(op)) if op.is_dma else eng] = op
        for k in writes:
            self.lastw[k] = op
            self.readers[k] = {}
        self.streams[eng].append(op)
        return op

    def barrier(self):
        ops = []
        for e in self.CE:
            for op in reversed(self.streams[e]):
                if not op.is_dma:
                    ops.append(op)
                    break
        last_dma = {}
        for e in self.streams:
            for op in self.streams[e]:
                if op.is_dma:
                    if op.sem not in last_dma or last_dma[op.sem].done < op.done:
                        last_dma[op.sem] = op
        ops.extend(last_dma.values())
        self.barrier_ops = ops

    def emit(self, block, eng_sems, dma_sems):
        for st in self.streams.values():
            for op in st:
                for d in op.deps:
                    if not d.is_dma:
                        d.signal = True
        for e in self.CE:
            r = 0
            for op in self.streams[e]:
                if op.signal and not op.is_dma:
                    r += 1
                op.rank = r

        def run(ename, eng):
            waited = {}
            for op in self.streams[ename]:
                for d in op.deps:
                    if d.is_dma:
                        sname, val, sem = ("dma", d.sem), d.done, dma_sems[d.sem]
                    else:
                        sname, val, sem = ("ce", d.eng), d.rank, eng_sems[d.eng]
                    if waited.get(sname, 0) >= val:
                        continue
                    eng.wait_ge(sem, val)
                    waited[sname] = val
                name, args, kwargs = op.fn
                ins = getattr(eng, name)(*args, **kwargs)
                if op.is_dma:
                    ins.then_inc(dma_sems[op.sem], 16)
                elif op.signal:
                    ins.then_inc(eng_sems[ename], 1)
            if ename == "sp":
                for s, cnt in self.final_dma:
                    eng.wait_ge(dma_sems[s], cnt)

        @block.sync
        def _(e):
            run("sp", e)

        @block.tensor
        def _(e):
            run("pe", e)

        @block.scalar
        def _(e):
            run("act", e)

        @block.vector
        def _(e):
            run("dve", e)

        @block.gpsimd
        def _(e):
            run("pool", e)


class Arena:
    def __init__(self, nc, nbytes):
        self.t = nc.alloc_sbuf_tensor("arena", [128, nbytes // 4], F32)
        self.ap = self.t.ap()
        self.off = 0
        self.cap = nbytes
        self.peak = 0

    def alloc(self, shape, dtype):
        n = int(np.prod(shape))
        esz = 4 if dtype == F32 else 2
        nbytes = (n * esz + 63) // 64 * 64
        assert self.off + nbytes <= self.cap, ("arena overflow", self.off, nbytes, self.cap)
        a = self.ap[:, self.off // 4:(self.off + nbytes) // 4]
        if dtype != F32:
            a = a.bitcast(dtype)
        a = a[:, 0:n]
        if len(shape) == 2:
            a = a.rearrange("p (a b) -> p a b", a=shape[0])
        elif len(shape) == 3:
            a = a.rearrange("p (a b c) -> p a b c", a=shape[0], b=shape[1])
        self.off += nbytes
        self.peak = max(self.peak, self.off)
        return a

    def mark(self):
        return self.off

    def release(self, m):
        self.off = m


def build_program(stages=None, dbg_h=False):
    if stages is None:
        stages = [(l, s) for l in range(NL) for s in range(3)]
    nc = bass.Bass("TRN2", target_bir_lowering=False)
    dt = {}

    def din(name, shape):
        dt[name] = nc.dram_tensor(name, list(shape), F32, kind="ExternalInput").ap()
        return dt[name]

    xT = din("xT", [KC, 128, T - NMETA])
    metaT = din("metaT", [KC, 128, NMETA])
    gvec = din("gvec", [128, NL * 6 * KC])
    ffn_win = [din(f"ffn{i}_win", [NL, JT, 128, KC, 256]) for i in (1, 2)]
    ffn_wout = [din(f"ffn{i}_wout", [NL, KC, 128, JT, 128]) for i in (1, 2)]
    mix_win = din("mix_win", [NL, MIXT, 128, KC, 128])
    mix_wout = din("mix_wout", [NL, KC, 128, KC, 128])
    pv_d = din("pv", [128, NL * NPV])
    lora_d = din("lora_w", [NL, 128, 512])
    lruw_d = din("lru_w", [NL, 2, 4, 64, 64])
    consts_d = din("consts", [128, NCONST])
    if dbg_h:
        outT = nc.dram_tensor("outT", [KC, 128, T], F32, kind="ExternalOutput").ap()
    else:
        outT = nc.dram_tensor("outT", [KC, 128, T - NMETA], F32, kind="ExternalOutput").ap()

    P = Prog(nc)
    A = Arena(nc, 204 * 1024)
    banks = [nc.alloc_psum_tensor(f"bank{i}", [128, 512], F32).ap() for i in range(8)]

    h = A.alloc([KC, T], F32)
    gsb = A.alloc([NL * 6 * KC], F32)
    ghalf = A.alloc([NL * 6 * KC], F32)
    ones_bf = A.alloc([128], BF16)

    def g_ap(l, i, c, half=False):
        idx = (l * 6 + i) * KC + c
        return (ghalf if half else gsb)[:, idx:idx + 1]

    def hk(c, b):
        return ("h", c, b)

    grp = []
    for c in range(KC):
        grp.append(P.add("sp", lambda e, c=c: e.dma_start(out=h[:, c, NMETA:], in_=xT[c]),
                         writes=[hk(c, b) for b in range(NBLK)], dma_sem="ld", nodeps=True))
        grp.append(P.add("sp", lambda e, c=c: e.dma_start(out=h[:, c, 0:NMETA], in_=metaT[c]),
                         writes=[hk(c, 0)], dma_sem="ld", nodeps=True))
    grp.append(P.add("sp", lambda e: e.dma_start(out=gsb, in_=gvec), writes=["gsb"], dma_sem="ld", nodeps=True))
    P.add("pool", lambda e: e.memset(ones_bf, 1.0), writes=["ones"])
    P.add("dve", lambda e: e.tensor_scalar(out=ghalf, in0=gsb, scalar1=0.5, scalar2=None, op0=ALU.mult),
          reads=["gsb"], writes=["ghalf"])

    def rstd_from_sumsq(ps_ap, rstd_ap, n, pskey, rkey, tmpkey=None):
        P.add("act", lambda e: e.activation(out=rstd_ap, in_=ps_ap, func=AF.Ln, bias=RMS_EPS, scale=1.0 / n),
              reads=[pskey], writes=[rkey])
        P.add("act", lambda e: e.activation(out=rstd_ap, in_=rstd_ap, func=AF.Exp, scale=-0.5), reads=[rkey], writes=[rkey])

    def ffn_stage(l, which):
        gi_pre, gi_post = (0, 1) if which == 0 else (4, 5)
        win_d, wout_d = ffn_win[which], ffn_wout[which]
        P.barrier()
        m0 = A.mark()
        NT = SB * NB
        xn2 = A.alloc([1, KC, NT], BF16)
        hid = A.alloc([JT, NT], BF16)
        o = A.alloc([KC, NT], F32)
        sq = A.alloc([1, KC, NB], BF16)
        rstd = A.alloc([2, NB], F32)
        sg = A.alloc([2, NB], F32)
        tmp = A.alloc([2, NB], F32)
        NWI, NWO = 2, 2
        wi = A.alloc([NWI, KC, 256], BF16)
        wo = A.alloc([NWO, JT, 128], BF16)
        psA = [banks[0], banks[1]]
        psB = [banks[2], banks[3]]
        psO = [banks[4], banks[5]]
        psN = [banks[6], banks[7]]
        uid = ("f", l, which)
        wn = {"i": 0, "o": 0}
        cnt = {"n": 0, "a": 0, "o": 0}
        def blocks_of(sbi):
            return list(range(sbi * SB, (sbi + 1) * SB))

        def pre(sbi):
            blocks = blocks_of(sbi)
            par = 0
            xn = xn2[:, par]
            for bi, b in enumerate(blocks):
                n = cnt["n"]; cnt["n"] += 1
                s = n % 2
                tsl = slice(b * NB, (b + 1) * NB)
                P.add("act", lambda e, s=s, tsl=tsl: e.activation(out=sq[:, 0], in_=h[:, :, tsl], func=AF.Square),
                      reads=[hk(c, b) for c in range(KC)], writes=[(uid, "sq", 0)])
                for c in range(KC):
                    P.add("pe", lambda e, s=s, c=c: e.matmul(psN[s][:, 0:NB], lhsT=ones_bf, rhs=sq[:, 0, c, :],
                                                            start=(c == 0), stop=(c == KC - 1)),
                          reads=[(uid, "sq", 0), "ones"], writes=[(uid, "psN", s)])
                rstd_from_sumsq(psN[s][:, 0:NB], rstd[:, s], D, (uid, "psN", s), (uid, "rstd", s))
                for c in range(KC):
                    P.add("dve", lambda e, s=s, c=c, tsl=tsl, bi=bi: e.scalar_tensor_tensor(
                        out=xn[:, c, bi * NB:(bi + 1) * NB], in0=h[:, c, tsl], scalar=g_ap(l, gi_pre, c),
                        in1=rstd[:, s], op0=ALU.mult, op1=ALU.mult),
                        reads=[hk(c, b), (uid, "rstd", s), "gsb"], writes=[(uid, "xn", par, c, bi)])

        def inp(sbi):
            blocks = blocks_of(sbi)
            par = 0
            xn = xn2[:, par]
            for j in range(JT):
                ws = wn["i"] % NWI; wn["i"] += 1
                P.add("pool", lambda e, ws=ws, j=j: e.dma_start(out=wi[:, ws], in_=win_d[l, j], max_dma_last_dim=4096),
                      writes=[(uid, "wi", ws)], dma_sem=f"wi{ws}")
                for bi, b in enumerate(blocks):
                    n = cnt["a"]; cnt["a"] += 1
                    s = n % 2
                    bsl = slice(bi * NB, (bi + 1) * NB)
                    for half, ps in ((0, psA), (1, psB)):
                        for k in range(KC):
                            P.add("pe", lambda e, ps=ps, s=s, k=k, ws=ws, half=half, bsl=bsl: e.matmul(
                                ps[s][:, 0:NB], lhsT=wi[:, ws, k, half * 128:(half + 1) * 128], rhs=xn[:, k, bsl],
                                start=(k == 0), stop=(k == KC - 1)),
                                reads=[(uid, "wi", ws), (uid, "xn", par, k, bi)], writes=[(uid, "ps", half, s)])
                    P.add("act", lambda e, s=s: e.activation(out=sg[:, s], in_=psA[s][:, 0:NB], func=AF.Silu),
                          reads=[(uid, "ps", 0, s)], writes=[(uid, "sg", s)])
                    P.add("dve", lambda e, s=s, j=j, bsl=bsl: e.tensor_tensor(
                        out=hid[:, j, bsl], in0=sg[:, s], in1=psB[s][:, 0:NB], op=ALU.mult),
                        reads=[(uid, "sg", s), (uid, "ps", 1, s)], writes=[(uid, "hid", j, bi)])

        def outp(sbi):
            blocks = blocks_of(sbi)
            for m in range(KC):
                ws = wn["o"] % NWO; wn["o"] += 1
                P.add("pool", lambda e, ws=ws, m=m: e.dma_start(out=wo[:, ws], in_=wout_d[l, m], max_dma_last_dim=4096),
                      writes=[(uid, "wo", ws)], dma_sem=f"wo{ws}")
                for bi, b in enumerate(blocks):
                    n = cnt["o"]; cnt["o"] += 1
                    s = n % 2
                    bsl = slice(bi * NB, (bi + 1) * NB)
                    for k in range(JT):
                        P.add("pe", lambda e, s=s, k=k, ws=ws, bsl=bsl: e.matmul(
                            psO[s][:, 0:NB], lhsT=wo[:, ws, k, :], rhs=hid[:, k, bsl],
                            start=(k == 0), stop=(k == JT - 1)),
                            reads=[(uid, "wo", ws), (uid, "hid", k, bi)], writes=[(uid, "psO", s)])
                    P.add("act", lambda e, s=s, m=m, bsl=bsl: e.activation(out=o[:, m, bsl], in_=psO[s][:, 0:NB], func=AF.Copy),
                          reads=[(uid, "psO", s)], writes=[(uid, "o", m, bi)])

        def post(sbi):
            blocks = blocks_of(sbi)
            for bi, b in enumerate(blocks):
                n = cnt["n"]; cnt["n"] += 1
                s = n % 2
                bsl = slice(bi * NB, (bi + 1) * NB)
                tsl = slice(b * NB, (b + 1) * NB)
                P.add("act", lambda e, s=s, bsl=bsl: e.activation(out=sq[:, 0], in_=o[:, :, bsl], func=AF.Square),
                      reads=[(uid, "o", m, bi) for m in range(KC)], writes=[(uid, "sq", 0)])
                for c in range(KC):
                    P.add("pe", lambda e, s=s, c=c: e.matmul(psN[s][:, 0:NB], lhsT=ones_bf, rhs=sq[:, 0, c, :],
                                                            start=(c == 0), stop=(c == KC - 1)),
                          reads=[(uid, "sq", 0), "ones"], writes=[(uid, "psN", s)])
                rstd_from_sumsq(psN[s][:, 0:NB], rstd[:, s], D, (uid, "psN", s), (uid, "rstd", s))
                for m in range(KC):
                    ts_ = (n * KC + m) % 2
                    P.add("dve", lambda e, s=s, m=m, bsl=bsl, ts_=ts_: e.scalar_tensor_tensor(
                        out=tmp[:, ts_], in0=o[:, m, bsl], scalar=g_ap(l, gi_post, m, half=True),
                        in1=rstd[:, s], op0=ALU.mult, op1=ALU.mult),
                        reads=[(uid, "o", m, bi), (uid, "rstd", s), "ghalf"], writes=[(uid, "tmp", ts_)])
                    P.add("pool", lambda e, m=m, tsl=tsl, ts_=ts_: e.tensor_tensor(
                        out=h[:, m, tsl], in0=h[:, m, tsl], in1=tmp[:, ts_], op=ALU.add),
                        reads=[hk(m, b), (uid, "tmp", ts_)], writes=[hk(m, b)])
        NSB = NBLK // SB
        pre(0)
        inp(0)
        for sbi in range(NSB):
            if sbi + 1 < NSB:
                pre(sbi + 1)
            outp(sbi)
            if sbi + 1 < NSB:
                inp(sbi + 1)
            post(sbi)
        A.release(m0)

    pv = A.alloc([NL * NPV], F32)
    consts = A.alloc([NCONST], F32)
    ident_f = consts[:, CO_ID:CO_ID + 128]
    mask1 = consts[0:C2, CO_M1:CO_M1 + 2 * C2]
    mask3 = consts[0:C2, CO_M3:CO_M3 + 2 * C2]
    mask2 = consts[0:C2, CO_M2:CO_M2 + C2]
    mask01 = consts[:, CO_01:CO_01 + NB]
    ident_bf = A.alloc([128], BF16)
    ones_bd = A.alloc([128], BF16)
    lru_halo = A.alloc([2, 3], F32)
    sc_halo = A.alloc([2, 2], F32)
    rw_halo = A.alloc([13], F32)
    lru_state = A.alloc([2], F32)
    S_buf = A.alloc([4, 2, 128], BF16)
    clam = A.alloc([4], F32)
    grp2 = []
    grp2.append(P.add("sp", lambda e: e.dma_start(out=pv, in_=pv_d), writes=["pv"], dma_sem="ld", nodeps=True))
    grp2.append(P.add("sp", lambda e: e.dma_start(out=consts, in_=consts_d), writes=["consts"], dma_sem="ld", nodeps=True))
    for op in grp + grp2:
        op.done = P.dma_cnt["ld"]
    P.add("act", lambda e: e.activation(out=ident_bf, in_=ident_f, func=AF.Copy), reads=["consts"], writes=["identbf"])
    P.add("pool", lambda e: e.memset(ones_bd, 0.0), writes=["ones_bd"])
    P.add("pool", lambda e: e.memset(ones_bd[0:64, 0:64], 1.0), writes=["ones_bd"])
    P.add("pool", lambda e: e.memset(ones_bd[64:128, 64:128], 1.0), writes=["ones_bd"])

    npv = A.alloc([NL * NPV], F32)
    P.add("dve", lambda e: e.tensor_scalar(out=npv, in0=pv, scalar1=-1.0, scalar2=None, op0=ALU.mult), reads=["pv"], writes=["pv"])

    def pvc(l, j):
        return pv[:, l * NPV + j:l * NPV + j + 1]

    def npvc(l, j):
        return npv[:, l * NPV + j:l * NPV + j + 1]

    def act_sigmoid(dst_ap, src_ap, reads, wkey, nbias=None, scale=1.0):
        if nbias is None:
            P.add("act", lambda e: e.activation(out=dst_ap, in_=src_ap, func=AF.Exp, scale=-scale), reads=reads, writes=[wkey])
        else:
            P.add("act", lambda e: e.activation(out=dst_ap, in_=src_ap, func=AF.Exp, scale=-scale, bias=nbias), reads=reads + ["pv"], writes=[wkey])
        P.add("act", lambda e: e.activation(out=dst_ap, in_=dst_ap, func=AF.Ln, bias=1.0), reads=[wkey], writes=[wkey])
        P.add("act", lambda e: e.activation(out=dst_ap, in_=dst_ap, func=AF.Exp, scale=-1.0), reads=[wkey], writes=[wkey])

    class Slot:
        def __init__(self, i, ap):
            self.i, self.ap, self.key = i, ap, ("slot", i)
            self.bf = ap.bitcast(BF16)

    def mixer_stage(l):
        P.barrier()
        m0 = A.mark()
        NSLOT = 29
        slots_ap = A.alloc([NSLOT, SLOTW], F32)
        free_slots = list(range(NSLOT))
        cur = {"tid": None}
        live = {}
        quota = {}

        def getslot():
            assert free_slots, "slot pool exhausted"
            i = free_slots.pop(0)
            sl = Slot(i, slots_ap[:, i, :])
            sl.owner = cur["tid"]
            live[sl.owner] = live.get(sl.owner, 0) + 1
            return sl

        def putslot(*ss):
            for s_ in ss:
                free_slots.append(s_.i)
                if s_.owner is not None or None in live:
                    live[s_.owner] = live.get(s_.owner, 0) - 1

        def endphase():
            t = cur["tid"]
            quota[t] = live.get(t, 0)

        def disown(*ss):
            for s_ in ss:
                live[s_.owner] = live.get(s_.owner, 0) - 1
                s_.owner = "handoff"
                live["handoff"] = live.get("handoff", 0) + 1

        u = A.alloc([KC, NB], BF16)
        ycat = A.alloc([2, KC, NB], BF16)
        sqb = A.alloc([KC, NB], BF16)
        sqo = A.alloc([KC, NB], BF16)
        NWI, NWO = 3, 2
        wi = A.alloc([NWI, KC, 128], BF16)
        wo = A.alloc([NWO, KC, 128], BF16)
        lora_sb = A.alloc([512], BF16)
        gate_w = A.alloc([2, 2, 128], BF16)
        AR_pad = A.alloc([2, NCHB, 2 * C2], BF16)
        BK_pad = A.alloc([NCHB, 2 * C2], BF16)
        Bh_pad = A.alloc([NCHB, C2], BF16)
        Kh_pad = A.alloc([NCHB, C2], BF16)
        V_pad = A.alloc([NCHB, C2], BF16)
        NRB = A.alloc([NCHB, 300], BF16)
        NTY = A.alloc([NCHB, 300], BF16)
        RKV = A.alloc([2, NCHB, 342], BF16)
        NNb = A.alloc([2, NCHB, 2 * C2], BF16)
        PT = A.alloc([NCHB, C2], BF16)
        QT = A.alloc([2, NCHB, C2], BF16)
        Mm = A.alloc([2, NCHB, 128], BF16)
        GHb = A.alloc([2, NCHB, 214], BF16)
        Pend = A.alloc([2, NCHB], F32)
        uid = ("m", l)
        pbank = {"n": 0}

        def nextbank():
            b = pbank["n"] % 8
            pbank["n"] += 1
            return banks[b], ("bank", b)

        P.add("pool", lambda e: e.memset(AR_pad, 0.0), writes=[(uid, "AR", q_, c) for q_ in range(2) for c in range(NCHB)])
        for buf, nm in ((BK_pad, "BK"), (Bh_pad, "Bh"), (Kh_pad, "Kh"), (V_pad, "Vp")):
            P.add("pool", lambda e, buf=buf: e.memset(buf, 0.0), writes=[(uid, nm, c) for c in range(NCHB)])
        P.add("pool", lambda e: e.memset(S_buf, 0.0), writes=[("S", hp, i) for hp in range(4) for i in range(2)])
        P.add("pool", lambda e: e.memset(lru_halo, 0.0), writes=["lru_halo"])
        P.add("pool", lambda e: e.memset(sc_halo, 0.0), writes=["sc_halo"])
        P.add("pool", lambda e: e.memset(rw_halo, 0.0), writes=[("rw_halo", m_) for m_ in range(13)])
        P.add("pool", lambda e: e.memset(lru_state, 0.0), writes=["lru_state"])
        P.add("pool", lambda e: e.memset(gate_w, 0.0), writes=[(uid, "gate_w")])
        mg = [P.add("pool", lambda e: e.dma_start(out=lora_sb, in_=lora_d[l]), writes=[(uid, "lora")], dma_sem="msm")]
        for ax in range(2):
            for g in range(4):
                i, hh = g // 2, g % 2
                mg.append(P.add("pool", lambda e, ax=ax, g=g, i=i, hh=hh: e.dma_start(
                    out=gate_w[hh * 64:(hh + 1) * 64, ax, i, hh * 64:(hh + 1) * 64], in_=lruw_d[l, ax, g]),
                    reads=[], writes=[(uid, "gate_w")], dma_sem="msm"))
        for op_ in mg:
            op_.done = P.dma_cnt["msm"]
        st_ = getslot()
        P.add("act", lambda e: e.activation(out=st_.ap[:, 0:2], in_=pv[:, l * NPV + 14:l * NPV + 16], func=AF.Exp, scale=-1.0),
              reads=["pv"], writes=[st_.key])
        P.add("act", lambda e: e.activation(out=st_.ap[:, 2:4], in_=st_.ap[:, 0:2], func=AF.Ln, bias=1.0),
              reads=[st_.key], writes=[st_.key])
        P.add("dve", lambda e: e.tensor_scalar(out=clam[:, 0:2], in0=st_.ap[:, 2:4], scalar1=-8.0, scalar2=None, op0=ALU.mult),
              reads=[st_.key], writes=["clam"])
        P.add("dve", lambda e: e.tensor_scalar(out=clam[:, 2:4], in0=st_.ap[:, 2:4], scalar1=-16.0, scalar2=None, op0=ALU.mult),
              reads=[st_.key], writes=["clam"])
        putslot(st_)

        wi_n = {"n": 0}
        wo_n = {"n": 0}

        def inproj_tile(m, dst_ap, dst_key):
            ws = wi_n["n"] % NWI
            wi_n["n"] += 1
            P.add("pool", lambda e: e.dma_start(out=wi[:, ws], in_=mix_win[l, m], max_dma_last_dim=4096),
                  writes=[(uid, "wi", ws)], dma_sem=f"wi{ws}")
            bk, bkey = nextbank()
            for k in range(KC):
                P.add("pe", lambda e, k=k: e.matmul(bk[:, 0:NB], lhsT=wi[:, ws, k, :], rhs=u[:, k, :],
                                                   start=(k == 0), stop=(k == KC - 1)),
                      reads=[(uid, "wi", ws), (uid, "u")], writes=[bkey])
            P.add("act", lambda e: e.activation(out=dst_ap, in_=bk[:, 0:NB], func=AF.Copy), reads=[bkey], writes=[dst_key])

        def group_rstd(src_ap, src_key, n, eps):
            sq_ = getslot()
            P.add("act", lambda e: e.activation(out=sq_.bf[:, 0:NB], in_=src_ap, func=AF.Square),
                  reads=[src_key], writes=[sq_.key])
            bk, bkey = nextbank()
            P.add("pe", lambda e: e.matmul(bk[:, 0:NB], lhsT=ones_bd, rhs=sq_.bf[:, 0:NB], start=True, stop=True),
                  reads=[sq_.key, "ones_bd"], writes=[bkey])
            rs = getslot()
            P.add("act", lambda e: e.activation(out=rs.ap[:, 0:NB], in_=bk[:, 0:NB], func=AF.Ln, bias=eps, scale=1.0 / n),
                  reads=[bkey], writes=[rs.key])
            P.add("act", lambda e: e.activation(out=rs.ap[:, 0:NB], in_=rs.ap[:, 0:NB], func=AF.Exp, scale=-0.5), reads=[rs.key], writes=[rs.key])
            putslot(sq_)
            return rs

        class Ev:
            def __init__(self):
                self.done = False

        def wait(ev):
            while not ev.done:
                yield "wait"

        def need(n):
            tid = cur["tid"]
            while True:
                others = sum(max(0, quota.get(t, 0) - live.get(t, 0)) for t in quota if t != tid)
                if len(free_slots) >= n + others:
                    break
                yield "wait"
            quota[tid] = live.get(tid, 0) + n

        NG = NBLK * 4
        ev_prep = [Ev() for _ in range(NG)]
        ev_A = [Ev() for _ in range(NG)]
        ev_QM = [Ev() for _ in range(NG)]
        ev_state = [Ev() for _ in range(NG)]
        ev_out = [Ev() for _ in range(NBLK)]
        ev_u = [Ev() for _ in range(NBLK)]
        ev_ls = [Ev() for _ in range(NBLK)]
        handoff = {}
        gen_banks = [0, 1, 2, 3, 4]
        gb = {"n": 0}

        def nextbank():
            b = gen_banks[gb["n"] % len(gen_banks)]
            gb["n"] += 1
            return banks[b], ("bank", b)

        def NRBk(c):
            return [(uid, "NRBm", c), (uid, "NRBt", c)]

        def NTYk(c):
            return [(uid, "NTYm", c), (uid, "NTYt", c)]

        def RKVk(c, q):
            return [(uid, "RKVm", q, c), (uid, "RKVt", q, c)]

        def inproj_tile(m, dst_ap, dst_key):
            ws = wi_n["n"] % NWI
            wi_n["n"] += 1
            P.add("pool", lambda e: e.dma_start(out=wi[:, ws], in_=mix_win[l, m], max_dma_last_dim=4096),
                  writes=[(uid, "wi", ws)], dma_sem=f"wi{ws}")
            bk, bkey = nextbank()
            for k in range(KC):
                P.add("pe", lambda e, k=k: e.matmul(bk[:, 0:NB], lhsT=wi[:, ws, k, :], rhs=u[:, k, :],
                                                   start=(k == 0), stop=(k == KC - 1)),
                      reads=[(uid, "wi", ws), (uid, "u")], writes=[bkey])
            P.add("act", lambda e: e.activation(out=dst_ap, in_=bk[:, 0:NB], func=AF.Copy), reads=[bkey], writes=[dst_key])

        def group_rstd(src_ap, src_key, n, eps):
            sq_ = getslot()
            P.add("act", lambda e: e.activation(out=sq_.bf[:, 0:NB], in_=src_ap, func=AF.Square),
                  reads=[src_key], writes=[sq_.key])
            bk, bkey = nextbank()
            P.add("pe", lambda e: e.matmul(bk[:, 0:NB], lhsT=ones_bd, rhs=sq_.bf[:, 0:NB], start=True, stop=True),
                  reads=[sq_.key, "ones_bd"], writes=[bkey])
            rs = getslot()
            P.add("act", lambda e: e.activation(out=rs.ap[:, 0:NB], in_=bk[:, 0:NB], func=AF.Ln, bias=eps, scale=1.0 / n),
                  reads=[bkey], writes=[rs.key])
            P.add("act", lambda e: e.activation(out=rs.ap[:, 0:NB], in_=rs.ap[:, 0:NB], func=AF.Exp, scale=-0.5), reads=[rs.key], writes=[rs.key])
            putslot(sq_)
            return rs

        def rw_tile(m):
            zz = getslot()
            inproj_tile(10 + m, zz.ap[:, 1:1 + NB], zz.key)
            P.add("act", lambda e: e.activation(out=zz.ap[:, 0:1], in_=rw_halo[:, m:m + 1], func=AF.Copy), reads=[("rw_halo", m)], writes=[zz.key])
            P.add("act", lambda e: e.activation(out=rw_halo[:, m:m + 1], in_=zz.ap[:, NB:NB + 1], func=AF.Copy), reads=[zz.key], writes=[("rw_halo", m)])
            dd = getslot()
            P.add("dve", lambda e: e.tensor_tensor(out=dd.ap[:, 0:NB], in0=zz.ap[:, 0:NB], in1=zz.ap[:, 1:1 + NB], op=ALU.subtract),
                  reads=[zz.key], writes=[dd.key])
            P.add("dve", lambda e: e.scalar_tensor_tensor(
                out=zz.ap[:, 1:1 + NB], in0=dd.ap[:, 0:NB], scalar=pvc(l, 26 + m), in1=zz.ap[:, 1:1 + NB], op0=ALU.mult, op1=ALU.add),
                reads=[zz.key, dd.key, "pv"], writes=[zz.key])
            putslot(dd)
            return zz

        def front():
            for b in range(NBLK):
                tsl = slice(b * NB, (b + 1) * NB)
                yq = b % 2
                if b >= 1:
                    yield from wait(ev_ls[b - 1])
                yield from need(1)
                P.add("act", lambda e: e.activation(out=sqb, in_=h[:, :, tsl], func=AF.Square),
                      reads=[hk(c, b) for c in range(KC)], writes=[(uid, "sqb")])
                bk, bkey = nextbank()
                for c in range(KC):
                    P.add("pe", lambda e, c=c: e.matmul(bk[:, 0:NB], lhsT=ones_bf, rhs=sqb[:, c, :], start=(c == 0), stop=(c == KC - 1)),
                          reads=[(uid, "sqb"), "ones"], writes=[bkey])
                rs = getslot()
                rstd_from_sumsq(bk[:, 0:NB], rs.ap[:, 0:NB], D, bkey, rs.key)
                for c in range(KC):
                    P.add("dve", lambda e, c=c: e.scalar_tensor_tensor(
                        out=u[:, c, :], in0=h[:, c, tsl], scalar=g_ap(l, 2, c), in1=rs.ap[:, 0:NB], op0=ALU.mult, op1=ALU.mult),
                        reads=[hk(c, b), rs.key, "gsb"], writes=[(uid, "u")])
                putslot(rs)
                endphase()
                yield
                ev_u[b].done = True
                yield from need(5)
                zlo = rw_tile(12)
                lob = getslot()
                ltmp = getslot()
                act_sigmoid(ltmp.ap[0:32, 0:NB], zlo.ap[0:32, 1:1 + NB], [zlo.key], ltmp.key, scale=2.0)
                P.add("dve", lambda e: e.tensor_scalar(out=lob.bf[0:32, 0:NB], in0=ltmp.ap[0:32, 0:NB], scalar1=2.0, scalar2=-1.0, op0=ALU.mult, op1=ALU.add),
                      reads=[ltmp.key], writes=[lob.key])
                P.add("act", lambda e: e.activation(out=lob.bf[32:64, 0:NB], in_=zlo.ap[32:64, 1:1 + NB], func=AF.Copy), reads=[zlo.key], writes=[lob.key])
                act_sigmoid(ltmp.ap[64:128, 0:NB], zlo.ap[64:128, 1:1 + NB], [zlo.key], ltmp.key)
                P.add("act", lambda e: e.activation(out=lob.bf[64:128, 0:NB], in_=ltmp.ap[64:128, 0:NB], func=AF.Copy), reads=[ltmp.key], writes=[lob.key])
                putslot(zlo, ltmp)
                endphase()
                yield
                for hp in range(4):
                    g = b * 4 + hp
                    q = g % 2
                    yield from need(13)
                    rr = rw_tile(hp)
                    yield
                    kx = rw_tile(4 + hp)
                    yield
                    vv = rw_tile(8 + hp)
                    yield
                    r_ap, k_ap, v_ap = rr.ap[:, 1:1 + NB], kx.ap[:, 1:1 + NB], vv.ap[:, 1:1 + NB]
                    cols = slice(hp * 128, (hp + 1) * 128)
                    logw, av, gv = getslot(), getslot(), getslot()
                    for (p0, p1, dst, fn, bcol) in ((0, 32, logw, AF.Sigmoid, 39 + hp), (32, 64, av, AF.Sigmoid, 43 + hp), (64, 128, gv, AF.Copy, None)):
                        bk, bkey = nextbank()
                        P.add("pe", lambda e, p0=p0, p1=p1, bk=bk: e.matmul(bk[:, 0:NB], lhsT=lora_sb[p0:p1, cols], rhs=lob.bf[p0:p1, 0:NB],
                                                                           start=True, stop=True),
                              reads=[(uid, "lora"), lob.key], writes=[bkey])
                        if bcol is None:
                            P.add("act", lambda e, dst=dst, bk=bk: e.activation(out=dst.ap[:, 0:NB], in_=bk[:, 0:NB], func=AF.Copy),
                                  reads=[bkey], writes=[dst.key])
                        else:
                            act_sigmoid(dst.ap[:, 0:NB], bk[:, 0:NB], [bkey], dst.key, nbias=npvc(l, bcol))
                    P.add("dve", lambda e: e.tensor_scalar(out=logw.ap[:, 0:NB], in0=logw.ap[:, 0:NB], scalar1=-0.6065306597126334, scalar2=None, op0=ALU.mult),
                          reads=[logw.key], writes=[logw.key])
                    yield
                    kk = getslot()
                    P.add("dve", lambda e: e.tensor_scalar(out=kk.ap[:, 0:NB], in0=k_ap, scalar1=pvc(l, 47 + hp), scalar2=None, op0=ALU.mult),
                          reads=[kx.key, "pv"], writes=[kk.key])
                    sq_ = getslot()
                    P.add("act", lambda e: e.activation(out=sq_.bf[:, 0:NB], in_=kk.ap[:, 0:NB], func=AF.Square),
                          reads=[kk.key], writes=[sq_.key])
                    bk, bkey = nextbank()
                    P.add("pe", lambda e, bk=bk: e.matmul(bk[:, 0:NB], lhsT=ones_bd, rhs=sq_.bf[:, 0:NB], start=True, stop=True),
                          reads=[sq_.key, "ones_bd"], writes=[bkey])
                    rn = getslot()
                    P.add("dve", lambda e, bk=bk: e.tensor_scalar(out=rn.ap[:, 0:NB], in0=bk[:, 0:NB], scalar1=1e-24, scalar2=None, op0=ALU.max),
                          reads=[bkey], writes=[rn.key])
                    P.add("act", lambda e: e.activation(out=rn.ap[:, 0:NB], in_=rn.ap[:, 0:NB], func=AF.Ln), reads=[rn.key], writes=[rn.key])
                    P.add("act", lambda e: e.activation(out=rn.ap[:, 0:NB], in_=rn.ap[:, 0:NB], func=AF.Exp, scale=-0.5), reads=[rn.key], writes=[rn.key])
                    P.add("dve", lambda e: e.tensor_tensor(out=kk.ap[:, 0:NB], in0=kk.ap[:, 0:NB], in1=rn.ap[:, 0:NB], op=ALU.mult),
                          reads=[rn.key, kk.key], writes=[kk.key])
                    putslot(sq_, rn)
                    yield
                    kp = getslot()
                    P.add("dve", lambda e: e.tensor_scalar(out=kp.ap[:, 0:NB], in0=av.ap[:, 0:NB], scalar1=-1.0, scalar2=pvc(l, 51 + hp),
                                                           op0=ALU.add, op1=ALU.mult), reads=[av.key, "pv"], writes=[kp.key])
                    P.add("dve", lambda e: e.scalar_tensor_tensor(out=kp.ap[:, 0:NB], in0=kp.ap[:, 0:NB], scalar=1.0, in1=k_ap,
                                                                  op0=ALU.add, op1=ALU.mult), reads=[kp.key, kx.key], writes=[kp.key])
                    bvec = getslot()
                    P.add("dve", lambda e: e.tensor_tensor(out=bvec.ap[:, 0:NB], in0=kk.ap[:, 0:NB], in1=av.ap[:, 0:NB], op=ALU.mult),
                          reads=[kk.key, av.key], writes=[bvec.key])
                    putslot(av)
                    rkb = getslot()
                    P.add("dve", lambda e: e.scalar_tensor_tensor(
                        out=rkb.bf[:, 0:NB], in0=r_ap, scalar=pvc(l, 55 + hp), in1=kp.ap[:, 0:NB], op0=ALU.mult, op1=ALU.mult),
                        reads=[rr.key, kp.key, "pv"], writes=[rkb.key])
                    bk, bkey = nextbank()
                    P.add("pe", lambda e, bk=bk: e.matmul(bk[:, 0:NB], lhsT=ones_bd, rhs=rkb.bf[:, 0:NB], start=True, stop=True),
                          reads=[rkb.key, "ones_bd"], writes=[bkey])
                    bonus = getslot()
                    P.add("dve", lambda e, bk=bk: e.tensor_tensor(out=bonus.ap[:, 0:NB], in0=bk[:, 0:NB], in1=v_ap, op=ALU.mult),
                          reads=[bkey, vv.key], writes=[bonus.key])
                    putslot(rkb)
                    yield
                    Ls = getslot()
                    P.add("dve", lambda e: e.tensor_tensor_scan(out=Ls.ap[:, 0:NB], data0=mask01, data1=logw.ap[:, 0:NB], initial=0.0,
                                                                op0=ALU.mult, op1=ALU.add), reads=[logw.key, "consts"], writes=[Ls.key])
                    L3 = Ls.ap[:, 0:NB].rearrange("p (c t) -> p c t", t=CH)
                    Lend = L3[:, :, CH - 1:CH]
                    if g >= 1:
                        yield from wait(ev_A[g - 1])
                    if g >= 2:
                        yield from wait(ev_QM[g - 2])

                    def padded(dst, col0, in0_ap, in1_ap, keys_in, wkeys, neg=False, eng="dve"):
                        for hh in range(2):
                            ps_ = slice(hh * 64, (hh + 1) * 64)
                            o_ap = dst[ps_, :, col0 + hh * CH:col0 + (hh + 1) * CH]
                            a0 = in0_ap[ps_].rearrange("p (c t) -> p c t", t=CH)
                            if in1_ap is None:
                                P.add(eng, lambda e: e.activation(out=o_ap, in_=a0, func=AF.Copy), reads=keys_in, writes=wkeys)
                            else:
                                a1 = in1_ap[ps_].rearrange("p (c t) -> p c t", t=CH)
                                if neg:
                                    P.add(eng, lambda e: e.scalar_tensor_tensor(out=o_ap, in0=a0, scalar=-1.0, in1=a1, op0=ALU.mult, op1=ALU.mult),
                                          reads=keys_in, writes=wkeys)
                                else:
                                    P.add(eng, lambda e: e.tensor_tensor(out=o_ap, in0=a0, in1=a1, op=ALU.mult), reads=keys_in, writes=wkeys)

                    ARq = AR_pad[:, q]
                    ARkeys = [(uid, "AR", q, c) for c in range(NCHB)]
                    E = getslot()
                    P.add("act", lambda e: e.activation(out=E.ap[:, 0:NB], in_=Ls.ap[:, 0:NB], func=AF.Exp), reads=[Ls.key], writes=[E.key])
                    padded(ARq, C2, r_ap, E.ap[:, 0:NB], [rr.key, E.key], ARkeys)
                    P.add("act", lambda e: e.activation(out=Pend[:, q, :], in_=L3[:, :, CH - 1], func=AF.Exp), reads=[Ls.key], writes=[(uid, "Pend", q)])
                    yield
                    E2 = getslot()
                    P.add("act", lambda e: e.activation(out=E2.ap[:, 0:NB], in_=Ls.ap[:, 0:NB], func=AF.Exp, scale=-1.0), reads=[Ls.key], writes=[E2.key])
                    padded(BK_pad, 0, bvec.ap[:, 0:NB], E2.ap[:, 0:NB], [bvec.key, E2.key], [(uid, "BK", c) for c in range(NCHB)])
                    padded(BK_pad, C2, kp.ap[:, 0:NB], E2.ap[:, 0:NB], [kp.key, E2.key], [(uid, "BK", c) for c in range(NCHB)], eng="dve")
                    yield
                    P.add("dve", lambda e: e.tensor_tensor(out=E.ap[:, 0:NB], in0=Ls.ap[:, 0:NB], in1=logw.ap[:, 0:NB], op=ALU.subtract),
                          reads=[Ls.key, logw.key], writes=[E.key])
                    P.add("act", lambda e: e.activation(out=E.ap[:, 0:NB], in_=E.ap[:, 0:NB], func=AF.Exp), reads=[E.key], writes=[E.key])
                    padded(ARq, 0, kk.ap[:, 0:NB], E.ap[:, 0:NB], [kk.key, E.key], ARkeys, neg=True)
                    yield
                    E3 = E2.ap[:, 0:NB].rearrange("p (c t) -> p c t", t=CH)
                    P.add("dve", lambda e: e.tensor_tensor(out=E3, in0=Lend.to_broadcast([128, NCHB, CH]), in1=L3, op=ALU.subtract),
                          reads=[Ls.key], writes=[E2.key])
                    P.add("act", lambda e: e.activation(out=E2.ap[:, 0:NB], in_=E2.ap[:, 0:NB], func=AF.Exp), reads=[E2.key], writes=[E2.key])
                    padded(Bh_pad, 0, bvec.ap[:, 0:NB], E2.ap[:, 0:NB], [bvec.key, E2.key], [(uid, "Bh", c) for c in range(NCHB)])
                    padded(Kh_pad, 0, kp.ap[:, 0:NB], E2.ap[:, 0:NB], [kp.key, E2.key], [(uid, "Kh", c) for c in range(NCHB)], eng="dve")
                    padded(V_pad, 0, v_ap, None, [vv.key], [(uid, "Vp", c) for c in range(NCHB)], eng="act")
                    putslot(E, E2, Ls, logw, kk, kp, bvec, rr, kx, vv)
                    disown(gv, bonus)
                    handoff[g] = (gv, bonus)
                    endphase()
                    ev_prep[g].done = True
                    yield
                putslot(lob)

        def ls_thread():
            for b in range(NBLK):
                tsl = slice(b * NB, (b + 1) * NB)
                yq = b % 2
                yield from wait(ev_u[b])
                if b >= 2:
                    yield from wait(ev_out[b - 2])
                for i in range(2):
                    yield from need(7)
                    xs = getslot()
                    gx = getslot()
                    inproj_tile(i, xs.ap[:, 3:3 + NB], xs.key)
                    inproj_tile(2 + i, gx.ap[:, 0:NB], gx.key)
                    yield
                    P.add("act", lambda e: e.activation(out=xs.ap[:, 0:3], in_=lru_halo[:, i, :], func=AF.Copy), reads=["lru_halo"], writes=[xs.key])
                    P.add("act", lambda e: e.activation(out=lru_halo[:, i, :], in_=xs.ap[:, NB:NB + 3], func=AF.Copy), reads=[xs.key], writes=["lru_halo"])
                    uc = getslot()
                    P.add("dve", lambda e: e.tensor_scalar(out=uc.ap[:, 0:NB], in0=xs.ap[:, 3:3 + NB], scalar1=pvc(l, i * 4 + 3), scalar2=pvc(l, 8 + i),
                                                           op0=ALU.mult, op1=ALU.add), reads=[xs.key, "pv"], writes=[uc.key])
                    for k in (2, 1, 0):
                        P.add("dve", lambda e, k=k: e.scalar_tensor_tensor(
                            out=uc.ap[:, 0:NB], in0=xs.ap[:, k:k + NB], scalar=pvc(l, i * 4 + k), in1=uc.ap[:, 0:NB],
                            op0=ALU.mult, op1=ALU.add), reads=[xs.key, uc.key, "pv"], writes=[uc.key])
                    putslot(xs)
                    yield
                    ucb = getslot()
                    P.add("act", lambda e: e.activation(out=ucb.bf[:, 0:NB], in_=uc.ap[:, 0:NB], func=AF.Copy), reads=[uc.key], writes=[ucb.key])
                    rg = getslot()
                    ig = getslot()
                    for ax, dst, bcol in ((0, rg, 10 + i), (1, ig, 12 + i)):
                        bk, bkey = nextbank()
                        P.add("pe", lambda e, ax=ax, bk=bk: e.matmul(bk[:, 0:NB], lhsT=gate_w[:, ax, i, :], rhs=ucb.bf[:, 0:NB], start=True, stop=True),
                              reads=[(uid, "gate_w"), ucb.key], writes=[bkey])
                        act_sigmoid(dst.ap[:, 0:NB], bk[:, 0:NB], [bkey], dst.key, nbias=npvc(l, bcol))
                    putslot(ucb)
                    yield
                    av = getslot()
                    a2 = getslot()
                    P.add("act", lambda e: e.activation(out=av.ap[:, 0:NB], in_=rg.ap[:, 0:NB], func=AF.Exp, scale=clam[:, i:i + 1]),
                          reads=[rg.key, "clam"], writes=[av.key])
                    P.add("act", lambda e: e.activation(out=a2.ap[:, 0:NB], in_=rg.ap[:, 0:NB], func=AF.Exp, scale=clam[:, 2 + i:3 + i]),
                          reads=[rg.key, "clam"], writes=[a2.key])
                    P.add("act", lambda e: e.activation(out=a2.ap[:, 0:NB], in_=a2.ap[:, 0:NB], func=AF.Ln, bias=1.0, scale=-1.0),
                          reads=[a2.key], writes=[a2.key])
                    P.add("act", lambda e: e.activation(out=a2.ap[:, 0:NB], in_=a2.ap[:, 0:NB], func=AF.Exp, scale=0.5),
                          reads=[a2.key], writes=[a2.key])
                    putslot(rg)
                    P.add("dve", lambda e: e.tensor_tensor(out=ig.ap[:, 0:NB], in0=ig.ap[:, 0:NB], in1=uc.ap[:, 0:NB], op=ALU.mult),
                          reads=[ig.key, uc.key], writes=[ig.key])
                    P.add("dve", lambda e: e.tensor_tensor(out=ig.ap[:, 0:NB], in0=ig.ap[:, 0:NB], in1=a2.ap[:, 0:NB], op=ALU.mult),
                          reads=[ig.key, a2.key], writes=[ig.key])
                    putslot(uc, a2)
                    hs = getslot()
                    P.add("dve", lambda e: e.tensor_tensor_scan(
                        out=hs.ap[:, 0:NB], data0=av.ap[:, 0:NB], data1=ig.ap[:, 0:NB], initial=lru_state[:, i:i + 1],
                        op0=ALU.mult, op1=ALU.add), reads=[av.key, ig.key, "lru_state"], writes=[hs.key])
                    P.add("act", lambda e: e.activation(out=lru_state[:, i:i + 1], in_=hs.ap[:, NB - 1:NB], func=AF.Copy), reads=[hs.key], writes=["lru_state"])
                    putslot(av, ig)
                    yield
                    t1 = getslot()
                    P.add("act", lambda e: e.activation(out=t1.ap[:, 0:NB], in_=gx.ap[:, 0:NB], func=AF.Square), reads=[gx.key], writes=[t1.key])
                    P.add("dve", lambda e: e.tensor_scalar(out=t1.ap[:, 0:NB], in0=t1.ap[:, 0:NB], scalar1=0.044715, scalar2=1.0,
                                                           op0=ALU.mult, op1=ALU.add), reads=[t1.key], writes=[t1.key])
                    P.add("dve", lambda e: e.tensor_tensor(out=t1.ap[:, 0:NB], in0=t1.ap[:, 0:NB], in1=gx.ap[:, 0:NB], op=ALU.mult),
                          reads=[gx.key, t1.key], writes=[t1.key])
                    act_sigmoid(t1.ap[:, 0:NB], t1.ap[:, 0:NB], [t1.key], t1.key, scale=1.5957691216)
                    P.add("dve", lambda e: e.tensor_tensor(out=t1.ap[:, 0:NB], in0=t1.ap[:, 0:NB], in1=gx.ap[:, 0:NB], op=ALU.mult),
                          reads=[gx.key, t1.key], writes=[t1.key])
                    P.add("dve", lambda e: e.tensor_tensor(out=t1.ap[:, 0:NB], in0=t1.ap[:, 0:NB], in1=hs.ap[:, 0:NB], op=ALU.mult),
                          reads=[hs.key, t1.key], writes=[t1.key])
                    putslot(gx, hs)
                    yield
                    rs = group_rstd(t1.ap[:, 0:NB], t1.key, 64, RMS_EPS)
                    P.add("dve", lambda e: e.scalar_tensor_tensor(
                        out=ycat[:, yq, i, :], in0=t1.ap[:, 0:NB], scalar=pvc(l, 16 + i), in1=rs.ap[:, 0:NB], op0=ALU.mult, op1=ALU.mult),
                        reads=[t1.key, rs.key, "pv"], writes=[(uid, "ycat", yq, i)])
                    putslot(t1, rs)
                    endphase()
                    yield
                for i in range(2):
                    yield from need(5)
                    Bs, Cs, Xs = getslot(), getslot(), getslot()
                    inproj_tile(4 + i, Bs.ap[:, 0:NB], Bs.key)
                    inproj_tile(6 + i, Cs.ap[:, 0:NB], Cs.key)
                    inproj_tile(8 + i, Xs.ap[:, 0:NB], Xs.key)
                    yield
                    vs = getslot()
                    P.add("dve", lambda e: e.tensor_tensor(out=vs.ap[:, 2:2 + NB], in0=Cs.ap[:, 0:NB], in1=Xs.ap[:, 0:NB], op=ALU.mult),
                          reads=[Cs.key, Xs.key], writes=[vs.key])
                    P.add("act", lambda e: e.activation(out=vs.ap[:, 0:2], in_=sc_halo[:, i, :], func=AF.Copy), reads=["sc_halo"], writes=[vs.key])
                    P.add("act", lambda e: e.activation(out=sc_halo[:, i, :], in_=vs.ap[:, NB:NB + 2], func=AF.Copy), reads=[vs.key], writes=["sc_halo"])
                    putslot(Cs, Xs)
                    yc = getslot()
                    P.add("dve", lambda e: e.tensor_scalar(out=yc.ap[:, 0:NB], in0=vs.ap[:, 2:2 + NB], scalar1=pvc(l, 18 + i * 3 + 2),
                                                           scalar2=None, op0=ALU.mult), reads=[vs.key, "pv"], writes=[yc.key])
                    for k in (1, 0):
                        P.add("dve", lambda e, k=k: e.scalar_tensor_tensor(
                            out=yc.ap[:, 0:NB], in0=vs.ap[:, k:k + NB], scalar=pvc(l, 18 + i * 3 + k), in1=yc.ap[:, 0:NB],
                            op0=ALU.mult, op1=ALU.add), reads=[vs.key, yc.key, "pv"], writes=[yc.key])
                    P.add("dve", lambda e: e.tensor_tensor(out=yc.ap[:, 0:NB], in0=yc.ap[:, 0:NB], in1=Bs.ap[:, 0:NB], op=ALU.mult),
                          reads=[Bs.key, yc.key], writes=[yc.key])
                    putslot(vs, Bs)
                    yield
                    rs = group_rstd(yc.ap[:, 0:NB], yc.key, 64, RMS_EPS)
                    P.add("dve", lambda e: e.scalar_tensor_tensor(
                        out=ycat[:, yq, 2 + i, :], in0=yc.ap[:, 0:NB], scalar=pvc(l, 24 + i), in1=rs.ap[:, 0:NB], op0=ALU.mult, op1=ALU.mult),
                        reads=[yc.key, rs.key, "pv"], writes=[(uid, "ycat", yq, 2 + i)])
                    putslot(yc, rs)
                    endphase()
                    yield
                ev_ls[b].done = True
                yield

        def chunk_thread():
            ev = {"n": 0}

            def evac_copy(out_ap, in_ap, reads, writes, wt=2):
                ev["n"] += 1
                if ev["n"] % wt != 0:
                    P.add("act", lambda e: e.activation(out=out_ap, in_=in_ap, func=AF.Copy), reads=reads, writes=writes)
                else:
                    P.add("dve", lambda e: e.tensor_copy(out=out_ap, in_=in_ap), reads=reads, writes=writes)

            for g in range(NG):
                q = g % 2
                yield from wait(ev_prep[g])
                if g >= 2:
                    yield from wait(ev_state[g - 2])
                ARq = AR_pad[:, q]
                def bc2(m, n):
                    return m.unsqueeze(1).to_broadcast([C2, n, m.shape[1]])

                for c0 in range(0, NCHB, 2):
                    cs = (c0, c0 + 1)
                    rk = [(uid, "BK", c) for c in cs] + [(uid, "AR", q, c) for c in cs]
                    bk, bkey = nextbank()
                    for j, c in enumerate(cs):
                        P.add("pe", lambda e: e.matmul(bk[0:C2, j * 2 * C2:(j + 1) * 2 * C2], lhsT=BK_pad[:, c, 0:C2], rhs=ARq[:, c, :], start=True, stop=True),
                              reads=rk, writes=[bkey])
                    P.add("dve", lambda e: e.tensor_tensor(out=NRB[0:C2, c0:c0 + 2, 0:2 * C2], in0=bk[0:C2, 0:4 * C2].rearrange("p (c x) -> p c x", x=2 * C2),
                                                           in1=bc2(mask1, 2), op=ALU.mult),
                          reads=[bkey, "consts"], writes=[(uid, "NRBm", c) for c in cs])
                    bk2, bkey2 = nextbank()
                    for j, c in enumerate(cs):
                        P.add("pe", lambda e: e.matmul(bk2[0:C2, j * 2 * C2:(j + 1) * 2 * C2], lhsT=ARq[:, c, 0:C2], rhs=BK_pad[:, c, :], start=True, stop=True),
                              reads=rk, writes=[bkey2])
                    P.add("dve", lambda e: e.tensor_tensor(out=NTY[0:C2, c0:c0 + 2, 0:2 * C2], in0=bk2[0:C2, 0:4 * C2].rearrange("p (c x) -> p c x", x=2 * C2),
                                                           in1=bc2(mask3, 2), op=ALU.mult),
                          reads=[bkey2, "consts"], writes=[(uid, "NTYm", c) for c in cs])
                    bk3, bkey3 = nextbank()
                    for j, c in enumerate(cs):
                        P.add("pe", lambda e: e.matmul(bk3[0:C2, j * 128:(j + 1) * 128], lhsT=Bh_pad[:, c, :], rhs=ident_bf, start=True, stop=True),
                              reads=[(uid, "Bh", c), "identbf"], writes=[bkey3])
                        P.add("pe", lambda e: e.matmul(bk3[0:C2, 256 + j * 128:256 + (j + 1) * 128], lhsT=ARq[:, c, 0:C2], rhs=ident_bf, start=True, stop=True),
                              reads=[(uid, "AR", q, c), "identbf"], writes=[bkey3])
                    P.add("act", lambda e: e.activation(out=NRB[0:C2, c0:c0 + 2, 2 * C2:300], in_=bk3[0:C2, 0:256].rearrange("p (c x) -> p c x", x=128), func=AF.Copy),
                          reads=[bkey3], writes=[(uid, "NRBt", c) for c in cs])
                    P.add("act", lambda e: e.activation(out=NTY[0:C2, c0:c0 + 2, 2 * C2:300], in_=bk3[0:C2, 256:512].rearrange("p (c x) -> p c x", x=128), func=AF.Copy),
                          reads=[bkey3], writes=[(uid, "NTYt", c) for c in cs])
                    bk4, bkey4 = nextbank()
                    for j, c in enumerate(cs):
                        P.add("pe", lambda e: e.matmul(bk4[0:C2, j * 256:j * 256 + 128], lhsT=Kh_pad[:, c, :], rhs=ident_bf, start=True, stop=True),
                              reads=[(uid, "Kh", c), "identbf"], writes=[bkey4])
                        P.add("pe", lambda e: e.matmul(bk4[0:C2, j * 256 + 128:(j + 1) * 256], lhsT=V_pad[:, c, :], rhs=ident_bf, start=True, stop=True),
                              reads=[(uid, "Vp", c), "identbf"], writes=[bkey4])
                    P.add("act", lambda e: e.activation(out=RKV[0:C2, q, c0:c0 + 2, C2:342], in_=bk4[0:C2, 0:512].rearrange("p (c x) -> p c x", x=256), func=AF.Copy),
                          reads=[bkey4], writes=[(uid, "RKVt", q, c) for c in cs])
                    yield
                for c0 in range(0, NCHB, 4):
                    bk5, bkey5 = nextbank()
                    for j in range(4):
                        c = c0 + j
                        P.add("pe", lambda e: e.matmul(bk5[0:C2, j * C2:(j + 1) * C2], lhsT=BK_pad[:, c, C2:2 * C2], rhs=ARq[:, c, C2:2 * C2], start=True, stop=True),
                              reads=[(uid, "BK", c), (uid, "AR", q, c)], writes=[bkey5])
                    P.add("dve", lambda e: e.tensor_tensor(out=RKV[0:C2, q, c0:c0 + 4, 0:C2], in0=bk5[0:C2, 0:4 * C2].rearrange("p (c x) -> p c x", x=C2),
                                                           in1=bc2(mask2, 4), op=ALU.mult),
                          reads=[bkey5, "consts"], writes=[(uid, "RKVm", q, c0 + j) for j in range(4)])
                    yield
                ev_A[g].done = True
                def nk_of(lev, c):
                    if lev == 0:
                        return NRB[0:C2, c, 0:C2], NTY[0:C2, c, 0:C2], [(uid, "NRBm", c), (uid, "NTYm", c)]
                    return NNb[0:C2, lev % 2, c, 0:C2], NNb[0:C2, lev % 2, c, C2:2 * C2], [(uid, "NN", lev % 2, c)]

                for lev in range(6):
                    for c0 in range(0, NCHB, 4):
                        bk, bkey = nextbank()
                        for j in range(4):
                            c = c0 + j
                            Nk, NTk, nkey = nk_of(lev, c)
                            pkey = (uid, "PT", c)
                            rhsP = ident_bf[0:C2, 0:C2] if lev == 0 else PT[0:C2, c, :]
                            P.add("pe", lambda e: e.matmul(bk[0:C2, j * C2:(j + 1) * C2], lhsT=ident_bf[0:C2, 0:C2], rhs=rhsP, start=True, stop=False),
                                  reads=[pkey, "identbf"], writes=[bkey])
                            P.add("pe", lambda e: e.matmul(bk[0:C2, j * C2:(j + 1) * C2], lhsT=NTk, rhs=rhsP, start=False, stop=True),
                                  reads=[pkey, "identbf"] + nkey, writes=[bkey])
                        evac_copy(PT[0:C2, c0:c0 + 4, :], bk[0:C2, 0:4 * C2].rearrange("p (c x) -> p c x", x=C2), [bkey],
                                  [(uid, "PT", c0 + j) for j in range(4)])
                        yield
                    if lev < 5:
                        for c0 in range(0, NCHB, 2):
                            bk2, bkey2 = nextbank()
                            for j in range(2):
                                c = c0 + j
                                Nk, NTk, nkey = nk_of(lev, c)
                                if lev < 4:
                                    P.add("pe", lambda e: e.matmul(bk2[0:C2, j * 2 * C2:j * 2 * C2 + C2], lhsT=NTk, rhs=Nk, start=True, stop=True),
                                          reads=nkey, writes=[bkey2])
                                    P.add("pe", lambda e: e.matmul(bk2[0:C2, j * 2 * C2 + C2:(j + 1) * 2 * C2], lhsT=Nk, rhs=NTk, start=True, stop=True),
                                          reads=nkey, writes=[bkey2])
                                else:
                                    P.add("pe", lambda e: e.matmul(bk2[0:C2, j * 2 * C2 + C2:(j + 1) * 2 * C2], lhsT=Nk, rhs=NTk, start=True, stop=True),
                                          reads=nkey, writes=[bkey2])
                            nxt_ = (lev + 1) % 2
                            if lev < 4:
                                evac_copy(NNb[0:C2, nxt_, c0:c0 + 2, :], bk2[0:C2, 0:4 * C2].rearrange("p (c x) -> p c x", x=2 * C2), [bkey2],
                                          [(uid, "NN", nxt_, c0 + j) for j in range(2)])
                            else:
                                evac_copy(NNb[0:C2, nxt_, c0:c0 + 2, C2:2 * C2],
                                          bk2[0:C2, 0:4 * C2].rearrange("p (c x) -> p c x", x=2 * C2)[:, :, C2:2 * C2], [bkey2],
                                          [(uid, "NN", nxt_, c0 + j) for j in range(2)])
                            yield
                for c0 in range(0, NCHB, 2):
                    bk, bkey = nextbank()
                    for j in range(2):
                        c = c0 + j
                        P.add("pe", lambda e: e.matmul(bk[0:C2, j * 214:(j + 1) * 214], lhsT=PT[0:C2, c, :], rhs=NTY[0:C2, c, C2:300], start=True, stop=True),
                              reads=NTYk(c) + [(uid, "PT", c)], writes=[bkey])
                    evac_copy(NTY[0:C2, c0:c0 + 2, C2:300], bk[0:C2, 0:428].rearrange("p (c x) -> p c x", x=214), [bkey],
                              NTYk(c0) + NTYk(c0 + 1))
                    yield
                for c0 in range(0, NCHB, 2):
                    cs = (c0, c0 + 1)
                    bk, bkey = nextbank()
                    for j, c in enumerate(cs):
                        o_ = j * 214
                        ak = (uid, "AR", q, c)
                        P.add("pe", lambda e: e.matmul(bk[:, o_:o_ + C2], lhsT=NTY[0:C2, c, 2 * C2:300], rhs=NRB[0:C2, c, C2:2 * C2], start=True, stop=False),
                              reads=NTYk(c) + NRBk(c), writes=[bkey])
                        P.add("pe", lambda e: e.matmul(bk[:, o_:o_ + C2], lhsT=ident_bf, rhs=ARq[:, c, C2:2 * C2], start=False, stop=True),
                              reads=[ak, "identbf"], writes=[bkey])
                        P.add("pe", lambda e: e.matmul(bk[:, o_ + C2:o_ + 214], lhsT=NTY[0:C2, c, 2 * C2:300], rhs=NRB[0:C2, c, 2 * C2:300], start=True, stop=True),
                              reads=NTYk(c) + NRBk(c), writes=[bkey])
                    evac_copy(QT[:, q, c0:c0 + 2, :], bk[:, 0:428].rearrange("p (c x) -> p c x", x=214)[:, :, 0:C2], [bkey],
                              [(uid, "QT", q, c) for c in cs], wt=2)
                    for j, c in enumerate(cs):
                        o_ = j * 214
                        P.add("dve", lambda e: e.scalar_tensor_tensor(out=Mm[:, q, c, :], in0=ident_f, scalar=Pend[:, q, c:c + 1], in1=bk[:, o_ + C2:o_ + 214],
                                                                      op0=ALU.mult, op1=ALU.add),
                              reads=[bkey, "consts", (uid, "Pend", q)], writes=[(uid, "Mm", q, c)])
                    bk2, bkey2 = nextbank()
                    for j, c in enumerate(cs):
                        o_ = j * 214
                        P.add("pe", lambda e: e.matmul(bk2[0:C2, o_:o_ + 214], lhsT=NTY[0:C2, c, C2:2 * C2], rhs=NRB[0:C2, c, C2:300], start=True, stop=False),
                              reads=NTYk(c) + NRBk(c), writes=[bkey2])
                        P.add("pe", lambda e: e.matmul(bk2[0:C2, o_:o_ + 214], lhsT=ident_bf[0:C2, 0:C2], rhs=RKV[0:C2, q, c, 0:214], start=False, stop=True),
                              reads=RKVk(c, q) + ["identbf"], writes=[bkey2])
                    evac_copy(GHb[0:C2, q, c0:c0 + 2, :], bk2[0:C2, 0:428].rearrange("p (c x) -> p c x", x=214), [bkey2],
                              [(uid, "GH", q, c) for c in cs], wt=2)
                    yield
                ev_QM[g].done = True

        def state_thread():
            sb = {"n": 0}
            for g in range(NG):
                b, hp = g // 4, g % 4
                q = g % 2
                yq = b % 2
                tsl = slice(b * NB, (b + 1) * NB)
                yield from wait(ev_QM[g])
                yield from need(5)
                gv, bonus = handoff.pop(g)
                yrw = getslot()
                ybk, ybkey = banks[5], ("bank", 5)
                for c in range(NCHB):
                    gidx = (b * NCHB + c)
                    cur, nxt = gidx % 2, (gidx + 1) % 2
                    S_cur, S_nxt = S_buf[:, hp, cur, :], S_buf[:, hp, nxt, :]
                    j = c % 4
                    P.add("pe", lambda e: e.matmul(ybk[:, j * C2:(j + 1) * C2], lhsT=RKV[0:C2, q, c, 214:342], rhs=GHb[0:C2, q, c, 0:C2], start=True, stop=False),
                          reads=RKVk(c, q) + [(uid, "GH", q, c)], writes=[ybkey])
                    P.add("pe", lambda e: e.matmul(ybk[:, j * C2:(j + 1) * C2], lhsT=S_cur, rhs=QT[:, q, c, :], start=False, stop=True),
                          reads=[("S", hp, cur), (uid, "QT", q, c)], writes=[ybkey])
                    if j == 3:
                        c0 = c - 3
                        for hh in range(2):
                            ps_ = slice(hh * 64, (hh + 1) * 64)
                            src = ybk[ps_, 0:4 * C2].rearrange("p (c x) -> p c x", x=C2)[:, :, hh * CH:(hh + 1) * CH]
                            dst = yrw.ap[ps_, c0 * CH:(c0 + 4) * CH].rearrange("p (c t) -> p c t", t=CH)
                            P.add("act", lambda e, src=src, dst=dst: e.activation(out=dst, in_=src, func=AF.Copy), reads=[ybkey], writes=[yrw.key])
                    sbi = 6 + (sb["n"] % 2)
                    sb["n"] += 1
                    sbk, sbkey = banks[sbi], ("bank", sbi)
                    P.add("pe", lambda e: e.matmul(sbk[:, 0:128], lhsT=GHb[0:C2, q, c, C2:214], rhs=RKV[0:C2, q, c, 214:342], start=True, stop=False),
                          reads=RKVk(c, q) + [(uid, "GH", q, c)], writes=[sbkey])
                    P.add("pe", lambda e: e.matmul(sbk[:, 0:128], lhsT=Mm[:, q, c, :], rhs=S_cur, start=False, stop=True),
                          reads=[("S", hp, cur), (uid, "Mm", q, c)], writes=[sbkey])
                    P.add("act", lambda e: e.activation(out=S_nxt, in_=sbk[:, 0:128], func=AF.Copy), reads=[sbkey], writes=[("S", hp, nxt)])
                    yield
                ev_state[g].done = True
                yb = getslot()
                P.add("act", lambda e: e.activation(out=yb.bf[:, 0:NB], in_=yrw.ap[:, 0:NB], func=AF.Copy), reads=[yrw.key], writes=[yb.key])
                bk, bkey = nextbank()
                P.add("pe", lambda e: e.matmul(bk[:, 0:NB], lhsT=ones_bd, rhs=yb.bf[:, 0:NB], start=True, stop=True),
                      reads=[yb.key, "ones_bd"], writes=[bkey])
                P.add("dve", lambda e: e.scalar_tensor_tensor(out=yrw.ap[:, 0:NB], in0=bk[:, 0:NB], scalar=-1.0 / 64, in1=yrw.ap[:, 0:NB],
                                                              op0=ALU.mult, op1=ALU.add), reads=[bkey, yrw.key], writes=[yrw.key])
                putslot(yb)
                yield
                rs = group_rstd(yrw.ap[:, 0:NB], yrw.key, 64, LNX_EPS)
                P.add("dve", lambda e: e.tensor_tensor(out=yrw.ap[:, 0:NB], in0=yrw.ap[:, 0:NB], in1=rs.ap[:, 0:NB], op=ALU.mult),
                      reads=[yrw.key, rs.key], writes=[yrw.key])
                P.add("dve", lambda e: e.tensor_scalar(out=yrw.ap[:, 0:NB], in0=yrw.ap[:, 0:NB], scalar1=pvc(l, 59 + hp), scalar2=pvc(l, 63 + hp),
                                                       op0=ALU.mult, op1=ALU.add), reads=[yrw.key, "pv"], writes=[yrw.key])
                P.add("dve", lambda e: e.tensor_tensor(out=yrw.ap[:, 0:NB], in0=yrw.ap[:, 0:NB], in1=bonus.ap[:, 0:NB], op=ALU.add),
                      reads=[yrw.key, bonus.key], writes=[yrw.key])
                P.add("dve", lambda e: e.tensor_tensor(out=ycat[:, yq, 4 + hp, :], in0=yrw.ap[:, 0:NB], in1=gv.ap[:, 0:NB], op=ALU.mult),
                      reads=[yrw.key, gv.key], writes=[(uid, "ycat", yq, 4 + hp)])
                putslot(rs, yrw, bonus, gv)
                endphase()
                yield
                if hp == 3:
                    yield from need(10)
                    om = [getslot() for _ in range(KC)]
                    for m in range(KC):
                        ws = wo_n["n"] % NWO
                        wo_n["n"] += 1
                        P.add("pool", lambda e: e.dma_start(out=wo[:, ws], in_=mix_wout[l, m], max_dma_last_dim=4096),
                              writes=[(uid, "wo", ws)], dma_sem=f"wo{ws}")
                        bk, bkey = nextbank()
                        for k in range(KC):
                            P.add("pe", lambda e, k=k: e.matmul(bk[:, 0:NB], lhsT=wo[:, ws, k, :], rhs=ycat[:, yq, k, :], start=(k == 0), stop=(k == KC - 1)),
                                  reads=[(uid, "wo", ws), (uid, "ycat", yq, k)], writes=[bkey])
                        P.add("act", lambda e: e.activation(out=om[m].ap[:, 0:NB], in_=bk[:, 0:NB], func=AF.Copy), reads=[bkey], writes=[om[m].key])
                        P.add("act", lambda e: e.activation(out=sqo[:, m, :], in_=bk[:, 0:NB], func=AF.Square), reads=[bkey], writes=[(uid, "sqo")])
                        yield
                    bk, bkey = nextbank()
                    for c in range(KC):
                        P.add("pe", lambda e, c=c: e.matmul(bk[:, 0:NB], lhsT=ones_bf, rhs=sqo[:, c, :], start=(c == 0), stop=(c == KC - 1)),
                              reads=[(uid, "sqo"), "ones"], writes=[bkey])
                    rs = getslot()
                    rstd_from_sumsq(bk[:, 0:NB], rs.ap[:, 0:NB], D, bkey, rs.key)
                    for m in range(KC):
                        P.add("dve", lambda e, m=m: e.scalar_tensor_tensor(out=om[m].ap[:, 0:NB], in0=om[m].ap[:, 0:NB], scalar=g_ap(l, 3, m),
                                                                           in1=rs.ap[:, 0:NB], op0=ALU.mult, op1=ALU.mult),
                              reads=[om[m].key, rs.key, "gsb"], writes=[om[m].key])
                        P.add("pool", lambda e, m=m: e.tensor_tensor(out=h[:, m, tsl], in0=h[:, m, tsl], in1=om[m].ap[:, 0:NB], op=ALU.add),
                              reads=[hk(m, b), om[m].key], writes=[hk(m, b)])
                    putslot(rs, *om)
                    endphase()
                    ev_out[b].done = True
                    yield

        threads = [("front", front()), ("ls", ls_thread()), ("chunk", chunk_thread()), ("state", state_thread())]
        stall = 0
        while threads:
            progressed = False
            for tid_th in list(threads):
                tid, th = tid_th
                cur["tid"] = tid
                try:
                    r = next(th)
                    if r != "wait":
                        progressed = True
                except StopIteration:
                    threads.remove(tid_th)
                    quota.pop(tid, None)
                    progressed = True
            stall = 0 if progressed else stall + 1
            assert stall < 10, "thread scheduler deadlock"
        cur["tid"] = None
        A.release(m0)

    for (l, s) in stages:
        if s == 0:
            ffn_stage(l, 0)
        elif s == 2:
            ffn_stage(l, 1)
        else:
            mixer_stage(l)

    P.barrier()
    for c in range(KC):
        if dbg_h:
            P.add("sp", lambda e, c=c: e.dma_start(out=outT[c], in_=h[:, c, :]),
                  reads=[hk(c, b) for b in range(NBLK)], dma_sem="st")
        else:
            P.add("sp", lambda e, c=c: e.dma_start(out=outT[c], in_=h[:, c, NMETA:]),
                  reads=[hk(c, b) for b in range(NBLK)], dma_sem="st")
    P.final_dma.append(("st", P.dma_cnt["st"]))

    sem_names = sorted(P.dma_cnt.keys())
    from contextlib import ExitStack
    with ExitStack() as es:
        eng_sems = {e: es.enter_context(nc.semaphore(f"s_{e}")) for e in Prog.CE}
        dma_sems = {s: es.enter_context(nc.semaphore(f"d_{s}")) for s in sem_names}
        block = es.enter_context(nc.Block())
        P.emit(block, eng_sems, dma_sems)
    print("arena peak bytes/partition:", A.peak, " ops:", {e: len(s) for e, s in P.streams.items()})
    return nc


def _tile_w_in_ffn(w):
    Lh = w.shape[0]
    w5 = w.reshape(Lh, KC, 128, 2, JT, 128)
    return np.ascontiguousarray(w5.transpose(0, 4, 2, 1, 3, 5)).reshape(Lh, JT, 128, KC, 256)


def _tile_w_out_ffn(w):
    Lh = w.shape[0]
    w5 = w.reshape(Lh, JT, 128, KC, 128)
    return np.ascontiguousarray(w5.transpose(0, 3, 2, 1, 4))


def _make_consts():
    cst = np.zeros((128, NCONST), np.float32)
    cst[:, CO_ID:CO_ID + 128] = np.eye(128, dtype=np.float32)
    su = np.triu(np.ones((CH, CH), np.float32), 1)
    ue = np.triu(np.ones((CH, CH), np.float32), 0)
    bd = lambda a: np.block([[a, np.zeros_like(a)], [np.zeros_like(a), a]])
    cst[0:C2, CO_M1:CO_M1 + C2] = bd(su)
    cst[0:C2, CO_M1 + C2:CO_M1 + 2 * C2] = bd(ue)
    cst[0:C2, CO_M3:CO_M3 + C2] = bd(su.T)
    cst[0:C2, CO_M3 + C2:CO_M3 + 2 * C2] = bd(su.T)
    cst[0:C2, CO_M2:CO_M2 + C2] = bd(ue)
    m01 = np.ones(NB, np.float32)
    m01[0::CH] = 0.0
    cst[:, CO_01:CO_01 + NB] = m01[None, :]
    return cst


def prep_inputs(inputs):
    x = np.asarray(inputs["x"], dtype=np.float32)
    shared = {}
    shared["metaT"] = np.ascontiguousarray(np.asarray(inputs["meta_tokens"], np.float32).T).reshape(KC, 128, NMETA)
    g = np.asarray(inputs["norm_g"], np.float32).reshape(NL, 6, KC, 128)
    shared["gvec"] = np.ascontiguousarray(g.transpose(3, 0, 1, 2)).reshape(128, NL * 6 * KC)
    shared["ffn1_win"] = _tile_w_in_ffn(np.asarray(inputs["ffn1_w_in"], np.float32))
    shared["ffn2_win"] = _tile_w_in_ffn(np.asarray(inputs["ffn2_w_in"], np.float32))
    shared["ffn1_wout"] = _tile_w_out_ffn(np.asarray(inputs["ffn1_w_out"], np.float32))
    shared["ffn2_wout"] = _tile_w_out_ffn(np.asarray(inputs["ffn2_w_out"], np.float32))
    f = lambda k: np.asarray(inputs[k], np.float32)
    mw = f("mix_w_in").reshape(NL, KC, 128, MIXT, 128)
    shared["mix_win"] = np.ascontiguousarray(mw.transpose(0, 3, 2, 1, 4))
    mo = f("mix_w_out").reshape(NL, KC, 128, KC, 128)
    shared["mix_wout"] = np.ascontiguousarray(mo.transpose(0, 3, 2, 1, 4))
    pvt = np.zeros((NL, NPV, 128), np.float32)
    for l in range(NL):
        cw = f("lru_conv_w")[l].reshape(4, 2, 128)
        for i in range(2):
            for k in range(4):
                pvt[l, i * 4 + k] = cw[k, i]
        pvt[l, 8:10] = f("lru_conv_b")[l].reshape(2, 128)
        pvt[l, 10:12] = f("lru_ba")[l].reshape(2, 128)
        pvt[l, 12:14] = f("lru_bx")[l].reshape(2, 128)
        pvt[l, 14:16] = f("lru_lambda")[l].reshape(2, 128)
        pvt[l, 16:18] = f("lru_norm_g")[l].reshape(2, 128)
        sw = f("sc_conv_w")[l].reshape(3, 2, 128)
        for i in range(2):
            for k in range(3):
                pvt[l, 18 + i * 3 + k] = sw[k, i]
        pvt[l, 24:26] = f("sc_norm_g")[l].reshape(2, 128)
        pvt[l, 26:39] = f("rwkv_mu")[l].reshape(13, 128)
        for j, nm in enumerate(("rwkv_w0", "rwkv_a0", "rwkv_k_k", "rwkv_k_a", "rwkv_r_k", "rwkv_lnx_w", "rwkv_lnx_b")):
            pvt[l, 39 + 4 * j:43 + 4 * j] = f(nm)[l].reshape(4, 128)
    shared["pv"] = np.ascontiguousarray(pvt.transpose(2, 0, 1)).reshape(128, NL * NPV)
    shared["lora_w"] = np.ascontiguousarray(np.concatenate([f("rwkv_w2"), f("rwkv_a2"), f("rwkv_g2")], axis=1))
    shared["lru_w"] = np.ascontiguousarray(np.stack([f("lru_wa"), f("lru_wx")], axis=1))
    shared["consts"] = _make_consts()
    in_maps = []
    for b in range(x.shape[0]):
        m = dict(shared)
        m["xT"] = np.ascontiguousarray(x[b].T).reshape(KC, 128, T - NMETA)
        in_maps.append(m)
    return in_maps


def kernel(**inputs):
    in_maps = prep_inputs(inputs)
    nc = build_program()
    res = run_bass_kernel_spmd(nc, in_maps, core_ids=list(range(NCORES)))
    outs = [np.asarray(r["outT"]).reshape(D, T - NMETA).T for r in res.results]
    return np.ascontiguousarray(np.stack(outs, axis=0)).astype(np.float32)
```

```python
import numpy as np
import concourse.bass as bass
import concourse.mybir as mybir
from concourse.bass_utils import run_bass_kernel_spmd

F32, BF16 = mybir.dt.float32, mybir.dt.bfloat16
AF = mybir.ActivationFunctionType
ALU = mybir.AluOpType

NCORES = 8
T = 2064
NMETA = 16
NB = 344
NBLK = 6
D = 1024
KC = 8
DFF = 2816
JT = 22
NL = 2
SB = 3
MIXT = 23
RMS_EPS = 1e-6
LNX_EPS = 64e-5
NPV = 67
CH = 43
NCHB = 8
C2 = 2 * CH
CO_ID, CO_M1, CO_M3, CO_M2, CO_01 = 0, 128, 300, 472, 558
NCONST = 558 + 344
SLOTW = 348


class Op:
    __slots__ = ("eng", "fn", "deps", "is_dma", "sem", "done", "signal", "rank", "tag", "rank_pos")


class _Rec:
    def __init__(self):
        self.call = None

    def __getattr__(self, name):
        def f(*args, **kwargs):
            self.call = (name, args, kwargs)
            return self
        return f


class Prog:
    CE = ("pe", "act", "dve", "pool")

    def __init__(self, nc):
        self.nc = nc
        self.streams = {e: [] for e in ("pe", "act", "dve", "pool", "sp")}
        self.lastw = {}
        self.readers = {}
        self.dma_cnt = {}
        self.barrier_ops = []
        self.final_dma = []

    def add(self, eng, fn, reads=(), writes=(), dma_sem=None, tag=None, nodeps=False):
        op = Op()
        rec = _Rec()
        fn(rec)
        assert rec.call is not None
        op.eng, op.fn, op.is_dma, op.sem = eng, rec.call, dma_sem is not None, dma_sem
        op.signal, op.rank, op.done, op.tag = False, None, None, tag
        if op.is_dma:
            self.dma_cnt[dma_sem] = self.dma_cnt.get(dma_sem, 0) + 16
            op.done = self.dma_cnt[dma_sem]
        deps = {}

        def want(d, raw):
            if d is op:
                return
            if d.is_dma:
                if op.is_dma and d.sem == op.sem and d.sem in ("msm",):
                    return
                key = ("dma", d.sem)
                if key not in deps or deps[key].done < d.done:
                    deps[key] = d
                return
            if (not op.is_dma) and d.eng == eng:
                if eng == "pe" or (not raw and eng != "pool"):
                    return
            key = ("ce", d.eng)
            if key not in deps or deps[key].rank_pos < d.rank_pos:
                deps[key] = d

        for k in reads:
            w = self.lastw.get(k)
            if w is not None:
                want(w, True)
            if isinstance(k, tuple) and k[0] == "bank":
                rd = self.readers.get(k)
                if rd:
                    for r in rd.values():
                        if r.eng != eng:
                            want(r, True)
        for k in writes:
            w = self.lastw.get(k)
            if w is not None:
                want(w, True)
            rd = self.readers.get(k)
            if rd:
                for r in rd.values():
                    want(r, False)
        for b in self.barrier_ops:
            want(b, True)
        if nodeps:
            deps = {}
        op.deps = list(deps.values())
        op.rank_pos = len(self.streams[eng])
        for k in reads:
            rd = self.readers.setdefault(k, {})
            rd[("dma", id(op)) if op.is_dma else eng] = op
        for k in writes:
            self.lastw[k] = op
            self.readers[k] = {}
        self.streams[eng].append(op)
        return op

    def barrier(self):
        ops = []
        for e in self.CE:
            for op in reversed(self.streams[e]):
                if not op.is_dma:
                    ops.append(op)
                    break
        last_dma = {}
        for e in self.streams:
            for op in self.streams[e]:
                if op.is_dma:
                    if op.sem not in last_dma or last_dma[op.sem].done < op.done:
                        last_dma[op.sem] = op
        ops.extend(last_dma.values())
        self.barrier_ops = ops

    def emit(self, block, eng_sems, dma_sems):
        for st in self.streams.values():
            for op in st:
                for d in op.deps:
                    if not d.is_dma:
                        d.signal = True
        for e in self.CE:
            r = 0
            for op in self.streams[e]:
                if op.signal and not op.is_dma:
                    r += 1
                op.rank = r

        def run(ename, eng):
            waited = {}
            for op in self.streams[ename]:
                for d in op.deps:
                    if d.is_dma:
                        sname, val, sem = ("dma", d.sem), d.done, dma_sems[d.sem]
                    else:
                        sname, val, sem = ("ce", d.eng), d.rank, eng_sems[d.eng]
                    if waited.get(sname, 0) >= val:
                        continue
                    eng.wait_ge(sem, val)
                    waited[sname] = val
                name, args, kwargs = op.fn
                ins = getattr(eng, name)(*args, **kwargs)
                if op.is_dma:
                    ins.then_inc(dma_sems[op.sem], 16)
                elif op.signal:
                    ins.then_inc(eng_sems[ename], 1)
            if ename == "sp":
                for s, cnt in self.final_dma:
                    eng.wait_ge(dma_sems[s], cnt)

        @block.sync
        def _(e):
            run("sp", e)

        @block.tensor
        def _(e):
            run("pe", e)

        @block.scalar
        def _(e):
            run("act", e)

        @block.vector
        def _(e):
            run("dve", e)

        @block.gpsimd
        def _(e):
            run("pool", e)


class Arena:
    def __init__(self, nc, nbytes):
        self.t = nc.alloc_sbuf_tensor("arena", [128, nbytes // 4], F32)
        self.ap = self.t.ap()
        self.off = 0
        self.cap = nbytes
        self.peak = 0

    def alloc(self, shape, dtype):
        n = int(np.prod(shape))
        esz = 4 if dtype == F32 else 2
        nbytes = (n * esz + 63) // 64 * 64
        assert self.off + nbytes <= self.cap, ("arena overflow", self.off, nbytes, self.cap)
        a = self.ap[:, self.off // 4:(self.off + nbytes) // 4]
        if dtype != F32:
            a = a.bitcast(dtype)
        a = a[:, 0:n]
        if len(shape) == 2:
            a = a.rearrange("p (a b) -> p a b", a=shape[0])
        elif len(shape) == 3:
            a = a.rearrange("p (a b c) -> p a b c", a=shape[0], b=shape[1])
        self.off += nbytes
        self.peak = max(self.peak, self.off)
        return a

    def mark(self):
        return self.off

    def release(self, m):
        self.off = m


def build_program(stages=None, dbg_h=False):
    if stages is None:
        stages = [(l, s) for l in range(NL) for s in range(3)]
    nc = bass.Bass("TRN2", target_bir_lowering=False)
    dt = {}

    def din(name, shape):
        dt[name] = nc.dram_tensor(name, list(shape), F32, kind="ExternalInput").ap()
        return dt[name]

    xT = din("xT", [KC, 128, T - NMETA])
    metaT = din("metaT", [KC, 128, NMETA])
    gvec = din("gvec", [128, NL * 6 * KC])
    ffn_win = [din(f"ffn{i}_win", [NL, JT, 128, KC, 256]) for i in (1, 2)]
    ffn_wout = [din(f"ffn{i}_wout", [NL, KC, 128, JT, 128]) for i in (1, 2)]
    mix_win = din("mix_win", [NL, MIXT, 128, KC, 128])
    mix_wout = din("mix_wout", [NL, KC, 128, KC, 128])
    pv_d = din("pv", [128, NL * NPV])
    lora_d = din("lora_w", [NL, 128, 512])
    lruw_d = din("lru_w", [NL, 2, 4, 64, 64])
    consts_d = din("consts", [128, NCONST])
    if dbg_h:
        outT = nc.dram_tensor("outT", [KC, 128, T], F32, kind="ExternalOutput").ap()
    else:
        outT = nc.dram_tensor("outT", [KC, 128, T - NMETA], F32, kind="ExternalOutput").ap()

    P = Prog(nc)
    A = Arena(nc, 204 * 1024)
    banks = [nc.alloc_psum_tensor(f"bank{i}", [128, 512], F32).ap() for i in range(8)]

    h = A.alloc([KC, T], F32)
    gsb = A.alloc([NL * 6 * KC], F32)
    ghalf = A.alloc([NL * 6 * KC], F32)
    ones_bf = A.alloc([128], BF16)

    def g_ap(l, i, c, half=False):
        idx = (l * 6 + i) * KC + c
        return (ghalf if half else gsb)[:, idx:idx + 1]

    def hk(c, b):
        return ("h", c, b)

    grp = []
    for c in range(KC):
        grp.append(P.add("sp", lambda e, c=c: e.dma_start(out=h[:, c, NMETA:], in_=xT[c]),
                         writes=[hk(c, b) for b in range(NBLK)], dma_sem="ld", nodeps=True))
        grp.append(P.add("sp", lambda e, c=c: e.dma_start(out=h[:, c, 0:NMETA], in_=metaT[c]),
                         writes=[hk(c, 0)], dma_sem="ld", nodeps=True))
    grp.append(P.add("sp", lambda e: e.dma_start(out=gsb, in_=gvec), writes=["gsb"], dma_sem="ld", nodeps=True))
    P.add("pool", lambda e: e.memset(ones_bf, 1.0), writes=["ones"])
    P.add("dve", lambda e: e.tensor_scalar(out=ghalf, in0=gsb, scalar1=0.5, scalar2=None, op0=ALU.mult),
          reads=["gsb"], writes=["ghalf"])

    def rstd_from_sumsq(ps_ap, rstd_ap, n, pskey, rkey, tmpkey=None):
        P.add("act", lambda e: e.activation(out=rstd_ap, in_=ps_ap, func=AF.Ln, bias=RMS_EPS, scale=1.0 / n),
              reads=[pskey], writes=[rkey])
        P.add("act", lambda e: e.activation(out=rstd_ap, in_=rstd_ap, func=AF.Exp, scale=-0.5), reads=[rkey], writes=[rkey])

    def ffn_stage(l, which):
        gi_pre, gi_post = (0, 1) if which == 0 else (4, 5)
        win_d, wout_d = ffn_win[which], ffn_wout[which]
        P.barrier()
        m0 = A.mark()
        NT = SB * NB
        xn2 = A.alloc([1, KC, NT], BF16)
        hid = A.alloc([JT, NT], BF16)
        o = A.alloc([KC, NT], F32)
        sq = A.alloc([1, KC, NB], BF16)
        rstd = A.alloc([2, NB], F32)
        sg = A.alloc([2, NB], F32)
        tmp = A.alloc([2, NB], F32)
        NWI, NWO = 3, 2
        wi = A.alloc([NWI, KC, 256], BF16)
        wo = A.alloc([NWO, JT, 128], BF16)
        psA = [banks[0], banks[1]]
        psB = [banks[2], banks[3]]
        psO = [banks[4], banks[5]]
        psN = [banks[6], banks[7]]
        uid = ("f", l, which)
        wn = {"i": 0, "o": 0}
        cnt = {"n": 0, "a": 0, "o": 0}
        def blocks_of(sbi):
            return list(range(sbi * SB, (sbi + 1) * SB))

        def pre(sbi):
            blocks = blocks_of(sbi)
            par = 0
            xn = xn2[:, par]
            for bi, b in enumerate(blocks):
                n = cnt["n"]; cnt["n"] += 1
                s = n % 2
                tsl = slice(b * NB, (b + 1) * NB)
                P.add("act", lambda e, s=s, tsl=tsl: e.activation(out=sq[:, 0], in_=h[:, :, tsl], func=AF.Square),
                      reads=[hk(c, b) for c in range(KC)], writes=[(uid, "sq", 0)])
                for c in range(KC):
                    P.add("pe", lambda e, s=s, c=c: e.matmul(psN[s][:, 0:NB], lhsT=ones_bf, rhs=sq[:, 0, c, :],
                                                            start=(c == 0), stop=(c == KC - 1)),
                          reads=[(uid, "sq", 0), "ones"], writes=[(uid, "psN", s)])
                rstd_from_sumsq(psN[s][:, 0:NB], rstd[:, s], D, (uid, "psN", s), (uid, "rstd", s))
                for c in range(KC):
                    P.add("dve", lambda e, s=s, c=c, tsl=tsl, bi=bi: e.scalar_tensor_tensor(
                        out=xn[:, c, bi * NB:(bi + 1) * NB], in0=h[:, c, tsl], scalar=g_ap(l, gi_pre, c),
                        in1=rstd[:, s], op0=ALU.mult, op1=ALU.mult),
                        reads=[hk(c, b), (uid, "rstd", s), "gsb"], writes=[(uid, "xn", par, c, bi)])

        def inp(sbi):
            blocks = blocks_of(sbi)
            par = 0
            xn = xn2[:, par]
            for j in range(JT):
                ws = wn["i"] % NWI; wn["i"] += 1
                P.add("pool", lambda e, ws=ws, j=j: e.dma_start(out=wi[:, ws], in_=win_d[l, j], max_dma_last_dim=4096),
                      writes=[(uid, "wi", ws)], dma_sem=f"wi{ws}")
                for bi, b in enumerate(blocks):
                    n = cnt["a"]; cnt["a"] += 1
                    s = n % 2
                    bsl = slice(bi * NB, (bi + 1) * NB)
                    for half, ps in ((0, psA), (1, psB)):
                        for k in range(KC):
                            P.add("pe", lambda e, ps=ps, s=s, k=k, ws=ws, half=half, bsl=bsl: e.matmul(
                                ps[s][:, 0:NB], lhsT=wi[:, ws, k, half * 128:(half + 1) * 128], rhs=xn[:, k, bsl],
                                start=(k == 0), stop=(k == KC - 1)),
                                reads=[(uid, "wi", ws), (uid, "xn", par, k, bi)], writes=[(uid, "ps", half, s)])
                    P.add("act", lambda e, s=s: e.activation(out=sg[:, s], in_=psA[s][:, 0:NB], func=AF.Silu),
                          reads=[(uid, "ps", 0, s)], writes=[(uid, "sg", s)])
                    P.add("dve", lambda e, s=s, j=j, bsl=bsl: e.tensor_tensor(
                        out=hid[:, j, bsl], in0=sg[:, s], in1=psB[s][:, 0:NB], op=ALU.mult),
                        reads=[(uid, "sg", s), (uid, "ps", 1, s)], writes=[(uid, "hid", j, bi)])

        def outp(sbi):
            blocks = blocks_of(sbi)
            for m in range(KC):
                ws = wn["o"] % NWO; wn["o"] += 1
                P.add("pool", lambda e, ws=ws, m=m: e.dma_start(out=wo[:, ws], in_=wout_d[l, m], max_dma_last_dim=4096),
                      writes=[(uid, "wo", ws)], dma_sem=f"wo{ws}")
                for bi, b in enumerate(blocks):
                    n = cnt["o"]; cnt["o"] += 1
                    s = n % 2
                    bsl = slice(bi * NB, (bi + 1) * NB)
                    for k in range(JT):
                        P.add("pe", lambda e, s=s, k=k, ws=ws, bsl=bsl: e.matmul(
                            psO[s][:, 0:NB], lhsT=wo[:, ws, k, :], rhs=hid[:, k, bsl],
                            start=(k == 0), stop=(k == JT - 1)),
                            reads=[(uid, "wo", ws), (uid, "hid", k, bi)], writes=[(uid, "psO", s)])
                    P.add("act", lambda e, s=s, m=m, bsl=bsl: e.activation(out=o[:, m, bsl], in_=psO[s][:, 0:NB], func=AF.Copy),
                          reads=[(uid, "psO", s)], writes=[(uid, "o", m, bi)])

        def post(sbi):
            blocks = blocks_of(sbi)
            for bi, b in enumerate(blocks):
                n = cnt["n"]; cnt["n"] += 1
                s = n % 2
                bsl = slice(bi * NB, (bi + 1) * NB)
                tsl = slice(b * NB, (b + 1) * NB)
                P.add("act", lambda e, s=s, bsl=bsl: e.activation(out=sq[:, 0], in_=o[:, :, bsl], func=AF.Square),
                      reads=[(uid, "o", m, bi) for m in range(KC)], writes=[(uid, "sq", 0)])
                for c in range(KC):
                    P.add("pe", lambda e, s=s, c=c: e.matmul(psN[s][:, 0:NB], lhsT=ones_bf, rhs=sq[:, 0, c, :],
                                                            start=(c == 0), stop=(c == KC - 1)),
                          reads=[(uid, "sq", 0), "ones"], writes=[(uid, "psN", s)])
                rstd_from_sumsq(psN[s][:, 0:NB], rstd[:, s], D, (uid, "psN", s), (uid, "rstd", s))
                for m in range(KC):
                    ts_ = (n * KC + m) % 2
                    P.add("dve", lambda e, s=s, m=m, bsl=bsl, ts_=ts_: e.scalar_tensor_tensor(
                        out=tmp[:, ts_], in0=o[:, m, bsl], scalar=g_ap(l, gi_post, m, half=True),
                        in1=rstd[:, s], op0=ALU.mult, op1=ALU.mult),
                        reads=[(uid, "o", m, bi), (uid, "rstd", s), "ghalf"], writes=[(uid, "tmp", ts_)])
                    P.add("pool", lambda e, m=m, tsl=tsl, ts_=ts_: e.tensor_tensor(
                        out=h[:, m, tsl], in0=h[:, m, tsl], in1=tmp[:, ts_], op=ALU.add),
                        reads=[hk(m, b), (uid, "tmp", ts_)], writes=[hk(m, b)])
        NSB = NBLK // SB
        pre(0)
        inp(0)
        for sbi in range(NSB):
            if sbi + 1 < NSB:
                pre(sbi + 1)
            outp(sbi)
            if sbi + 1 < NSB:
                inp(sbi + 1)
            post(sbi)
        A.release(m0)

    pv = A.alloc([NL * NPV], F32)
    consts = A.alloc([NCONST], F32)
    ident_f = consts[:, CO_ID:CO_ID + 128]
    mask1 = consts[0:C2, CO_M1:CO_M1 + 2 * C2]
    mask3 = consts[0:C2, CO_M3:CO_M3 + 2 * C2]
    mask2 = consts[0:C2, CO_M2:CO_M2 + C2]
    mask01 = consts[:, CO_01:CO_01 + NB]
    ident_bf = A.alloc([128], BF16)
    ones_bd = A.alloc([128], BF16)
    lru_halo = A.alloc([2, 3], F32)
    sc_halo = A.alloc([2, 2], F32)
    rw_halo = A.alloc([13], F32)
    lru_state = A.alloc([2], F32)
    S_buf = A.alloc([4, 2, 128], BF16)
    clam = A.alloc([4], F32)
    grp2 = []
    grp2.append(P.add("sp", lambda e: e.dma_start(out=pv, in_=pv_d), writes=["pv"], dma_sem="ld", nodeps=True))
    grp2.append(P.add("sp", lambda e: e.dma_start(out=consts, in_=consts_d), writes=["consts"], dma_sem="ld", nodeps=True))
    for op in grp + grp2:
        op.done = P.dma_cnt["ld"]
    P.add("act", lambda e: e.activation(out=ident_bf, in_=ident_f, func=AF.Copy), reads=["consts"], writes=["identbf"])
    P.add("pool", lambda e: e.memset(ones_bd, 0.0), writes=["ones_bd"])
    P.add("pool", lambda e: e.memset(ones_bd[0:64, 0:64], 1.0), writes=["ones_bd"])
    P.add("pool", lambda e: e.memset(ones_bd[64:128, 64:128], 1.0), writes=["ones_bd"])

    npv = A.alloc([NL * NPV], F32)
    P.add("dve", lambda e: e.tensor_scalar(out=npv, in0=pv, scalar1=-1.0, scalar2=None, op0=ALU.mult), reads=["pv"], writes=["pv"])

    def pvc(l, j):
        return pv[:, l * NPV + j:l * NPV + j + 1]

    def npvc(l, j):
        return npv[:, l * NPV + j:l * NPV + j + 1]

    def act_sigmoid(dst_ap, src_ap, reads, wkey, nbias=None, scale=1.0):
        if nbias is None:
            P.add("act", lambda e: e.activation(out=dst_ap, in_=src_ap, func=AF.Exp, scale=-scale), reads=reads, writes=[wkey])
        else:
            P.add("act", lambda e: e.activation(out=dst_ap, in_=src_ap, func=AF.Exp, scale=-scale, bias=nbias), reads=reads + ["pv"], writes=[wkey])
        P.add("act", lambda e: e.activation(out=dst_ap, in_=dst_ap, func=AF.Ln, bias=1.0), reads=[wkey], writes=[wkey])
        P.add("act", lambda e: e.activation(out=dst_ap, in_=dst_ap, func=AF.Exp, scale=-1.0), reads=[wkey], writes=[wkey])

    class Slot:
        def __init__(self, i, ap):
            self.i, self.ap, self.key = i, ap, ("slot", i)
            self.bf = ap.bitcast(BF16)

    def mixer_stage(l):
        P.barrier()
        m0 = A.mark()
        NSLOT = 29
        slots_ap = A.alloc([NSLOT, SLOTW], F32)
        free_slots = list(range(NSLOT))
        cur = {"tid": None}
        live = {}
        quota = {}

        def getslot():
            assert free_slots, "slot pool exhausted"
            i = free_slots.pop(0)
            sl = Slot(i, slots_ap[:, i, :])
            sl.owner = cur["tid"]
            live[sl.owner] = live.get(sl.owner, 0) + 1
            return sl

        def putslot(*ss):
            for s_ in ss:
                free_slots.append(s_.i)
                if s_.owner is not None or None in live:
                    live[s_.owner] = live.get(s_.owner, 0) - 1

        def endphase():
            t = cur["tid"]
            quota[t] = live.get(t, 0)

        def disown(*ss):
            for s_ in ss:
                live[s_.owner] = live.get(s_.owner, 0) - 1
                s_.owner = "handoff"
                live["handoff"] = live.get("handoff", 0) + 1

        u = A.alloc([KC, NB], BF16)
        ycat = A.alloc([2, KC, NB], BF16)
        sqb = A.alloc([KC, NB], BF16)
        sqo = A.alloc([KC, NB], BF16)
        NWI, NWO = 3, 2
        wi = A.alloc([NWI, KC, 128], BF16)
        wo = A.alloc([NWO, KC, 128], BF16)
        lora_sb = A.alloc([512], BF16)
        gate_w = A.alloc([2, 2, 128], BF16)
        AR_pad = A.alloc([2, NCHB, 2 * C2], BF16)
        BK_pad = A.alloc([NCHB, 2 * C2], BF16)
        Bh_pad = A.alloc([NCHB, C2], BF16)
        Kh_pad = A.alloc([NCHB, C2], BF16)
        V_pad = A.alloc([NCHB, C2], BF16)
        NRB = A.alloc([NCHB, 300], BF16)
        NTY = A.alloc([NCHB, 300], BF16)
        RKV = A.alloc([2, NCHB, 342], BF16)
        NNb = A.alloc([2, NCHB, 2 * C2], BF16)
        PT = A.alloc([NCHB, C2], BF16)
        QT = A.alloc([2, NCHB, C2], BF16)
        Mm = A.alloc([2, NCHB, 128], BF16)
        GHb = A.alloc([2, NCHB, 214], BF16)
        Pend = A.alloc([2, NCHB], F32)
        uid = ("m", l)
        pbank = {"n": 0}

        def nextbank():
            b = pbank["n"] % 8
            pbank["n"] += 1
            return banks[b], ("bank", b)

        P.add("pool", lambda e: e.memset(AR_pad, 0.0), writes=[(uid, "AR", q_, c) for q_ in range(2) for c in range(NCHB)])
        for buf, nm in ((BK_pad, "BK"), (Bh_pad, "Bh"), (Kh_pad, "Kh"), (V_pad, "Vp")):
            P.add("pool", lambda e, buf=buf: e.memset(buf, 0.0), writes=[(uid, nm, c) for c in range(NCHB)])
        P.add("pool", lambda e: e.memset(S_buf, 0.0), writes=[("S", hp, i) for hp in range(4) for i in range(2)])
        P.add("pool", lambda e: e.memset(lru_halo, 0.0), writes=["lru_halo"])
        P.add("pool", lambda e: e.memset(sc_halo, 0.0), writes=["sc_halo"])
        P.add("pool", lambda e: e.memset(rw_halo, 0.0), writes=[("rw_halo", m_) for m_ in range(13)])
        P.add("pool", lambda e: e.memset(lru_state, 0.0), writes=["lru_state"])
        P.add("pool", lambda e: e.memset(gate_w, 0.0), writes=[(uid, "gate_w")])
        mg = [P.add("pool", lambda e: e.dma_start(out=lora_sb, in_=lora_d[l]), writes=[(uid, "lora")], dma_sem="msm")]
        for ax in range(2):
            for g in range(4):
                i, hh = g // 2, g % 2
                mg.append(P.add("pool", lambda e, ax=ax, g=g, i=i, hh=hh: e.dma_start(
                    out=gate_w[hh * 64:(hh + 1) * 64, ax, i, hh * 64:(hh + 1) * 64], in_=lruw_d[l, ax, g]),
                    reads=[], writes=[(uid, "gate_w")], dma_sem="msm"))
        for op_ in mg:
            op_.done = P.dma_cnt["msm"]
        st_ = getslot()
        P.add("act", lambda e: e.activation(out=st_.ap[:, 0:2], in_=pv[:, l * NPV + 14:l * NPV + 16], func=AF.Exp, scale=-1.0),
              reads=["pv"], writes=[st_.key])
        P.add("act", lambda e: e.activation(out=st_.ap[:, 2:4], in_=st_.ap[:, 0:2], func=AF.Ln, bias=1.0),
              reads=[st_.key], writes=[st_.key])
        P.add("dve", lambda e: e.tensor_scalar(out=clam[:, 0:2], in0=st_.ap[:, 2:4], scalar1=-8.0, scalar2=None, op0=ALU.mult),
              reads=[st_.key], writes=["clam"])
        P.add("dve", lambda e: e.tensor_scalar(out=clam[:, 2:4], in0=st_.ap[:, 2:4], scalar1=-16.0, scalar2=None, op0=ALU.mult),
              reads=[st_.key], writes=["clam"])
        putslot(st_)

        wi_n = {"n": 0}
        wo_n = {"n": 0}

        def inproj_tile(m, dst_ap, dst_key):
            ws = wi_n["n"] % NWI
            wi_n["n"] += 1
            P.add("pool", lambda e: e.dma_start(out=wi[:, ws], in_=mix_win[l, m], max_dma_last_dim=4096),
                  writes=[(uid, "wi", ws)], dma_sem=f"wi{ws}")
            bk, bkey = nextbank()
            for k in range(KC):
                P.add("pe", lambda e, k=k: e.matmul(bk[:, 0:NB], lhsT=wi[:, ws, k, :], rhs=u[:, k, :],
                                                   start=(k == 0), stop=(k == KC - 1)),
                      reads=[(uid, "wi", ws), (uid, "u")], writes=[bkey])
            P.add("act", lambda e: e.activation(out=dst_ap, in_=bk[:, 0:NB], func=AF.Copy), reads=[bkey], writes=[dst_key])

        def group_rstd(src_ap, src_key, n, eps):
            sq_ = getslot()
            P.add("act", lambda e: e.activation(out=sq_.bf[:, 0:NB], in_=src_ap, func=AF.Square),
                  reads=[src_key], writes=[sq_.key])
            bk, bkey = nextbank()
            P.add("pe", lambda e: e.matmul(bk[:, 0:NB], lhsT=ones_bd, rhs=sq_.bf[:, 0:NB], start=True, stop=True),
                  reads=[sq_.key, "ones_bd"], writes=[bkey])
            rs = getslot()
            P.add("act", lambda e: e.activation(out=rs.ap[:, 0:NB], in_=bk[:, 0:NB], func=AF.Ln, bias=eps, scale=1.0 / n),
                  reads=[bkey], writes=[rs.key])
            P.add("act", lambda e: e.activation(out=rs.ap[:, 0:NB], in_=rs.ap[:, 0:NB], func=AF.Exp, scale=-0.5), reads=[rs.key], writes=[rs.key])
            putslot(sq_)
            return rs

        class Ev:
            def __init__(self):
                self.done = False

        def wait(ev):
            while not ev.done:
                yield "wait"

        def need(n):
            tid = cur["tid"]
            while True:
                others = sum(max(0, quota.get(t, 0) - live.get(t, 0)) for t in quota if t != tid)
                if len(free_slots) >= n + others:
                    break
                yield "wait"
            quota[tid] = live.get(tid, 0) + n

        NG = NBLK * 4
        ev_prep = [Ev() for _ in range(NG)]
        ev_A = [Ev() for _ in range(NG)]
        ev_QM = [Ev() for _ in range(NG)]
        ev_state = [Ev() for _ in range(NG)]
        ev_out = [Ev() for _ in range(NBLK)]
        ev_u = [Ev() for _ in range(NBLK)]
        ev_ls = [Ev() for _ in range(NBLK)]
        handoff = {}
        gen_banks = [0, 1, 2, 3, 4, 7]
        gb = {"n": 0}

        def nextbank():
            b = gen_banks[gb["n"] % len(gen_banks)]
            gb["n"] += 1
            return banks[b], ("bank", b)

        def NRBk(c):
            return [(uid, "NRBm", c), (uid, "NRBt", c)]

        def NTYk(c):
            return [(uid, "NTYm", c), (uid, "NTYt", c)]

        def RKVk(c, q):
            return [(uid, "RKVm", q, c), (uid, "RKVt", q, c)]

        def inproj_tile(m, dst_ap, dst_key):
            ws = wi_n["n"] % NWI
            wi_n["n"] += 1
            P.add("pool", lambda e: e.dma_start(out=wi[:, ws], in_=mix_win[l, m], max_dma_last_dim=4096),
                  writes=[(uid, "wi", ws)], dma_sem=f"wi{ws}")
            bk, bkey = nextbank()
            for k in range(KC):
                P.add("pe", lambda e, k=k: e.matmul(bk[:, 0:NB], lhsT=wi[:, ws, k, :], rhs=u[:, k, :],
                                                   start=(k == 0), stop=(k == KC - 1)),
                      reads=[(uid, "wi", ws), (uid, "u")], writes=[bkey])
            P.add("act", lambda e: e.activation(out=dst_ap, in_=bk[:, 0:NB], func=AF.Copy), reads=[bkey], writes=[dst_key])

        def group_rstd(src_ap, src_key, n, eps):
            sq_ = getslot()
            P.add("act", lambda e: e.activation(out=sq_.bf[:, 0:NB], in_=src_ap, func=AF.Square),
                  reads=[src_key], writes=[sq_.key])
            bk, bkey = nextbank()
            P.add("pe", lambda e: e.matmul(bk[:, 0:NB], lhsT=ones_bd, rhs=sq_.bf[:, 0:NB], start=True, stop=True),
                  reads=[sq_.key, "ones_bd"], writes=[bkey])
            rs = getslot()
            P.add("act", lambda e: e.activation(out=rs.ap[:, 0:NB], in_=bk[:, 0:NB], func=AF.Ln, bias=eps, scale=1.0 / n),
                  reads=[bkey], writes=[rs.key])
            P.add("act", lambda e: e.activation(out=rs.ap[:, 0:NB], in_=rs.ap[:, 0:NB], func=AF.Exp, scale=-0.5), reads=[rs.key], writes=[rs.key])
            putslot(sq_)
            return rs

        def rw_tile(m):
            zz = getslot()
            inproj_tile(10 + m, zz.ap[:, 1:1 + NB], zz.key)
            P.add("act", lambda e: e.activation(out=zz.ap[:, 0:1], in_=rw_halo[:, m:m + 1], func=AF.Copy), reads=[("rw_halo", m)], writes=[zz.key])
            P.add("act", lambda e: e.activation(out=rw_halo[:, m:m + 1], in_=zz.ap[:, NB:NB + 1], func=AF.Copy), reads=[zz.key], writes=[("rw_halo", m)])
            dd = getslot()
            P.add("dve", lambda e: e.tensor_tensor(out=dd.ap[:, 0:NB], in0=zz.ap[:, 0:NB], in1=zz.ap[:, 1:1 + NB], op=ALU.subtract),
                  reads=[zz.key], writes=[dd.key])
            P.add("dve", lambda e: e.scalar_tensor_tensor(
                out=zz.ap[:, 1:1 + NB], in0=dd.ap[:, 0:NB], scalar=pvc(l, 26 + m), in1=zz.ap[:, 1:1 + NB], op0=ALU.mult, op1=ALU.add),
                reads=[zz.key, dd.key, "pv"], writes=[zz.key])
            putslot(dd)
            return zz

        def front():
            for b in range(NBLK):
                tsl = slice(b * NB, (b + 1) * NB)
                yq = b % 2
                if b >= 1:
                    yield from wait(ev_ls[b - 1])
                yield from need(1)
                P.add("act", lambda e: e.activation(out=sqb, in_=h[:, :, tsl], func=AF.Square),
                      reads=[hk(c, b) for c in range(KC)], writes=[(uid, "sqb")])
                bk, bkey = nextbank()
                for c in range(KC):
                    P.add("pe", lambda e, c=c: e.matmul(bk[:, 0:NB], lhsT=ones_bf, rhs=sqb[:, c, :], start=(c == 0), stop=(c == KC - 1)),
                          reads=[(uid, "sqb"), "ones"], writes=[bkey])
                rs = getslot()
                rstd_from_sumsq(bk[:, 0:NB], rs.ap[:, 0:NB], D, bkey, rs.key)
                for c in range(KC):
                    P.add("dve", lambda e, c=c: e.scalar_tensor_tensor(
                        out=u[:, c, :], in0=h[:, c, tsl], scalar=g_ap(l, 2, c), in1=rs.ap[:, 0:NB], op0=ALU.mult, op1=ALU.mult),
                        reads=[hk(c, b), rs.key, "gsb"], writes=[(uid, "u")])
                putslot(rs)
                endphase()
                yield
                ev_u[b].done = True
                yield from need(5)
                zlo = rw_tile(12)
                lob = getslot()
                ltmp = getslot()
                act_sigmoid(ltmp.ap[0:32, 0:NB], zlo.ap[0:32, 1:1 + NB], [zlo.key], ltmp.key, scale=2.0)
                P.add("dve", lambda e: e.tensor_scalar(out=lob.bf[0:32, 0:NB], in0=ltmp.ap[0:32, 0:NB], scalar1=2.0, scalar2=-1.0, op0=ALU.mult, op1=ALU.add),
                      reads=[ltmp.key], writes=[lob.key])
                P.add("act", lambda e: e.activation(out=lob.bf[32:64, 0:NB], in_=zlo.ap[32:64, 1:1 + NB], func=AF.Copy), reads=[zlo.key], writes=[lob.key])
                act_sigmoid(ltmp.ap[64:128, 0:NB], zlo.ap[64:128, 1:1 + NB], [zlo.key], ltmp.key)
                P.add("act", lambda e: e.activation(out=lob.bf[64:128, 0:NB], in_=ltmp.ap[64:128, 0:NB], func=AF.Copy), reads=[ltmp.key], writes=[lob.key])
                putslot(zlo, ltmp)
                endphase()
                yield
                for hp in range(4):
                    g = b * 4 + hp
                    q = g % 2
                    yield from need(13)
                    rr = rw_tile(hp)
                    yield
                    kx = rw_tile(4 + hp)
                    yield
                    vv = rw_tile(8 + hp)
                    yield
                    r_ap, k_ap, v_ap = rr.ap[:, 1:1 + NB], kx.ap[:, 1:1 + NB], vv.ap[:, 1:1 + NB]
                    cols = slice(hp * 128, (hp + 1) * 128)
                    logw, av, gv = getslot(), getslot(), getslot()
                    for (p0, p1, dst, fn, bcol) in ((0, 32, logw, AF.Sigmoid, 39 + hp), (32, 64, av, AF.Sigmoid, 43 + hp), (64, 128, gv, AF.Copy, None)):
                        bk, bkey = nextbank()
                        P.add("pe", lambda e, p0=p0, p1=p1, bk=bk: e.matmul(bk[:, 0:NB], lhsT=lora_sb[p0:p1, cols], rhs=lob.bf[p0:p1, 0:NB],
                                                                           start=True, stop=True),
                              reads=[(uid, "lora"), lob.key], writes=[bkey])
                        if bcol is None:
                            P.add("act", lambda e, dst=dst, bk=bk: e.activation(out=dst.ap[:, 0:NB], in_=bk[:, 0:NB], func=AF.Copy),
                                  reads=[bkey], writes=[dst.key])
                        else:
                            act_sigmoid(dst.ap[:, 0:NB], bk[:, 0:NB], [bkey], dst.key, nbias=npvc(l, bcol))
                    P.add("dve", lambda e: e.tensor_scalar(out=logw.ap[:, 0:NB], in0=logw.ap[:, 0:NB], scalar1=-0.6065306597126334, scalar2=None, op0=ALU.mult),
                          reads=[logw.key], writes=[logw.key])
                    yield
                    kk = getslot()
                    P.add("dve", lambda e: e.tensor_scalar(out=kk.ap[:, 0:NB], in0=k_ap, scalar1=pvc(l, 47 + hp), scalar2=None, op0=ALU.mult),
                          reads=[kx.key, "pv"], writes=[kk.key])
                    sq_ = getslot()
                    P.add("act", lambda e: e.activation(out=sq_.bf[:, 0:NB], in_=kk.ap[:, 0:NB], func=AF.Square),
                          reads=[kk.key], writes=[sq_.key])
                    bk, bkey = nextbank()
                    P.add("pe", lambda e, bk=bk: e.matmul(bk[:, 0:NB], lhsT=ones_bd, rhs=sq_.bf[:, 0:NB], start=True, stop=True),
                          reads=[sq_.key, "ones_bd"], writes=[bkey])
                    rn = getslot()
                    P.add("dve", lambda e, bk=bk: e.tensor_scalar(out=rn.ap[:, 0:NB], in0=bk[:, 0:NB], scalar1=1e-24, scalar2=None, op0=ALU.max),
                          reads=[bkey], writes=[rn.key])
                    P.add("act", lambda e: e.activation(out=rn.ap[:, 0:NB], in_=rn.ap[:, 0:NB], func=AF.Ln), reads=[rn.key], writes=[rn.key])
                    P.add("act", lambda e: e.activation(out=rn.ap[:, 0:NB], in_=rn.ap[:, 0:NB], func=AF.Exp, scale=-0.5), reads=[rn.key], writes=[rn.key])
                    P.add("dve", lambda e: e.tensor_tensor(out=kk.ap[:, 0:NB], in0=kk.ap[:, 0:NB], in1=rn.ap[:, 0:NB], op=ALU.mult),
                          reads=[rn.key, kk.key], writes=[kk.key])
                    putslot(sq_, rn)
                    yield
                    kp = getslot()
                    P.add("dve", lambda e: e.tensor_scalar(out=kp.ap[:, 0:NB], in0=av.ap[:, 0:NB], scalar1=-1.0, scalar2=pvc(l, 51 + hp),
                                                           op0=ALU.add, op1=ALU.mult), reads=[av.key, "pv"], writes=[kp.key])
                    P.add("dve", lambda e: e.scalar_tensor_tensor(out=kp.ap[:, 0:NB], in0=kp.ap[:, 0:NB], scalar=1.0, in1=k_ap,
                                                                  op0=ALU.add, op1=ALU.mult), reads=[kp.key, kx.key], writes=[kp.key])
                    bvec = getslot()
                    P.add("dve", lambda e: e.tensor_tensor(out=bvec.ap[:, 0:NB], in0=kk.ap[:, 0:NB], in1=av.ap[:, 0:NB], op=ALU.mult),
                          reads=[kk.key, av.key], writes=[bvec.key])
                    putslot(av)
                    rkb = getslot()
                    P.add("dve", lambda e: e.scalar_tensor_tensor(
                        out=rkb.bf[:, 0:NB], in0=r_ap, scalar=pvc(l, 55 + hp), in1=kp.ap[:, 0:NB], op0=ALU.mult, op1=ALU.mult),
                        reads=[rr.key, kp.key, "pv"], writes=[rkb.key])
                    bk, bkey = nextbank()
                    P.add("pe", lambda e, bk=bk: e.matmul(bk[:, 0:NB], lhsT=ones_bd, rhs=rkb.bf[:, 0:NB], start=True, stop=True),
                          reads=[rkb.key, "ones_bd"], writes=[bkey])
                    bonus = getslot()
                    P.add("dve", lambda e, bk=bk: e.tensor_tensor(out=bonus.ap[:, 0:NB], in0=bk[:, 0:NB], in1=v_ap, op=ALU.mult),
                          reads=[bkey, vv.key], writes=[bonus.key])
                    putslot(rkb)
                    yield
                    Ls = getslot()
                    P.add("dve", lambda e: e.tensor_tensor_scan(out=Ls.ap[:, 0:NB], data0=mask01, data1=logw.ap[:, 0:NB], initial=0.0,
                                                                op0=ALU.mult, op1=ALU.add), reads=[logw.key, "consts"], writes=[Ls.key])
                    L3 = Ls.ap[:, 0:NB].rearrange("p (c t) -> p c t", t=CH)
                    Lend = L3[:, :, CH - 1:CH]
                    if g >= 1:
                        yield from wait(ev_A[g - 1])
                    if g >= 2:
                        yield from wait(ev_QM[g - 2])

                    def padded(dst, col0, in0_ap, in1_ap, keys_in, wkeys, neg=False, eng="dve"):
                        for hh in range(2):
                            ps_ = slice(hh * 64, (hh + 1) * 64)
                            o_ap = dst[ps_, :, col0 + hh * CH:col0 + (hh + 1) * CH]
                            a0 = in0_ap[ps_].rearrange("p (c t) -> p c t", t=CH)
                            if in1_ap is None:
                                P.add(eng, lambda e: e.activation(out=o_ap, in_=a0, func=AF.Copy), reads=keys_in, writes=wkeys)
                            else:
                                a1 = in1_ap[ps_].rearrange("p (c t) -> p c t", t=CH)
                                if neg:
                                    P.add(eng, lambda e: e.scalar_tensor_tensor(out=o_ap, in0=a0, scalar=-1.0, in1=a1, op0=ALU.mult, op1=ALU.mult),
                                          reads=keys_in, writes=wkeys)
                                else:
                                    P.add(eng, lambda e: e.tensor_tensor(out=o_ap, in0=a0, in1=a1, op=ALU.mult), reads=keys_in, writes=wkeys)

                    ARq = AR_pad[:, q]
                    ARkeys = [(uid, "AR", q, c) for c in range(NCHB)]
                    E = getslot()
                    P.add("act", lambda e: e.activation(out=E.ap[:, 0:NB], in_=Ls.ap[:, 0:NB], func=AF.Exp), reads=[Ls.key], writes=[E.key])
                    padded(ARq, C2, r_ap, E.ap[:, 0:NB], [rr.key, E.key], ARkeys)
                    P.add("act", lambda e: e.activation(out=Pend[:, q, :], in_=L3[:, :, CH - 1], func=AF.Exp), reads=[Ls.key], writes=[(uid, "Pend", q)])
                    yield
                    E2 = getslot()
                    P.add("act", lambda e: e.activation(out=E2.ap[:, 0:NB], in_=Ls.ap[:, 0:NB], func=AF.Exp, scale=-1.0), reads=[Ls.key], writes=[E2.key])
                    padded(BK_pad, 0, bvec.ap[:, 0:NB], E2.ap[:, 0:NB], [bvec.key, E2.key], [(uid, "BK", c) for c in range(NCHB)])
                    padded(BK_pad, C2, kp.ap[:, 0:NB], E2.ap[:, 0:NB], [kp.key, E2.key], [(uid, "BK", c) for c in range(NCHB)], eng="dve")
                    yield
                    P.add("dve", lambda e: e.tensor_tensor(out=E.ap[:, 0:NB], in0=Ls.ap[:, 0:NB], in1=logw.ap[:, 0:NB], op=ALU.subtract),
                          reads=[Ls.key, logw.key], writes=[E.key])
                    P.add("act", lambda e: e.activation(out=E.ap[:, 0:NB], in_=E.ap[:, 0:NB], func=AF.Exp), reads=[E.key], writes=[E.key])
                    padded(ARq, 0, kk.ap[:, 0:NB], E.ap[:, 0:NB], [kk.key, E.key], ARkeys, neg=True)
                    yield
                    E3 = E2.ap[:, 0:NB].rearrange("p (c t) -> p c t", t=CH)
                    P.add("dve", lambda e: e.tensor_tensor(out=E3, in0=Lend.to_broadcast([128, NCHB, CH]), in1=L3, op=ALU.subtract),
                          reads=[Ls.key], writes=[E2.key])
                    P.add("act", lambda e: e.activation(out=E2.ap[:, 0:NB], in_=E2.ap[:, 0:NB], func=AF.Exp), reads=[E2.key], writes=[E2.key])
                    padded(Bh_pad, 0, bvec.ap[:, 0:NB], E2.ap[:, 0:NB], [bvec.key, E2.key], [(uid, "Bh", c) for c in range(NCHB)])
                    padded(Kh_pad, 0, kp.ap[:, 0:NB], E2.ap[:, 0:NB], [kp.key, E2.key], [(uid, "Kh", c) for c in range(NCHB)], eng="dve")
                    padded(V_pad, 0, v_ap, None, [vv.key], [(uid, "Vp", c) for c in range(NCHB)], eng="act")
                    putslot(E, E2, Ls, logw, kk, kp, bvec, rr, kx, vv)
                    disown(gv, bonus)
                    handoff[g] = (gv, bonus)
                    endphase()
                    ev_prep[g].done = True
                    yield
                putslot(lob)

        def ls_thread():
            for b in range(NBLK):
                tsl = slice(b * NB, (b + 1) * NB)
                yq = b % 2
                yield from wait(ev_u[b])
                if b >= 2:
                    yield from wait(ev_out[b - 2])
                for i in range(2):
                    yield from need(7)
                    xs = getslot()
                    gx = getslot()
                    inproj_tile(i, xs.ap[:, 3:3 + NB], xs.key)
                    inproj_tile(2 + i, gx.ap[:, 0:NB], gx.key)
                    yield
                    P.add("act", lambda e: e.activation(out=xs.ap[:, 0:3], in_=lru_halo[:, i, :], func=AF.Copy), reads=["lru_halo"], writes=[xs.key])
                    P.add("act", lambda e: e.activation(out=lru_halo[:, i, :], in_=xs.ap[:, NB:NB + 3], func=AF.Copy), reads=[xs.key], writes=["lru_halo"])
                    uc = getslot()
                    P.add("dve", lambda e: e.tensor_scalar(out=uc.ap[:, 0:NB], in0=xs.ap[:, 3:3 + NB], scalar1=pvc(l, i * 4 + 3), scalar2=pvc(l, 8 + i),
                                                           op0=ALU.mult, op1=ALU.add), reads=[xs.key, "pv"], writes=[uc.key])
                    for k in (2, 1, 0):
                        P.add("dve", lambda e, k=k: e.scalar_tensor_tensor(
                            out=uc.ap[:, 0:NB], in0=xs.ap[:, k:k + NB], scalar=pvc(l, i * 4 + k), in1=uc.ap[:, 0:NB],
                            op0=ALU.mult, op1=ALU.add), reads=[xs.key, uc.key, "pv"], writes=[uc.key])
                    putslot(xs)
                    yield
                    ucb = getslot()
                    P.add("act", lambda e: e.activation(out=ucb.bf[:, 0:NB], in_=uc.ap[:, 0:NB], func=AF.Copy), reads=[uc.key], writes=[ucb.key])
                    rg = getslot()
                    ig = getslot()
                    for ax, dst, bcol in ((0, rg, 10 + i), (1, ig, 12 + i)):
                        bk, bkey = nextbank()
                        P.add("pe", lambda e, ax=ax, bk=bk: e.matmul(bk[:, 0:NB], lhsT=gate_w[:, ax, i, :], rhs=ucb.bf[:, 0:NB], start=True, stop=True),
                              reads=[(uid, "gate_w"), ucb.key], writes=[bkey])
                        act_sigmoid(dst.ap[:, 0:NB], bk[:, 0:NB], [bkey], dst.key, nbias=npvc(l, bcol))
                    putslot(ucb)
                    yield
                    av = getslot()
                    a2 = getslot()
                    P.add("act", lambda e: e.activation(out=av.ap[:, 0:NB], in_=rg.ap[:, 0:NB], func=AF.Exp, scale=clam[:, i:i + 1]),
                          reads=[rg.key, "clam"], writes=[av.key])
                    P.add("act", lambda e: e.activation(out=a2.ap[:, 0:NB], in_=rg.ap[:, 0:NB], func=AF.Exp, scale=clam[:, 2 + i:3 + i]),
                          reads=[rg.key, "clam"], writes=[a2.key])
                    P.add("act", lambda e: e.activation(out=a2.ap[:, 0:NB], in_=a2.ap[:, 0:NB], func=AF.Ln, bias=1.0, scale=-1.0),
                          reads=[a2.key], writes=[a2.key])
                    P.add("act", lambda e: e.activation(out=a2.ap[:, 0:NB], in_=a2.ap[:, 0:NB], func=AF.Exp, scale=0.5),
                          reads=[a2.key], writes=[a2.key])
                    putslot(rg)
                    P.add("dve", lambda e: e.tensor_tensor(out=ig.ap[:, 0:NB], in0=ig.ap[:, 0:NB], in1=uc.ap[:, 0:NB], op=ALU.mult),
                          reads=[ig.key, uc.key], writes=[ig.key])
                    P.add("dve", lambda e: e.tensor_tensor(out=ig.ap[:, 0:NB], in0=ig.ap[:, 0:NB], in1=a2.ap[:, 0:NB], op=ALU.mult),
                          reads=[ig.key, a2.key], writes=[ig.key])
                    putslot(uc, a2)
                    hs = getslot()
                    P.add("dve", lambda e: e.tensor_tensor_scan(
                        out=hs.ap[:, 0:NB], data0=av.ap[:, 0:NB], data1=ig.ap[:, 0:NB], initial=lru_state[:, i:i + 1],
                        op0=ALU.mult, op1=ALU.add), reads=[av.key, ig.key, "lru_state"], writes=[hs.key])
                    P.add("act", lambda e: e.activation(out=lru_state[:, i:i + 1], in_=hs.ap[:, NB - 1:NB], func=AF.Copy), reads=[hs.key], writes=["lru_state"])
                    putslot(av, ig)
                    yield
                    t1 = getslot()
                    P.add("act", lambda e: e.activation(out=t1.ap[:, 0:NB], in_=gx.ap[:, 0:NB], func=AF.Square), reads=[gx.key], writes=[t1.key])
                    P.add("dve", lambda e: e.tensor_scalar(out=t1.ap[:, 0:NB], in0=t1.ap[:, 0:NB], scalar1=0.044715, scalar2=1.0,
                                                           op0=ALU.mult, op1=ALU.add), reads=[t1.key], writes=[t1.key])
                    P.add("dve", lambda e: e.tensor_tensor(out=t1.ap[:, 0:NB], in0=t1.ap[:, 0:NB], in1=gx.ap[:, 0:NB], op=ALU.mult),
                          reads=[gx.key, t1.key], writes=[t1.key])
                    act_sigmoid(t1.ap[:, 0:NB], t1.ap[:, 0:NB], [t1.key], t1.key, scale=1.5957691216)
                    P.add("dve", lambda e: e.tensor_tensor(out=t1.ap[:, 0:NB], in0=t1.ap[:, 0:NB], in1=gx.ap[:, 0:NB], op=ALU.mult),
                          reads=[gx.key, t1.key], writes=[t1.key])
                    P.add("dve", lambda e: e.tensor_tensor(out=t1.ap[:, 0:NB], in0=t1.ap[:, 0:NB], in1=hs.ap[:, 0:NB], op=ALU.mult),
                          reads=[hs.key, t1.key], writes=[t1.key])
                    putslot(gx, hs)
                    yield
                    rs = group_rstd(t1.ap[:, 0:NB], t1.key, 64, RMS_EPS)
                    P.add("dve", lambda e: e.scalar_tensor_tensor(
                        out=ycat[:, yq, i, :], in0=t1.ap[:, 0:NB], scalar=pvc(l, 16 + i), in1=rs.ap[:, 0:NB], op0=ALU.mult, op1=ALU.mult),
                        reads=[t1.key, rs.key, "pv"], writes=[(uid, "ycat", yq, i)])
                    putslot(t1, rs)
                    endphase()
                    yield
                for i in range(2):
                    yield from need(5)
                    Bs, Cs, Xs = getslot(), getslot(), getslot()
                    inproj_tile(4 + i, Bs.ap[:, 0:NB], Bs.key)
                    inproj_tile(6 + i, Cs.ap[:, 0:NB], Cs.key)
                    inproj_tile(8 + i, Xs.ap[:, 0:NB], Xs.key)
                    yield
                    vs = getslot()
                    P.add("dve", lambda e: e.tensor_tensor(out=vs.ap[:, 2:2 + NB], in0=Cs.ap[:, 0:NB], in1=Xs.ap[:, 0:NB], op=ALU.mult),
                          reads=[Cs.key, Xs.key], writes=[vs.key])
                    P.add("act", lambda e: e.activation(out=vs.ap[:, 0:2], in_=sc_halo[:, i, :], func=AF.Copy), reads=["sc_halo"], writes=[vs.key])
                    P.add("act", lambda e: e.activation(out=sc_halo[:, i, :], in_=vs.ap[:, NB:NB + 2], func=AF.Copy), reads=[vs.key], writes=["sc_halo"])
                    putslot(Cs, Xs)
                    yc = getslot()
                    P.add("dve", lambda e: e.tensor_scalar(out=yc.ap[:, 0:NB], in0=vs.ap[:, 2:2 + NB], scalar1=pvc(l, 18 + i * 3 + 2),
                                                           scalar2=None, op0=ALU.mult), reads=[vs.key, "pv"], writes=[yc.key])
                    for k in (1, 0):
                        P.add("dve", lambda e, k=k: e.scalar_tensor_tensor(
                            out=yc.ap[:, 0:NB], in0=vs.ap[:, k:k + NB], scalar=pvc(l, 18 + i * 3 + k), in1=yc.ap[:, 0:NB],
                            op0=ALU.mult, op1=ALU.add), reads=[vs.key, yc.key, "pv"], writes=[yc.key])
                    P.add("dve", lambda e: e.tensor_tensor(out=yc.ap[:, 0:NB], in0=yc.ap[:, 0:NB], in1=Bs.ap[:, 0:NB], op=ALU.mult),
                          reads=[Bs.key, yc.key], writes=[yc.key])
                    putslot(vs, Bs)
                    yield
                    rs = group_rstd(yc.ap[:, 0:NB], yc.key, 64, RMS_EPS)
                    P.add("dve", lambda e: e.scalar_tensor_tensor(
                        out=ycat[:, yq, 2 + i, :], in0=yc.ap[:, 0:NB], scalar=pvc(l, 24 + i), in1=rs.ap[:, 0:NB], op0=ALU.mult, op1=ALU.mult),
                        reads=[yc.key, rs.key, "pv"], writes=[(uid, "ycat", yq, 2 + i)])
                    putslot(yc, rs)
                    endphase()
                    yield
                ev_ls[b].done = True
                yield

        def chunk_thread():
            ev = {"n": 0}

            def evac_copy(out_ap, in_ap, reads, writes, wt=2):
                ev["n"] += 1
                if ev["n"] % wt != 0:
                    P.add("act", lambda e: e.activation(out=out_ap, in_=in_ap, func=AF.Copy), reads=reads, writes=writes)
                else:
                    P.add("dve", lambda e: e.tensor_copy(out=out_ap, in_=in_ap), reads=reads, writes=writes)

            for g in range(NG):
                q = g % 2
                yield from wait(ev_prep[g])
                if g >= 2:
                    yield from wait(ev_state[g - 2])
                ARq = AR_pad[:, q]
                def bc2(m, n):
                    return m.unsqueeze(1).to_broadcast([C2, n, m.shape[1]])

                for c0 in range(0, NCHB, 2):
                    cs = (c0, c0 + 1)
                    rk = [(uid, "BK", c) for c in cs] + [(uid, "AR", q, c) for c in cs]
                    bk, bkey = nextbank()
                    for j, c in enumerate(cs):
                        P.add("pe", lambda e: e.matmul(bk[0:C2, j * 2 * C2:(j + 1) * 2 * C2], lhsT=BK_pad[:, c, 0:C2], rhs=ARq[:, c, :], start=True, stop=True),
                              reads=rk, writes=[bkey])
                    P.add("dve", lambda e: e.tensor_tensor(out=NRB[0:C2, c0:c0 + 2, 0:2 * C2], in0=bk[0:C2, 0:4 * C2].rearrange("p (c x) -> p c x", x=2 * C2),
                                                           in1=bc2(mask1, 2), op=ALU.mult),
                          reads=[bkey, "consts"], writes=[(uid, "NRBm", c) for c in cs])
                    bk2, bkey2 = nextbank()
                    for j, c in enumerate(cs):
                        P.add("pe", lambda e: e.matmul(bk2[0:C2, j * 2 * C2:(j + 1) * 2 * C2], lhsT=ARq[:, c, 0:C2], rhs=BK_pad[:, c, :], start=True, stop=True),
                              reads=rk, writes=[bkey2])
                    P.add("dve", lambda e: e.tensor_tensor(out=NTY[0:C2, c0:c0 + 2, 0:2 * C2], in0=bk2[0:C2, 0:4 * C2].rearrange("p (c x) -> p c x", x=2 * C2),
                                                           in1=bc2(mask3, 2), op=ALU.mult),
                          reads=[bkey2, "consts"], writes=[(uid, "NTYm", c) for c in cs])
                    bk3, bkey3 = nextbank()
                    for j, c in enumerate(cs):
                        P.add("pe", lambda e: e.matmul(bk3[0:C2, j * 128:(j + 1) * 128], lhsT=Bh_pad[:, c, :], rhs=ident_bf, start=True, stop=True),
                              reads=[(uid, "Bh", c), "identbf"], writes=[bkey3])
                        P.add("pe", lambda e: e.matmul(bk3[0:C2, 256 + j * 128:256 + (j + 1) * 128], lhsT=ARq[:, c, 0:C2], rhs=ident_bf, start=True, stop=True),
                              reads=[(uid, "AR", q, c), "identbf"], writes=[bkey3])
                    P.add("act", lambda e: e.activation(out=NRB[0:C2, c0:c0 + 2, 2 * C2:300], in_=bk3[0:C2, 0:256].rearrange("p (c x) -> p c x", x=128), func=AF.Copy),
                          reads=[bkey3], writes=[(uid, "NRBt", c) for c in cs])
                    P.add("act", lambda e: e.activation(out=NTY[0:C2, c0:c0 + 2, 2 * C2:300], in_=bk3[0:C2, 256:512].rearrange("p (c x) -> p c x", x=128), func=AF.Copy),
                          reads=[bkey3], writes=[(uid, "NTYt", c) for c in cs])
                    bk4, bkey4 = nextbank()
                    for j, c in enumerate(cs):
                        P.add("pe", lambda e: e.matmul(bk4[0:C2, j * 256:j * 256 + 128], lhsT=Kh_pad[:, c, :], rhs=ident_bf, start=True, stop=True),
                              reads=[(uid, "Kh", c), "identbf"], writes=[bkey4])
                        P.add("pe", lambda e: e.matmul(bk4[0:C2, j * 256 + 128:(j + 1) * 256], lhsT=V_pad[:, c, :], rhs=ident_bf, start=True, stop=True),
                              reads=[(uid, "Vp", c), "identbf"], writes=[bkey4])
                    P.add("act", lambda e: e.activation(out=RKV[0:C2, q, c0:c0 + 2, C2:342], in_=bk4[0:C2, 0:512].rearrange("p (c x) -> p c x", x=256), func=AF.Copy),
                          reads=[bkey4], writes=[(uid, "RKVt", q, c) for c in cs])
                    yield
                for c0 in range(0, NCHB, 4):
                    bk5, bkey5 = nextbank()
                    for j in range(4):
                        c = c0 + j
                        P.add("pe", lambda e: e.matmul(bk5[0:C2, j * C2:(j + 1) * C2], lhsT=BK_pad[:, c, C2:2 * C2], rhs=ARq[:, c, C2:2 * C2], start=True, stop=True),
                              reads=[(uid, "BK", c), (uid, "AR", q, c)], writes=[bkey5])
                    P.add("dve", lambda e: e.tensor_tensor(out=RKV[0:C2, q, c0:c0 + 4, 0:C2], in0=bk5[0:C2, 0:4 * C2].rearrange("p (c x) -> p c x", x=C2),
                                                           in1=bc2(mask2, 4), op=ALU.mult),
                          reads=[bkey5, "consts"], writes=[(uid, "RKVm", q, c0 + j) for j in range(4)])
                    yield
                ev_A[g].done = True
                def nk_of(lev, c):
                    if lev == 0:
                        return NRB[0:C2, c, 0:C2], NTY[0:C2, c, 0:C2], [(uid, "NRBm", c), (uid, "NTYm", c)]
                    return NNb[0:C2, lev % 2, c, 0:C2], NNb[0:C2, lev % 2, c, C2:2 * C2], [(uid, "NN", lev % 2, c)]

                for lev in range(6):
                    for c0 in range(0, NCHB, 4):
                        bk, bkey = nextbank()
                        for j in range(4):
                            c = c0 + j
                            Nk, NTk, nkey = nk_of(lev, c)
                            pkey = (uid, "PT", c)
                            rhsP = ident_bf[0:C2, 0:C2] if lev == 0 else PT[0:C2, c, :]
                            P.add("pe", lambda e: e.matmul(bk[0:C2, j * C2:(j + 1) * C2], lhsT=ident_bf[0:C2, 0:C2], rhs=rhsP, start=True, stop=False),
                                  reads=[pkey, "identbf"], writes=[bkey])
                            P.add("pe", lambda e: e.matmul(bk[0:C2, j * C2:(j + 1) * C2], lhsT=NTk, rhs=rhsP, start=False, stop=True),
                                  reads=[pkey, "identbf"] + nkey, writes=[bkey])
                        evac_copy(PT[0:C2, c0:c0 + 4, :], bk[0:C2, 0:4 * C2].rearrange("p (c x) -> p c x", x=C2), [bkey],
                                  [(uid, "PT", c0 + j) for j in range(4)])
                        yield
                    if lev < 5:
                        for c0 in range(0, NCHB, 2):
                            bk2, bkey2 = nextbank()
                            for j in range(2):
                                c = c0 + j
                                Nk, NTk, nkey = nk_of(lev, c)
                                if lev < 4:
                                    P.add("pe", lambda e: e.matmul(bk2[0:C2, j * 2 * C2:j * 2 * C2 + C2], lhsT=NTk, rhs=Nk, start=True, stop=True),
                                          reads=nkey, writes=[bkey2])
                                    P.add("pe", lambda e: e.matmul(bk2[0:C2, j * 2 * C2 + C2:(j + 1) * 2 * C2], lhsT=Nk, rhs=NTk, start=True, stop=True),
                                          reads=nkey, writes=[bkey2])
                                else:
                                    P.add("pe", lambda e: e.matmul(bk2[0:C2, j * 2 * C2 + C2:(j + 1) * 2 * C2], lhsT=Nk, rhs=NTk, start=True, stop=True),
                                          reads=nkey, writes=[bkey2])
                            nxt_ = (lev + 1) % 2
                            if lev < 4:
                                evac_copy(NNb[0:C2, nxt_, c0:c0 + 2, :], bk2[0:C2, 0:4 * C2].rearrange("p (c x) -> p c x", x=2 * C2), [bkey2],
                                          [(uid, "NN", nxt_, c0 + j) for j in range(2)])
                            else:
                                evac_copy(NNb[0:C2, nxt_, c0:c0 + 2, C2:2 * C2],
                                          bk2[0:C2, 0:4 * C2].rearrange("p (c x) -> p c x", x=2 * C2)[:, :, C2:2 * C2], [bkey2],
                                          [(uid, "NN", nxt_, c0 + j) for j in range(2)])
                            yield
                for c0 in range(0, NCHB, 2):
                    bk, bkey = nextbank()
                    for j in range(2):
                        c = c0 + j
                        P.add("pe", lambda e: e.matmul(bk[0:C2, j * 214:(j + 1) * 214], lhsT=PT[0:C2, c, :], rhs=NTY[0:C2, c, C2:300], start=True, stop=True),
                              reads=NTYk(c) + [(uid, "PT", c)], writes=[bkey])
                    evac_copy(NTY[0:C2, c0:c0 + 2, C2:300], bk[0:C2, 0:428].rearrange("p (c x) -> p c x", x=214), [bkey],
                              NTYk(c0) + NTYk(c0 + 1))
                    yield
                for c in range(NCHB):
                    ak = (uid, "AR", q, c)
                    bk, bkey = nextbank()
                    P.add("pe", lambda e: e.matmul(bk[:, 0:C2], lhsT=NTY[0:C2, c, 2 * C2:300], rhs=NRB[0:C2, c, C2:2 * C2], start=True, stop=False),
                          reads=NTYk(c) + NRBk(c), writes=[bkey])
                    P.add("pe", lambda e: e.matmul(bk[:, 0:C2], lhsT=ident_bf, rhs=ARq[:, c, C2:2 * C2], start=False, stop=True),
                          reads=[ak, "identbf"], writes=[bkey])
                    P.add("pe", lambda e: e.matmul(bk[:, C2:214], lhsT=NTY[0:C2, c, 2 * C2:300], rhs=NRB[0:C2, c, 2 * C2:300], start=True, stop=True),
                          reads=NTYk(c) + NRBk(c), writes=[bkey])
                    evac_copy(QT[:, q, c, :], bk[:, 0:C2], [bkey], [(uid, "QT", q, c)], wt=2)
                    P.add("dve", lambda e: e.scalar_tensor_tensor(out=Mm[:, q, c, :], in0=ident_f, scalar=Pend[:, q, c:c + 1], in1=bk[:, C2:214],
                                                                  op0=ALU.mult, op1=ALU.add),
                          reads=[bkey, "consts", (uid, "Pend", q)], writes=[(uid, "Mm", q, c)])
                    bk2, bkey2 = nextbank()
                    P.add("pe", lambda e: e.matmul(bk2[0:C2, 0:214], lhsT=NTY[0:C2, c, C2:2 * C2], rhs=NRB[0:C2, c, C2:300], start=True, stop=False),
                          reads=NTYk(c) + NRBk(c), writes=[bkey2])
                    P.add("pe", lambda e: e.matmul(bk2[0:C2, 0:214], lhsT=ident_bf[0:C2, 0:C2], rhs=RKV[0:C2, q, c, 0:214], start=False, stop=True),
                          reads=RKVk(c, q) + ["identbf"], writes=[bkey2])
                    evac_copy(GHb[0:C2, q, c, :], bk2[0:C2, 0:214], [bkey2], [(uid, "GH", q, c)], wt=2)
                    yield
                ev_QM[g].done = True

        def state_thread():
            sb = {"n": 0}
            for g in range(NG):
                b, hp = g // 4, g % 4
                q = g % 2
                yq = b % 2
                tsl = slice(b * NB, (b + 1) * NB)
                yield from wait(ev_QM[g])
                yield from need(5)
                gv, bonus = handoff.pop(g)
                yrw = getslot()
                ybk, ybkey = banks[5], ("bank", 5)
                for c in range(NCHB):
                    gidx = (b * NCHB + c)
                    cur, nxt = gidx % 2, (gidx + 1) % 2
                    S_cur, S_nxt = S_buf[:, hp, cur, :], S_buf[:, hp, nxt, :]
                    j = c % 4
                    P.add("pe", lambda e: e.matmul(ybk[:, j * C2:(j + 1) * C2], lhsT=RKV[0:C2, q, c, 214:342], rhs=GHb[0:C2, q, c, 0:C2], start=True, stop=False),
                          reads=RKVk(c, q) + [(uid, "GH", q, c)], writes=[ybkey])
                    P.add("pe", lambda e: e.matmul(ybk[:, j * C2:(j + 1) * C2], lhsT=S_cur, rhs=QT[:, q, c, :], start=False, stop=True),
                          reads=[("S", hp, cur), (uid, "QT", q, c)], writes=[ybkey])
                    if j == 3:
                        c0 = c - 3
                        for hh in range(2):
                            ps_ = slice(hh * 64, (hh + 1) * 64)
                            src = ybk[ps_, 0:4 * C2].rearrange("p (c x) -> p c x", x=C2)[:, :, hh * CH:(hh + 1) * CH]
                            dst = yrw.ap[ps_, c0 * CH:(c0 + 4) * CH].rearrange("p (c t) -> p c t", t=CH)
                            P.add("act", lambda e, src=src, dst=dst: e.activation(out=dst, in_=src, func=AF.Copy), reads=[ybkey], writes=[yrw.key])
                    sbi = 6
                    sb["n"] += 1
                    sbk, sbkey = banks[sbi], ("bank", sbi)
                    P.add("pe", lambda e: e.matmul(sbk[:, 0:128], lhsT=GHb[0:C2, q, c, C2:214], rhs=RKV[0:C2, q, c, 214:342], start=True, stop=False),
                          reads=RKVk(c, q) + [(uid, "GH", q, c)], writes=[sbkey])
                    P.add("pe", lambda e: e.matmul(sbk[:, 0:128], lhsT=Mm[:, q, c, :], rhs=S_cur, start=False, stop=True),
                          reads=[("S", hp, cur), (uid, "Mm", q, c)], writes=[sbkey])
                    P.add("act", lambda e: e.activation(out=S_nxt, in_=sbk[:, 0:128], func=AF.Copy), reads=[sbkey], writes=[("S", hp, nxt)])
                    yield
                ev_state[g].done = True
                yb = getslot()
                P.add("act", lambda e: e.activation(out=yb.bf[:, 0:NB], in_=yrw.ap[:, 0:NB], func=AF.Copy), reads=[yrw.key], writes=[yb.key])
                bk, bkey = nextbank()
                P.add("pe", lambda e: e.matmul(bk[:, 0:NB], lhsT=ones_bd, rhs=yb.bf[:, 0:NB], start=True, stop=True),
                      reads=[yb.key, "ones_bd"], writes=[bkey])
                P.add("dve", lambda e: e.scalar_tensor_tensor(out=yrw.ap[:, 0:NB], in0=bk[:, 0:NB], scalar=-1.0 / 64, in1=yrw.ap[:, 0:NB],
                                                              op0=ALU.mult, op1=ALU.add), reads=[bkey, yrw.key], writes=[yrw.key])
                putslot(yb)
                yield
                rs = group_rstd(yrw.ap[:, 0:NB], yrw.key, 64, LNX_EPS)
                P.add("dve", lambda e: e.tensor_tensor(out=yrw.ap[:, 0:NB], in0=yrw.ap[:, 0:NB], in1=rs.ap[:, 0:NB], op=ALU.mult),
                      reads=[yrw.key, rs.key], writes=[yrw.key])
                P.add("dve", lambda e: e.tensor_scalar(out=yrw.ap[:, 0:NB], in0=yrw.ap[:, 0:NB], scalar1=pvc(l, 59 + hp), scalar2=pvc(l, 63 + hp),
                                                       op0=ALU.mult, op1=ALU.add), reads=[yrw.key, "pv"], writes=[yrw.key])
                P.add("dve", lambda e: e.tensor_tensor(out=yrw.ap[:, 0:NB], in0=yrw.ap[:, 0:NB], in1=bonus.ap[:, 0:NB], op=ALU.add),
                      reads=[yrw.key, bonus.key], writes=[yrw.key])
                P.add("dve", lambda e: e.tensor_tensor(out=ycat[:, yq, 4 + hp, :], in0=yrw.ap[:, 0:NB], in1=gv.ap[:, 0:NB], op=ALU.mult),
                      reads=[yrw.key, gv.key], writes=[(uid, "ycat", yq, 4 + hp)])
                putslot(rs, yrw, bonus, gv)
                endphase()
                yield
                if hp == 3:
                    yield from need(10)
                    om = [getslot() for _ in range(KC)]
                    for m in range(KC):
                        ws = wo_n["n"] % NWO
                        wo_n["n"] += 1
                        P.add("pool", lambda e: e.dma_start(out=wo[:, ws], in_=mix_wout[l, m], max_dma_last_dim=4096),
                              writes=[(uid, "wo", ws)], dma_sem=f"wo{ws}")
                        bk, bkey = nextbank()
                        for k in range(KC):
                            P.add("pe", lambda e, k=k: e.matmul(bk[:, 0:NB], lhsT=wo[:, ws, k, :], rhs=ycat[:, yq, k, :], start=(k == 0), stop=(k == KC - 1)),
                                  reads=[(uid, "wo", ws), (uid, "ycat", yq, k)], writes=[bkey])
                        P.add("act", lambda e: e.activation(out=om[m].ap[:, 0:NB], in_=bk[:, 0:NB], func=AF.Copy), reads=[bkey], writes=[om[m].key])
                        P.add("act", lambda e: e.activation(out=sqo[:, m, :], in_=bk[:, 0:NB], func=AF.Square), reads=[bkey], writes=[(uid, "sqo")])
                        yield
                    bk, bkey = nextbank()
                    for c in range(KC):
                        P.add("pe", lambda e, c=c: e.matmul(bk[:, 0:NB], lhsT=ones_bf, rhs=sqo[:, c, :], start=(c == 0), stop=(c == KC - 1)),
                              reads=[(uid, "sqo"), "ones"], writes=[bkey])
                    rs = getslot()
                    rstd_from_sumsq(bk[:, 0:NB], rs.ap[:, 0:NB], D, bkey, rs.key)
                    for m in range(KC):
                        P.add("dve", lambda e, m=m: e.scalar_tensor_tensor(out=om[m].ap[:, 0:NB], in0=om[m].ap[:, 0:NB], scalar=g_ap(l, 3, m),
                                                                           in1=rs.ap[:, 0:NB], op0=ALU.mult, op1=ALU.mult),
                              reads=[om[m].key, rs.key, "gsb"], writes=[om[m].key])
                        P.add("pool", lambda e, m=m: e.tensor_tensor(out=h[:, m, tsl], in0=h[:, m, tsl], in1=om[m].ap[:, 0:NB], op=ALU.add),
                              reads=[hk(m, b), om[m].key], writes=[hk(m, b)])
                    putslot(rs, *om)
                    endphase()
                    ev_out[b].done = True
                    yield

        threads = [("front", front()), ("ls", ls_thread()), ("chunk", chunk_thread()), ("state", state_thread())]
        stall = 0
        while threads:
            progressed = False
            for tid_th in list(threads):
                tid, th = tid_th
                cur["tid"] = tid
                try:
                    r = next(th)
                    if r != "wait":
                        progressed = True
                except StopIteration:
                    threads.remove(tid_th)
                    quota.pop(tid, None)
                    progressed = True
            stall = 0 if progressed else stall + 1
            assert stall < 10, "thread scheduler deadlock"
        cur["tid"] = None
        A.release(m0)

    for (l, s) in stages:
        if s == 0:
            ffn_stage(l, 0)
        elif s == 2:
            ffn_stage(l, 1)
        else:
            mixer_stage(l)

    P.barrier()
    for c in range(KC):
        if dbg_h:
            P.add("sp", lambda e, c=c: e.dma_start(out=outT[c], in_=h[:, c, :]),
                  reads=[hk(c, b) for b in range(NBLK)], dma_sem="st")
        else:
            P.add("sp", lambda e, c=c: e.dma_start(out=outT[c], in_=h[:, c, NMETA:]),
                  reads=[hk(c, b) for b in range(NBLK)], dma_sem="st")
    P.final_dma.append(("st", P.dma_cnt["st"]))

    sem_names = sorted(P.dma_cnt.keys())
    from contextlib import ExitStack
    with ExitStack() as es:
        eng_sems = {e: es.enter_context(nc.semaphore(f"s_{e}")) for e in Prog.CE}
        dma_sems = {s: es.enter_context(nc.semaphore(f"d_{s}")) for s in sem_names}
        block = es.enter_context(nc.Block())
        P.emit(block, eng_sems, dma_sems)
    print("arena peak bytes/partition:", A.peak, " ops:", {e: len(s) for e, s in P.streams.items()})
    return nc


def _tile_w_in_ffn(w):
    Lh = w.shape[0]
    w5 = w.reshape(Lh, KC, 128, 2, JT, 128)
    return np.ascontiguousarray(w5.transpose(0, 4, 2, 1, 3, 5)).reshape(Lh, JT, 128, KC, 256)


def _tile_w_out_ffn(w):
    Lh = w.shape[0]
    w5 = w.reshape(Lh, JT, 128, KC, 128)
    return np.ascontiguousarray(w5.transpose(0, 3, 2, 1, 4))


def _make_consts():
    cst = np.zeros((128, NCONST), np.float32)
    cst[:, CO_ID:CO_ID + 128] = np.eye(128, dtype=np.float32)
    su = np.triu(np.ones((CH, CH), np.float32), 1)
    ue = np.triu(np.ones((CH, CH), np.float32), 0)
    bd = lambda a: np.block([[a, np.zeros_like(a)], [np.zeros_like(a), a]])
    cst[0:C2, CO_M1:CO_M1 + C2] = bd(su)
    cst[0:C2, CO_M1 + C2:CO_M1 + 2 * C2] = bd(ue)
    cst[0:C2, CO_M3:CO_M3 + C2] = bd(su.T)
    cst[0:C2, CO_M3 + C2:CO_M3 + 2 * C2] = bd(su.T)
    cst[0:C2, CO_M2:CO_M2 + C2] = bd(ue)
    m01 = np.ones(NB, np.float32)
    m01[0::CH] = 0.0
    cst[:, CO_01:CO_01 + NB] = m01[None, :]
    return cst


def prep_inputs(inputs):
    x = np.asarray(inputs["x"], dtype=np.float32)
    shared = {}
    shared["metaT"] = np.ascontiguousarray(np.asarray(inputs["meta_tokens"], np.float32).T).reshape(KC, 128, NMETA)
    g = np.asarray(inputs["norm_g"], np.float32).reshape(NL, 6, KC, 128)
    shared["gvec"] = np.ascontiguousarray(g.transpose(3, 0, 1, 2)).reshape(128, NL * 6 * KC)
    shared["ffn1_win"] = _tile_w_in_ffn(np.asarray(inputs["ffn1_w_in"], np.float32))
    shared["ffn2_win"] = _tile_w_in_ffn(np.asarray(inputs["ffn2_w_in"], np.float32))
    shared["ffn1_wout"] = _tile_w_out_ffn(np.asarray(inputs["ffn1_w_out"], np.float32))
    shared["ffn2_wout"] = _tile_w_out_ffn(np.asarray(inputs["ffn2_w_out"], np.float32))
    f = lambda k: np.asarray(inputs[k], np.float32)
    mw = f("mix_w_in").reshape(NL, KC, 128, MIXT, 128)
    shared["mix_win"] = np.ascontiguousarray(mw.transpose(0, 3, 2, 1, 4))
    mo = f("mix_w_out").reshape(NL, KC, 128, KC, 128)
    shared["mix_wout"] = np.ascontiguousarray(mo.transpose(0, 3, 2, 1, 4))
    pvt = np.zeros((NL, NPV, 128), np.float32)
    for l in range(NL):
        cw = f("lru_conv_w")[l].reshape(4, 2, 128)
        for i in range(2):
            for k in range(4):
                pvt[l, i * 4 + k] = cw[k, i]
        pvt[l, 8:10] = f("lru_conv_b")[l].reshape(2, 128)
        pvt[l, 10:12] = f("lru_ba")[l].reshape(2, 128)
        pvt[l, 12:14] = f("lru_bx")[l].reshape(2, 128)
        pvt[l, 14:16] = f("lru_lambda")[l].reshape(2, 128)
        pvt[l, 16:18] = f("lru_norm_g")[l].reshape(2, 128)
        sw = f("sc_conv_w")[l].reshape(3, 2, 128)
        for i in range(2):
            for k in range(3):
                pvt[l, 18 + i * 3 + k] = sw[k, i]
        pvt[l, 24:26] = f("sc_norm_g")[l].reshape(2, 128)
        pvt[l, 26:39] = f("rwkv_mu")[l].reshape(13, 128)
        for j, nm in enumerate(("rwkv_w0", "rwkv_a0", "rwkv_k_k", "rwkv_k_a", "rwkv_r_k", "rwkv_lnx_w", "rwkv_lnx_b")):
            pvt[l, 39 + 4 * j:43 + 4 * j] = f(nm)[l].reshape(4, 128)
    shared["pv"] = np.ascontiguousarray(pvt.transpose(2, 0, 1)).reshape(128, NL * NPV)
    shared["lora_w"] = np.ascontiguousarray(np.concatenate([f("rwkv_w2"), f("rwkv_a2"), f("rwkv_g2")], axis=1))
    shared["lru_w"] = np.ascontiguousarray(np.stack([f("lru_wa"), f("lru_wx")], axis=1))
    shared["consts"] = _make_consts()
    in_maps = []
    for b in range(x.shape[0]):
        m = dict(shared)
        m["xT"] = np.ascontiguousarray(x[b].T).reshape(KC, 128, T - NMETA)
        in_maps.append(m)
    return in_maps


def kernel(**inputs):
    in_maps = prep_inputs(inputs)
    nc = build_program()
    res = run_bass_kernel_spmd(nc, in_maps, core_ids=list(range(NCORES)))
    outs = [np.asarray(r["outT"]).reshape(D, T - NMETA).T for r in res.results]
    return np.ascontiguousarray(np.stack(outs, axis=0)).astype(np.float32)
```
